# Optimizing a Trainium2 kernel written in Bass

```python
import math
import jax
import jax.numpy as jnp
from jax import lax
import numpy as np

D_MODEL = 2048
BATCH = 4
SEQ = 2048
DEPTH = 4
DEC_BATCH = 8
DEC_SEQ = 1
PAST_LEN = 16384
PAGE_SIZE = 128

N_MIXERS = 3
LAYER_MIXER = tuple(i % N_MIXERS for i in range(DEPTH))
N_NSA = LAYER_MIXER.count(0)
N_CONV = LAYER_MIXER.count(1)
N_SSM = LAYER_MIXER.count(2)

N_HEADS = 16
HEAD_DIM = D_MODEL // N_HEADS
KV_GROUPS = 4
GROUP_SIZE = N_HEADS // KV_GROUPS
N_BRANCH = 3
CMP_BLOCK = 32
CMP_STRIDE = 16
CMP_RATIO = CMP_BLOCK // CMP_STRIDE
SEL_BLOCK = 64
SEL_RATIO = SEL_BLOCK // CMP_STRIDE
N_SELECT = 16
WINDOW = 512
FORCE_BONUS = 1.0e4
SEL_Q_BLOCK = 16
WIN_Q_BLOCK = 128

N_BUCKETS = 32
MAX_EXACT = 16
REL_MAX_DIST = 128

CONV_WIDTH = 31

SSM_GROUP_CH = 16
SSM_GROUPS = D_MODEL // SSM_GROUP_CH
SSM_STATE = 64
DT_MIN = 0.001
DT_MAX = 0.1

D_FF = 5632
FFN_CONV_WIDTH = 3

RMS_EPS = 1e-6
LN_EPS = 1e-5

kernel_name = "nsa_conformer_s5_hybrid_step"


def rms_norm(x, g):
    xf = x.astype(jnp.float32)
    y = xf * lax.rsqrt(jnp.mean(xf * xf, axis=-1, keepdims=True) + RMS_EPS)
    return (y * g.astype(jnp.float32)).astype(x.dtype)


def layer_norm(x, g, b):
    xf = x.astype(jnp.float32)
    mu = jnp.mean(xf, axis=-1, keepdims=True)
    var = jnp.mean(jnp.square(xf - mu), axis=-1, keepdims=True)
    y = (xf - mu) * lax.rsqrt(var + LN_EPS) * g.astype(jnp.float32) + b.astype(jnp.float32)
    return y.astype(x.dtype)


def t5_bucket(rel):
    n = jnp.maximum(rel, 0)
    nf = jnp.maximum(n, MAX_EXACT).astype(jnp.float32)
    big = MAX_EXACT + (jnp.log(nf / MAX_EXACT) / math.log(REL_MAX_DIST / MAX_EXACT)
                       * (N_BUCKETS - MAX_EXACT)).astype(jnp.int32)
    return jnp.where(n < MAX_EXACT, n, jnp.minimum(big, N_BUCKETS - 1))


def masked_softmax(logits, mask):
    logits = jnp.where(mask, logits.astype(jnp.float32), -1e30)
    return jnp.where(mask, jax.nn.softmax(logits, axis=-1), 0.0)


def attend_shared(q, k, v, qpos, kpos, mask, rel_bias):
    nq, nk = q.shape[1], k.shape[1]
    bias = rel_bias[t5_bucket(qpos[:, None] - kpos[None, :])]
    bias = bias.reshape(nq, nk, KV_GROUPS, GROUP_SIZE).transpose(2, 3, 0, 1)
    logits = (jnp.einsum('bqgrd,bngd->bgrqn', q, k).astype(jnp.float32) * HEAD_DIM ** -0.5
              + bias.astype(jnp.float32))
    p = masked_softmax(logits, mask)
    o = jnp.einsum('bgrqn,bngd->bqgrd', p.astype(v.dtype), v)
    return o, p


def attend_gathered(q, kv, qpos, kpos, ok, rel_bias):
    g_idx = jnp.arange(KV_GROUPS)[None, :, None, None]
    bias = rel_bias.reshape(N_BUCKETS, KV_GROUPS, GROUP_SIZE)[t5_bucket(qpos[None, None, :, None] - kpos), g_idx]
    bias = jnp.moveaxis(bias, -1, 2)
    logits = (jnp.einsum('bqgrd,bgqnd->bgrqn', q, kv[..., 0, :]).astype(jnp.float32) * HEAD_DIM ** -0.5
              + bias.astype(jnp.float32))
    mask = (ok & (kpos <= qpos[None, None, :, None]))[:, :, None]
    p = masked_softmax(logits, mask)
    return jnp.einsum('bgrqn,bgqnd->bqgrd', p.astype(kv.dtype), kv[..., 1, :])


def window_mask(qpos, kpos):
    k, qq = kpos[None, :], qpos[:, None]
    return (k >= 0) & (k <= qq) & (k >= qq - WINDOW)


def compress_kv(kv, pe, w1, w2):
    b, l = kv.shape[:2]
    n_chunk = l // CMP_STRIDE
    n_blk = n_chunk - CMP_RATIO + 1
    ch = kv.reshape(b, n_chunk, CMP_STRIDE, KV_GROUPS, 2, HEAD_DIM)
    h = 0.0
    for j in range(CMP_RATIO):
        sl = slice(j * CMP_STRIDE, (j + 1) * CMP_STRIDE)
        h = h + jnp.einsum('bnsgcd,csde->bngce', ch[:, j:j + n_blk] + pe[sl][:, None], w1[:, sl])
    return jnp.einsum('bngce,ced->bngcd', jax.nn.gelu(h), w2)


def select_blocks(imp, qpos, n_sel_blocks):
    n_cmp = imp.shape[-1]
    coef = np.convolve(np.ones(SEL_RATIO), np.ones(CMP_RATIO)).astype(np.float32)
    imp_pad = jnp.pad(imp, ((0, 0), (0, 0), (0, 0), (CMP_RATIO - 1, SEL_RATIO * n_sel_blocks - n_cmp)))
    gidx = SEL_RATIO * np.arange(n_sel_blocks)[:, None] + np.arange(coef.shape[0])[None, :]
    p_sel = jnp.einsum('bgqjo,o->bgqj', imp_pad[..., gidx], jnp.asarray(coef))
    j = jnp.arange(n_sel_blocks)[None, :]
    cur = (qpos // SEL_BLOCK)[:, None]
    valid = j <= cur
    forced = (j == 0) | (j == cur) | (j == cur - 1)
    score = jnp.where(valid, p_sel + jnp.where(forced, FORCE_BONUS, 0.0), -jnp.inf)
    vals, idx = lax.top_k(score, min(N_SELECT, n_sel_blocks))
    return idx, jnp.isfinite(vals)


def sel_positions(idx):
    pos = idx[..., None] * SEL_BLOCK + jnp.arange(SEL_BLOCK, dtype=idx.dtype)
    return pos.reshape(idx.shape[:-1] + (-1,))


def unblock(o):
    return jnp.moveaxis(o, 0, 1).reshape((o.shape[1], -1) + o.shape[3:])


def nsa_project(x, w_q, w_kv, w_gate):
    b, t, _ = x.shape
    q = (x @ w_q).reshape(b, t, KV_GROUPS, GROUP_SIZE, HEAD_DIM)
    kv = (x @ w_kv).reshape(b, t, N_BRANCH, KV_GROUPS, 2, HEAD_DIM)
    gate = jax.nn.sigmoid((x @ w_gate).astype(jnp.float32))
    gate = gate.reshape(b, t, KV_GROUPS, GROUP_SIZE, N_BRANCH).astype(x.dtype)
    return q, kv[:, :, 0], kv[:, :, 1], kv[:, :, 2], gate


def nsa_combine(o_cmp, o_slc, o_win, gate, w_o):
    b, t = o_cmp.shape[:2]
    o = o_cmp * gate[..., 0:1] + o_slc * gate[..., 1:2] + o_win * gate[..., 2:3]
    return o.reshape(b, t, N_HEADS * HEAD_DIM) @ w_o


def nsa_prompt(x, rel_bias, w_q, w_kv, cmp_pe, cmp_w1, cmp_w2, w_gate, w_o):
    b, t, _ = x.shape
    q, kv_cmp, kv_slc, kv_win, gate = nsa_project(x, w_q, w_kv, w_gate)
    qpos = jnp.arange(t, dtype=jnp.int32)
    kc = compress_kv(kv_cmp, cmp_pe, cmp_w1, cmp_w2)
    cpos = jnp.arange(kc.shape[1], dtype=jnp.int32) * CMP_STRIDE + (CMP_BLOCK - 1)
    o_cmp, p_cmp = attend_shared(q, kc[..., 0, :], kc[..., 1, :], qpos, cpos,
                                 cpos[None, :] <= qpos[:, None], rel_bias)
    idx, ok = select_blocks(p_cmp.sum(axis=2), qpos, -(-t // SEL_BLOCK))
    bi = jnp.arange(b)[:, None, None, None]
    gi = jnp.arange(KV_GROUPS)[None, :, None, None]

    def sel_chunk(c):
        s0 = c * SEL_Q_BLOCK
        qc = lax.dynamic_slice_in_dim(q, s0, SEL_Q_BLOCK, axis=1)
        ic = lax.dynamic_slice_in_dim(idx, s0, SEL_Q_BLOCK, axis=2)
        okc = lax.dynamic_slice_in_dim(ok, s0, SEL_Q_BLOCK, axis=2)
        kpos = sel_positions(ic)
        kv = kv_slc[bi, kpos, gi]
        return attend_gathered(qc, kv, s0 + jnp.arange(SEL_Q_BLOCK, dtype=jnp.int32), kpos,
                               jnp.repeat(okc, SEL_BLOCK, axis=-1), rel_bias)

    o_slc = unblock(lax.map(sel_chunk, jnp.arange(t // SEL_Q_BLOCK, dtype=jnp.int32)))
    kv_pad = jnp.pad(kv_win, ((0, 0), (WINDOW, 0), (0, 0), (0, 0), (0, 0)))
    span = WINDOW + WIN_Q_BLOCK

    def win_block(c):
        s0 = c * WIN_Q_BLOCK
        qc = lax.dynamic_slice_in_dim(q, s0, WIN_Q_BLOCK, axis=1)
        kvc = lax.dynamic_slice_in_dim(kv_pad, s0, span, axis=1)
        qp = s0 + jnp.arange(WIN_Q_BLOCK, dtype=jnp.int32)
        kp = s0 - WINDOW + jnp.arange(span, dtype=jnp.int32)
        return attend_shared(qc, kvc[..., 0, :], kvc[..., 1, :], qp, kp, window_mask(qp, kp), rel_bias)[0]

    o_win = unblock(lax.map(win_block, jnp.arange(t // WIN_Q_BLOCK, dtype=jnp.int32)))
    y = nsa_combine(o_cmp, o_slc, o_win, gate, w_o)
    return y, kv_cmp, kv_slc, kv_win[:, t - min(WINDOW, t):]


def nsa_sample(x, page_table, pool_cmp, pool_slc, win_buf, rel_bias,
               w_q, w_kv, cmp_pe, cmp_w1, cmp_w2, w_gate, w_o):
    b, s, _ = x.shape
    q, kv_cmp, kv_slc, kv_win, gate = nsa_project(x, w_q, w_kv, w_gate)
    qpos = PAST_LEN + jnp.arange(s, dtype=jnp.int32)
    total = PAST_LEN + s
    past_cmp = pool_cmp[page_table].reshape(b, PAST_LEN, KV_GROUPS, 2, HEAD_DIM)
    full = jnp.concatenate([past_cmp, kv_cmp], axis=1)
    full = jnp.pad(full, ((0, 0), (0, (-total) % CMP_STRIDE), (0, 0), (0, 0), (0, 0)))
    kc = compress_kv(full, cmp_pe, cmp_w1, cmp_w2)
    cpos = jnp.arange(kc.shape[1], dtype=jnp.int32) * CMP_STRIDE + (CMP_BLOCK - 1)
    o_cmp, p_cmp = attend_shared(q, kc[..., 0, :], kc[..., 1, :], qpos, cpos,
                                 cpos[None, :] <= qpos[:, None], rel_bias)
    idx, ok = select_blocks(p_cmp.sum(axis=2), qpos, -(-total // SEL_BLOCK))
    kpos = sel_positions(idx)
    bi = jnp.arange(b)[:, None, None, None]
    gi = jnp.arange(KV_GROUPS)[None, :, None, None]
    lp = jnp.minimum(kpos, PAST_LEN - 1)
    kv_past = pool_slc[page_table[bi, lp // PAGE_SIZE], lp % PAGE_SIZE, gi]
    kv_new = kv_slc[bi, jnp.clip(kpos - PAST_LEN, 0, s - 1), gi]
    kv_sel = jnp.where((kpos < PAST_LEN)[..., None, None], kv_past, kv_new)
    o_slc = attend_gathered(q, kv_sel, qpos, kpos, jnp.repeat(ok, SEL_BLOCK, axis=-1), rel_bias)
    win = jnp.concatenate([win_buf, kv_win], axis=1)
    kp = PAST_LEN - win_buf.shape[1] + jnp.arange(win.shape[1], dtype=jnp.int32)
    o_win = attend_shared(q, win[..., 0, :], win[..., 1, :], qpos, kp, window_mask(qpos, kp), rel_bias)[0]
    y = nsa_combine(o_cmp, o_slc, o_win, gate, w_o)
    return y, kv_cmp, kv_slc, win[:, win.shape[1] - min(WINDOW, total):]


def causal_dwconv(u, hist, w, bias):
    full = jnp.concatenate([hist, u], axis=1)
    y = lax.conv_general_dilated(full, w[:, None, :], window_strides=(1,), padding='VALID',
                                 dimension_numbers=('NWC', 'WIO', 'NWC'), feature_group_count=u.shape[-1])
    return y + bias, full[:, full.shape[1] - (w.shape[0] - 1):]


def conformer_conv(x, hist, w_pw1, dw, dw_b, ln_g, ln_b, w_pw2):
    a = x @ w_pw1
    u = a[..., :D_MODEL] * jax.nn.sigmoid(a[..., D_MODEL:])
    h, new_hist = causal_dwconv(u, hist, dw, dw_b)
    h = jax.nn.silu(layer_norm(h, ln_g, ln_b))
    return h @ w_pw2, new_hist


def s5_mixer(x, h_re, h_im, a_re, a_im, log_dt, b_re, b_im, c_re, c_im, d_skip, w_glu):
    bsz, t, _ = x.shape
    f32 = jnp.float32
    dt = jnp.exp(log_dt.astype(f32))[:, None]
    ar, ai = a_re.astype(f32), a_im.astype(f32)
    mag = jnp.exp(ar * dt)
    abar_re, abar_im = mag * jnp.cos(ai * dt), mag * jnp.sin(ai * dt)
    den = ar * ar + ai * ai
    coef_re = ((abar_re - 1.0) * ar + abar_im * ai) / den
    coef_im = (abar_im * ar - (abar_re - 1.0) * ai) / den
    br, bim = b_re.astype(f32), b_im.astype(f32)
    bb_re = coef_re[..., None] * br - coef_im[..., None] * bim
    bb_im = coef_re[..., None] * bim + coef_im[..., None] * br
    u = x.astype(f32).reshape(bsz, t, SSM_GROUPS, SSM_GROUP_CH)
    bu_re = jnp.einsum('btgc,gpc->tbgp', u, bb_re)
    bu_im = jnp.einsum('btgc,gpc->tbgp', u, bb_im)
    h0r, h0i = h_re.astype(f32), h_im.astype(f32)
    bu_re = bu_re.at[0].add(abar_re * h0r - abar_im * h0i)
    bu_im = bu_im.at[0].add(abar_re * h0i + abar_im * h0r)
    a_seq_re = jnp.broadcast_to(abar_re, (t, 1) + abar_re.shape)
    a_seq_im = jnp.broadcast_to(abar_im, (t, 1) + abar_im.shape)

    def combine(e1, e2):
        a1r, a1i, b1r, b1i = e1
        a2r, a2i, b2r, b2i = e2
        return (a2r * a1r - a2i * a1i, a2r * a1i + a2i * a1r,
                a2r * b1r - a2i * b1i + b2r, a2r * b1i + a2i * b1r + b2i)

    _, _, hs_re, hs_im = lax.associative_scan(combine, (a_seq_re, a_seq_im, bu_re, bu_im), axis=0)
    y = (jnp.einsum('tbgp,gcp->btgc', hs_re, c_re.astype(f32))
         - jnp.einsum('tbgp,gcp->btgc', hs_im, c_im.astype(f32)))
    y = y.reshape(bsz, t, D_MODEL) + d_skip.astype(f32) * x.astype(f32)
    z = y.astype(x.dtype) @ w_glu
    return z[..., :D_MODEL] * jax.nn.sigmoid(z[..., D_MODEL:]), hs_re[-1], hs_im[-1]


def conv_ffn(x, hist, w_up, dw, dw_b, w_down):
    up = x @ w_up
    g, new_hist = causal_dwconv(up[..., :D_FF], hist, dw, dw_b)
    return (jax.nn.gelu(g) * up[..., D_FF:]) @ w_down, new_hist


def setup_inputs(seed: int = 0) -> dict:
    key = jax.random.key(seed)
    ks = iter(jax.random.split(key, 64))

    def nrm(shape, scale):
        return scale * jax.random.normal(next(ks), shape, jnp.float32)

    n_pages = PAST_LEN // PAGE_SIZE
    n_used = DEC_BATCH * n_pages
    n_phys = n_used + max(1, n_used // 4)
    wb = min(WINDOW, PAST_LEN)
    D = D_MODEL
    inp = {}
    inp['x_prompt'] = nrm((BATCH, SEQ, D), 1.0)
    inp['x_sample'] = nrm((DEC_BATCH, DEC_SEQ, D), 1.0)
    inp['cache_cmp_kv'] = nrm((N_NSA, n_phys, PAGE_SIZE, KV_GROUPS, 2, HEAD_DIM), 1.0)
    inp['cache_slc_kv'] = nrm((N_NSA, n_phys, PAGE_SIZE, KV_GROUPS, 2, HEAD_DIM), 1.0)
    inp['cache_win_kv'] = nrm((N_NSA, DEC_BATCH, wb, KV_GROUPS, 2, HEAD_DIM), 1.0)
    inp['state_conv'] = nrm((N_CONV, DEC_BATCH, CONV_WIDTH - 1, D), 0.5)
    inp['state_ssm_re'] = nrm((N_SSM, DEC_BATCH, SSM_GROUPS, SSM_STATE), 0.3)
    inp['state_ssm_im'] = nrm((N_SSM, DEC_BATCH, SSM_GROUPS, SSM_STATE), 0.3)
    inp['state_ffn_conv'] = nrm((DEPTH, DEC_BATCH, FFN_CONV_WIDTH - 1, D_FF), 1.0)
    inp['page_table'] = jax.random.permutation(next(ks), n_phys)[:n_used].reshape(DEC_BATCH, n_pages).astype(jnp.int32)
    inp['norm_gain'] = 1.0 + nrm((DEPTH, 4, D), 0.05)
    inp['rel_bias'] = nrm((N_BUCKETS, N_HEADS), 0.5)
    inp['nsa_w_q'] = nrm((N_NSA, D, N_HEADS * HEAD_DIM), D ** -0.5)
    inp['nsa_w_kv'] = nrm((N_NSA, D, N_BRANCH * KV_GROUPS * 2 * HEAD_DIM), D ** -0.5)
    inp['nsa_cmp_pe'] = nrm((N_NSA, CMP_BLOCK, 2, HEAD_DIM), 0.5)
    inp['nsa_cmp_w1'] = nrm((N_NSA, 2, CMP_BLOCK, HEAD_DIM, HEAD_DIM), (CMP_BLOCK * HEAD_DIM) ** -0.5)
    inp['nsa_cmp_w2'] = nrm((N_NSA, 2, HEAD_DIM, HEAD_DIM), HEAD_DIM ** -0.5)
    inp['nsa_w_gate'] = nrm((N_NSA, D, N_HEADS * N_BRANCH), D ** -0.5)
    inp['nsa_w_o'] = nrm((N_NSA, N_HEADS * HEAD_DIM, D), (N_HEADS * HEAD_DIM) ** -0.5)
    inp['conv_w_pw1'] = nrm((N_CONV, D, 2 * D), D ** -0.5)
    inp['conv_dw'] = nrm((N_CONV, CONV_WIDTH, D), CONV_WIDTH ** -0.5)
    inp['conv_dw_b'] = nrm((N_CONV, D), 0.02)
    inp['conv_ln_g'] = 1.0 + nrm((N_CONV, D), 0.05)
    inp['conv_ln_b'] = nrm((N_CONV, D), 0.02)
    inp['conv_w_pw2'] = nrm((N_CONV, D, D), D ** -0.5)
    n_idx = jnp.arange(SSM_STATE, dtype=jnp.float32)
    inp['ssm_a_re'] = -0.5 + nrm((N_SSM, SSM_GROUPS, SSM_STATE), 0.01)
    inp['ssm_a_im'] = math.pi * n_idx + nrm((N_SSM, SSM_GROUPS, SSM_STATE), 0.01)
    inp['ssm_log_dt'] = jax.random.uniform(next(ks), (N_SSM, SSM_GROUPS), jnp.float32,
                                           math.log(DT_MIN), math.log(DT_MAX))
    inp['ssm_b_re'] = nrm((N_SSM, SSM_GROUPS, SSM_STATE, SSM_GROUP_CH), (2 * SSM_GROUP_CH) ** -0.5)
    inp['ssm_b_im'] = nrm((N_SSM, SSM_GROUPS, SSM_STATE, SSM_GROUP_CH), (2 * SSM_GROUP_CH) ** -0.5)
    inp['ssm_c_re'] = nrm((N_SSM, SSM_GROUPS, SSM_GROUP_CH, SSM_STATE), (2 * SSM_STATE) ** -0.5)
    inp['ssm_c_im'] = nrm((N_SSM, SSM_GROUPS, SSM_GROUP_CH, SSM_STATE), (2 * SSM_STATE) ** -0.5)
    inp['ssm_d'] = nrm((N_SSM, D), 1.0)
    inp['ssm_w_glu'] = nrm((N_SSM, D, 2 * D), D ** -0.5)
    inp['ffn_w_up'] = nrm((DEPTH, D, 2 * D_FF), D ** -0.5)
    inp['ffn_dw'] = nrm((DEPTH, FFN_CONV_WIDTH, D_FF), FFN_CONV_WIDTH ** -0.5)
    inp['ffn_dw_b'] = nrm((DEPTH, D_FF), 0.02)
    inp['ffn_w_down'] = nrm((DEPTH, D_FF, D), D_FF ** -0.5)
    return inp


def reference(x_prompt, x_sample, cache_cmp_kv, cache_slc_kv, cache_win_kv, state_conv, state_ssm_re,
              state_ssm_im, state_ffn_conv, page_table, norm_gain, rel_bias, nsa_w_q, nsa_w_kv, nsa_cmp_pe,
              nsa_cmp_w1, nsa_cmp_w2, nsa_w_gate, nsa_w_o, conv_w_pw1, conv_dw, conv_dw_b, conv_ln_g, conv_ln_b,
              conv_w_pw2, ssm_a_re, ssm_a_im, ssm_log_dt, ssm_b_re, ssm_b_im, ssm_c_re, ssm_c_im, ssm_d,
              ssm_w_glu, ffn_w_up, ffn_dw, ffn_dw_b, ffn_w_down):
    xp, xs = x_prompt, x_sample
    bp = xp.shape[0]
    cmp_p, cmp_s, slc_p, slc_s, win_p, win_s = [], [], [], [], [], []
    conv_p, conv_s = [], []
    ssm_re_p, ssm_re_s, ssm_im_p, ssm_im_s = [], [], [], []
    ffn_p, ffn_s = [], []
    for i in range(DEPTH):
        m = LAYER_MIXER[i]
        j = LAYER_MIXER[:i].count(m)
        hp = rms_norm(xp, norm_gain[i, 0])
        hs = rms_norm(xs, norm_gain[i, 0])
        if m == 0:
            w = (nsa_w_q[j], nsa_w_kv[j], nsa_cmp_pe[j], nsa_cmp_w1[j], nsa_cmp_w2[j], nsa_w_gate[j], nsa_w_o[j])
            mp, kc_p, ks_p, kw_p = nsa_prompt(hp, rel_bias, *w)
            ms, kc_s, ks_s, kw_s = nsa_sample(hs, page_table, cache_cmp_kv[j], cache_slc_kv[j],
                                              cache_win_kv[j], rel_bias, *w)
            cmp_p.append(kc_p); cmp_s.append(kc_s)
            slc_p.append(ks_p); slc_s.append(ks_s)
            win_p.append(kw_p); win_s.append(kw_s)
        elif m == 1:
            w = (conv_w_pw1[j], conv_dw[j], conv_dw_b[j], conv_ln_g[j], conv_ln_b[j], conv_w_pw2[j])
            mp, cp = conformer_conv(hp, jnp.zeros((bp, CONV_WIDTH - 1, D_MODEL), hp.dtype), *w)
            ms, cs = conformer_conv(hs, state_conv[j], *w)
            conv_p.append(cp); conv_s.append(cs)
        else:
            w = (ssm_a_re[j], ssm_a_im[j], ssm_log_dt[j], ssm_b_re[j], ssm_b_im[j], ssm_c_re[j], ssm_c_im[j],
                 ssm_d[j], ssm_w_glu[j])
            zero_state = jnp.zeros((bp, SSM_GROUPS, SSM_STATE), jnp.float32)
            mp, hr_p, hi_p = s5_mixer(hp, zero_state, zero_state, *w)
            ms, hr_s, hi_s = s5_mixer(hs, state_ssm_re[j], state_ssm_im[j], *w)
            ssm_re_p.append(hr_p); ssm_re_s.append(hr_s)
            ssm_im_p.append(hi_p); ssm_im_s.append(hi_s)
        xp = xp + rms_norm(mp, norm_gain[i, 1])
        xs = xs + rms_norm(ms, norm_gain[i, 1])
        wf = (ffn_w_up[i], ffn_dw[i], ffn_dw_b[i], ffn_w_down[i])
        fp, fh_p = conv_ffn(rms_norm(xp, norm_gain[i, 2]),
                            jnp.zeros((bp, FFN_CONV_WIDTH - 1, D_FF), xp.dtype), *wf)
        fs, fh_s = conv_ffn(rms_norm(xs, norm_gain[i, 2]), state_ffn_conv[i], *wf)
        ffn_p.append(fh_p); ffn_s.append(fh_s)
        xp = xp + rms_norm(fp, norm_gain[i, 3])
        xs = xs + rms_norm(fs, norm_gain[i, 3])
    return (xp, xs,
            jnp.stack(cmp_p), jnp.stack(cmp_s), jnp.stack(slc_p), jnp.stack(slc_s),
            jnp.stack(win_p), jnp.stack(win_s), jnp.stack(conv_p), jnp.stack(conv_s),
            jnp.stack(ssm_re_p), jnp.stack(ssm_re_s), jnp.stack(ssm_im_p), jnp.stack(ssm_im_s),
            jnp.stack(ffn_p), jnp.stack(ffn_s))
```

```python
import contextlib
import math
import numpy as np
import concourse.bass as bass
import concourse.mybir as mybir
from concourse.bass_utils import run_bass_kernel_spmd

F32 = mybir.dt.float32
BF16 = mybir.dt.bfloat16
I32 = mybir.dt.int32
U32 = mybir.dt.uint32
AF = mybir.ActivationFunctionType
ALU = mybir.AluOpType
AX = mybir.AxisListType

D = 2048
DC = 16
DFF = 5632
FC = 44
DEPTH = 4
SW = 8
EPOCH = 20000
RMS_EPS = 1e-6
LN_EPS = 1e-5


def _bucket(d):
    d = np.asarray(d, dtype=np.int64)
    n = np.maximum(d, 0)
    nf = np.maximum(n, 16).astype(np.float32)
    big = 16 + (np.log(nf / np.float32(16)) / np.float32(math.log(8)) * np.float32(16)).astype(np.int32)
    return np.where(n < 16, n, np.minimum(big, 31)).astype(np.int64)


def _onehot(d, valid):
    b = np.where(valid, _bucket(d), 32)
    oh = np.zeros((33,) + b.shape, np.float32)
    np.put_along_axis(oh, b[None], 1.0, axis=0)
    return oh


def structural_consts():
    c = {}
    n = np.arange(128)[:, None]
    q = np.arange(128)[None, :]
    kinds = []
    kinds.append(_onehot(q - n, (q - n) >= 0))
    kinds.append(_onehot(128 + q - n, np.ones((128, 128), bool)))
    kinds.append(_onehot(np.full((128, 128), 1000), np.ones((128, 128), bool)))
    kinds.append(_onehot(512 + q - n, n >= q))
    c["oh_pk"] = np.stack(kinds, 1).reshape(33, 4 * 128 * 128)
    m = np.arange(248)[:, None] - 120
    d = q - 16 * m - 31
    c["oh_cmp"] = _onehot(d, d >= 0).reshape(33, 248 * 128)
    p = np.arange(128)
    tiles = []
    for t in range(8):
        cc = t * 128 + p
        d = 16384 - 16 * cc - 15
        tiles.append(_onehot(d, cc >= 1))
    for t in range(4):
        idx = t * 128 + p
        tiles.append(_onehot(512 - idx, np.ones(128, bool)))
    tiles.append(_onehot(np.zeros(128, np.int64), p == 0))
    tiles.append(_onehot(128 - (p % 64), np.ones(128, bool)))
    tiles.append(_onehot(64 - (p % 64), np.ones(128, bool)))
    tiles.append(_onehot(np.full(128, 1000), np.ones(128, bool)))
    tiles.append(_onehot(np.zeros(128, np.int64), p == 0))
    c["oh_s"] = np.stack(tiles, 1).astype(np.float32)
    coef = np.array([1, 2, 2, 2, 1], np.float32)
    mp = np.zeros((128, 32), np.float32)
    for nn in range(127):
        for j in range(32):
            o = nn + 1 - 4 * j
            if 0 <= o <= 4:
                mp[nn, j] = coef[o]
    c["msel_p"] = mp
    am = np.zeros((8, 128, 32), np.float32)
    for i in range(8, 16):
        for ql in range(128):
            cur = (i * 128 + ql) // 64
            for j in range(32):
                if j > cur:
                    am[i - 8, ql, j] = -1e30
                elif j == 0 or j == cur or j == cur - 1:
                    am[i - 8, ql, j] = 1e4
    c["addmask_p"] = am
    ms = np.zeros((1024, 257), np.float32)
    for cc in range(1024):
        for j in range(257):
            o = cc - 4 * j
            if 0 <= o <= 4:
                ms[cc, j] = coef[o]
    c["msel_s"] = ms
    ams = np.zeros((4, 257), np.float32)
    ams[:, [0, 255, 256]] = 1e4
    c["addmask_s"] = ams
    ee = np.zeros((32, 16, 128), np.float32)
    for kt in range(16):
        for nn in range(128):
            ee[(kt * 128 + nn) // 64, kt, nn] = 1.0
    c["eexp"] = ee
    sel = np.zeros((2, 4, 128), np.float32)
    sel[0, :, :64] = 1.0
    sel[1, :, 64:] = 1.0
    c["selhalf"] = sel
    c["eye4"] = np.eye(4, dtype=np.float32)
    c["iota_pg"] = np.tile(np.arange(128, dtype=np.float32)[None, :], (128, 1))
    c["iota_row"] = (np.arange(128) % 64).astype(np.float32)[:, None].copy()
    c["iota_p"] = np.arange(128, dtype=np.float32)[:, None].copy()
    c["gcol"] = np.tile((np.arange(32) // 8).astype(np.float32)[None, :], (128, 1))
    return c


STRUCT_SHAPES = dict(oh_pk=[33, 65536], oh_cmp=[33, 248 * 128], oh_s=[33, 17, 128], msel_p=[128, 32],
                     addmask_p=[8, 128, 32], msel_s=[1024, 257], addmask_s=[4, 257], eexp=[32, 16, 128],
                     selhalf=[2, 4, 128], eye4=[4, 4], iota_pg=[128, 128], iota_row=[128, 1], iota_p=[128, 1], gcol=[128, 32])

class Buf:
    __slots__ = ("name", "last_w", "readers")

    def __init__(self, name):
        self.name = name
        self.last_w = None
        self.readers = []


class TK:
    def __init__(self, nc, es, same_engine_sync=True):
        self.nc = nc
        self.es = es
        self.eng = {"pe": nc.tensor, "dve": nc.vector, "act": nc.scalar, "pool": nc.gpsimd, "sp": nc.sync}
        self.cur = {}
        self.nsem = 0
        for q in self.eng:
            self.cur[q] = [self._newsem(q), 0]
        self.seen = {q: {} for q in self.eng}
        self.dpool = {}
        for q, n in (("sp", 20), ("act", 8), ("pool", 12)):
            self.dpool[q] = [[self._newsem("d" + q), 0] for _ in range(n)]
        self.dnext = {q: 0 for q in self.dpool}
        self.same = same_engine_sync
        self.ninstr = 0
        self.nwait = 0

    def _newsem(self, tag):
        self.nsem += 1
        return self.es.enter_context(self.nc.semaphore("s_%s_%d" % (tag, self.nsem)))

    def _wait(self, q, ev):
        sem, val, owner = ev
        key = id(sem)
        if self.seen[q].get(key, 0) >= val:
            return
        self.eng[q].wait_ge(sem, val)
        self.nwait += 1
        self.seen[q][key] = val

    def _deps(self, q, reads, writes, is_dma):
        evs = []
        for b in reads:
            if b.last_w is not None:
                evs.append(b.last_w)
        for b in writes:
            if b.last_w is not None:
                evs.append(b.last_w)
            evs.extend(b.readers)
        for ev in evs:
            if ev[2] == q and not is_dma:
                if q == "pe" or not self.same:
                    continue
            self._wait(q, ev)

    def _record(self, ev, reads, writes):
        for b in writes:
            b.last_w = ev
            b.readers = []
        for b in reads:
            b.readers = [e for e in b.readers if not (e[2] == ev[2] and e[0] is ev[0])] + [ev]

    def op(self, q, fn, reads=(), writes=()):
        self._deps(q, reads, writes, False)
        c = self.cur[q]
        if c[1] >= EPOCH:
            c[0] = self._newsem(q)
            c[1] = 0
        ins = fn(self.eng[q])
        c[1] += 1
        ins.then_inc(c[0], 1)
        ev = (c[0], c[1], q)
        self._record(ev, reads, writes)
        self.ninstr += 1
        return ev

    def dma(self, q, fn, reads=(), writes=()):
        self._deps(q, reads, writes, True)
        pool = self.dpool[q]
        i = self.dnext[q]
        self.dnext[q] = (i + 1) % len(pool)
        s = pool[i]
        if s[1] > 0:
            self._wait(q, (s[0], s[1], "dma" + q))
        ins = fn(self.eng[q])
        s[1] += 16
        ins.then_inc(s[0], 16)
        ev = (s[0], s[1], "dma" + q)
        self._record(ev, reads, writes)
        self.ninstr += 1
        return ev

    def barrier(self):
        evs = []
        for q, c in self.cur.items():
            if c[1] > 0:
                evs.append((c[0], c[1], q))
        for q, pool in self.dpool.items():
            for s in pool:
                if s[1] > 0:
                    evs.append((s[0], s[1], "dma" + q))
        for q in self.eng:
            for ev in evs:
                if ev[2] == q:
                    continue
                self._wait(q, ev)


class Builder:
    def __init__(self, T=2048, layers=(0, 1, 2, 3), do_mixer=True, do_ffn=True, dbg=False, nphys=1280):
        self.NPHYS = nphys
        self.T = T
        self.NT = T // 512
        self.TT = T + SW
        self.layers = layers
        self.do_mixer = do_mixer
        self.do_ffn = do_ffn
        self.dbg = dbg
        self.nc = bass.Bass("TRN2", target_bir_lowering=False)
        self.es = contextlib.ExitStack()
        self.tk = TK(self.nc, self.es)
        self.tiles = [(i * 512, 512) for i in range(self.NT)] + [(T, SW)]
        self.dq = 0
        self.wcache = {}

    def din(self, name, shape, dt=F32):
        return self.nc.dram_tensor(name, list(shape), dt, kind="ExternalInput").ap()

    def dout(self, name, shape, dt=F32):
        return self.nc.dram_tensor(name, list(shape), dt, kind="ExternalOutput").ap()

    def dscr(self, name, shape, dt=F32):
        return self.nc.dram_tensor(name, list(shape), dt, kind="Internal").ap()

    def sb(self, name, shape, dt=F32, stack=None):
        self.uid = getattr(self, "uid", 0) + 1
        return (stack or self.es).enter_context(self.nc.sbuf_tensor("%s_u%d" % (name, self.uid), list(shape), dt))

    def ps(self, name, shape, dt=F32, stack=None):
        return (stack or self.es).enter_context(self.nc.psum_tensor(name, list(shape), dt))

    def op(self, q, fn, r=(), w=()):
        return self.tk.op(q, fn, r, w)

    def dma(self, fn, r=(), w=(), q=None):
        if q is None:
            q = "sp"
        return self.tk.dma(q, fn, r, w)

    def declare_io(self):
        T = self.T
        d = self.din
        self.x_p = d("x_p", [T, D])
        self.x_s = d("x_s", [1, D])
        self.norm_gain = d("norm_gain", [DEPTH * 4, D])
        self.ffn_w_up = d("ffn_w_up", [DEPTH, D, 2 * DFF])
        self.ffn_dw = d("ffn_dw", [DEPTH * 3, DFF])
        self.ffn_dw_b = d("ffn_dw_b", [DEPTH, DFF])
        self.ffn_w_down = d("ffn_w_down", [DEPTH, DFF, D])
        self.state_ffn = d("state_ffn", [DEPTH, 2, DFF])
        self.conv_w_pw1 = d("conv_w_pw1", [D, 2 * D])
        self.conv_dw = d("conv_dw", [31, D])
        self.conv_dw_b = d("conv_dw_b", [1, D])
        self.conv_ln_g = d("conv_ln_g", [1, D])
        self.conv_ln_b = d("conv_ln_b", [1, D])
        self.conv_w_pw2 = d("conv_w_pw2", [D, D])
        self.state_conv = d("state_conv", [30, D])
        self.ssm_a_re = d("ssm_a_re", [128, 64])
        self.ssm_a_im = d("ssm_a_im", [128, 64])
        self.ssm_log_dt = d("ssm_log_dt", [1, 128])
        self.ssm_b_re = d("ssm_b_re", [128, 64, 16])
        self.ssm_b_im = d("ssm_b_im", [128, 64, 16])
        self.ssm_c_re = d("ssm_c_re", [128, 16, 64])
        self.ssm_c_im = d("ssm_c_im", [128, 16, 64])
        self.ssm_d = d("ssm_d", [1, D])
        self.ssm_w_glu = d("ssm_w_glu", [D, 2 * D])
        self.state_ssm_re = d("state_ssm_re", [128, 64])
        self.state_ssm_im = d("state_ssm_im", [128, 64])
        o = self.dout
        self.ssm_re_p = o("ssm_re_p", [128, 64])
        self.ssm_re_s = o("ssm_re_s", [128, 64])
        self.ssm_im_p = o("ssm_im_p", [128, 64])
        self.ssm_im_s = o("ssm_im_s", [128, 64])
        self.conv_p = o("conv_p", [30, D])
        self.conv_s = o("conv_s", [30, D])
        self.y_p = o("y_p", [T, D])
        self.y_s = o("y_s", [1, D])
        self.ffn_p = o("ffn_p", [DEPTH, 2, DFF])
        self.ffn_s = o("ffn_s", [DEPTH, 2, DFF])
        self.nsa_declare()
        self.XR = self.dscr("XR", [DC, 128, self.TT])
        self.bXR = [Buf("XR%d" % i) for i in range(len(self.tiles))]

    def setup_consts(self):
        tk = self.tk
        self.ident = self.sb("ident", [128, 128], F32)
        self.ident_b = self.sb("ident_b", [128, 128], BF16)
        self.ones_b = self.sb("ones_b", [128, 128], BF16)
        self.gains = self.sb("gains", [128, DEPTH * 4, DC], F32)
        self.b_const = Buf("const")
        nc = self.nc
        self.iot = self.sb("iot", [128, 128], F32)
        self.op("pool", lambda e: e.iota(self.iot[:], pattern=[[1, 128]], base=0, channel_multiplier=-1,
                                         allow_small_or_imprecise_dtypes=True), w=[self.b_const])
        self.op("dve", lambda e: e.tensor_scalar(out=self.ident[:], in0=self.iot[:], scalar1=0.0, scalar2=None,
                                                 op0=ALU.is_equal), r=[self.b_const], w=[self.b_const])
        self.op("dve", lambda e: e.tensor_copy(out=self.ident_b[:], in_=self.ident[:]), r=[self.b_const],
                w=[self.b_const])
        self.op("dve", lambda e: e.memset(self.ones_b[:], 1.0), w=[self.b_const])
        self.ones_f = self.sb("ones_f", [128, 128], F32)
        self.op("dve", lambda e: e.memset(self.ones_f[:], 1.0), w=[self.b_const])
        self.dma(lambda e: e.dma_start(out=self.gains[:], in_=self.norm_gain.rearrange("l (c p) -> p l c", p=128),
                                       allow_slow_non_contiguous=True), w=[self.b_const])
        self.psb = [self.ps("psb%d" % i, [128, 512], F32) for i in range(8)]
        self.bps = [Buf("ps%d" % i) for i in range(8)]

    def phase_input(self):
        with contextlib.ExitStack() as st:
            NB = 2
            tin = [self.sb("tin%d" % i, [128, D], F32, st) for i in range(NB)]
            tout = [self.sb("tout%d" % i, [128, DC, 128], F32, st) for i in range(NB)]
            btin = [Buf("tin") for _ in range(NB)]
            btout = [Buf("tout") for _ in range(NB)]
            XRv = self.XR.rearrange("c p t -> p c t")
            nblk = self.T // 128
            for b in range(nblk + 1):
                k = b % NB
                samp = (b == nblk)
                rows = 1 if samp else 128
                if samp:
                    self.op("pool", lambda e: e.memset(tin[k][:], 0.0), w=[btin[k]])
                    self.dma(lambda e: e.dma_start(out=tin[k][0:1, :], in_=self.x_s[0:1, :]), w=[btin[k]])
                else:
                    self.dma(lambda e: e.dma_start(out=tin[k][:], in_=self.x_p[b * 128:(b + 1) * 128, :]),
                             w=[btin[k]])
                for g in range(4):
                    pb = 4 + (g % 2)
                    for j in range(4):
                        c = g * 4 + j
                        self.op("pe", lambda e: e.transpose(out=self.psb[pb][:, j * 128:(j + 1) * 128],
                                                            in_=tin[k][:, c * 128:(c + 1) * 128],
                                                            identity=self.ident[:]),
                                r=[btin[k], self.b_const], w=[self.bps[pb]])
                    eng = "act" if g % 2 == 0 else "dve"
                    if eng == "act":
                        self.op("act", lambda e: e.copy(out=tout[k][:, g * 4:(g + 1) * 4, :],
                                                        in_=self.psb[pb][:].rearrange("p (j t) -> p j t", j=4)),
                                r=[self.bps[pb]], w=[btout[k]])
                    else:
                        self.op("dve", lambda e: e.tensor_copy(out=tout[k][:, g * 4:(g + 1) * 4, :],
                                                               in_=self.psb[pb][:].rearrange("p (j t) -> p j t", j=4)),
                                r=[self.bps[pb]], w=[btout[k]])
                if samp:
                    ti = len(self.tiles) - 1
                    self.dma(lambda e: e.dma_start(out=XRv[:, :, self.T:self.T + SW], in_=tout[k][:, :, 0:SW]),
                             r=[btout[k]], w=[self.bXR[ti]])
                else:
                    ti = b // 4
                    self.dma(lambda e: e.dma_start(out=XRv[:, :, b * 128:(b + 1) * 128], in_=tout[k][:]),
                             r=[btout[k]], w=[self.bXR[ti]])
            self.tk.barrier()

    def phase_output(self):
        with contextlib.ExitStack() as st:
            NB = 2
            tin = [self.sb("oin%d" % i, [128, DC, 128], F32, st) for i in range(NB)]
            tout = [self.sb("oout%d" % i, [128, D], F32, st) for i in range(NB)]
            btin = [Buf("oin") for _ in range(NB)]
            btout = [Buf("oout") for _ in range(NB)]
            XRv = self.XR.rearrange("c p t -> p c t")
            nblk = self.T // 128
            for b in range(nblk + 1):
                k = b % NB
                samp = (b == nblk)
                if samp:
                    ti = len(self.tiles) - 1
                    self.op("pool", lambda e: e.memset(tin[k][:], 0.0), w=[btin[k]])
                    self.dma(lambda e: e.dma_start(out=tin[k][:, :, 0:SW], in_=XRv[:, :, self.T:self.T + SW]),
                             r=[self.bXR[ti]], w=[btin[k]])
                else:
                    ti = b // 4
                    self.dma(lambda e: e.dma_start(out=tin[k][:], in_=XRv[:, :, b * 128:(b + 1) * 128]),
                             r=[self.bXR[ti]], w=[btin[k]])
                for g in range(4):
                    pb = 4 + (g % 2)
                    for j in range(4):
                        c = g * 4 + j
                        self.op("pe", lambda e: e.transpose(out=self.psb[pb][:, j * 128:(j + 1) * 128],
                                                            in_=tin[k][:, c, :], identity=self.ident[:]),
                                r=[btin[k], self.b_const], w=[self.bps[pb]])
                    if g % 2 == 0:
                        self.op("act", lambda e: e.copy(out=tout[k][:, g * 512:(g + 1) * 512], in_=self.psb[pb][:]),
                                r=[self.bps[pb]], w=[btout[k]])
                    else:
                        self.op("dve", lambda e: e.tensor_copy(out=tout[k][:, g * 512:(g + 1) * 512],
                                                               in_=self.psb[pb][:]),
                                r=[self.bps[pb]], w=[btout[k]])
                if samp:
                    self.dma(lambda e: e.dma_start(out=self.y_s[0:1, :], in_=tout[k][0:1, :]), r=[btout[k]])
                else:
                    self.dma(lambda e: e.dma_start(out=self.y_p[b * 128:(b + 1) * 128, :], in_=tout[k][:]),
                             r=[btout[k]])
            self.tk.barrier()

    def alloc_rowlocal(self, st):
        self.xr = self.sb("xr", [128, DC, 512], F32, st)
        self.b_xr = Buf("xr")
        self.xn = self.sb("xn", [128, DC, 512], BF16, st)
        self.b_xn = Buf("xn")
        self.yy = self.sb("yy", [128, DC, 512], F32, st)
        self.b_yy = Buf("yy")
        self.rstd = self.sb("rstd", [128, 512], F32, st)
        self.b_rstd = Buf("rstd")
        self.NWB = 4
        self.NWF = 2
        self.wf = [self.sb("wf%d" % i, [128, 8, 2, 128], F32, st) for i in range(self.NWF)]
        self.wb = [self.sb("wb%d" % i, [128, 8, 2, 128], BF16, st) for i in range(self.NWB)]
        self.b_wf = [Buf("wf") for _ in range(self.NWF)]
        self.b_wb = [Buf("wb") for _ in range(self.NWB)]
        self.wctr = 0
        self.wfctr = 0

    def load_xr(self, ti):
        t0, W = self.tiles[ti]
        XRv = self.XR.rearrange("c p t -> p c t")
        self.dma(lambda e: e.dma_start(out=self.xr[:, :, :W], in_=XRv[:, :, t0:t0 + W]),
                 r=[self.bXR[ti]], w=[self.b_xr])

    def store_xr(self, ti):
        t0, W = self.tiles[ti]
        XRv = self.XR.rearrange("c p t -> p c t")
        self.dma(lambda e: e.dma_start(out=XRv[:, :, t0:t0 + W], in_=self.xr[:, :, :W]),
                 r=[self.b_xr], w=[self.bXR[ti]])

    def rms_stats(self, src, bsrc, W, sq, bsq):
        self.op("act", lambda e: e.activation(out=sq[:, :, :W], in_=src[:, :, :W], func=AF.Square),
                r=[bsrc], w=[bsq])
        pb = 4
        for c in range(DC):
            self.op("pe", lambda e: e.matmul(self.psb[pb][:, :W], lhsT=self.ones_b[:], rhs=sq[:, c, :W],
                                             start=(c == 0), stop=(c == DC - 1)),
                    r=[bsq, self.b_const], w=[self.bps[pb]])
        self.op("act", lambda e: e.activation(out=self.rstd[:, :W], in_=self.psb[pb][:, :W], func=AF.Sqrt,
                                              bias=self.eps_rms[:, 0:1], scale=1.0 / D),
                r=[self.bps[pb], self.b_const], w=[self.b_rstd])
        self.op("dve", lambda e: e.reciprocal(out=self.rstd[:, :W], in_=self.rstd[:, :W]),
                r=[self.b_rstd], w=[self.b_rstd])

    def pre_norm(self, ti, gidx, xn_f32=None, b_xnf=None):
        t0, W = self.tiles[ti]
        self.load_xr(ti)
        self.rms_stats(self.xr, self.b_xr, W, self.xn, self.b_xn)
        for c in range(DC):
            if xn_f32 is not None:
                self.op("dve", lambda e: e.scalar_tensor_tensor(out=xn_f32[:, c, :W], in0=self.xr[:, c, :W],
                                                                scalar=self.gains[:, gidx, c:c + 1],
                                                                in1=self.rstd[:, :W], op0=ALU.mult, op1=ALU.mult),
                        r=[self.b_xr, self.b_rstd, self.b_const], w=[b_xnf])
                self.op("act", lambda e: e.copy(out=self.xn[:, c, :W], in_=xn_f32[:, c, :W]),
                        r=[b_xnf], w=[self.b_xn])
            else:
                self.op("dve", lambda e: e.scalar_tensor_tensor(out=self.xn[:, c, :W], in0=self.xr[:, c, :W],
                                                                scalar=self.gains[:, gidx, c:c + 1],
                                                                in1=self.rstd[:, :W], op0=ALU.mult, op1=ALU.mult),
                        r=[self.b_xr, self.b_rstd, self.b_const], w=[self.b_xn])

    def post_norm_residual(self, ti, gidx):
        t0, W = self.tiles[ti]
        self.rms_stats(self.yy, self.b_yy, W, self.xn, self.b_xn)
        for c in range(DC):
            self.op("dve", lambda e: e.scalar_tensor_tensor(out=self.yy[:, c, :W], in0=self.yy[:, c, :W],
                                                            scalar=self.gains[:, gidx, c:c + 1],
                                                            in1=self.rstd[:, :W], op0=ALU.mult, op1=ALU.mult),
                    r=[self.b_yy, self.b_rstd, self.b_const], w=[self.b_yy])
            self.op("pool", lambda e: e.tensor_tensor(out=self.xr[:, c, :W], in0=self.xr[:, c, :W],
                                                      in1=self.yy[:, c, :W], op=ALU.add),
                    r=[self.b_yy, self.b_xr], w=[self.b_xr])
        self.store_xr(ti)

    def gemm(self, wview, KC, npairs, xin, bxin, W, epilogue, wkey=None, first=True):
        nkb = (KC + 7) // 8
        cache = None
        if wkey is not None:
            if wkey not in self.wcache:
                self.wcache[wkey] = (self.dscr("wc_" + wkey, [npairs * nkb, 128, 2048], BF16), Buf("wc_" + wkey))
            cache, b_cache = self.wcache[wkey]
        for p in range(npairs):
            pa = (p % 2) * 2
            pbk = pa + 1
            for kb in range(nkb):
                k0 = kb * 8
                kn = min(8, KC - k0)
                i = self.wctr % self.NWB
                self.wctr += 1
                blk = p * nkb + kb
                if first or cache is None:
                    fi = self.wfctr % self.NWF
                    self.wfctr += 1
                    for h in range(2):
                        src = wview(p, k0, kn, h)
                        self.dma(lambda e: e.dma_start(out=self.wf[fi][:, :kn, h, :], in_=src), w=[self.b_wf[fi]], q="sp")
                    ceng = "pool" if (self.wfctr % 2 == 0) else "act"
                    if ceng == "pool":
                        self.op("pool", lambda e: e.tensor_copy(out=self.wb[i][:, :kn], in_=self.wf[fi][:, :kn]),
                                r=[self.b_wf[fi]], w=[self.b_wb[i]])
                    else:
                        self.op("act", lambda e: e.copy(out=self.wb[i][:, :kn], in_=self.wf[fi][:, :kn]),
                                r=[self.b_wf[fi]], w=[self.b_wb[i]])
                    if cache is not None:
                        self.dma(lambda e: e.dma_start(out=cache[blk, :, 0:kn * 256],
                                                       in_=self.wb[i][:, :kn].rearrange("p k h n -> p (k h n)")),
                                 r=[self.b_wb[i]], w=[b_cache], q=ceng)
                else:
                    self.dma(lambda e: e.dma_start(out=self.wb[i][:, :kn].rearrange("p k h n -> p (k h n)"),
                                                   in_=cache[blk, :, 0:kn * 256]), r=[b_cache], w=[self.b_wb[i]], q="sp")
                for kk in range(kn):
                    kc = k0 + kk
                    for h, pbank in ((0, pa), (1, pbk)):
                        self.op("pe", lambda e: e.matmul(self.psb[pbank][:, :W], lhsT=self.wb[i][:, kk, h, :],
                                                         rhs=xin[:, kc, :W], start=(kc == 0), stop=(kc == KC - 1)),
                                r=[self.b_wb[i], bxin], w=[self.bps[pbank]])
            epilogue(p, self.psb[pa], self.bps[pa], self.psb[pbk], self.bps[pbk])

    def ffn_layer(self, li):
        tk = self.tk
        with contextlib.ExitStack() as st:
            self.alloc_rowlocal(st)
            hh = self.sb("hh", [128, FC, 512], BF16, st)
            b_hh = Buf("hh")
            ghist = self.sb("ghist", [128, 2, FC], F32, st)
            b_gh = Buf("ghist")
            gbuf = [self.sb("gbuf%d" % i, [128, 514], F32, st) for i in range(2)]
            b_gb = [Buf("gbuf") for _ in range(2)]
            acc = [self.sb("acc%d" % i, [128, 512], F32, st) for i in range(2)]
            b_acc = [Buf("acc") for _ in range(2)]
            dwT = self.sb("dwT", [128, 3, FC], F32, st)
            dbT = self.sb("dbT", [128, FC], F32, st)
            b_dw = Buf("dw")
            hist_in = self.sb("hist_in", [88, 128], F32, st)
            b_hi = Buf("hi")
            hsT = self.sb("hsT", [128, 2, FC], F32, st)
            b_hs = Buf("hsT")
            hout = self.sb("hout", [88, 128], F32, st)
            b_ho = Buf("hout")
            self.dma(lambda e: e.dma_start(out=dwT[:], in_=self.ffn_dw[li * 3:(li + 1) * 3, :]
                                           .rearrange("w (c p) -> p w c", p=128), allow_slow_non_contiguous=True),
                     w=[b_dw])
            self.dma(lambda e: e.dma_start(out=dbT[:], in_=self.ffn_dw_b[li:li + 1, :]
                                           .rearrange("o (c p) -> p (o c)", p=128), allow_slow_non_contiguous=True),
                     w=[b_dw])
            self.dma(lambda e: e.dma_start(out=hist_in[:], in_=self.state_ffn[li].rearrange("t (c p) -> (t c) p", p=128)),
                     w=[b_hi])
            self.op("pe", lambda e: e.transpose(out=self.psb[5][:, 0:88], in_=hist_in[:], identity=self.ident[0:88, 0:88]),
                    r=[b_hi, self.b_const], w=[self.bps[5]])
            self.op("dve", lambda e: e.tensor_copy(out=hsT[:].rearrange("p t c -> p (t c)"), in_=self.psb[5][:, 0:88]),
                    r=[self.bps[5]], w=[b_hs])
            self.op("dve", lambda e: e.memset(ghist[:], 0.0), w=[b_gh])

            w_up = self.ffn_w_up[li].rearrange("(kc p) (h f) -> p kc h f", p=128, h=2)
            w_dn = self.ffn_w_down[li].rearrange("(kc p) (f h n) -> p kc f h n", p=128, h=2, n=128)

            for ti in range(len(self.tiles)):
                t0, W = self.tiles[ti]
                samp = (ti == len(self.tiles) - 1)
                self.pre_norm(ti, li * 4 + 2)
                if samp:
                    self.op("dve", lambda e: e.tensor_copy(out=ghist[:], in_=hsT[:]), r=[b_hs], w=[b_gh])

                def ep_up(p, psA, bA, psB, bB):
                    k = p % 2
                    self.op("act", lambda e: e.copy(out=gbuf[k][:, 2:2 + W], in_=psA[:, :W]), r=[bA], w=[b_gb[k]])
                    self.op("dve", lambda e: e.tensor_copy(out=gbuf[k][:, 0:2], in_=ghist[:, :, p]),
                            r=[b_gh], w=[b_gb[k]])
                    self.op("act", lambda e: e.activation(out=acc[k][:, :W], in_=gbuf[k][:, 2:2 + W], func=AF.Identity,
                                                          bias=dbT[:, p:p + 1], scale=dwT[:, 2, p:p + 1]),
                            r=[b_gb[k], b_dw], w=[b_acc[k]])
                    self.op("dve", lambda e: e.scalar_tensor_tensor(out=acc[k][:, :W], in0=gbuf[k][:, 1:1 + W],
                                                                    scalar=dwT[:, 1, p:p + 1], in1=acc[k][:, :W],
                                                                    op0=ALU.mult, op1=ALU.add),
                            r=[b_gb[k], b_dw, b_acc[k]], w=[b_acc[k]])
                    self.op("dve", lambda e: e.scalar_tensor_tensor(out=acc[k][:, :W], in0=gbuf[k][:, 0:W],
                                                                    scalar=dwT[:, 0, p:p + 1], in1=acc[k][:, :W],
                                                                    op0=ALU.mult, op1=ALU.add),
                            r=[b_gb[k], b_dw, b_acc[k]], w=[b_acc[k]])
                    if samp:
                        self.op("act", lambda e: e.copy(out=ghist[:, 0, p:p + 1], in_=gbuf[k][:, 1:2]),
                                r=[b_gb[k]], w=[b_gh])
                        self.op("act", lambda e: e.copy(out=ghist[:, 1, p:p + 1], in_=gbuf[k][:, 2:3]),
                                r=[b_gb[k]], w=[b_gh])
                    else:
                        self.op("act", lambda e: e.copy(out=ghist[:, :, p], in_=gbuf[k][:, W:W + 2]),
                                r=[b_gb[k]], w=[b_gh])
                    self.op("act", lambda e: e.activation(out=acc[k][:, :W], in_=acc[k][:, :W],
                                                          func=AF.Gelu_apprx_tanh),
                            r=[b_acc[k]], w=[b_acc[k]])
                    self.op("dve", lambda e: e.tensor_tensor(out=hh[:, p, :W], in0=acc[k][:, :W], in1=psB[:, :W],
                                                             op=ALU.mult),
                            r=[b_acc[k], bB], w=[b_hh])

                self.gemm(lambda p, k0, kn, h: w_up[:, k0:k0 + kn, h, p * 128:(p + 1) * 128], DC, FC,
                          self.xn, self.b_xn, W, ep_up, wkey='up%d' % li, first=(ti == 0))

                def ep_dn(p, psA, bA, psB, bB):
                    self.op("act", lambda e: e.copy(out=self.yy[:, 2 * p, :W], in_=psA[:, :W]), r=[bA], w=[self.b_yy])
                    self.op("dve", lambda e: e.tensor_copy(out=self.yy[:, 2 * p + 1, :W], in_=psB[:, :W]), r=[bB],
                            w=[self.b_yy])

                self.gemm(lambda p, k0, kn, h: w_dn[:, k0:k0 + kn, p, h, :], FC, DC // 2, hh, b_hh, W, ep_dn, wkey='dn%d' % li, first=(ti == 0))
                self.post_norm_residual(ti, li * 4 + 3)

                if ti == self.NT - 1 or samp:
                    dst = self.ffn_s if samp else self.ffn_p
                    self.op("pe", lambda e: e.transpose(out=self.psb[5][0:88, 0:128],
                                                        in_=ghist[:].rearrange("p t c -> p (t c)"),
                                                        identity=self.ident[:]),
                            r=[b_gh, self.b_const], w=[self.bps[5]])
                    self.op("dve", lambda e: e.tensor_copy(out=hout[:], in_=self.psb[5][0:88, 0:128]),
                            r=[self.bps[5]], w=[b_ho])
                    self.dma(lambda e: e.dma_start(out=dst[li].rearrange("t (c p) -> (t c) p", p=128), in_=hout[:]),
                             r=[b_ho])
            self.tk.barrier()

    def conf_layer(self, li):
        with contextlib.ExitStack() as st:
            self.alloc_rowlocal(st)
            CW = 31
            ubuf = self.sb("ubuf", [128, DC, 30 + 512], F32, st)
            b_ub = Buf("ubuf")
            sig = [self.sb("sig%d" % i, [128, 512], F32, st) for i in range(2)]
            b_sig = [Buf("sig") for _ in range(2)]
            dwT = self.sb("cdwT", [128, CW, DC], F32, st)
            prm = self.sb("cprm", [128, 3, DC], F32, st)
            b_dw = Buf("cdw")
            mean = self.sb("mean", [128, 512], F32, st)
            b_mean = Buf("mean")
            msq = self.sb("msq", [128, 512], F32, st)
            b_msq = Buf("msq")
            hio = self.sb("hio", [30, D], F32, st)
            b_hio = Buf("hio")
            cacc = self.sb("cacc", [128, 512], F32, st); b_cacc = Buf("cacc")
            ctmp = [self.sb("ctmp%d" % i, [128, 512], F32, st) for i in range(2)]; b_ctmp = [Buf("ctmp") for _ in range(2)]
            eps_ln = self.sb("eps_ln", [128, 1], F32, st)
            self.op("dve", lambda e: e.memset(eps_ln[:], LN_EPS), w=[b_dw])
            self.dma(lambda e: e.dma_start(out=dwT[:], in_=self.conv_dw.rearrange("w (c p) -> p w c", p=128),
                                           allow_slow_non_contiguous=True), w=[b_dw])
            for k, src in enumerate((self.conv_dw_b, self.conv_ln_g, self.conv_ln_b)):
                self.dma(lambda e: e.dma_start(out=prm[:, k, :], in_=src.rearrange("o (c p) -> p (o c)", p=128),
                                               allow_slow_non_contiguous=True), w=[b_dw])
            self.op("dve", lambda e: e.memset(ubuf[:, :, 0:30], 0.0), w=[b_ub])
            w1 = self.conv_w_pw1.rearrange("(kc p) (h f) -> p kc h f", p=128, h=2)
            w2 = self.conv_w_pw2.rearrange("(kc p) (f h n) -> p kc f h n", p=128, h=2, n=128)
            hc = self.yy
            b_hc = self.b_yy
            for ti in range(len(self.tiles)):
                t0, W = self.tiles[ti]
                samp = (ti == len(self.tiles) - 1)
                self.pre_norm(ti, li * 4 + 0)
                if samp:
                    self.dma(lambda e: e.dma_start(out=hio[:], in_=self.state_conv[:, :]), w=[b_hio])
                    for c in range(DC):
                        pb = 4 + (c % 2)
                        self.op("pe", lambda e: e.transpose(out=self.psb[pb][:, 0:30], in_=hio[:, c * 128:(c + 1) * 128],
                                                            identity=self.ident[0:30, 0:30]),
                                r=[b_hio, self.b_const], w=[self.bps[pb]])
                        self.op("dve", lambda e: e.tensor_copy(out=ubuf[:, c, 0:30], in_=self.psb[pb][:, 0:30]),
                                r=[self.bps[pb]], w=[b_ub])

                def ep1(p, psA, bA, psB, bB):
                    k = p % 2
                    self.op("act", lambda e: e.activation(out=sig[k][:, :W], in_=psB[:, :W], func=AF.Sigmoid),
                            r=[bB], w=[b_sig[k]])
                    self.op("dve", lambda e: e.tensor_tensor(out=ubuf[:, p, 30:30 + W], in0=psA[:, :W],
                                                             in1=sig[k][:, :W], op=ALU.mult),
                            r=[bA, b_sig[k]], w=[b_ub])

                self.gemm(lambda p, k0, kn, h: w1[:, k0:k0 + kn, h, p * 128:(p + 1) * 128], DC, DC,
                          self.xn, self.b_xn, W, ep1, wkey='pw1', first=(ti == 0))
                for c in range(DC):
                    self.op("act", lambda e: e.activation(out=hc[:, c, :W], in_=ubuf[:, c, 30:30 + W], func=AF.Identity,
                                                          bias=prm[:, 0, c:c + 1], scale=dwT[:, 30, c:c + 1]),
                            r=[b_ub, b_dw], w=[b_hc])
                    for w in range(15):
                        self.op("dve", lambda e: e.scalar_tensor_tensor(out=hc[:, c, :W], in0=ubuf[:, c, w:w + W],
                                                                        scalar=dwT[:, w, c:c + 1], in1=hc[:, c, :W],
                                                                        op0=ALU.mult, op1=ALU.add),
                                r=[b_ub, b_dw, b_hc], w=[b_hc])
                    self.op("act", lambda e: e.activation(out=cacc[:, :W], in_=ubuf[:, c, 15:15 + W], func=AF.Copy,
                                                          scale=dwT[:, 15, c:c + 1]), r=[b_ub, b_dw], w=[b_cacc])
                    for w in range(16, 30):
                        kx = w % 2
                        self.op("act", lambda e: e.activation(out=ctmp[kx][:, :W], in_=ubuf[:, c, w:w + W], func=AF.Copy,
                                                              scale=dwT[:, w, c:c + 1]), r=[b_ub, b_dw], w=[b_ctmp[kx]])
                        self.op("pool", lambda e: e.tensor_tensor(out=cacc[:, :W], in0=cacc[:, :W], in1=ctmp[kx][:, :W], op=ALU.add),
                                r=[b_ctmp[kx], b_cacc], w=[b_cacc])
                    self.op("pool", lambda e: e.tensor_tensor(out=hc[:, c, :W], in0=hc[:, c, :W], in1=cacc[:, :W], op=ALU.add),
                            r=[b_cacc, b_hc], w=[b_hc])
                if ti == self.NT - 1 or samp:
                    c0 = 1 if samp else W
                    for c in range(DC):
                        pb = 4 + (c % 2)
                        self.op("pe", lambda e: e.transpose(out=self.psb[pb][0:30, 0:128], in_=ubuf[:, c, c0:c0 + 30],
                                                            identity=self.ident[:]),
                                r=[b_ub, self.b_const], w=[self.bps[pb]])
                        self.op("act", lambda e: e.copy(out=hio[:, c * 128:(c + 1) * 128], in_=self.psb[pb][0:30, 0:128]),
                                r=[self.bps[pb]], w=[b_hio])
                    dst = self.conv_s if samp else self.conv_p
                    self.dma(lambda e: e.dma_start(out=dst[:, :], in_=hio[:]), r=[b_hio])
                if not samp:
                    self.op("pool", lambda e: e.tensor_copy(out=ubuf[:, :, 0:30], in_=ubuf[:, :, W:W + 30]),
                            r=[b_ub], w=[b_ub])
                self.op("act", lambda e: e.activation(out=self.xn[:, :, :W], in_=hc[:, :, :W], func=AF.Square),
                        r=[b_hc], w=[self.b_xn])
                for c in range(DC):
                    self.op("pe", lambda e: e.matmul(self.psb[4][:, :W], lhsT=self.ones_f[:], rhs=hc[:, c, :W],
                                                     start=(c == 0), stop=(c == DC - 1)),
                            r=[b_hc, self.b_const], w=[self.bps[4]])
                for c in range(DC):
                    self.op("pe", lambda e: e.matmul(self.psb[5][:, :W], lhsT=self.ones_b[:], rhs=self.xn[:, c, :W],
                                                     start=(c == 0), stop=(c == DC - 1)),
                            r=[self.b_xn, self.b_const], w=[self.bps[5]])
                self.op("act", lambda e: e.activation(out=mean[:, :W], in_=self.psb[4][:, :W], func=AF.Copy,
                                                      scale=1.0 / D), r=[self.bps[4]], w=[b_mean])
                self.op("dve", lambda e: e.tensor_tensor(out=msq[:, :W], in0=mean[:, :W], in1=mean[:, :W], op=ALU.mult),
                        r=[b_mean], w=[b_msq])
                self.op("dve", lambda e: e.scalar_tensor_tensor(out=msq[:, :W], in0=self.psb[5][:, :W], scalar=1.0 / D,
                                                                in1=msq[:, :W], op0=ALU.mult, op1=ALU.subtract),
                        r=[self.bps[5], b_msq], w=[b_msq])
                self.op("act", lambda e: e.activation(out=self.rstd[:, :W], in_=msq[:, :W], func=AF.Sqrt,
                                                      bias=eps_ln[:, 0:1], scale=1.0), r=[b_msq, b_dw], w=[self.b_rstd])
                self.op("dve", lambda e: e.reciprocal(out=self.rstd[:, :W], in_=self.rstd[:, :W]),
                        r=[self.b_rstd], w=[self.b_rstd])
                for c in range(DC):
                    self.op("pool", lambda e: e.tensor_tensor(out=hc[:, c, :W], in0=hc[:, c, :W], in1=mean[:, :W],
                                                              op=ALU.subtract), r=[b_hc, b_mean], w=[b_hc])
                    self.op("dve", lambda e: e.tensor_tensor(out=hc[:, c, :W], in0=hc[:, c, :W], in1=self.rstd[:, :W],
                                                             op=ALU.mult), r=[b_hc, self.b_rstd], w=[b_hc])
                    self.op("act", lambda e: e.activation(out=self.xn[:, c, :W], in_=hc[:, c, :W], func=AF.Silu,
                                                          bias=prm[:, 2, c:c + 1], scale=prm[:, 1, c:c + 1]),
                            r=[b_hc, b_dw], w=[self.b_xn])

                def ep2(p, psA, bA, psB, bB):
                    self.op("act", lambda e: e.copy(out=self.yy[:, 2 * p, :W], in_=psA[:, :W]), r=[bA], w=[self.b_yy])
                    self.op("dve", lambda e: e.tensor_copy(out=self.yy[:, 2 * p + 1, :W], in_=psB[:, :W]), r=[bB],
                            w=[self.b_yy])

                self.gemm(lambda p, k0, kn, h: w2[:, k0:k0 + kn, p, h, :], DC, DC // 2, self.xn, self.b_xn, W, ep2, wkey='pw2', first=(ti == 0))
                self.post_norm_residual(ti, li * 4 + 1)
            self.tk.barrier()

    def nat_to_scan(self, src_dram_flat, dst, b_dst, tmp64, b_tmp):
        self.dma(lambda e: e.dma_start(out=tmp64[:], in_=src_dram_flat.rearrange("(s g2) p -> s (g2 p)", g2=2)),
                 w=[b_tmp])
        self.op("pe", lambda e: e.transpose(out=self.psb[5][:, 0:64], in_=tmp64[:], identity=self.ident[0:64, 0:64]),
                r=[b_tmp, self.b_const], w=[self.bps[5]])
        self.op("dve", lambda e: e.tensor_copy(out=dst[:], in_=self.psb[5][:, 0:64]), r=[self.bps[5]], w=[b_dst])

    def s5_layer(self, li):
        TWO_PI = 2.0 * math.pi
        with contextlib.ExitStack() as st:
            self.alloc_rowlocal(st)
            rho = self.sb("rho", [128, 64], F32, st)
            c1 = self.sb("c1", [128, 64], F32, st)
            s1 = self.sb("s1", [128, 64], F32, st)
            hpr = self.sb("hpr", [128, 64], F32, st)
            hpi = self.sb("hpi", [128, 64], F32, st)
            k0r = self.sb("k0r", [128, 64], F32, st)
            k0i = self.sb("k0i", [128, 64], F32, st)
            ktm = self.sb("ktm", [128, 64], F32, st)
            dsk = self.sb("dsk", [128, DC], F32, st)
            b_prm = Buf("s5prm")
            b_hp = Buf("hp")
            b_k0 = Buf("k0")
            tmp64 = self.sb("tmp64", [64, 128], F32, st)
            b_t64 = Buf("t64")
            BBs = self.dscr("BBs", [2, 128, 16, 64])
            CCs = self.dscr("CCs", [2, 128, 64, 16])
            PRs = self.dscr("PRs", [3, 128, 64])
            WBs = self.dscr("WBs", [2, 16, 128, 512], BF16)
            WCs = self.dscr("WCs", [2, 16, 128, 512], BF16)
            TAB = self.dscr("TAB", [2, 64, 128, 512])
            b_dr = Buf("s5dram")
            self.dma(lambda e: e.dma_start(out=dsk[:], in_=self.ssm_d.rearrange("o (c p) -> p (o c)", p=128),
                                           allow_slow_non_contiguous=True), w=[b_prm])
            with contextlib.ExitStack() as st2:
                def T2(name, shape, dt=F32):
                    return self.sb(name, shape, dt, st2)
                ar = T2("p_ar", [128, 64]); ai = T2("p_ai", [128, 64]); ldt = T2("p_ldt", [128, 1])
                dt_ = T2("p_dt", [128, 1]); mag = T2("p_mag", [128, 64]); th = T2("p_th", [128, 64])
                r0 = T2("p_r0", [128, 64]); ri_ = T2("p_ri", [128, 64], I32); rf = T2("p_rf", [128, 64])
                m1 = T2("p_m1", [128, 64]); m2 = T2("p_m2", [128, 64])
                cs = T2("p_cs", [128, 64]); sn = T2("p_sn", [128, 64])
                abr = T2("p_abr", [128, 64]); abi = T2("p_abi", [128, 64]); den = T2("p_den", [128, 64])
                cfr = T2("p_cfr", [128, 64]); cfi = T2("p_cfi", [128, 64])
                bre = T2("p_bre", [128, 64, 16]); bim = T2("p_bim", [128, 64, 16])
                bt1 = T2("p_bt1", [128, 64, 16]); bt2 = T2("p_bt2", [128, 64, 16])
                bbT = T2("p_bbT", [128, 16, 64])
                cin = T2("p_cin", [128, 16, 64]); ccT = T2("p_ccT", [128, 64, 16])
                bp = Buf("prep")
                V = "dve"
                self.dma(lambda e: e.dma_start(out=ar[:], in_=self.ssm_a_re[:, :]), w=[bp])
                self.dma(lambda e: e.dma_start(out=ai[:], in_=self.ssm_a_im[:, :]), w=[bp])
                self.dma(lambda e: e.dma_start(out=ldt[:], in_=self.ssm_log_dt.rearrange("o g -> g o"),
                                               allow_slow_non_contiguous=True), w=[bp])
                self.dma(lambda e: e.dma_start(out=bre[:], in_=self.ssm_b_re[:, :, :]), w=[bp])
                self.dma(lambda e: e.dma_start(out=bim[:], in_=self.ssm_b_im[:, :, :]), w=[bp])
                o = lambda q, f: self.op(q, f, r=[bp], w=[bp])
                o("act", lambda e: e.activation(out=dt_[:], in_=ldt[:], func=AF.Exp))
                o(V, lambda e: e.tensor_scalar(out=mag[:], in0=ar[:], scalar1=dt_[:, 0:1], scalar2=None, op0=ALU.mult))
                o("act", lambda e: e.activation(out=mag[:], in_=mag[:], func=AF.Exp))
                o(V, lambda e: e.tensor_scalar(out=th[:], in0=ai[:], scalar1=dt_[:, 0:1], scalar2=1.0 / TWO_PI,
                                               op0=ALU.mult, op1=ALU.mult))

                def sin_turns(dst, off):
                    o(V, lambda e: e.tensor_scalar(out=r0[:], in0=th[:], scalar1=off, scalar2=None, op0=ALU.add))
                    o(V, lambda e: e.tensor_copy(out=ri_[:], in_=r0[:]))
                    o(V, lambda e: e.tensor_copy(out=rf[:], in_=ri_[:]))
                    o(V, lambda e: e.tensor_tensor(out=r0[:], in0=r0[:], in1=rf[:], op=ALU.subtract))
                    o(V, lambda e: e.tensor_scalar(out=m1[:], in0=r0[:], scalar1=0.5, scalar2=None, op0=ALU.is_gt))
                    o(V, lambda e: e.tensor_scalar(out=m2[:], in0=r0[:], scalar1=-0.5, scalar2=None, op0=ALU.is_lt))
                    o(V, lambda e: e.tensor_tensor(out=r0[:], in0=r0[:], in1=m1[:], op=ALU.subtract))
                    o(V, lambda e: e.tensor_tensor(out=r0[:], in0=r0[:], in1=m2[:], op=ALU.add))
                    o("act", lambda e: e.activation(out=dst[:], in_=r0[:], func=AF.Sin, scale=TWO_PI))

                sin_turns(sn, 0.0)
                sin_turns(cs, 0.25)
                o(V, lambda e: e.tensor_tensor(out=abr[:], in0=mag[:], in1=cs[:], op=ALU.mult))
                o(V, lambda e: e.tensor_tensor(out=abi[:], in0=mag[:], in1=sn[:], op=ALU.mult))
                o(V, lambda e: e.tensor_tensor(out=den[:], in0=ar[:], in1=ar[:], op=ALU.mult))
                o(V, lambda e: e.tensor_tensor(out=m1[:], in0=ai[:], in1=ai[:], op=ALU.mult))
                o(V, lambda e: e.tensor_tensor(out=den[:], in0=den[:], in1=m1[:], op=ALU.add))
                o(V, lambda e: e.reciprocal(out=den[:], in_=den[:]))
                o(V, lambda e: e.tensor_scalar(out=m2[:], in0=abr[:], scalar1=-1.0, scalar2=None, op0=ALU.add))
                o(V, lambda e: e.tensor_tensor(out=cfr[:], in0=m2[:], in1=ar[:], op=ALU.mult))
                o(V, lambda e: e.tensor_tensor(out=m1[:], in0=abi[:], in1=ai[:], op=ALU.mult))
                o(V, lambda e: e.tensor_tensor(out=cfr[:], in0=cfr[:], in1=m1[:], op=ALU.add))
                o(V, lambda e: e.tensor_tensor(out=cfr[:], in0=cfr[:], in1=den[:], op=ALU.mult))
                o(V, lambda e: e.tensor_tensor(out=cfi[:], in0=abi[:], in1=ar[:], op=ALU.mult))
                o(V, lambda e: e.tensor_tensor(out=m1[:], in0=m2[:], in1=ai[:], op=ALU.mult))
                o(V, lambda e: e.tensor_tensor(out=cfi[:], in0=cfi[:], in1=m1[:], op=ALU.subtract))
                o(V, lambda e: e.tensor_tensor(out=cfi[:], in0=cfi[:], in1=den[:], op=ALU.mult))
                for k, src in enumerate((mag, cs, sn)):
                    self.dma(lambda e: e.dma_start(out=PRs[k], in_=src[:]), r=[bp], w=[b_dr])
                self.tk.barrier()
                for k, dst in enumerate((rho, c1, s1)):
                    self.nat_to_scan(PRs[k], dst, b_prm, tmp64, b_t64)
                cfr_b = cfr[:].unsqueeze(2).to_broadcast([128, 64, 16])
                cfi_b = cfi[:].unsqueeze(2).to_broadcast([128, 64, 16])
                for rix in range(2):
                    if rix == 0:
                        o(V, lambda e: e.tensor_tensor(out=bt1[:], in0=bre[:], in1=cfr_b, op=ALU.mult))
                        o(V, lambda e: e.tensor_tensor(out=bt2[:], in0=bim[:], in1=cfi_b, op=ALU.mult))
                        o(V, lambda e: e.tensor_tensor(out=bbT[:].rearrange("g c p -> g p c"), in0=bt1[:], in1=bt2[:],
                                                       op=ALU.subtract))
                    else:
                        o(V, lambda e: e.tensor_tensor(out=bt1[:], in0=bim[:], in1=cfr_b, op=ALU.mult))
                        o(V, lambda e: e.tensor_tensor(out=bt2[:], in0=bre[:], in1=cfi_b, op=ALU.mult))
                        o(V, lambda e: e.tensor_tensor(out=bbT[:].rearrange("g c p -> g p c"), in0=bt1[:], in1=bt2[:],
                                                       op=ALU.add))
                    self.dma(lambda e: e.dma_start(out=BBs[rix], in_=bbT[:]), r=[bp], w=[b_dr])
                    self.tk.barrier()
                for rix, src in enumerate((self.ssm_c_re, self.ssm_c_im)):
                    self.dma(lambda e: e.dma_start(out=cin[:], in_=src[:, :, :]), w=[bp])
                    o("act", lambda e: e.activation(out=ccT[:].rearrange("g p c -> g c p"), in_=cin[:], func=AF.Copy,
                                                    scale=(1.0 if rix == 0 else -1.0)))
                    self.dma(lambda e: e.dma_start(out=CCs[rix], in_=ccT[:]), r=[bp], w=[b_dr])
                    self.tk.barrier()
            self.tk.barrier()
            with contextlib.ExitStack() as st2:
                def T2(name, shape, dt=F32):
                    return self.sb(name, shape, dt, st2)
                bp = Buf("prep2")
                o = lambda q, f: self.op(q, f, r=[bp], w=[bp])
                V = "dve"
                bbl = T2("q_bbl", [128, 16, 64]); ccl = T2("q_ccl", [128, 64, 16])
                mkB = T2("q_mkB", [128, 8]); mkC = T2("q_mkC", [128, 4, 8]); mt = T2("q_mt", [128, 4, 8])
                wbt = T2("q_wbt", [128, 8, 64], BF16); wct = T2("q_wct", [128, 4, 8, 16], BF16)
                b_wt = Buf("wt")
                o("pool", lambda e: e.iota(mkB[:], pattern=[[-16, 8]], base=0, channel_multiplier=1,
                                           allow_small_or_imprecise_dtypes=True))
                o(V, lambda e: e.tensor_scalar(out=mt[:, 0, :], in0=mkB[:], scalar1=0.0, scalar2=None, op0=ALU.is_ge))
                o(V, lambda e: e.tensor_scalar(out=mkB[:], in0=mkB[:], scalar1=15.0, scalar2=None, op0=ALU.is_le))
                o(V, lambda e: e.tensor_tensor(out=mkB[:], in0=mkB[:], in1=mt[:, 0, :], op=ALU.mult))
                o("pool", lambda e: e.iota(mkC[:], pattern=[[-128, 4], [64, 8]], base=0, channel_multiplier=-1,
                                           allow_small_or_imprecise_dtypes=True))
                o(V, lambda e: e.tensor_scalar(out=mt[:], in0=mkC[:], scalar1=-63.0, scalar2=None, op0=ALU.is_ge))
                o(V, lambda e: e.tensor_scalar(out=mkC[:], in0=mkC[:], scalar1=0.0, scalar2=None, op0=ALU.is_le))
                o(V, lambda e: e.tensor_tensor(out=mkC[:], in0=mkC[:], in1=mt[:], op=ALU.mult))
                for rix in range(2):
                    self.dma(lambda e: e.dma_start(out=bbl[:], in_=BBs[rix].rearrange("(k g8) c p -> (g8 c) k p", g8=8)),
                             r=[b_dr], w=[bp])
                    self.dma(lambda e: e.dma_start(out=ccl[:], in_=CCs[rix].rearrange("(s g2) p c -> (g2 p) s c", g2=2)),
                             r=[b_dr], w=[bp])
                    for k in range(16):
                        self.op(V, lambda e: e.tensor_tensor(out=wbt[:], in0=bbl[:, k, :].unsqueeze(1).to_broadcast([128, 8, 64]),
                                                             in1=mkB[:].unsqueeze(2).to_broadcast([128, 8, 64]), op=ALU.mult),
                                r=[bp], w=[b_wt])
                        self.dma(lambda e: e.dma_start(out=WBs[rix, k], in_=wbt[:].rearrange("p a b -> p (a b)")),
                                 r=[b_wt], w=[b_dr])
                        for j in range(4):
                            sidx = 4 * k + j
                            self.op(V, lambda e: e.tensor_tensor(out=wct[:, j], in0=ccl[:, sidx, :].unsqueeze(1).to_broadcast([128, 8, 16]),
                                                                 in1=mkC[:, j, :].unsqueeze(2).to_broadcast([128, 8, 16]), op=ALU.mult),
                                    r=[bp], w=[b_wt])
                        self.dma(lambda e: e.dma_start(out=WCs[rix, k], in_=wct[:].rearrange("p j a b -> p (j a b)")),
                                 r=[b_wt], w=[b_dr])
                tc_ = T2("q_tc", [128, 8, 512]); ts_ = T2("q_ts", [128, 8, 512])
                a1 = T2("q_a1", [128, 8, 256]); a2 = T2("q_a2", [128, 8, 256])
                pc = T2("q_pc", [128, 8]); psn = T2("q_ps", [128, 8]); pt1 = T2("q_pt1", [128, 8]); pt2 = T2("q_pt2", [128, 8])
                b_tb = Buf("tb")
                ot = lambda q, f: self.op(q, f, r=[b_tb, b_prm], w=[b_tb])
                for sg in range(8):
                    ot(V, lambda e: e.memset(tc_[:, :, 0:1], 1.0))
                    ot(V, lambda e: e.memset(ts_[:, :, 0:1], 0.0))
                    ot(V, lambda e: e.tensor_copy(out=pc[:], in_=c1[:, sg * 8:(sg + 1) * 8]))
                    ot(V, lambda e: e.tensor_copy(out=psn[:], in_=s1[:, sg * 8:(sg + 1) * 8]))
                    m = 1
                    while m < 512:
                        pcb = pc[:].unsqueeze(2).to_broadcast([128, 8, m])
                        psb_ = psn[:].unsqueeze(2).to_broadcast([128, 8, m])
                        ot(V, lambda e: e.tensor_tensor(out=a1[:, :, :m], in0=tc_[:, :, 0:m], in1=pcb, op=ALU.mult))
                        ot(V, lambda e: e.tensor_tensor(out=a2[:, :, :m], in0=ts_[:, :, 0:m], in1=psb_, op=ALU.mult))
                        ot(V, lambda e: e.tensor_tensor(out=tc_[:, :, m:2 * m], in0=a1[:, :, :m], in1=a2[:, :, :m], op=ALU.subtract))
                        ot(V, lambda e: e.tensor_tensor(out=a1[:, :, :m], in0=tc_[:, :, 0:m], in1=psb_, op=ALU.mult))
                        ot(V, lambda e: e.tensor_tensor(out=a2[:, :, :m], in0=ts_[:, :, 0:m], in1=pcb, op=ALU.mult))
                        ot(V, lambda e: e.tensor_tensor(out=ts_[:, :, m:2 * m], in0=a1[:, :, :m], in1=a2[:, :, :m], op=ALU.add))
                        ot(V, lambda e: e.tensor_tensor(out=pt1[:], in0=pc[:], in1=pc[:], op=ALU.mult))
                        ot(V, lambda e: e.tensor_tensor(out=pt2[:], in0=psn[:], in1=psn[:], op=ALU.mult))
                        ot(V, lambda e: e.tensor_tensor(out=psn[:], in0=psn[:], in1=pc[:], op=ALU.mult))
                        ot(V, lambda e: e.tensor_scalar(out=psn[:], in0=psn[:], scalar1=2.0, scalar2=None, op0=ALU.mult))
                        ot(V, lambda e: e.tensor_tensor(out=pc[:], in0=pt1[:], in1=pt2[:], op=ALU.subtract))
                        m *= 2
                    self.dma(lambda e: e.dma_start(out=TAB[0, sg * 8:(sg + 1) * 8].rearrange("s p t -> p s t"), in_=tc_[:]),
                             r=[b_tb], w=[b_dr])
                    self.dma(lambda e: e.dma_start(out=TAB[1, sg * 8:(sg + 1) * 8].rearrange("s p t -> p s t"), in_=ts_[:]),
                             r=[b_tb], w=[b_dr], q="act")
                self.tk.barrier()
            self.tk.barrier()
            NB = 2
            NBT = 3
            tabc = [self.sb("tabc%d" % i, [128, 512], F32, st) for i in range(NBT)]
            tabs = [self.sb("tabs%d" % i, [128, 512], F32, st) for i in range(NBT)]
            b_tab = [Buf("tab") for _ in range(NBT)]
            bsr = [self.sb("bsr%d" % i, [128, 512], F32, st) for i in range(NB)]
            bsi = [self.sb("bsi%d" % i, [128, 512], F32, st) for i in range(NB)]
            b_bs = [Buf("bs") for _ in range(NB)]
            qa = [self.sb("sqa%d" % i, [128, 512], F32, st) for i in range(4)]
            qb = [self.sb("sqb%d" % i, [128, 512], F32, st) for i in range(4)]
            b_qa = Buf("qa"); b_qb = Buf("qb")
            vre = [self.sb("vre%d" % i, [128, 512], F32, st) for i in range(NB)]
            vim = [self.sb("vim%d" % i, [128, 512], F32, st) for i in range(NB)]
            b_vv = [Buf("v") for _ in range(NB)]
            kre = self.sb("kre", [128, 512], F32, st); kim = self.sb("kim", [128, 512], F32, st)
            hre = self.sb("hre", [128, 512], F32, st); him = self.sb("him", [128, 512], F32, st)
            b_v = Buf("v"); b_k = Buf("k"); b_h = Buf("h")
            hrb = [self.sb("hrb%d" % i, [128, 512], BF16, st) for i in range(NB)]
            hib = [self.sb("hib%d" % i, [128, 512], BF16, st) for i in range(NB)]
            b_hb = [Buf("hb") for _ in range(NB)]
            wBk = [self.sb("wBk%d" % i, [128, 2, 512], BF16, st) for i in range(NB)]
            wCk = [self.sb("wCk%d" % i, [128, 2, 512], BF16, st) for i in range(NB)]
            b_wk = [Buf("wk") for _ in range(NB)]
            yb = self.sb("yb", [128, DC, 512], BF16, st)
            b_yb = Buf("yb")
            sig = [self.sb("s5sig%d" % i, [128, 512], F32, st) for i in range(2)]
            b_sig = [Buf("sig") for _ in range(2)]
            hfo = self.sb("hfo", [64, 128], F32, st)
            b_hfo = Buf("hfo")
            xnf = self.yy
            b_xnf = self.b_yy
            wg = self.ssm_w_glu.rearrange("(kc p) (h f) -> p kc h f", p=128, h=2)
            self.op("dve", lambda e: e.memset(hpr[:], 0.0), w=[b_hp])
            self.op("dve", lambda e: e.memset(hpi[:], 0.0), w=[b_hp])
            sctr = 0
            for ti in range(len(self.tiles)):
                t0, W = self.tiles[ti]
                samp = (ti == len(self.tiles) - 1)
                if samp:
                    self.nat_to_scan(self.state_ssm_re, hpr, b_hp, tmp64, b_t64)
                    self.nat_to_scan(self.state_ssm_im, hpi, b_hp, tmp64, b_t64)
                self.pre_norm(ti, li * 4 + 0, xn_f32=xnf, b_xnf=b_xnf)
                ok = lambda f: self.op("dve", f, r=[b_hp, b_prm, b_k0], w=[b_k0])
                ok(lambda e: e.tensor_tensor(out=k0r[:], in0=c1[:], in1=hpr[:], op=ALU.mult))
                ok(lambda e: e.tensor_tensor(out=ktm[:], in0=s1[:], in1=hpi[:], op=ALU.mult))
                ok(lambda e: e.tensor_tensor(out=k0r[:], in0=k0r[:], in1=ktm[:], op=ALU.subtract))
                ok(lambda e: e.tensor_tensor(out=k0i[:], in0=s1[:], in1=hpr[:], op=ALU.mult))
                ok(lambda e: e.tensor_tensor(out=ktm[:], in0=c1[:], in1=hpi[:], op=ALU.mult))
                ok(lambda e: e.tensor_tensor(out=k0i[:], in0=k0i[:], in1=ktm[:], op=ALU.add))
                lastc = 0 if samp else W - 1

                def stageA(sidx):
                    k, j = sidx // 4, sidx % 4
                    kb = k % NB
                    i2 = sidx % NB
                    if j == 0:
                        self.dma(lambda e: e.dma_start(out=wBk[kb][:], in_=WBs[:, k].rearrange("r p n -> p r n")),
                                 r=[b_dr], w=[b_wk[kb]])
                        self.dma(lambda e: e.dma_start(out=wCk[kb][:], in_=WCs[:, k].rearrange("r p n -> p r n")),
                                 r=[b_dr], w=[b_wk[kb]])
                    i4 = sidx % NBT
                    self.dma(lambda e: e.dma_start(out=tabc[i4][:, :W], in_=TAB[0, sidx, :, 0:W]), r=[b_dr], w=[b_tab[i4]])
                    self.dma(lambda e: e.dma_start(out=tabs[i4][:, :W], in_=TAB[1, sidx, :, 0:W]), r=[b_dr], w=[b_tab[i4]])
                    pr, pi_ = (i2 * 2, i2 * 2 + 1)
                    self.op("pe", lambda e: e.matmul(self.psb[pr][:, :W], lhsT=wBk[kb][:, 0, j * 128:(j + 1) * 128],
                                                     rhs=self.xn[:, k, :W], start=True, stop=True),
                            r=[b_wk[kb], self.b_xn], w=[self.bps[pr]])
                    self.op("pe", lambda e: e.matmul(self.psb[pi_][:, :W], lhsT=wBk[kb][:, 1, j * 128:(j + 1) * 128],
                                                     rhs=self.xn[:, k, :W], start=True, stop=True),
                            r=[b_wk[kb], self.b_xn], w=[self.bps[pi_]])
                    self.op("act", lambda e: e.copy(out=bsr[i2][:, :W], in_=self.psb[pr][:, :W]), r=[self.bps[pr]], w=[b_bs[i2]])
                    self.op("act", lambda e: e.copy(out=bsi[i2][:, :W], in_=self.psb[pi_][:, :W]), r=[self.bps[pi_]], w=[b_bs[i2]])
                    TCt, TSt = tabc[i4], tabs[i4]
                    rd = [b_bs[i2], b_tab[i4]]
                    self.op("pool", lambda e: e.tensor_tensor(out=qa[0][:, :W], in0=TSt[:, :W], in1=bsi[i2][:, :W], op=ALU.mult), r=rd, w=[b_qa])
                    self.op("pool", lambda e: e.tensor_tensor(out=qa[1][:, :W], in0=TCt[:, :W], in1=bsr[i2][:, :W], op=ALU.mult), r=rd, w=[b_qa])
                    self.op("pool", lambda e: e.tensor_tensor(out=qa[2][:, :W], in0=TSt[:, :W], in1=bsr[i2][:, :W], op=ALU.mult), r=rd, w=[b_qa])
                    self.op("pool", lambda e: e.tensor_tensor(out=qa[3][:, :W], in0=TCt[:, :W], in1=bsi[i2][:, :W], op=ALU.mult), r=rd, w=[b_qa])
                    self.op("pool", lambda e: e.tensor_tensor(out=vre[i2][:, :W], in0=qa[1][:, :W], in1=qa[0][:, :W], op=ALU.add), r=[b_qa], w=[b_vv[i2]])
                    self.op("pool", lambda e: e.tensor_tensor(out=vim[i2][:, :W], in0=qa[3][:, :W], in1=qa[2][:, :W], op=ALU.subtract), r=[b_qa], w=[b_vv[i2]])

                def stageB(sidx):
                    k, j = sidx // 4, sidx % 4
                    kb = k % NB
                    i2 = sidx % NB
                    ybank = 6 + (k % 2)
                    i4 = sidx % NBT
                    TCt, TSt = tabc[i4], tabs[i4]
                    rb = rho[:, sidx:sidx + 1].to_broadcast([128, W])
                    self.op("dve", lambda e: e.tensor_tensor_scan(out=kre[:, :W], data0=rb, data1=vre[i2][:, :W],
                                                                  initial=k0r[:, sidx:sidx + 1], op0=ALU.mult, op1=ALU.add),
                            r=[b_vv[i2], b_k0, b_prm], w=[b_k])
                    self.op("dve", lambda e: e.tensor_tensor_scan(out=kim[:, :W], data0=rb, data1=vim[i2][:, :W],
                                                                  initial=k0i[:, sidx:sidx + 1], op0=ALU.mult, op1=ALU.add),
                            r=[b_vv[i2], b_k0, b_prm], w=[b_k])
                    rk = [b_k, b_tab[i4]]
                    self.op("pool", lambda e: e.tensor_tensor(out=qb[0][:, :W], in0=TSt[:, :W], in1=kim[:, :W], op=ALU.mult), r=rk, w=[b_qb])
                    self.op("dve", lambda e: e.tensor_tensor(out=qb[1][:, :W], in0=TCt[:, :W], in1=kre[:, :W], op=ALU.mult), r=rk, w=[b_qb])
                    self.op("dve", lambda e: e.tensor_tensor(out=qb[2][:, :W], in0=TSt[:, :W], in1=kre[:, :W], op=ALU.mult), r=rk, w=[b_qb])
                    self.op("dve", lambda e: e.tensor_tensor(out=qb[3][:, :W], in0=TCt[:, :W], in1=kim[:, :W], op=ALU.mult), r=rk, w=[b_qb])
                    self.op("dve", lambda e: e.tensor_tensor(out=hre[:, :W], in0=qb[1][:, :W], in1=qb[0][:, :W], op=ALU.subtract), r=[b_qb], w=[b_h])
                    self.op("dve", lambda e: e.tensor_tensor(out=him[:, :W], in0=qb[2][:, :W], in1=qb[3][:, :W], op=ALU.add), r=[b_qb], w=[b_h])
                    self.op("act", lambda e: e.copy(out=hrb[i2][:, :W], in_=hre[:, :W]), r=[b_h], w=[b_hb[i2]])
                    self.op("act", lambda e: e.copy(out=hib[i2][:, :W], in_=him[:, :W]), r=[b_h], w=[b_hb[i2]])
                    self.op("act", lambda e: e.copy(out=hpr[:, sidx:sidx + 1], in_=hre[:, lastc:lastc + 1]), r=[b_h, b_k0], w=[b_hp])
                    self.op("act", lambda e: e.copy(out=hpi[:, sidx:sidx + 1], in_=him[:, lastc:lastc + 1]), r=[b_h, b_k0], w=[b_hp])
                    self.op("pe", lambda e: e.matmul(self.psb[ybank][:, :W], lhsT=wCk[kb][:, 0, j * 128:(j + 1) * 128],
                                                     rhs=hrb[i2][:, :W], start=(j == 0), stop=False),
                            r=[b_wk[kb], b_hb[i2]], w=[self.bps[ybank]])
                    self.op("pe", lambda e: e.matmul(self.psb[ybank][:, :W], lhsT=wCk[kb][:, 1, j * 128:(j + 1) * 128],
                                                     rhs=hib[i2][:, :W], start=False, stop=(j == 3)),
                            r=[b_wk[kb], b_hb[i2]], w=[self.bps[ybank]])
                    if j == 3:
                        self.op("dve", lambda e: e.scalar_tensor_tensor(out=yb[:, k, :W], in0=xnf[:, k, :W], scalar=dsk[:, k:k + 1],
                                                                        in1=self.psb[ybank][:, :W], op0=ALU.mult, op1=ALU.add),
                                r=[b_xnf, b_prm, self.bps[ybank]], w=[b_yb])

                stageA(0)
                for sidx in range(64):
                    if sidx + 1 < 64:
                        stageA(sidx + 1)
                    stageB(sidx)

                def epg(p, psA, bA, psB, bB):
                    kk = p % 2
                    self.op("act", lambda e: e.activation(out=sig[kk][:, :W], in_=psB[:, :W], func=AF.Sigmoid),
                            r=[bB], w=[b_sig[kk]])
                    self.op("dve", lambda e: e.tensor_tensor(out=self.yy[:, p, :W], in0=psA[:, :W], in1=sig[kk][:, :W],
                                                             op=ALU.mult), r=[bA, b_sig[kk]], w=[self.b_yy])

                self.gemm(lambda p, k0, kn, h: wg[:, k0:k0 + kn, h, p * 128:(p + 1) * 128], DC, DC, yb, b_yb, W, epg, wkey='glu', first=(ti == 0))
                self.post_norm_residual(ti, li * 4 + 1)
                if ti == self.NT - 1 or samp:
                    for rix, (src, dstp, dsts) in enumerate(((hpr, self.ssm_re_p, self.ssm_re_s), (hpi, self.ssm_im_p, self.ssm_im_s))):
                        dst = dsts if samp else dstp
                        self.op("pe", lambda e: e.transpose(out=self.psb[5][0:64, 0:128], in_=src[:], identity=self.ident[:]),
                                r=[b_hp, self.b_const], w=[self.bps[5]])
                        self.op("dve", lambda e: e.tensor_copy(out=hfo[:], in_=self.psb[5][0:64, 0:128]), r=[self.bps[5]], w=[b_hfo])
                        self.dma(lambda e: e.dma_start(out=dst.rearrange("(s g2) p -> s (g2 p)", g2=2), in_=hfo[:]), r=[b_hfo])
            self.tk.barrier()

    def nsa_declare(self):
        d = self.din
        o = self.dout
        T = self.T
        for k, shp in STRUCT_SHAPES.items():
            setattr(self, "c_" + k, d(k, shp))
        self.rel_bias = d("rel_bias", [32, 16])
        self.page_table = d("page_table", [1, 128], I32)
        self.nsa_w_q = d("nsa_w_q", [2, D, D])
        self.nsa_w_kv = d("nsa_w_kv", [2, D, 3072])
        self.nsa_cmp_pe = d("nsa_cmp_pe", [2, 32, 2, 128])
        self.nsa_cmp_w1 = d("nsa_cmp_w1", [2, 2, 32, 128, 128])
        self.nsa_cmp_w2 = d("nsa_cmp_w2", [2, 2, 128, 128])
        self.nsa_w_gate = d("nsa_w_gate", [2, D, 48])
        self.nsa_w_o = d("nsa_w_o", [2, D, D])
        self.cache_cmp = [d("cache_cmp%d" % i, [self.NPHYS * 128, 1024]) for i in range(2)]
        self.cache_slc = [d("cache_slc%d" % i, [self.NPHYS * 128 * 4, 256]) for i in range(2)]
        self.cache_win = d("cache_win", [2, 512, 1024])
        WP = min(512, T)
        self.WP = WP
        self.kvo_p = [o(n, [2, T, 1024]) for n in ("cmp_kv_p", "slc_kv_p")] + [o("win_kv_p", [2, WP, 1024])]
        self.kvo_s = [o(n, [2, 1, 1024]) for n in ("cmp_kv_s", "slc_kv_s")] + [o("win_kv_s", [2, 512, 1024])]
        TT = self.TT
        self.QT = self.dscr("QT", [16, 128, TT], BF16)
        self.KVT = self.dscr("KVT", [3, 4, 2, 128, TT], BF16)
        self.VTOK = self.dscr("VTOK", [3, 4, TT, 128], BF16)
        self.GT = self.dscr("GT", [TT, 48])
        self.OT = self.dscr("OT", [16, 128, TT], BF16)
        self.BD = self.dscr("BD", [16, 4, 128, 128])
        self.BCs = self.dscr("BCs", [16, 248, 128])
        self.b_nsa_dr = Buf("nsadram")
        self.bias_built = False

    def build_bias_tables(self, st):
        with contextlib.ExitStack() as s2:
            relb = self.sb("relb", [33, 16], F32, s2)
            b_rb = Buf("relb")
            self.op("dve", lambda e: e.memset(relb[:], -30000.0), w=[b_rb])
            self.dma(lambda e: e.dma_start(out=relb[0:32, :], in_=self.rel_bias[:, :]), w=[b_rb])
            NB = 2
            ohb = [self.sb("ohb%d" % i, [33, 2048], F32, s2) for i in range(NB)]
            b_oh = [Buf("oh") for _ in range(NB)]
            ob = [self.sb("ob%d" % i, [16, 2048], F32, s2) for i in range(NB)]
            b_ob = [Buf("ob") for _ in range(NB)]
            ctr = 0
            for src, dst, n in ((self.c_oh_pk, self.BD.rearrange("h k n q -> h (k n q)"), 65536),
                                (self.c_oh_cmp, self.BCs.rearrange("h m q -> h (m q)"), 248 * 128)):
                for c0 in range(0, n, 2048):
                    cw = min(2048, n - c0)
                    i = ctr % NB
                    ctr += 1
                    self.dma(lambda e: e.dma_start(out=ohb[i][:, :cw], in_=src[:, c0:c0 + cw]), w=[b_oh[i]])
                    for k in range(0, cw, 512):
                        kw = min(512, cw - k)
                        pb = 4 + ((k // 512) % 2)
                        self.op("pe", lambda e: e.matmul(self.psb[pb][0:16, :kw], lhsT=relb[:, :], rhs=ohb[i][:, k:k + kw],
                                                         start=True, stop=True), r=[b_rb, b_oh[i]], w=[self.bps[pb]])
                        self.op("act", lambda e: e.copy(out=ob[i][:, k:k + kw], in_=self.psb[pb][0:16, :kw]),
                                r=[self.bps[pb]], w=[b_ob[i]])
                    self.dma(lambda e: e.dma_start(out=dst[:, c0:c0 + cw], in_=ob[i][:, :cw]), r=[b_ob[i]],
                             w=[self.b_nsa_dr], q="act")
            ohs = self.sb("ohs", [33, 17, 128], F32, s2)
            self.dma(lambda e: e.dma_start(out=ohs[:], in_=self.c_oh_s[:, :, :]), w=[b_oh[0]])
            for t in range(17):
                pb = 4 + (t % 2)
                self.op("pe", lambda e: e.matmul(self.psb[pb][:, 0:16], lhsT=ohs[:, t, :], rhs=relb[:, :], start=True, stop=True),
                        r=[b_rb, b_oh[0]], w=[self.bps[pb]])
                self.op("dve", lambda e: e.tensor_copy(out=self.BS[:, t, :], in_=self.psb[pb][:, 0:16]), r=[self.bps[pb]],
                        w=[self.b_BS])
            self.tk.barrier()

    def nsa_layer(self, li):
        j = 0 if li == 0 else 1
        T = self.T
        NQ = T // 128
        SC = 128.0 ** -0.5
        tk = self.tk
        if not self.bias_built:
            self.BS = self.sb("BS", [128, 17, 16], F32)
            self.b_BS = Buf("BS")
            self.build_bias_tables(None)
            self.bias_built = True
        with contextlib.ExitStack() as st:
            self.alloc_rowlocal(st)
            qt = self.sb("qt", [128, 16, 512], BF16, st); b_qt = Buf("qt")
            ktmp = [self.sb("ktmp%d" % i, [128, 2, 512], F32, st) for i in range(2)]; b_kt = [Buf("kt") for _ in range(2)]
            kbf = [self.sb("kbf%d" % i, [128, 2, 512], BF16, st) for i in range(2)]; b_kb = [Buf("kb") for _ in range(2)]
            stg = [self.sb("stg%d" % i, [128, 4, 2, 128], F32, st) for i in range(2)]; b_stg = [Buf("stg") for _ in range(2)]
            vbf = [self.sb("vbf%d" % i, [128, 4, 128], BF16, st) for i in range(2)]; b_vb = [Buf("vb") for _ in range(2)]
            wgf = self.sb("wgf", [128, 16, 48], F32, st); wgb = self.sb("wgb", [128, 16, 48], BF16, st); b_wg = Buf("wg")
            gsb = self.sb("gsb", [48, 512], F32, st); b_gs = Buf("gs")
            gtk = self.sb("gtk", [128, 4, 48], F32, st); b_gt = Buf("gtk")
            self.dma(lambda e: e.dma_start(out=wgf[:], in_=self.nsa_w_gate[j].rearrange("(kc p) n -> p kc n", p=128)), w=[b_wg])
            self.op("pool", lambda e: e.tensor_copy(out=wgb[:], in_=wgf[:]), r=[b_wg], w=[b_wg])
            wq = self.nsa_w_q[j].rearrange("(kc p) (f h n) -> p kc f h n", p=128, h=2, n=128)
            wkv = self.nsa_w_kv[j].rearrange("(kc p) (f h n) -> p kc f h n", p=128, h=2, n=128)
            QTv = self.QT.rearrange("h p t -> p h t")
            for ti in range(len(self.tiles)):
                t0, W = self.tiles[ti]
                samp = (ti == len(self.tiles) - 1)
                self.pre_norm(ti, li * 4 + 0)

                def epq(p, psA, bA, psB, bB):
                    self.op("act", lambda e: e.activation(out=qt[:, 2 * p, :W], in_=psA[:, :W], func=AF.Copy, scale=SC), r=[bA], w=[b_qt])
                    self.op("dve", lambda e: e.tensor_scalar(out=qt[:, 2 * p + 1, :W], in0=psB[:, :W], scalar1=SC, scalar2=None,
                                                             op0=ALU.mult), r=[bB], w=[b_qt])
                self.gemm(lambda p, k0, kn, h: wq[:, k0:k0 + kn, p, h, :], DC, 8, self.xn, self.b_xn, W, epq, wkey='wq%d' % j, first=(ti == 0))
                self.dma(lambda e: e.dma_start(out=QTv[:, :, t0:t0 + W], in_=qt[:, :, :W]), r=[b_qt], w=[self.b_nsa_dr])

                def epkv(p, psA, bA, psB, bB):
                    k = p % 2
                    br, g = p // 4, p % 4
                    self.op("act", lambda e: e.copy(out=ktmp[k][:, 0, :W], in_=psA[:, :W]), r=[bA], w=[b_kt[k]])
                    self.op("dve", lambda e: e.tensor_copy(out=ktmp[k][:, 1, :W], in_=psB[:, :W]), r=[bB], w=[b_kt[k]])
                    self.op("act", lambda e: e.copy(out=kbf[k][:, :, :W], in_=ktmp[k][:, :, :W]), r=[b_kt[k]], w=[b_kb[k]])
                    self.dma(lambda e: e.dma_start(out=self.KVT[br, g].rearrange("c p t -> p c t")[:, :, t0:t0 + W],
                                                   in_=kbf[k][:, :, :W]), r=[b_kb[k]], w=[self.b_nsa_dr], q="act")
                    nb = 1 if samp else 4
                    bw = W if samp else 128
                    for c in range(2):
                        pb = 4 + c
                        for tb in range(nb):
                            self.op("pe", lambda e: e.transpose(out=self.psb[pb][0:bw, tb * 128:(tb + 1) * 128],
                                                                in_=ktmp[k][:, c, tb * 128:tb * 128 + bw], identity=self.ident[:]),
                                    r=[b_kt[k], self.b_const], w=[self.bps[pb]])
                        if c == 0:
                            self.op("act", lambda e: e.copy(out=stg[k][0:bw, 0:nb, 0, :],
                                                            in_=self.psb[pb][0:bw, 0:nb * 128].rearrange("p (a d) -> p a d", d=128)),
                                    r=[self.bps[pb]], w=[b_stg[k]])
                        else:
                            self.op("dve", lambda e: e.tensor_copy(out=stg[k][0:bw, 0:nb, 1, :],
                                                                   in_=self.psb[pb][0:bw, 0:nb * 128].rearrange("p (a d) -> p a d", d=128)),
                                    r=[self.bps[pb]], w=[b_stg[k]])
                            self.op("dve", lambda e: e.tensor_copy(out=vbf[k][0:bw, 0:nb, :], in_=stg[k][0:bw, 0:nb, 1, :]),
                                    r=[b_stg[k]], w=[b_vb[k]])
                    if samp:
                        if br < 2:
                            self.dma(lambda e: e.dma_start(out=self.kvo_s[br][j, 0:1, g * 256:(g + 1) * 256],
                                                           in_=stg[k][0:1, 0, :, :].rearrange("p c d -> p (c d)")), r=[b_stg[k]])
                        else:
                            self.dma(lambda e: e.dma_start(out=self.kvo_s[2][j, 511:512, g * 256:(g + 1) * 256],
                                                           in_=stg[k][0:1, 0, :, :].rearrange("p c d -> p (c d)")), r=[b_stg[k]])
                        self.dma(lambda e: e.dma_start(out=self.VTOK[br, g, T:T + W, :], in_=vbf[k][0:W, 0, :]),
                                 r=[b_vb[k]], w=[self.b_nsa_dr], q="act")
                    else:
                        if br < 2:
                            dst = self.kvo_p[br][j, t0:t0 + 512, g * 256:(g + 1) * 256].rearrange("(a p) (c d) -> p a c d", p=128, c=2)
                            for c in range(2):
                                self.dma(lambda e: e.dma_start(out=dst[:, :, c, :], in_=stg[k][:, :, c, :]), r=[b_stg[k]])
                        elif t0 + 512 > T - self.WP:
                            r0 = t0 - (T - self.WP)
                            dst = self.kvo_p[2][j, r0:r0 + 512, g * 256:(g + 1) * 256].rearrange("(a p) (c d) -> p a c d", p=128, c=2)
                            for c in range(2):
                                self.dma(lambda e: e.dma_start(out=dst[:, :, c, :], in_=stg[k][:, :, c, :]), r=[b_stg[k]])
                        self.dma(lambda e: e.dma_start(out=self.VTOK[br, g, t0:t0 + 512, :].rearrange("(a p) d -> p a d", p=128),
                                                       in_=vbf[k][:, :, :]), r=[b_vb[k]], w=[self.b_nsa_dr], q="act")
                self.gemm(lambda p, k0, kn, h: wkv[:, k0:k0 + kn, p, h, :], DC, 12, self.xn, self.b_xn, W, epkv, wkey='wkv%d' % j, first=(ti == 0))
                for kc in range(DC):
                    self.op("pe", lambda e: e.matmul(self.psb[5][0:48, :W], lhsT=wgb[:, kc, :], rhs=self.xn[:, kc, :W],
                                                     start=(kc == 0), stop=(kc == DC - 1)), r=[b_wg, self.b_xn], w=[self.bps[5]])
                self.op("act", lambda e: e.activation(out=gsb[:, :W], in_=self.psb[5][0:48, :W], func=AF.Sigmoid), r=[self.bps[5]], w=[b_gs])
                nb = 1 if samp else 4
                bw = W if samp else 128
                for tb in range(nb):
                    self.op("pe", lambda e: e.transpose(out=self.psb[4][0:bw, tb * 48:(tb + 1) * 48], in_=gsb[:, tb * 128:tb * 128 + bw],
                                                        identity=self.ident[0:48, 0:48]), r=[b_gs, self.b_const], w=[self.bps[4]])
                self.op("dve", lambda e: e.tensor_copy(out=gtk[0:bw, 0:nb, :], in_=self.psb[4][0:bw, 0:nb * 48].rearrange("p (a n) -> p a n", n=48)),
                        r=[self.bps[4]], w=[b_gt])
                self.dma(lambda e: e.dma_start(out=self.GT[t0:t0 + nb * bw, :].rearrange("(a p) n -> p a n", p=bw), in_=gtk[0:bw, 0:nb, :]),
                         r=[b_gt], w=[self.b_nsa_dr])
            self.tk.barrier()
        self.dma(lambda e: e.dma_start(out=self.kvo_s[2][j, 0:511, :], in_=self.cache_win[j, 1:512, :]))
        with contextlib.ExitStack() as st:
            w1f = self.sb("w1f", [128, 32, 128], F32, st)
            w1b = self.sb("w1b", [128, 2, 32, 128], BF16, st)
            w2f = self.sb("w2f", [128, 2, 128], F32, st)
            w2b = self.sb("w2b", [128, 2, 128], BF16, st)
            pef = self.sb("pef", [64, 128], F32, st)
            peT = self.sb("peT", [128, 32, 2], BF16, st)
            pebias = self.sb("pebias", [128, 2], F32, st)
            b_cw = Buf("cw")
            for c in range(2):
                self.dma(lambda e: e.dma_start(out=w1f[:], in_=self.nsa_cmp_w1[j, c].rearrange("s d e -> d s e")), w=[b_cw])
                self.op("pool", lambda e: e.tensor_copy(out=w1b[:, c], in_=w1f[:]), r=[b_cw], w=[b_cw])
            self.dma(lambda e: e.dma_start(out=w2f[:], in_=self.nsa_cmp_w2[j].rearrange("c e d -> e c d")), w=[b_cw])
            self.op("pool", lambda e: e.tensor_copy(out=w2b[:], in_=w2f[:]), r=[b_cw], w=[b_cw])
            self.dma(lambda e: e.dma_start(out=pef[:], in_=self.nsa_cmp_pe[j].rearrange("s c d -> (s c) d")), w=[b_cw])
            self.op("pe", lambda e: e.transpose(out=self.psb[5][:, 0:64], in_=pef[:], identity=self.ident[0:64, 0:64]),
                    r=[b_cw, self.b_const], w=[self.bps[5]])
            self.op("dve", lambda e: e.tensor_copy(out=peT[:].rearrange("p s c -> p (s c)"), in_=self.psb[5][:, 0:64]), r=[self.bps[5]], w=[b_cw])
            for c in range(2):
                for s in range(32):
                    self.op("pe", lambda e: e.matmul(self.psb[5][:, 64 + c:65 + c], lhsT=w1b[:, c, s, :], rhs=peT[:, s, c:c + 1],
                                                     start=(s == 0 and c == 0), stop=(s == 31)), r=[b_cw], w=[self.bps[5]])
            self.op("dve", lambda e: e.tensor_copy(out=pebias[:], in_=self.psb[5][:, 64:66]), r=[self.bps[5]], w=[b_cw])
            ght = [self.sb("ght%d" % i, [128, 128], BF16, st) for i in range(2)]
            b_gh = [Buf("ght") for _ in range(2)]

            def compress(xT_ap_fn, bx, ncols, kdst, vdst, bdst, ctr0):
                for c in range(2):
                    i = (ctr0 + c) % 2
                    pb = i
                    for s in range(32):
                        self.op("pe", lambda e: e.matmul(self.psb[pb][:, 0:ncols], lhsT=w1b[:, c, s, :], rhs=xT_ap_fn(c, s),
                                                         start=(s == 0), stop=(s == 31)), r=[b_cw, bx], w=[self.bps[pb]])
                    self.op("act", lambda e: e.activation(out=ght[i][:, 0:ncols], in_=self.psb[pb][:, 0:ncols], func=AF.Gelu_apprx_tanh,
                                                          bias=pebias[:, c:c + 1], scale=1.0), r=[self.bps[pb], b_cw], w=[b_gh[i]])
                    pb2 = 2 + i
                    if c == 0:
                        self.op("pe", lambda e: e.matmul(self.psb[pb2][:, 0:ncols], lhsT=w2b[:, 0, :], rhs=ght[i][:, 0:ncols], start=True, stop=True),
                                r=[b_cw, b_gh[i]], w=[self.bps[pb2]])
                        self.op("dve", lambda e: e.tensor_copy(out=kdst, in_=self.psb[pb2][:, 0:ncols]), r=[self.bps[pb2]], w=[bdst])
                    else:
                        self.op("pe", lambda e: e.matmul(self.psb[pb2][0:ncols, 0:128], lhsT=ght[i][:, 0:ncols], rhs=w2b[:, 1, :], start=True, stop=True),
                                r=[b_cw, b_gh[i]], w=[self.bps[pb2]])
                        self.op("dve", lambda e: e.tensor_copy(out=vdst, in_=self.psb[pb2][0:ncols, 0:128]), r=[self.bps[pb2]], w=[bdst])

            with contextlib.ExitStack() as s2:
                NCB = T // 16 - 1
                xT = self.sb("cxT", [128, 2, T], BF16, s2); b_xT = Buf("cxT")
                kcT = self.sb("kcT", [128, 4, 128], BF16, s2)
                vc = self.sb("vc", [128, 4, 128], BF16, s2)
                b_kc = Buf("kc")
                self.op("dve", lambda e: e.memset(kcT[:], 0.0), w=[b_kc])
                self.op("dve", lambda e: e.memset(vc[:], 0.0), w=[b_kc])
                for g in range(4):
                    self.dma(lambda e: e.dma_start(out=xT[:], in_=self.KVT[0, g].rearrange("c p t -> p c t")[:, :, 0:T]),
                             r=[self.b_nsa_dr], w=[b_xT])
                    compress(lambda c, s: xT[:, c, s:s + 16 * (NCB - 1) + 1:16], b_xT, NCB, kcT[:, g, 0:NCB], vc[0:NCB, g, :], b_kc, 2 * g)
                ks = [self.sb("ks%d" % b, [128, T], BF16, s2) for b in range(2)]
                vs = [self.sb("vs%d" % b, [128, NQ, 128], BF16, s2) for b in range(2)]
                b_kv = Buf("kvres")
                bdt = self.sb("bdt", [128, 4, 4, 128], F32, s2)
                b_bd = Buf("bdt")
                gts = self.sb("gts", [128, NQ, 48], F32, s2); b_gts = Buf("gts")
                self.dma(lambda e: e.dma_start(out=gts[:], in_=self.GT[0:T, :].rearrange("(a p) n -> p a n", p=128)), r=[self.b_nsa_dr], w=[b_gts])
                mselp = self.sb("mselp", [128, 32], F32, s2)
                amp = self.sb("amp", [128, 8, 32], F32, s2)
                eexf = self.sb("eexf", [32, 16, 128], F32, s2)
                eexb = self.sb("eexb", [32, 16, 128], BF16, s2)
                b_sc = Buf("selc")
                self.dma(lambda e: e.dma_start(out=mselp[:], in_=self.c_msel_p[:, :]), w=[b_sc])
                self.dma(lambda e: e.dma_start(out=amp[:], in_=self.c_addmask_p.rearrange("i q j -> q i j")), w=[b_sc])
                self.dma(lambda e: e.dma_start(out=eexf[:], in_=self.c_eexp[:, :, :]), w=[b_sc])
                self.op("dve", lambda e: e.tensor_copy(out=eexb[:], in_=eexf[:]), r=[b_sc], w=[b_sc])
                qs = [self.sb("qs%d" % i, [128, 4, 128], BF16, s2) for i in range(2)]; b_qs = [Buf("qs") for _ in range(2)]
                bcm = [self.sb("bcm%d" % i, [128, 4, 128], F32, s2) for i in range(2)]; b_bcm = [Buf("bcm") for _ in range(2)]
                sT = [self.sb("sT%d" % i, [128, 512], F32, s2) for i in range(3)]; b_sT = [Buf("sT") for _ in range(3)]
                pT = [self.sb("pT%d" % i, [128, 512], BF16, s2) for i in range(3)]; b_pT = [Buf("pT") for _ in range(3)]
                SB_ = [0, 1, 7]
                pf = self.sb("pf", [128, 512], F32, s2); b_pf = Buf("pf")
                rcs = self.sb("rcs", [128, 512], F32, s2); b_rc = Buf("rcs")
                impT = self.sb("impT", [128, 128], F32, s2); b_imp = Buf("imp")
                psl = self.sb("psl", [32, 128], F32, s2); b_psl = Buf("psl")
                sco = self.sb("sco", [128, 32], F32, s2); sco2 = self.sb("sco2", [128, 32], F32, s2)
                mx1 = self.sb("mx1", [128, 8], F32, s2); mx2 = self.sb("mx2", [128, 8], F32, s2)
                ngm = self.sb("ngm", [128, 32], F32, s2); b_sco = Buf("sco")
                ngT = self.sb("ngT", [32, 128], BF16, s2); b_ngT = Buf("ngT")
                osb = self.sb("osb", [128, 4, 128], F32, s2); b_osb = Buf("osb")
                wsc = self.sb("wsc", [128, 4], F32, s2); b_wsc = Buf("wsc")
                otb = [self.sb("otb%d" % i, [128, 4, 128], BF16, s2) for i in range(2)]; b_otb = [Buf("otb") for _ in range(2)]
                for g in range(4):
                    for b_ in range(2):
                        self.dma(lambda e: e.dma_start(out=ks[b_][:], in_=self.KVT[1 + b_, g, 0, :, 0:T]), r=[self.b_nsa_dr], w=[b_kv])
                        self.dma(lambda e: e.dma_start(out=vs[b_][:], in_=self.VTOK[1 + b_, g, 0:T, :].rearrange("(a p) d -> p a d", p=128)),
                                 r=[self.b_nsa_dr], w=[b_kv])
                    for kd in range(4):
                        self.dma(lambda e: e.dma_start(out=bdt[:, kd], in_=self.BD[4 * g:4 * g + 4, kd].rearrange("h n q -> n h q")),
                                 r=[self.b_nsa_dr], w=[b_bd])
                    units = []
                    banks = {}
                    octr = 0
                    for i in range(NQ):
                        for br in (0, 2, 1):
                            if br == 0:
                                kts = [0]
                            elif br == 1:
                                kts = list(range(0, i + 1))
                            else:
                                kts = list(range(max(0, i - 4), i + 1))
                            banks[(i, br)] = (2, 0) if octr % 2 == 0 else (6, 8)
                            octr += 1
                            for ki, kt in enumerate(kts):
                                units.append((i, br, kt, ki, len(kts)))

                    def emit_qk(n):
                        i, br, kt, ki, nk = units[n]
                        qi = i % 2
                        u = n % 3
                        sbk = SB_[u]
                        if br == 0:
                            self.dma(lambda e: e.dma_start(out=qs[qi][:], in_=self.QT[4 * g:4 * g + 4, :, i * 128:(i + 1) * 128].rearrange("h p t -> p h t")),
                                     r=[self.b_nsa_dr], w=[b_qs[qi]])
                            self.dma(lambda e: e.dma_start(out=bcm[qi][:], in_=self.BCs[4 * g:4 * g + 4, 120 - 8 * i:248 - 8 * i, :].rearrange("h n q -> n h q")),
                                     r=[self.b_nsa_dr], w=[b_bcm[qi]])
                        qrhs = qs[qi][:].rearrange("p r q -> p (r q)")
                        if br == 0:
                            klhs, kvb = kcT[:, g, :], b_kc
                        else:
                            klhs, kvb = ks[br - 1][:, kt * 128:(kt + 1) * 128], b_kv
                        msk = (br == 1 and i >= 8)
                        self.op("pe", lambda e: e.matmul(self.psb[sbk][:, :], lhsT=klhs, rhs=qrhs, start=True, stop=(not msk)),
                                r=[kvb, b_qs[qi]], w=[self.bps[sbk]])
                        if msk:
                            for r in range(4):
                                self.op("pe", lambda e: e.matmul(self.psb[sbk][:, r * 128:(r + 1) * 128], lhsT=eexb[:, kt, :], rhs=ngT[:, :],
                                                                 start=False, stop=(r == 3)), r=[b_sc, b_ngT], w=[self.bps[sbk]])

                    def emit_rest(n):
                        i, br, kt, ki, nk = units[n]
                        qi = i % 2
                        u = n % 3
                        sbk = SB_[u]
                        ob_, oc_ = banks[(i, br)]
                        os_ = 3
                        dosel = (i >= 8)
                        if br == 0:
                            vrhs, kvb = vc[:, g, :], b_kc
                            bias_ap, bb = bcm[qi][:].rearrange("p r q -> p (r q)"), b_bcm[qi]
                        else:
                            vrhs, kvb = vs[br - 1][:, kt, :], b_kv
                            if kt == i:
                                kd = 0
                            elif kt == i - 1:
                                kd = 1
                            elif br == 2 and kt == i - 4:
                                kd = 3
                            else:
                                kd = 2
                            bias_ap, bb = bdt[:, kd].rearrange("p r q -> p (r q)"), b_bd
                        self.op("dve", lambda e: e.tensor_tensor(out=sT[u][:], in0=self.psb[sbk][:, :], in1=bias_ap, op=ALU.add),
                                r=[self.bps[sbk], bb], w=[b_sT[u]])
                        if br == 0:
                            self.op("act", lambda e: e.activation(out=pf[:], in_=sT[u][:], func=AF.Exp), r=[b_sT[u]], w=[b_pf])
                            self.op("pool", lambda e: e.tensor_copy(out=pT[u][:], in_=pf[:]), r=[b_pf], w=[b_pT[u]])
                        else:
                            self.op("act", lambda e: e.activation(out=pT[u][:], in_=sT[u][:], func=AF.Exp), r=[b_sT[u]], w=[b_pT[u]])
                        for r in range(4):
                            self.op("pe", lambda e: e.matmul(self.psb[ob_][:, r * 128:(r + 1) * 128], lhsT=pT[u][:, r * 128:(r + 1) * 128], rhs=vrhs,
                                                             start=(ki == 0 and r == 0), stop=(ki == nk - 1)),
                                    r=[b_pT[u], kvb], w=[self.bps[ob_]])
                            self.op("pe", lambda e: e.matmul(self.psb[os_][:, oc_ + r:oc_ + r + 1], lhsT=pT[u][:, r * 128:(r + 1) * 128], rhs=self.ones_b[:, 0:1],
                                                             start=(ki == 0 and r == 0), stop=(ki == nk - 1)),
                                    r=[b_pT[u], self.b_const], w=[self.bps[os_]])
                        if br == 0 and dosel:
                            self.op("pe", lambda e: e.matmul(self.psb[4][:, :], lhsT=self.ones_f[:], rhs=pf[:], start=True, stop=True),
                                    r=[b_pf, self.b_const], w=[self.bps[4]])
                            self.op("dve", lambda e: e.tensor_scalar(out=rcs[:], in0=self.psb[4][:, :], scalar1=1e-30, scalar2=None, op0=ALU.add),
                                    r=[self.bps[4]], w=[b_rc])
                            self.op("dve", lambda e: e.reciprocal(out=rcs[:], in_=rcs[:]), r=[b_rc], w=[b_rc])
                            self.op("dve", lambda e: e.tensor_tensor(out=rcs[:], in0=rcs[:], in1=pf[:], op=ALU.mult), r=[b_rc, b_pf], w=[b_rc])
                            self.op("dve", lambda e: e.tensor_reduce(out=impT[:], in_=rcs[:].rearrange("p (r q) -> p q r", r=4), axis=AX.X, op=ALU.add),
                                    r=[b_rc], w=[b_imp])
                            self.op("pe", lambda e: e.matmul(self.psb[5][0:32, 0:128], lhsT=mselp[:, :], rhs=impT[:], start=True, stop=True),
                                    r=[b_imp, b_sc], w=[self.bps[5]])
                            self.op("act", lambda e: e.copy(out=psl[:], in_=self.psb[5][0:32, 0:128]), r=[self.bps[5]], w=[b_psl])
                            self.op("pe", lambda e: e.transpose(out=self.psb[5][:, 128:160], in_=psl[:], identity=self.ident[0:32, 0:32]),
                                    r=[b_psl, self.b_const], w=[self.bps[5]])
                            self.op("dve", lambda e: e.tensor_tensor(out=sco[:], in0=self.psb[5][:, 128:160], in1=amp[:, i - 8, :], op=ALU.add),
                                    r=[self.bps[5], b_sc], w=[b_sco])
                            osc = lambda f: self.op("dve", f, r=[b_sco], w=[b_sco])
                            osc(lambda e: e.max(out=mx1[:], in_=sco[:]))
                            osc(lambda e: e.match_replace(out=sco2[:], in_to_replace=mx1[:], in_values=sco[:], imm_value=-3.0e38))
                            osc(lambda e: e.max(out=mx2[:], in_=sco2[:]))
                            osc(lambda e: e.tensor_scalar(out=ngm[:], in0=sco[:], scalar1=mx2[:, 7:8], scalar2=None, op0=ALU.is_ge))
                            osc(lambda e: e.tensor_scalar(out=ngm[:], in0=ngm[:], scalar1=30000.0, scalar2=-30000.0, op0=ALU.mult, op1=ALU.add))
                            self.op("pe", lambda e: e.transpose(out=self.psb[5][0:32, 256:384], in_=ngm[:], identity=self.ident[:]),
                                    r=[b_sco, self.b_const], w=[self.bps[5]])
                            self.op("act", lambda e: e.copy(out=ngT[:], in_=self.psb[5][0:32, 256:384]), r=[self.bps[5]], w=[b_ngT])
                        if ki == nk - 1:
                            self.op("dve", lambda e: e.tensor_scalar(out=wsc[:], in0=self.psb[os_][:, oc_:oc_ + 4], scalar1=1e-30, scalar2=None, op0=ALU.add),
                                    r=[self.bps[os_]], w=[b_wsc])
                            self.op("dve", lambda e: e.reciprocal(out=wsc[:], in_=wsc[:]), r=[b_wsc], w=[b_wsc])
                            gv = gts[:, i, :].rearrange("p (h b) -> p h b", b=3)[:, 4 * g:4 * g + 4, br]
                            self.op("dve", lambda e: e.tensor_tensor(out=wsc[:], in0=wsc[:], in1=gv, op=ALU.mult), r=[b_wsc, b_gts], w=[b_wsc])
                            for r in range(4):
                                if br == 0:
                                    self.op("dve", lambda e: e.tensor_scalar(out=osb[:, r, :], in0=self.psb[ob_][:, r * 128:(r + 1) * 128], scalar1=wsc[:, r:r + 1],
                                                                             scalar2=None, op0=ALU.mult), r=[self.bps[ob_], b_wsc], w=[b_osb])
                                else:
                                    self.op("dve", lambda e: e.scalar_tensor_tensor(out=osb[:, r, :], in0=self.psb[ob_][:, r * 128:(r + 1) * 128], scalar=wsc[:, r:r + 1],
                                                                                    in1=osb[:, r, :], op0=ALU.mult, op1=ALU.add),
                                            r=[self.bps[ob_], b_wsc, b_osb], w=[b_osb])
                            if br == 1:
                                for r in range(4):
                                    self.op("pe", lambda e: e.transpose(out=self.psb[4][:, r * 128:(r + 1) * 128], in_=osb[:, r, :], identity=self.ident[:]),
                                            r=[b_osb, self.b_const], w=[self.bps[4]])
                                self.op("act", lambda e: e.copy(out=otb[qi][:].rearrange("p r q -> p (r q)"), in_=self.psb[4][:, :]), r=[self.bps[4]], w=[b_otb[qi]])
                                self.dma(lambda e: e.dma_start(out=self.OT[4 * g:4 * g + 4, :, i * 128:(i + 1) * 128].rearrange("h p t -> p h t"), in_=otb[qi][:]),
                                         r=[b_otb[qi]], w=[self.b_nsa_dr], q="act")

                    emit_qk(0)
                    emit_qk(1)
                    for n in range(len(units)):
                        if n + 2 < len(units):
                            emit_qk(n + 2)
                        emit_rest(n)
                self.tk.barrier()
            self.nsa_sample(j, st, compress, b_cw)
            self.tk.barrier()
        with contextlib.ExitStack() as st:
            self.alloc_rowlocal(st)
            oin = self.sb("oin_", [128, 16, 512], BF16, st); b_oin = Buf("oin")
            wo = self.nsa_w_o[j].rearrange("(kc p) (f h n) -> p kc f h n", p=128, h=2, n=128)
            OTv = self.OT.rearrange("h p t -> p h t")
            for ti in range(len(self.tiles)):
                t0, W = self.tiles[ti]
                self.load_xr(ti)
                self.dma(lambda e: e.dma_start(out=oin[:, :, :W], in_=OTv[:, :, t0:t0 + W]), r=[self.b_nsa_dr], w=[b_oin], q="act")

                def epo(p, psA, bA, psB, bB):
                    self.op("act", lambda e: e.copy(out=self.yy[:, 2 * p, :W], in_=psA[:, :W]), r=[bA], w=[self.b_yy])
                    self.op("dve", lambda e: e.tensor_copy(out=self.yy[:, 2 * p + 1, :W], in_=psB[:, :W]), r=[bB], w=[self.b_yy])
                self.gemm(lambda p, k0, kn, h: wo[:, k0:k0 + kn, p, h, :], DC, 8, oin, b_oin, W, epo, wkey='wo%d' % j, first=(ti == 0))
                self.post_norm_residual(ti, li * 4 + 1)
            self.tk.barrier()

    def nsa_sample(self, j, st, compress, b_cw):
        T = self.T
        with contextlib.ExitStack() as s2:
            S = lambda n, shp, dt=F32: self.sb(n, shp, dt, s2)
            b_c = Buf("sconst")
            pti = S("pti", [128, 128], I32); ptf = S("ptf", [128, 128]); idxf = S("idxf", [128, 128]); idxa = S("idxa", [128, 128], I32)
            iop = S("iop", [128, 1]); ior = S("ior", [128, 1]); iopg = S("iopg", [128, 128]); gcol = S("gcol", [128, 32])
            selh = S("selh", [4, 2, 128]); eye4 = S("eye4", [4, 4]); msels = S("msels", [128, 8, 257]); ams = S("ams", [4, 257])
            L = lambda dst, src: self.dma(lambda e: e.dma_start(out=dst, in_=src), w=[b_c])
            L(pti[:], self.page_table.partition_broadcast(128))
            L(iop[:], self.c_iota_p[:, :]); L(ior[:], self.c_iota_row[:, :]); L(iopg[:], self.c_iota_pg[:, :]); L(gcol[:], self.c_gcol[:, :])
            L(selh[:], self.c_selhalf.rearrange("s g p -> g s p")); L(eye4[:], self.c_eye4[:, :])
            L(msels[:], self.c_msel_s.rearrange("(t n) j -> n t j", n=128)); L(ams[:], self.c_addmask_s[:, :])
            oc = lambda q, f: self.op(q, f, r=[b_c], w=[b_c])
            oc("dve", lambda e: e.tensor_copy(out=ptf[:], in_=pti[:]))
            oc("dve", lambda e: e.tensor_scalar(out=idxf[:], in0=ptf[:], scalar1=128.0, scalar2=iop[:, 0:1], op0=ALU.mult, op1=ALU.add))
            oc("dve", lambda e: e.tensor_copy(out=idxa[:], in_=idxf[:]))
            qsT = S("qsT", [128, 16], BF16); gsm = S("gsm", [4, 4, 3]); b_q = Buf("qs_s")
            self.dma(lambda e: e.dma_start(out=qsT[:].unsqueeze(2), in_=self.QT.rearrange("h p t -> p h t")[:, :, T:T + 1],
                                           allow_slow_non_contiguous=True), r=[self.b_nsa_dr], w=[b_q])
            self.dma(lambda e: e.dma_start(out=gsm[:], in_=self.GT[T:T + 1, :].rearrange("o (g r b) -> (o r) g b", g=4, r=4),
                                           allow_slow_non_contiguous=True), r=[self.b_nsa_dr], w=[b_q])
            XT = S("XT", [128, 8, 2064], BF16); b_XT = Buf("XT")
            pg = [S("pg%d" % i, [128, 1024]) for i in range(2)]; b_pg = [Buf("pg") for _ in range(2)]
            kcs = S("kcs", [128, 4, 1024], BF16); vcs = S("vcs", [128, 8, 4, 128], BF16); b_kcs = Buf("kcs")
            self.op("pool", lambda e: e.memset(XT[:, :, 0:16], 0.0), w=[b_XT])
            cmp_rows = self.cache_cmp[j]
            for G8 in range(8):
                if G8 > 0:
                    self.op("pool", lambda e: e.tensor_copy(out=XT[:, :, 0:16], in_=XT[:, :, 2048:2064]), r=[b_XT], w=[b_XT])
                for p16 in range(16):
                    page = G8 * 16 + p16
                    k = page % 2
                    self.tk.dma("pool", lambda e: e.indirect_dma_start(out=pg[k][:, :], out_offset=None, in_=cmp_rows[:, :],
                                                                       in_offset=bass.IndirectOffsetOnAxis(ap=idxa[:, page:page + 1], axis=0)),
                                [b_c], [b_pg[k]])
                    for half in range(2):
                        pb = 4 + half
                        for q4 in range(4):
                            gc = half * 4 + q4
                            self.op("pe", lambda e: e.transpose(out=self.psb[pb][:, q4 * 128:(q4 + 1) * 128], in_=pg[k][:, gc * 128:(gc + 1) * 128],
                                                                identity=self.ident[:]), r=[b_pg[k], self.b_const], w=[self.bps[pb]])
                        dst = XT[:, half * 4:(half + 1) * 4, 16 + p16 * 128:16 + (p16 + 1) * 128]
                        src = self.psb[pb][:, :].rearrange("p (a n) -> p a n", a=4)
                        if half == 0:
                            self.op("act", lambda e: e.copy(out=dst, in_=src), r=[self.bps[pb]], w=[b_XT])
                        else:
                            self.op("dve", lambda e: e.tensor_copy(out=dst, in_=src), r=[self.bps[pb]], w=[b_XT])
                for g in range(4):
                    compress(lambda c, s: XT[:, g * 2 + c, s:s + 16 * 127 + 1:16], b_XT, 128,
                             kcs[:, g, G8 * 128:(G8 + 1) * 128], vcs[:, G8, g, :], b_kcs, 2 * g)
            sTs = S("sTs", [128, 144]); pfs = S("pfs", [128, 144]); pns = S("pns", [128, 144]); pbs = S("pbs", [128, 144], BF16)
            tot = S("tot", [128, 16]); b_a = Buf("satt")
            imp = S("imp", [128, 8, 4])

            def softmax_cols(ncol, nt, bias_ap, psbank):
                pass

            for t in range(8):
                for g in range(4):
                    self.op("pe", lambda e: e.matmul(self.psb[0][:, t * 16 + 4 * g:t * 16 + 4 * g + 4], lhsT=kcs[:, g, t * 128:(t + 1) * 128],
                                                     rhs=qsT[:, 4 * g:4 * g + 4], start=True, stop=True), r=[b_kcs, b_q], w=[self.bps[0]])
            oa = lambda q, f, extra=(): self.op(q, f, r=[b_a] + list(extra), w=[b_a])
            oa("dve", lambda e: e.tensor_tensor(out=sTs[:, 0:128], in0=self.psb[0][:, 0:128], in1=self.BS[:, 0:8, :].rearrange("p t h -> p (t h)"),
                                                op=ALU.add), [self.bps[0], self.b_BS])
            oa("act", lambda e: e.activation(out=pfs[:, 0:128], in_=sTs[:, 0:128], func=AF.Exp))
            self.op("pe", lambda e: e.matmul(self.psb[4][:, 0:128], lhsT=self.ones_f[:], rhs=pfs[:, 0:128], start=True, stop=True),
                    r=[b_a, self.b_const], w=[self.bps[4]])
            oa("dve", lambda e: e.tensor_reduce(out=tot[:], in_=self.psb[4][:, 0:128].rearrange("p (t h) -> p h t", h=16), axis=AX.X, op=ALU.add),
               [self.bps[4]])
            oa("dve", lambda e: e.reciprocal(out=tot[:], in_=tot[:]))
            oa("dve", lambda e: e.tensor_tensor(out=pns[:, 0:128].rearrange("p (t h) -> p t h", h=16), in0=pfs[:, 0:128].rearrange("p (t h) -> p t h", h=16),
                                                in1=tot[:].unsqueeze(1).to_broadcast([128, 8, 16]), op=ALU.mult))
            oa("act", lambda e: e.copy(out=pbs[:, 0:128], in_=pns[:, 0:128]))
            oa("dve", lambda e: e.tensor_reduce(out=imp[:], in_=pns[:, 0:128].rearrange("p (t g r) -> p t g r", g=4, r=4), axis=AX.X, op=ALU.add))
            first = True
            for g in range(4):
                for t in range(8):
                    self.op("pe", lambda e: e.matmul(self.psb[2][0:4, g * 128:(g + 1) * 128], lhsT=pbs[:, t * 16 + 4 * g:t * 16 + 4 * g + 4],
                                                     rhs=vcs[:, t, g, :], start=first, stop=(t == 7)), r=[b_a, b_kcs], w=[self.bps[2]])
                    first = False
            osm = S("osm", [4, 4, 128]); otmp = S("otmp", [4, 4, 128]); b_os = Buf("osm")
            self.op("dve", lambda e: e.tensor_tensor(out=osm[:], in0=self.psb[2][0:4, :].rearrange("p (g d) -> p g d", g=4),
                                                     in1=gsm[:, :, 0:1].to_broadcast([4, 4, 128]), op=ALU.mult), r=[self.bps[2], b_q], w=[b_os])
            for t in range(8):
                self.op("pe", lambda e: e.matmul(self.psb[5][0:4, 0:257], lhsT=imp[:, t, :], rhs=msels[:, t, :], start=(t == 0), stop=(t == 7)),
                        r=[b_a, b_c], w=[self.bps[5]])
            sco = S("ssco", [4, 257]); sco2 = S("ssco2", [4, 257]); mx1 = S("smx1", [4, 8]); mx2 = S("smx2", [4, 8])
            ixu = S("ixu", [4, 16], U32); ixf = S("ixf", [4, 16]); bm = S("bm", [4, 2, 4, 8]); b_s = Buf("ssel")
            osl = lambda q, f, extra=(): self.op(q, f, r=[b_s] + list(extra), w=[b_s])
            osl("dve", lambda e: e.tensor_tensor(out=sco[:], in0=self.psb[5][0:4, 0:257], in1=ams[:], op=ALU.add), [self.bps[5], b_c])
            osl("dve", lambda e: e.max(out=mx1[:], in_=sco[:]))
            osl("dve", lambda e: e.match_replace(out=sco2[:], in_to_replace=mx1[:], in_values=sco[:], imm_value=-3.0e38))
            osl("dve", lambda e: e.max(out=mx2[:], in_=sco2[:]))
            osl("dve", lambda e: e.max_index(out=ixu[:, 0:8], in_max=mx1[:], in_values=sco[:]))
            osl("dve", lambda e: e.max_index(out=ixu[:, 8:16], in_max=mx2[:], in_values=sco2[:]))
            osl("dve", lambda e: e.tensor_copy(out=ixf[:], in_=ixu[:]))
            for s2i in range(2):
                osl("dve", lambda e: e.tensor_tensor(out=bm[:, s2i], in0=ixf[:, s2i:16:2].unsqueeze(1).to_broadcast([4, 4, 8]),
                                                     in1=eye4[:, :].unsqueeze(2).to_broadcast([4, 4, 8]), op=ALU.mult), [b_c])
                self.op("pe", lambda e: e.matmul(self.psb[5][:, 300:332], lhsT=selh[:, s2i, :], rhs=bm[:, s2i].rearrange("p g s -> p (g s)"),
                                                 start=(s2i == 0), stop=(s2i == 1)), r=[b_s, b_c], w=[self.bps[5]])
            jvf = S("jvf", [128, 32]); jvi = S("jvi", [128, 32], I32); jhi = S("jhi", [128, 32], I32); jpi = S("jpi", [128, 32], I32)
            jhf = S("jhf", [128, 32]); jpf = S("jpf", [128, 32]); ptsel = S("ptsel", [128, 32]); rowf = S("rowf", [128, 32]); rowi = S("rowi", [128, 32], I32)
            i254 = S("i254", [128, 32]); i255 = S("i255", [128, 32]); i256 = S("i256", [128, 32])
            osl("dve", lambda e: e.tensor_copy(out=jvf[:], in_=self.psb[5][:, 300:332]), [self.bps[5]])
            osl("dve", lambda e: e.tensor_copy(out=jvi[:], in_=jvf[:]))
            osl("dve", lambda e: e.tensor_single_scalar(out=jhi[:], in_=jvi[:], scalar=1, op=ALU.logical_shift_right))
            osl("dve", lambda e: e.tensor_single_scalar(out=jpi[:], in_=jvi[:], scalar=1, op=ALU.bitwise_and))
            osl("dve", lambda e: e.tensor_copy(out=jhf[:], in_=jhi[:]))
            osl("dve", lambda e: e.tensor_copy(out=jpf[:], in_=jpi[:]))
            with contextlib.ExitStack() as s3:
                ohp = self.sb("ohp", [128, 32, 128], F32, s3)
                osl("dve", lambda e: e.tensor_tensor(out=ohp[:], in0=jhf[:].unsqueeze(2).to_broadcast([128, 32, 128]),
                                                     in1=iopg[:].unsqueeze(1).to_broadcast([128, 32, 128]), op=ALU.is_equal), [b_c])
                osl("dve", lambda e: e.tensor_tensor(out=ohp[:], in0=ohp[:], in1=ptf[:].unsqueeze(1).to_broadcast([128, 32, 128]), op=ALU.mult), [b_c])
                osl("dve", lambda e: e.tensor_reduce(out=ptsel[:], in_=ohp[:], axis=AX.X, op=ALU.add))
                self.tk.barrier()
            osl("dve", lambda e: e.tensor_scalar(out=rowf[:], in0=ptsel[:], scalar1=128.0, scalar2=ior[:, 0:1], op0=ALU.mult, op1=ALU.add), [b_c])
            osl("dve", lambda e: e.scalar_tensor_tensor(out=rowf[:], in0=jpf[:], scalar=64.0, in1=rowf[:], op0=ALU.mult, op1=ALU.add))
            osl("dve", lambda e: e.scalar_tensor_tensor(out=rowf[:], in0=rowf[:], scalar=4.0, in1=gcol[:], op0=ALU.mult, op1=ALU.add), [b_c])
            osl("dve", lambda e: e.tensor_copy(out=rowi[:], in_=rowf[:]))
            for tile_, val in ((i254, 254.0), (i255, 255.0), (i256, 256.0)):
                osl("dve", lambda e: e.tensor_scalar(out=tile_[:], in0=jvf[:], scalar1=val, scalar2=None, op0=ALU.is_equal))
            bsl = S("bsl", [128, 4, 9, 4]); dv = S("dv", [128, 2, 16]); tb1 = S("tb1", [128, 8, 4])
            osl("dve", lambda e: e.tensor_tensor(out=dv[:, 0, :], in0=self.BS[:, 13, :], in1=self.BS[:, 15, :], op=ALU.subtract), [self.b_BS])
            osl("dve", lambda e: e.tensor_tensor(out=dv[:, 1, :], in0=self.BS[:, 14, :], in1=self.BS[:, 15, :], op=ALU.subtract), [self.b_BS])
            for g in range(4):
                tgt = bsl[:, g, 0:8, :]
                osl("dve", lambda e: e.tensor_copy(out=tgt, in_=self.BS[:, 15, 4 * g:4 * g + 4].unsqueeze(1).to_broadcast([128, 8, 4])), [self.b_BS])
                for k, ind in enumerate((i254, i255)):
                    osl("dve", lambda e: e.tensor_tensor(out=tb1[:], in0=ind[:, g * 8:(g + 1) * 8].unsqueeze(2).to_broadcast([128, 8, 4]),
                                                         in1=dv[:, k, 4 * g:4 * g + 4].unsqueeze(1).to_broadcast([128, 8, 4]), op=ALU.mult))
                    osl("dve", lambda e: e.tensor_tensor(out=tgt, in0=tgt, in1=tb1[:], op=ALU.add))
                osl("dve", lambda e: e.scalar_tensor_tensor(out=tgt, in0=i256[:, g * 8:(g + 1) * 8].unsqueeze(2).to_broadcast([128, 8, 4]), scalar=-30000.0,
                                                            in1=tgt, op0=ALU.mult, op1=ALU.add))
                osl("dve", lambda e: e.tensor_copy(out=bsl[:, g, 8, :], in_=self.BS[:, 16, 4 * g:4 * g + 4]), [self.b_BS])
            ksel = [S("ksel%d" % i, [128, 256]) for i in range(2)]; b_ks = [Buf("ksel") for _ in range(2)]
            KsT = S("KsT", [128, 4, 9, 128], BF16); Vs = S("Vs", [128, 4, 9, 128], BF16); b_KV = Buf("KVs")
            self.op("pool", lambda e: e.memset(KsT[:, :, 8, :], 0.0), w=[b_KV])
            self.op("pool", lambda e: e.memset(Vs[:, :, 8, :], 0.0), w=[b_KV])
            slc_rows = self.cache_slc[j]
            for g in range(4):
                self.dma(lambda e: e.dma_start(out=KsT[:, g, 8, 0:1], in_=self.KVT[1, g, 0, :, T:T + 1], allow_slow_non_contiguous=True),
                         r=[self.b_nsa_dr], w=[b_KV])
                self.dma(lambda e: e.dma_start(out=Vs[0:1, g, 8, :], in_=self.VTOK[1, g, T:T + 1, :]), r=[self.b_nsa_dr], w=[b_KV])
                for sp in range(8):
                    col = g * 8 + sp
                    k = col % 2
                    self.tk.dma("pool", lambda e: e.indirect_dma_start(out=ksel[k][:, :], out_offset=None, in_=slc_rows[:, :],
                                                                       in_offset=bass.IndirectOffsetOnAxis(ap=rowi[:, col:col + 1], axis=0)),
                                [b_s], [b_ks[k]])
                    pb = 4 + (col % 2)
                    self.op("pe", lambda e: e.transpose(out=self.psb[pb][:, 0:128], in_=ksel[k][:, 0:128], identity=self.ident[:]),
                            r=[b_ks[k], self.b_const], w=[self.bps[pb]])
                    self.op("act", lambda e: e.copy(out=KsT[:, g, sp, :], in_=self.psb[pb][:, 0:128]), r=[self.bps[pb]], w=[b_KV])
                    self.op("dve", lambda e: e.tensor_copy(out=Vs[:, g, sp, :], in_=ksel[k][:, 128:256]), r=[b_ks[k]], w=[b_KV])

            def small_attn(KT_, V_, bKV, nt, bias_ap, brx):
                ncol = 4 * nt * 4
                for g in range(4):
                    for t in range(nt):
                        c0 = (g * nt + t) * 4
                        self.op("pe", lambda e: e.matmul(self.psb[1][:, c0:c0 + 4], lhsT=KT_[:, g, t, :], rhs=qsT[:, 4 * g:4 * g + 4], start=True, stop=True),
                                r=[bKV, b_q], w=[self.bps[1]])
                oa("dve", lambda e: e.tensor_tensor(out=sTs[:, 0:ncol], in0=self.psb[1][:, 0:ncol], in1=bias_ap, op=ALU.add), [self.bps[1], b_s, self.b_BS])
                oa("act", lambda e: e.activation(out=pfs[:, 0:ncol], in_=sTs[:, 0:ncol], func=AF.Exp))
                self.op("pe", lambda e: e.matmul(self.psb[4][:, 0:ncol], lhsT=self.ones_f[:], rhs=pfs[:, 0:ncol], start=True, stop=True),
                        r=[b_a, self.b_const], w=[self.bps[4]])
                oa("dve", lambda e: e.tensor_reduce(out=tot[:].rearrange("p (g r) -> p g r", g=4),
                                                    in_=self.psb[4][:, 0:ncol].rearrange("p (g t r) -> p g r t", g=4, r=4), axis=AX.X, op=ALU.add), [self.bps[4]])
                oa("dve", lambda e: e.reciprocal(out=tot[:], in_=tot[:]))
                oa("dve", lambda e: e.tensor_tensor(out=pns[:, 0:ncol].rearrange("p (g t r) -> p g t r", g=4, r=4),
                                                    in0=pfs[:, 0:ncol].rearrange("p (g t r) -> p g t r", g=4, r=4),
                                                    in1=tot[:].rearrange("p (g r) -> p g r", g=4).unsqueeze(2).to_broadcast([128, 4, nt, 4]), op=ALU.mult))
                oa("act", lambda e: e.copy(out=pbs[:, 0:ncol], in_=pns[:, 0:ncol]))
                first = True
                for g in range(4):
                    for t in range(nt):
                        c0 = (g * nt + t) * 4
                        self.op("pe", lambda e: e.matmul(self.psb[3][0:4, g * 128:(g + 1) * 128], lhsT=pbs[:, c0:c0 + 4], rhs=V_[:, g, t, :],
                                                         start=first, stop=(t == nt - 1)), r=[b_a, bKV], w=[self.bps[3]])
                        first = False
                self.op("dve", lambda e: e.tensor_tensor(out=otmp[:], in0=self.psb[3][0:4, :].rearrange("p (g d) -> p g d", g=4),
                                                         in1=gsm[:, :, brx:brx + 1].to_broadcast([4, 4, 128]), op=ALU.mult), r=[self.bps[3], b_q, b_os], w=[b_os])
                self.op("dve", lambda e: e.tensor_tensor(out=osm[:], in0=osm[:], in1=otmp[:], op=ALU.add), r=[b_os], w=[b_os])

            small_attn(KsT, Vs, b_KV, 9, bsl[:].rearrange("p g t r -> p (g t r)"), 1)
            wld = [S("wld%d" % i, [128, 1024]) for i in range(2)]; b_wl = [Buf("wld") for _ in range(2)]
            KwT = S("KwT", [128, 4, 5, 128], BF16); Vw = S("Vw", [128, 4, 5, 128], BF16); b_KW = Buf("KVw")
            bwl = S("bwl", [128, 4, 5, 4])
            self.op("pool", lambda e: e.memset(KwT[:, :, 4, :], 0.0), w=[b_KW])
            self.op("pool", lambda e: e.memset(Vw[:, :, 4, :], 0.0), w=[b_KW])
            for g in range(4):
                self.dma(lambda e: e.dma_start(out=KwT[:, g, 4, 0:1], in_=self.KVT[2, g, 0, :, T:T + 1], allow_slow_non_contiguous=True),
                         r=[self.b_nsa_dr], w=[b_KW])
                self.dma(lambda e: e.dma_start(out=Vw[0:1, g, 4, :], in_=self.VTOK[2, g, T:T + 1, :]), r=[self.b_nsa_dr], w=[b_KW])
                osl("dve", lambda e: e.tensor_copy(out=bwl[:, g], in_=self.BS[:, 8:13, 4 * g:4 * g + 4]), [self.b_BS])
            for t in range(4):
                k = t % 2
                self.dma(lambda e: e.dma_start(out=wld[k][:], in_=self.cache_win[j, t * 128:(t + 1) * 128, :]), w=[b_wl[k]])
                for g in range(4):
                    pb = 4 + (g % 2)
                    self.op("pe", lambda e: e.transpose(out=self.psb[pb][:, 0:128], in_=wld[k][:, g * 256:g * 256 + 128], identity=self.ident[:]),
                            r=[b_wl[k], self.b_const], w=[self.bps[pb]])
                    self.op("act", lambda e: e.copy(out=KwT[:, g, t, :], in_=self.psb[pb][:, 0:128]), r=[self.bps[pb]], w=[b_KW])
                    self.op("dve", lambda e: e.tensor_copy(out=Vw[:, g, t, :], in_=wld[k][:, g * 256 + 128:(g + 1) * 256]), r=[b_wl[k]], w=[b_KW])
            small_attn(KwT, Vw, b_KW, 5, bwl[:].rearrange("p g t r -> p (g t r)"), 2)
            ots = S("ots", [128, 16, SW], BF16); b_ot = Buf("ots")
            self.op("pool", lambda e: e.memset(ots[:], 0.0), w=[b_ot])
            for g in range(4):
                self.op("pe", lambda e: e.transpose(out=self.psb[4][:, 4 * g:4 * g + 4], in_=osm[:, g, :], identity=self.ident[0:4, 0:4]),
                        r=[b_os, self.b_const], w=[self.bps[4]])
            self.op("dve", lambda e: e.tensor_copy(out=ots[:, :, 0:1], in_=self.psb[4][:, 0:16].unsqueeze(2)), r=[self.bps[4]], w=[b_ot])
            self.dma(lambda e: e.dma_start(out=self.OT.rearrange("h p t -> p h t")[:, :, T:T + SW], in_=ots[:]), r=[b_ot], w=[self.b_nsa_dr])
            self.tk.barrier()

    def build(self):
        self.declare_io()
        self.setup_consts()
        self.eps_rms = self.sb("eps_rms", [128, 1], F32)
        self.op("dve", lambda e: e.memset(self.eps_rms[:], RMS_EPS), w=[self.b_const])
        self.phase_input()
        for li in self.layers:
            if self.do_mixer:
                m = li % 3
                if m == 1:
                    self.conf_layer(li)
                elif m == 2:
                    self.s5_layer(li)
                else:
                    self.nsa_layer(li)
            if self.do_ffn:
                self.ffn_layer(li)
        self.phase_output()
        self.tk.barrier()
        self.es.close()
        return self.nc


_NC_CACHE = {}


def kernel(x_prompt, x_sample, cache_cmp_kv, cache_slc_kv, cache_win_kv, state_conv, state_ssm_re,
           state_ssm_im, state_ffn_conv, page_table, norm_gain, rel_bias, nsa_w_q, nsa_w_kv, nsa_cmp_pe,
           nsa_cmp_w1, nsa_cmp_w2, nsa_w_gate, nsa_w_o, conv_w_pw1, conv_dw, conv_dw_b, conv_ln_g, conv_ln_b,
           conv_w_pw2, ssm_a_re, ssm_a_im, ssm_log_dt, ssm_b_re, ssm_b_im, ssm_c_re, ssm_c_im, ssm_d,
           ssm_w_glu, ffn_w_up, ffn_dw, ffn_dw_b, ffn_w_down):
    A = lambda a: np.ascontiguousarray(np.asarray(a))
    x_prompt = A(x_prompt)
    B, T, _ = x_prompt.shape
    NS = x_sample.shape[0]
    nphys = cache_cmp_kv.shape[1]
    n_cores = 8
    key = (T, nphys)
    if key not in _NC_CACHE:
        _NC_CACHE[key] = Builder(T=T, nphys=nphys).build()
    nc = _NC_CACHE[key]
    f = np.float32
    shared = dict(
        norm_gain=A(norm_gain).reshape(16, D), rel_bias=A(rel_bias),
        nsa_w_q=A(nsa_w_q), nsa_w_kv=A(nsa_w_kv), nsa_cmp_pe=A(nsa_cmp_pe), nsa_cmp_w1=A(nsa_cmp_w1),
        nsa_cmp_w2=A(nsa_cmp_w2), nsa_w_gate=A(nsa_w_gate), nsa_w_o=A(nsa_w_o),
        conv_w_pw1=A(conv_w_pw1)[0], conv_dw=A(conv_dw)[0], conv_dw_b=A(conv_dw_b).reshape(1, D),
        conv_ln_g=A(conv_ln_g).reshape(1, D), conv_ln_b=A(conv_ln_b).reshape(1, D), conv_w_pw2=A(conv_w_pw2)[0],
        ssm_a_re=A(ssm_a_re)[0], ssm_a_im=A(ssm_a_im)[0], ssm_log_dt=A(ssm_log_dt).reshape(1, 128),
        ssm_b_re=A(ssm_b_re)[0], ssm_b_im=A(ssm_b_im)[0], ssm_c_re=A(ssm_c_re)[0], ssm_c_im=A(ssm_c_im)[0],
        ssm_d=A(ssm_d).reshape(1, D), ssm_w_glu=A(ssm_w_glu)[0],
        ffn_w_up=A(ffn_w_up), ffn_dw=A(ffn_dw).reshape(12, DFF), ffn_dw_b=A(ffn_dw_b), ffn_w_down=A(ffn_w_down),
        cache_cmp0=A(cache_cmp_kv)[0].reshape(nphys * 128, 1024), cache_cmp1=A(cache_cmp_kv)[1].reshape(nphys * 128, 1024),
        cache_slc0=A(cache_slc_kv)[0].reshape(nphys * 128 * 4, 256), cache_slc1=A(cache_slc_kv)[1].reshape(nphys * 128 * 4, 256),
    )
    shared.update(structural_consts())
    x_sample = A(x_sample); cache_win_kv = A(cache_win_kv); state_conv = A(state_conv)
    state_ssm_re = A(state_ssm_re); state_ssm_im = A(state_ssm_im); state_ffn_conv = A(state_ffn_conv)
    page_table = A(page_table).astype(np.int32)
    in_maps = []
    for c in range(n_cores):
        b = c % B
        s = c % NS
        m = dict(shared)
        m.update(dict(
            x_p=x_prompt[b], x_s=x_sample[s].reshape(1, D),
            cache_win=A(cache_win_kv[:, s]).reshape(2, 512, 1024),
            state_conv=A(state_conv[0, s]), state_ssm_re=A(state_ssm_re[0, s]), state_ssm_im=A(state_ssm_im[0, s]),
            state_ffn=A(state_ffn_conv[:, s]), page_table=page_table[s:s + 1],
        ))
        in_maps.append(m)
    res = run_bass_kernel_spmd(nc, in_maps, core_ids=list(range(n_cores)))
    R = res.results
    P = lambda name: np.stack([R[b][name] for b in range(B)], 0)
    Sm = lambda name: np.stack([R[s][name] for s in range(NS)], 0)
    WP = min(512, T)
    y_p = P("y_p")
    y_s = Sm("y_s")
    kvp = lambda n, rows: np.moveaxis(P(n), 0, 1).reshape(2, B, rows, 4, 2, 128)
    kvs = lambda n, rows: np.moveaxis(Sm(n), 0, 1).reshape(2, NS, rows, 4, 2, 128)
    outs = (
        y_p, y_s,
        kvp("cmp_kv_p", T), kvs("cmp_kv_s", 1), kvp("slc_kv_p", T), kvs("slc_kv_s", 1),
        kvp("win_kv_p", WP), kvs("win_kv_s", 512),
        P("conv_p")[None], Sm("conv_s")[None],
        P("ssm_re_p")[None], Sm("ssm_re_s")[None], P("ssm_im_p")[None], Sm("ssm_im_s")[None],
        np.moveaxis(P("ffn_p"), 0, 1), np.moveaxis(Sm("ffn_s"), 0, 1),
    )
    return tuple(np.ascontiguousarray(o.astype(np.float32)) for o in outs)
```

```python
import contextlib
import math
import numpy as np
import concourse.bass as bass
import concourse.mybir as mybir
from concourse.bass_utils import run_bass_kernel_spmd

F32 = mybir.dt.float32
BF16 = mybir.dt.bfloat16
I32 = mybir.dt.int32
U32 = mybir.dt.uint32
AF = mybir.ActivationFunctionType
ALU = mybir.AluOpType
AX = mybir.AxisListType

D = 2048
DC = 16
DFF = 5632
FC = 44
DEPTH = 4
SW = 8
EPOCH = 20000
RMS_EPS = 1e-6
LN_EPS = 1e-5


def _bucket(d):
    d = np.asarray(d, dtype=np.int64)
    n = np.maximum(d, 0)
    nf = np.maximum(n, 16).astype(np.float32)
    big = 16 + (np.log(nf / np.float32(16)) / np.float32(math.log(8)) * np.float32(16)).astype(np.int32)
    return np.where(n < 16, n, np.minimum(big, 31)).astype(np.int64)


def _onehot(d, valid):
    b = np.where(valid, _bucket(d), 32)
    oh = np.zeros((33,) + b.shape, np.float32)
    np.put_along_axis(oh, b[None], 1.0, axis=0)
    return oh


def structural_consts():
    c = {}
    n = np.arange(128)[:, None]
    q = np.arange(128)[None, :]
    kinds = []
    kinds.append(_onehot(q - n, (q - n) >= 0))
    kinds.append(_onehot(128 + q - n, np.ones((128, 128), bool)))
    kinds.append(_onehot(np.full((128, 128), 1000), np.ones((128, 128), bool)))
    kinds.append(_onehot(512 + q - n, n >= q))
    c["oh_pk"] = np.stack(kinds, 1).reshape(33, 4 * 128 * 128)
    m = np.arange(248)[:, None] - 120
    d = q - 16 * m - 31
    c["oh_cmp"] = _onehot(d, d >= 0).reshape(33, 248 * 128)
    p = np.arange(128)
    tiles = []
    for t in range(8):
        cc = t * 128 + p
        d = 16384 - 16 * cc - 15
        tiles.append(_onehot(d, cc >= 1))
    for t in range(4):
        idx = t * 128 + p
        tiles.append(_onehot(512 - idx, np.ones(128, bool)))
    tiles.append(_onehot(np.zeros(128, np.int64), p == 0))
    tiles.append(_onehot(128 - (p % 64), np.ones(128, bool)))
    tiles.append(_onehot(64 - (p % 64), np.ones(128, bool)))
    tiles.append(_onehot(np.full(128, 1000), np.ones(128, bool)))
    tiles.append(_onehot(np.zeros(128, np.int64), p == 0))
    c["oh_s"] = np.stack(tiles, 1).astype(np.float32)
    coef = np.array([1, 2, 2, 2, 1], np.float32)
    mp = np.zeros((128, 32), np.float32)
    for nn in range(127):
        for j in range(32):
            o = nn + 1 - 4 * j
            if 0 <= o <= 4:
                mp[nn, j] = coef[o]
    c["msel_p"] = mp
    am = np.zeros((8, 128, 32), np.float32)
    for i in range(8, 16):
        for ql in range(128):
            cur = (i * 128 + ql) // 64
            for j in range(32):
                if j > cur:
                    am[i - 8, ql, j] = -1e30
                elif j == 0 or j == cur or j == cur - 1:
                    am[i - 8, ql, j] = 1e4
    c["addmask_p"] = am
    ms = np.zeros((1024, 257), np.float32)
    for cc in range(1024):
        for j in range(257):
            o = cc - 4 * j
            if 0 <= o <= 4:
                ms[cc, j] = coef[o]
    c["msel_s"] = ms
    ams = np.zeros((4, 257), np.float32)
    ams[:, [0, 255, 256]] = 1e4
    c["addmask_s"] = ams
    ee = np.zeros((32, 16, 128), np.float32)
    for kt in range(16):
        for nn in range(128):
            ee[(kt * 128 + nn) // 64, kt, nn] = 1.0
    c["eexp"] = ee
    sel = np.zeros((2, 4, 128), np.float32)
    sel[0, :, :64] = 1.0
    sel[1, :, 64:] = 1.0
    c["selhalf"] = sel
    c["eye4"] = np.eye(4, dtype=np.float32)
    c["iota_pg"] = np.tile(np.arange(128, dtype=np.float32)[None, :], (128, 1))
    c["iota_row"] = (np.arange(128) % 64).astype(np.float32)[:, None].copy()
    c["iota_p"] = np.arange(128, dtype=np.float32)[:, None].copy()
    c["gcol"] = np.tile((np.arange(32) // 8).astype(np.float32)[None, :], (128, 1))
    return c


STRUCT_SHAPES = dict(oh_pk=[33, 65536], oh_cmp=[33, 248 * 128], oh_s=[33, 17, 128], msel_p=[128, 32],
                     addmask_p=[8, 128, 32], msel_s=[1024, 257], addmask_s=[4, 257], eexp=[32, 16, 128],
                     selhalf=[2, 4, 128], eye4=[4, 4], iota_pg=[128, 128], iota_row=[128, 1], iota_p=[128, 1], gcol=[128, 32])

class Buf:
    __slots__ = ("name", "last_w", "readers")

    def __init__(self, name):
        self.name = name
        self.last_w = None
        self.readers = []


class TK:
    def __init__(self, nc, es, same_engine_sync=True):
        self.nc = nc
        self.es = es
        self.eng = {"pe": nc.tensor, "dve": nc.vector, "act": nc.scalar, "pool": nc.gpsimd, "sp": nc.sync}
        self.cur = {}
        self.nsem = 0
        for q in self.eng:
            self.cur[q] = [self._newsem(q), 0]
        self.seen = {q: {} for q in self.eng}
        self.dpool = {}
        for q, n in (("sp", 20), ("act", 8), ("pool", 12)):
            self.dpool[q] = [[self._newsem("d" + q), 0] for _ in range(n)]
        self.dnext = {q: 0 for q in self.dpool}
        self.same = same_engine_sync
        self.ninstr = 0
        self.nwait = 0

    def _newsem(self, tag):
        self.nsem += 1
        return self.es.enter_context(self.nc.semaphore("s_%s_%d" % (tag, self.nsem)))

    def _wait(self, q, ev):
        sem, val, owner = ev
        key = id(sem)
        if self.seen[q].get(key, 0) >= val:
            return
        self.eng[q].wait_ge(sem, val)
        self.nwait += 1
        self.seen[q][key] = val

    def _deps(self, q, reads, writes, is_dma):
        evs = []
        for b in reads:
            if b.last_w is not None:
                evs.append(b.last_w)
        for b in writes:
            if b.last_w is not None:
                evs.append(b.last_w)
            evs.extend(b.readers)
        for ev in evs:
            if ev[2] == q and not is_dma:
                if q == "pe" or not self.same:
                    continue
            self._wait(q, ev)

    def _record(self, ev, reads, writes):
        for b in writes:
            b.last_w = ev
            b.readers = []
        for b in reads:
            b.readers = [e for e in b.readers if not (e[2] == ev[2] and e[0] is ev[0])] + [ev]

    def op(self, q, fn, reads=(), writes=()):
        self._deps(q, reads, writes, False)
        c = self.cur[q]
        if c[1] >= EPOCH:
            c[0] = self._newsem(q)
            c[1] = 0
        ins = fn(self.eng[q])
        c[1] += 1
        ins.then_inc(c[0], 1)
        ev = (c[0], c[1], q)
        self._record(ev, reads, writes)
        self.ninstr += 1
        return ev

    def dma(self, q, fn, reads=(), writes=()):
        self._deps(q, reads, writes, True)
        pool = self.dpool[q]
        i = self.dnext[q]
        self.dnext[q] = (i + 1) % len(pool)
        s = pool[i]
        if s[1] > 0:
            self._wait(q, (s[0], s[1], "dma" + q))
        ins = fn(self.eng[q])
        s[1] += 16
        ins.then_inc(s[0], 16)
        ev = (s[0], s[1], "dma" + q)
        self._record(ev, reads, writes)
        self.ninstr += 1
        return ev

    def barrier(self):
        evs = []
        for q, c in self.cur.items():
            if c[1] > 0:
                evs.append((c[0], c[1], q))
        for q, pool in self.dpool.items():
            for s in pool:
                if s[1] > 0:
                    evs.append((s[0], s[1], "dma" + q))
        for q in self.eng:
            for ev in evs:
                if ev[2] == q:
                    continue
                self._wait(q, ev)


class Builder:
    def __init__(self, T=2048, layers=(0, 1, 2, 3), do_mixer=True, do_ffn=True, dbg=False, nphys=1280):
        self.NPHYS = nphys
        self.T = T
        self.NT = T // 512
        self.TT = T + SW
        self.layers = layers
        self.do_mixer = do_mixer
        self.do_ffn = do_ffn
        self.dbg = dbg
        self.nc = bass.Bass("TRN2", target_bir_lowering=False)
        self.es = contextlib.ExitStack()
        self.tk = TK(self.nc, self.es)
        self.tiles = [(i * 512, 512) for i in range(self.NT)] + [(T, SW)]
        self.dq = 0
        self.wcache = {}

    def din(self, name, shape, dt=F32):
        return self.nc.dram_tensor(name, list(shape), dt, kind="ExternalInput").ap()

    def dout(self, name, shape, dt=F32):
        return self.nc.dram_tensor(name, list(shape), dt, kind="ExternalOutput").ap()

    def dscr(self, name, shape, dt=F32):
        return self.nc.dram_tensor(name, list(shape), dt, kind="Internal").ap()

    def sb(self, name, shape, dt=F32, stack=None):
        self.uid = getattr(self, "uid", 0) + 1
        return (stack or self.es).enter_context(self.nc.sbuf_tensor("%s_u%d" % (name, self.uid), list(shape), dt))

    def ps(self, name, shape, dt=F32, stack=None):
        return (stack or self.es).enter_context(self.nc.psum_tensor(name, list(shape), dt))

    def op(self, q, fn, r=(), w=()):
        return self.tk.op(q, fn, r, w)

    def dma(self, fn, r=(), w=(), q=None):
        if q is None:
            q = "sp"
        return self.tk.dma(q, fn, r, w)

    def declare_io(self):
        T = self.T
        d = self.din
        self.x_p = d("x_p", [T, D])
        self.x_s = d("x_s", [1, D])
        self.norm_gain = d("norm_gain", [DEPTH * 4, D])
        self.ffn_w_up = d("ffn_w_up", [DEPTH, D, 2 * DFF])
        self.ffn_dw = d("ffn_dw", [DEPTH * 3, DFF])
        self.ffn_dw_b = d("ffn_dw_b", [DEPTH, DFF])
        self.ffn_w_down = d("ffn_w_down", [DEPTH, DFF, D])
        self.state_ffn = d("state_ffn", [DEPTH, 2, DFF])
        self.conv_w_pw1 = d("conv_w_pw1", [D, 2 * D])
        self.conv_dw = d("conv_dw", [31, D])
        self.conv_dw_b = d("conv_dw_b", [1, D])
        self.conv_ln_g = d("conv_ln_g", [1, D])
        self.conv_ln_b = d("conv_ln_b", [1, D])
        self.conv_w_pw2 = d("conv_w_pw2", [D, D])
        self.state_conv = d("state_conv", [30, D])
        self.ssm_a_re = d("ssm_a_re", [128, 64])
        self.ssm_a_im = d("ssm_a_im", [128, 64])
        self.ssm_log_dt = d("ssm_log_dt", [1, 128])
        self.ssm_b_re = d("ssm_b_re", [128, 64, 16])
        self.ssm_b_im = d("ssm_b_im", [128, 64, 16])
        self.ssm_c_re = d("ssm_c_re", [128, 16, 64])
        self.ssm_c_im = d("ssm_c_im", [128, 16, 64])
        self.ssm_d = d("ssm_d", [1, D])
        self.ssm_w_glu = d("ssm_w_glu", [D, 2 * D])
        self.state_ssm_re = d("state_ssm_re", [128, 64])
        self.state_ssm_im = d("state_ssm_im", [128, 64])
        o = self.dout
        self.ssm_re_p = o("ssm_re_p", [128, 64])
        self.ssm_re_s = o("ssm_re_s", [128, 64])
        self.ssm_im_p = o("ssm_im_p", [128, 64])
        self.ssm_im_s = o("ssm_im_s", [128, 64])
        self.conv_p = o("conv_p", [30, D])
        self.conv_s = o("conv_s", [30, D])
        self.y_p = o("y_p", [T, D])
        self.y_s = o("y_s", [1, D])
        self.ffn_p = o("ffn_p", [DEPTH, 2, DFF])
        self.ffn_s = o("ffn_s", [DEPTH, 2, DFF])
        self.nsa_declare()
        self.XR = self.dscr("XR", [DC, 128, self.TT])
        self.bXR = [Buf("XR%d" % i) for i in range(len(self.tiles))]

    def setup_consts(self):
        tk = self.tk
        self.ident = self.sb("ident", [128, 128], F32)
        self.ident_b = self.sb("ident_b", [128, 128], BF16)
        self.ones_b = self.sb("ones_b", [128, 128], BF16)
        self.gains = self.sb("gains", [128, DEPTH * 4, DC], F32)
        self.b_const = Buf("const")
        nc = self.nc
        self.iot = self.sb("iot", [128, 128], F32)
        self.op("pool", lambda e: e.iota(self.iot[:], pattern=[[1, 128]], base=0, channel_multiplier=-1,
                                         allow_small_or_imprecise_dtypes=True), w=[self.b_const])
        self.op("dve", lambda e: e.tensor_scalar(out=self.ident[:], in0=self.iot[:], scalar1=0.0, scalar2=None,
                                                 op0=ALU.is_equal), r=[self.b_const], w=[self.b_const])
        self.op("dve", lambda e: e.tensor_copy(out=self.ident_b[:], in_=self.ident[:]), r=[self.b_const],
                w=[self.b_const])
        self.op("dve", lambda e: e.memset(self.ones_b[:], 1.0), w=[self.b_const])
        self.ones_f = self.sb("ones_f", [128, 128], F32)
        self.op("dve", lambda e: e.memset(self.ones_f[:], 1.0), w=[self.b_const])
        self.dma(lambda e: e.dma_start(out=self.gains[:], in_=self.norm_gain.rearrange("l (c p) -> p l c", p=128),
                                       allow_slow_non_contiguous=True), w=[self.b_const])
        self.psb = [self.ps("psb%d" % i, [128, 512], F32) for i in range(8)]
        self.bps = [Buf("ps%d" % i) for i in range(8)]

    def phase_input(self):
        with contextlib.ExitStack() as st:
            NB = 2
            tin = [self.sb("tin%d" % i, [128, D], F32, st) for i in range(NB)]
            tout = [self.sb("tout%d" % i, [128, DC, 128], F32, st) for i in range(NB)]
            btin = [Buf("tin") for _ in range(NB)]
            btout = [Buf("tout") for _ in range(NB)]
            XRv = self.XR.rearrange("c p t -> p c t")
            nblk = self.T // 128
            for b in range(nblk + 1):
                k = b % NB
                samp = (b == nblk)
                rows = 1 if samp else 128
                if samp:
                    self.op("pool", lambda e: e.memset(tin[k][:], 0.0), w=[btin[k]])
                    self.dma(lambda e: e.dma_start(out=tin[k][0:1, :], in_=self.x_s[0:1, :]), w=[btin[k]])
                else:
                    self.dma(lambda e: e.dma_start(out=tin[k][:], in_=self.x_p[b * 128:(b + 1) * 128, :]),
                             w=[btin[k]])
                for g in range(4):
                    pb = 4 + (g % 2)
                    for j in range(4):
                        c = g * 4 + j
                        self.op("pe", lambda e: e.transpose(out=self.psb[pb][:, j * 128:(j + 1) * 128],
                                                            in_=tin[k][:, c * 128:(c + 1) * 128],
                                                            identity=self.ident[:]),
                                r=[btin[k], self.b_const], w=[self.bps[pb]])
                    eng = "act" if g % 2 == 0 else "dve"
                    if eng == "act":
                        self.op("act", lambda e: e.copy(out=tout[k][:, g * 4:(g + 1) * 4, :],
                                                        in_=self.psb[pb][:].rearrange("p (j t) -> p j t", j=4)),
                                r=[self.bps[pb]], w=[btout[k]])
                    else:
                        self.op("dve", lambda e: e.tensor_copy(out=tout[k][:, g * 4:(g + 1) * 4, :],
                                                               in_=self.psb[pb][:].rearrange("p (j t) -> p j t", j=4)),
                                r=[self.bps[pb]], w=[btout[k]])
                if samp:
                    ti = len(self.tiles) - 1
                    self.dma(lambda e: e.dma_start(out=XRv[:, :, self.T:self.T + SW], in_=tout[k][:, :, 0:SW]),
                             r=[btout[k]], w=[self.bXR[ti]])
                else:
                    ti = b // 4
                    self.dma(lambda e: e.dma_start(out=XRv[:, :, b * 128:(b + 1) * 128], in_=tout[k][:]),
                             r=[btout[k]], w=[self.bXR[ti]])
            self.tk.barrier()

    def phase_output(self):
        with contextlib.ExitStack() as st:
            NB = 2
            tin = [self.sb("oin%d" % i, [128, DC, 128], F32, st) for i in range(NB)]
            tout = [self.sb("oout%d" % i, [128, D], F32, st) for i in range(NB)]
            btin = [Buf("oin") for _ in range(NB)]
            btout = [Buf("oout") for _ in range(NB)]
            XRv = self.XR.rearrange("c p t -> p c t")
            nblk = self.T // 128
            for b in range(nblk + 1):
                k = b % NB
                samp = (b == nblk)
                if samp:
                    ti = len(self.tiles) - 1
                    self.op("pool", lambda e: e.memset(tin[k][:], 0.0), w=[btin[k]])
                    self.dma(lambda e: e.dma_start(out=tin[k][:, :, 0:SW], in_=XRv[:, :, self.T:self.T + SW]),
                             r=[self.bXR[ti]], w=[btin[k]])
                else:
                    ti = b // 4
                    self.dma(lambda e: e.dma_start(out=tin[k][:], in_=XRv[:, :, b * 128:(b + 1) * 128]),
                             r=[self.bXR[ti]], w=[btin[k]])
                for g in range(4):
                    pb = 4 + (g % 2)
                    for j in range(4):
                        c = g * 4 + j
                        self.op("pe", lambda e: e.transpose(out=self.psb[pb][:, j * 128:(j + 1) * 128],
                                                            in_=tin[k][:, c, :], identity=self.ident[:]),
                                r=[btin[k], self.b_const], w=[self.bps[pb]])
                    if g % 2 == 0:
                        self.op("act", lambda e: e.copy(out=tout[k][:, g * 512:(g + 1) * 512], in_=self.psb[pb][:]),
                                r=[self.bps[pb]], w=[btout[k]])
                    else:
                        self.op("dve", lambda e: e.tensor_copy(out=tout[k][:, g * 512:(g + 1) * 512],
                                                               in_=self.psb[pb][:]),
                                r=[self.bps[pb]], w=[btout[k]])
                if samp:
                    self.dma(lambda e: e.dma_start(out=self.y_s[0:1, :], in_=tout[k][0:1, :]), r=[btout[k]])
                else:
                    self.dma(lambda e: e.dma_start(out=self.y_p[b * 128:(b + 1) * 128, :], in_=tout[k][:]),
                             r=[btout[k]])
            self.tk.barrier()

    def alloc_rowlocal(self, st):
        self.xr = self.sb("xr", [128, DC, 512], F32, st)
        self.b_xr = Buf("xr")
        self.xn = self.sb("xn", [128, DC, 512], BF16, st)
        self.b_xn = Buf("xn")
        self.yy = self.sb("yy", [128, DC, 512], F32, st)
        self.b_yy = Buf("yy")
        self.rstd = self.sb("rstd", [128, 512], F32, st)
        self.b_rstd = Buf("rstd")
        self.NWB = 4
        self.NWF = 2
        self.wf = [self.sb("wf%d" % i, [128, 8, 2, 128], F32, st) for i in range(self.NWF)]
        self.wb = [self.sb("wb%d" % i, [128, 8, 2, 128], BF16, st) for i in range(self.NWB)]
        self.b_wf = [Buf("wf") for _ in range(self.NWF)]
        self.b_wb = [Buf("wb") for _ in range(self.NWB)]
        self.wctr = 0
        self.wfctr = 0

    def load_xr(self, ti):
        t0, W = self.tiles[ti]
        XRv = self.XR.rearrange("c p t -> p c t")
        self.dma(lambda e: e.dma_start(out=self.xr[:, :, :W], in_=XRv[:, :, t0:t0 + W]),
                 r=[self.bXR[ti]], w=[self.b_xr])

    def store_xr(self, ti):
        t0, W = self.tiles[ti]
        XRv = self.XR.rearrange("c p t -> p c t")
        self.dma(lambda e: e.dma_start(out=XRv[:, :, t0:t0 + W], in_=self.xr[:, :, :W]),
                 r=[self.b_xr], w=[self.bXR[ti]])

    def rms_stats(self, src, bsrc, W, sq, bsq):
        self.op("act", lambda e: e.activation(out=sq[:, :, :W], in_=src[:, :, :W], func=AF.Square),
                r=[bsrc], w=[bsq])
        pb = 4
        for c in range(DC):
            self.op("pe", lambda e: e.matmul(self.psb[pb][:, :W], lhsT=self.ones_b[:], rhs=sq[:, c, :W],
                                             start=(c == 0), stop=(c == DC - 1)),
                    r=[bsq, self.b_const], w=[self.bps[pb]])
        self.op("act", lambda e: e.activation(out=self.rstd[:, :W], in_=self.psb[pb][:, :W], func=AF.Sqrt,
                                              bias=self.eps_rms[:, 0:1], scale=1.0 / D),
                r=[self.bps[pb], self.b_const], w=[self.b_rstd])
        self.op("dve", lambda e: e.reciprocal(out=self.rstd[:, :W], in_=self.rstd[:, :W]),
                r=[self.b_rstd], w=[self.b_rstd])

    def pre_norm(self, ti, gidx, xn_f32=None, b_xnf=None):
        t0, W = self.tiles[ti]
        self.load_xr(ti)
        self.rms_stats(self.xr, self.b_xr, W, self.xn, self.b_xn)
        for c in range(DC):
            if xn_f32 is not None:
                self.op("dve", lambda e: e.scalar_tensor_tensor(out=xn_f32[:, c, :W], in0=self.xr[:, c, :W],
                                                                scalar=self.gains[:, gidx, c:c + 1],
                                                                in1=self.rstd[:, :W], op0=ALU.mult, op1=ALU.mult),
                        r=[self.b_xr, self.b_rstd, self.b_const], w=[b_xnf])
                self.op("act", lambda e: e.copy(out=self.xn[:, c, :W], in_=xn_f32[:, c, :W]),
                        r=[b_xnf], w=[self.b_xn])
            else:
                self.op("dve", lambda e: e.scalar_tensor_tensor(out=self.xn[:, c, :W], in0=self.xr[:, c, :W],
                                                                scalar=self.gains[:, gidx, c:c + 1],
                                                                in1=self.rstd[:, :W], op0=ALU.mult, op1=ALU.mult),
                        r=[self.b_xr, self.b_rstd, self.b_const], w=[self.b_xn])

    def post_norm_residual(self, ti, gidx):
        t0, W = self.tiles[ti]
        self.rms_stats(self.yy, self.b_yy, W, self.xn, self.b_xn)
        for c in range(DC):
            self.op("dve", lambda e: e.scalar_tensor_tensor(out=self.yy[:, c, :W], in0=self.yy[:, c, :W],
                                                            scalar=self.gains[:, gidx, c:c + 1],
                                                            in1=self.rstd[:, :W], op0=ALU.mult, op1=ALU.mult),
                    r=[self.b_yy, self.b_rstd, self.b_const], w=[self.b_yy])
            self.op("pool", lambda e: e.tensor_tensor(out=self.xr[:, c, :W], in0=self.xr[:, c, :W],
                                                      in1=self.yy[:, c, :W], op=ALU.add),
                    r=[self.b_yy, self.b_xr], w=[self.b_xr])
        self.store_xr(ti)

    def gemm(self, wview, KC, npairs, xin, bxin, W, epilogue, wkey=None, first=True):
        nkb = (KC + 7) // 8
        cache = None
        if wkey is not None:
            if wkey not in self.wcache:
                self.wcache[wkey] = (self.dscr("wc_" + wkey, [npairs * nkb, 128, 2048], BF16), Buf("wc_" + wkey))
            cache, b_cache = self.wcache[wkey]
        for p in range(npairs):
            pa = (p % 2) * 2
            pbk = pa + 1
            for kb in range(nkb):
                k0 = kb * 8
                kn = min(8, KC - k0)
                i = self.wctr % self.NWB
                self.wctr += 1
                blk = p * nkb + kb
                if first or cache is None:
                    fi = self.wfctr % self.NWF
                    self.wfctr += 1
                    for h in range(2):
                        src = wview(p, k0, kn, h)
                        self.dma(lambda e: e.dma_start(out=self.wf[fi][:, :kn, h, :], in_=src), w=[self.b_wf[fi]], q="sp")
                    ceng = "pool" if (self.wfctr % 2 == 0) else "act"
                    if ceng == "pool":
                        self.op("pool", lambda e: e.tensor_copy(out=self.wb[i][:, :kn], in_=self.wf[fi][:, :kn]),
                                r=[self.b_wf[fi]], w=[self.b_wb[i]])
                    else:
                        self.op("act", lambda e: e.copy(out=self.wb[i][:, :kn], in_=self.wf[fi][:, :kn]),
                                r=[self.b_wf[fi]], w=[self.b_wb[i]])
                    if cache is not None:
                        self.dma(lambda e: e.dma_start(out=cache[blk, :, 0:kn * 256],
                                                       in_=self.wb[i][:, :kn].rearrange("p k h n -> p (k h n)")),
                                 r=[self.b_wb[i]], w=[b_cache], q=ceng)
                else:
                    self.dma(lambda e: e.dma_start(out=self.wb[i][:, :kn].rearrange("p k h n -> p (k h n)"),
                                                   in_=cache[blk, :, 0:kn * 256]), r=[b_cache], w=[self.b_wb[i]], q="sp")
                for kk in range(kn):
                    kc = k0 + kk
                    for h, pbank in ((0, pa), (1, pbk)):
                        self.op("pe", lambda e: e.matmul(self.psb[pbank][:, :W], lhsT=self.wb[i][:, kk, h, :],
                                                         rhs=xin[:, kc, :W], start=(kc == 0), stop=(kc == KC - 1)),
                                r=[self.b_wb[i], bxin], w=[self.bps[pbank]])
            epilogue(p, self.psb[pa], self.bps[pa], self.psb[pbk], self.bps[pbk])

    def ffn_layer(self, li):
        tk = self.tk
        with contextlib.ExitStack() as st:
            self.alloc_rowlocal(st)
            hh = self.sb("hh", [128, FC, 512], BF16, st)
            b_hh = Buf("hh")
            ghist = self.sb("ghist", [128, 2, FC], F32, st)
            b_gh = Buf("ghist")
            gbuf = [self.sb("gbuf%d" % i, [128, 514], F32, st) for i in range(2)]
            b_gb = [Buf("gbuf") for _ in range(2)]
            acc = [self.sb("acc%d" % i, [128, 512], F32, st) for i in range(2)]
            b_acc = [Buf("acc") for _ in range(2)]
            dwT = self.sb("dwT", [128, 3, FC], F32, st)
            dbT = self.sb("dbT", [128, FC], F32, st)
            b_dw = Buf("dw")
            hist_in = self.sb("hist_in", [88, 128], F32, st)
            b_hi = Buf("hi")
            hsT = self.sb("hsT", [128, 2, FC], F32, st)
            b_hs = Buf("hsT")
            hout = self.sb("hout", [88, 128], F32, st)
            b_ho = Buf("hout")
            self.dma(lambda e: e.dma_start(out=dwT[:], in_=self.ffn_dw[li * 3:(li + 1) * 3, :]
                                           .rearrange("w (c p) -> p w c", p=128), allow_slow_non_contiguous=True),
                     w=[b_dw])
            self.dma(lambda e: e.dma_start(out=dbT[:], in_=self.ffn_dw_b[li:li + 1, :]
                                           .rearrange("o (c p) -> p (o c)", p=128), allow_slow_non_contiguous=True),
                     w=[b_dw])
            self.dma(lambda e: e.dma_start(out=hist_in[:], in_=self.state_ffn[li].rearrange("t (c p) -> (t c) p", p=128)),
                     w=[b_hi])
            self.op("pe", lambda e: e.transpose(out=self.psb[5][:, 0:88], in_=hist_in[:], identity=self.ident[0:88, 0:88]),
                    r=[b_hi, self.b_const], w=[self.bps[5]])
            self.op("dve", lambda e: e.tensor_copy(out=hsT[:].rearrange("p t c -> p (t c)"), in_=self.psb[5][:, 0:88]),
                    r=[self.bps[5]], w=[b_hs])
            self.op("dve", lambda e: e.memset(ghist[:], 0.0), w=[b_gh])

            w_up = self.ffn_w_up[li].rearrange("(kc p) (h f) -> p kc h f", p=128, h=2)
            w_dn = self.ffn_w_down[li].rearrange("(kc p) (f h n) -> p kc f h n", p=128, h=2, n=128)

            for ti in range(len(self.tiles)):
                t0, W = self.tiles[ti]
                samp = (ti == len(self.tiles) - 1)
                self.pre_norm(ti, li * 4 + 2)
                if samp:
                    self.op("dve", lambda e: e.tensor_copy(out=ghist[:], in_=hsT[:]), r=[b_hs], w=[b_gh])

                def ep_up(p, psA, bA, psB, bB):
                    k = p % 2
                    self.op("act", lambda e: e.copy(out=gbuf[k][:, 2:2 + W], in_=psA[:, :W]), r=[bA], w=[b_gb[k]])
                    self.op("dve", lambda e: e.tensor_copy(out=gbuf[k][:, 0:2], in_=ghist[:, :, p]),
                            r=[b_gh], w=[b_gb[k]])
                    self.op("act", lambda e: e.activation(out=acc[k][:, :W], in_=gbuf[k][:, 2:2 + W], func=AF.Identity,
                                                          bias=dbT[:, p:p + 1], scale=dwT[:, 2, p:p + 1]),
                            r=[b_gb[k], b_dw], w=[b_acc[k]])
                    self.op("dve", lambda e: e.scalar_tensor_tensor(out=acc[k][:, :W], in0=gbuf[k][:, 1:1 + W],
                                                                    scalar=dwT[:, 1, p:p + 1], in1=acc[k][:, :W],
                                                                    op0=ALU.mult, op1=ALU.add),
                            r=[b_gb[k], b_dw, b_acc[k]], w=[b_acc[k]])
                    self.op("dve", lambda e: e.scalar_tensor_tensor(out=acc[k][:, :W], in0=gbuf[k][:, 0:W],
                                                                    scalar=dwT[:, 0, p:p + 1], in1=acc[k][:, :W],
                                                                    op0=ALU.mult, op1=ALU.add),
                            r=[b_gb[k], b_dw, b_acc[k]], w=[b_acc[k]])
                    if samp:
                        self.op("act", lambda e: e.copy(out=ghist[:, 0, p:p + 1], in_=gbuf[k][:, 1:2]),
                                r=[b_gb[k]], w=[b_gh])
                        self.op("act", lambda e: e.copy(out=ghist[:, 1, p:p + 1], in_=gbuf[k][:, 2:3]),
                                r=[b_gb[k]], w=[b_gh])
                    else:
                        self.op("act", lambda e: e.copy(out=ghist[:, :, p], in_=gbuf[k][:, W:W + 2]),
                                r=[b_gb[k]], w=[b_gh])
                    self.op("act", lambda e: e.activation(out=acc[k][:, :W], in_=acc[k][:, :W],
                                                          func=AF.Gelu_apprx_tanh),
                            r=[b_acc[k]], w=[b_acc[k]])
                    self.op("dve", lambda e: e.tensor_tensor(out=hh[:, p, :W], in0=acc[k][:, :W], in1=psB[:, :W],
                                                             op=ALU.mult),
                            r=[b_acc[k], bB], w=[b_hh])

                self.gemm(lambda p, k0, kn, h: w_up[:, k0:k0 + kn, h, p * 128:(p + 1) * 128], DC, FC,
                          self.xn, self.b_xn, W, ep_up, wkey='up%d' % li, first=(ti == 0))

                def ep_dn(p, psA, bA, psB, bB):
                    self.op("act", lambda e: e.copy(out=self.yy[:, 2 * p, :W], in_=psA[:, :W]), r=[bA], w=[self.b_yy])
                    self.op("dve", lambda e: e.tensor_copy(out=self.yy[:, 2 * p + 1, :W], in_=psB[:, :W]), r=[bB],
                            w=[self.b_yy])

                self.gemm(lambda p, k0, kn, h: w_dn[:, k0:k0 + kn, p, h, :], FC, DC // 2, hh, b_hh, W, ep_dn, wkey='dn%d' % li, first=(ti == 0))
                self.post_norm_residual(ti, li * 4 + 3)

                if ti == self.NT - 1 or samp:
                    dst = self.ffn_s if samp else self.ffn_p
                    self.op("pe", lambda e: e.transpose(out=self.psb[5][0:88, 0:128],
                                                        in_=ghist[:].rearrange("p t c -> p (t c)"),
                                                        identity=self.ident[:]),
                            r=[b_gh, self.b_const], w=[self.bps[5]])
                    self.op("dve", lambda e: e.tensor_copy(out=hout[:], in_=self.psb[5][0:88, 0:128]),
                            r=[self.bps[5]], w=[b_ho])
                    self.dma(lambda e: e.dma_start(out=dst[li].rearrange("t (c p) -> (t c) p", p=128), in_=hout[:]),
                             r=[b_ho])
            self.tk.barrier()

    def conf_layer(self, li):
        with contextlib.ExitStack() as st:
            self.alloc_rowlocal(st)
            CW = 31
            ubuf = self.sb("ubuf", [128, DC, 30 + 512], F32, st)
            b_ub = Buf("ubuf")
            sig = [self.sb("sig%d" % i, [128, 512], F32, st) for i in range(2)]
            b_sig = [Buf("sig") for _ in range(2)]
            dwT = self.sb("cdwT", [128, CW, DC], F32, st)
            prm = self.sb("cprm", [128, 3, DC], F32, st)
            b_dw = Buf("cdw")
            mean = self.sb("mean", [128, 512], F32, st)
            b_mean = Buf("mean")
            msq = self.sb("msq", [128, 512], F32, st)
            b_msq = Buf("msq")
            hio = self.sb("hio", [30, D], F32, st)
            b_hio = Buf("hio")
            cacc = self.sb("cacc", [128, 512], F32, st); b_cacc = Buf("cacc")
            ctmp = [self.sb("ctmp%d" % i, [128, 512], F32, st) for i in range(2)]; b_ctmp = [Buf("ctmp") for _ in range(2)]
            eps_ln = self.sb("eps_ln", [128, 1], F32, st)
            self.op("dve", lambda e: e.memset(eps_ln[:], LN_EPS), w=[b_dw])
            self.dma(lambda e: e.dma_start(out=dwT[:], in_=self.conv_dw.rearrange("w (c p) -> p w c", p=128),
                                           allow_slow_non_contiguous=True), w=[b_dw])
            for k, src in enumerate((self.conv_dw_b, self.conv_ln_g, self.conv_ln_b)):
                self.dma(lambda e: e.dma_start(out=prm[:, k, :], in_=src.rearrange("o (c p) -> p (o c)", p=128),
                                               allow_slow_non_contiguous=True), w=[b_dw])
            self.op("dve", lambda e: e.memset(ubuf[:, :, 0:30], 0.0), w=[b_ub])
            w1 = self.conv_w_pw1.rearrange("(kc p) (h f) -> p kc h f", p=128, h=2)
            w2 = self.conv_w_pw2.rearrange("(kc p) (f h n) -> p kc f h n", p=128, h=2, n=128)
            hc = self.yy
            b_hc = self.b_yy
            for ti in range(len(self.tiles)):
                t0, W = self.tiles[ti]
                samp = (ti == len(self.tiles) - 1)
                self.pre_norm(ti, li * 4 + 0)
                if samp:
                    self.dma(lambda e: e.dma_start(out=hio[:], in_=self.state_conv[:, :]), w=[b_hio])
                    for c in range(DC):
                        pb = 4 + (c % 2)
                        self.op("pe", lambda e: e.transpose(out=self.psb[pb][:, 0:30], in_=hio[:, c * 128:(c + 1) * 128],
                                                            identity=self.ident[0:30, 0:30]),
                                r=[b_hio, self.b_const], w=[self.bps[pb]])
                        self.op("dve", lambda e: e.tensor_copy(out=ubuf[:, c, 0:30], in_=self.psb[pb][:, 0:30]),
                                r=[self.bps[pb]], w=[b_ub])

                def ep1(p, psA, bA, psB, bB):
                    k = p % 2
                    self.op("act", lambda e: e.activation(out=sig[k][:, :W], in_=psB[:, :W], func=AF.Sigmoid),
                            r=[bB], w=[b_sig[k]])
                    self.op("dve", lambda e: e.tensor_tensor(out=ubuf[:, p, 30:30 + W], in0=psA[:, :W],
                                                             in1=sig[k][:, :W], op=ALU.mult),
                            r=[bA, b_sig[k]], w=[b_ub])

                self.gemm(lambda p, k0, kn, h: w1[:, k0:k0 + kn, h, p * 128:(p + 1) * 128], DC, DC,
                          self.xn, self.b_xn, W, ep1, wkey='pw1', first=(ti == 0))
                for c in range(DC):
                    self.op("act", lambda e: e.activation(out=hc[:, c, :W], in_=ubuf[:, c, 30:30 + W], func=AF.Identity,
                                                          bias=prm[:, 0, c:c + 1], scale=dwT[:, 30, c:c + 1]),
                            r=[b_ub, b_dw], w=[b_hc])
                    for w in range(15):
                        self.op("dve", lambda e: e.scalar_tensor_tensor(out=hc[:, c, :W], in0=ubuf[:, c, w:w + W],
                                                                        scalar=dwT[:, w, c:c + 1], in1=hc[:, c, :W],
                                                                        op0=ALU.mult, op1=ALU.add),
                                r=[b_ub, b_dw, b_hc], w=[b_hc])
                    self.op("act", lambda e: e.activation(out=cacc[:, :W], in_=ubuf[:, c, 15:15 + W], func=AF.Copy,
                                                          scale=dwT[:, 15, c:c + 1]), r=[b_ub, b_dw], w=[b_cacc])
                    for w in range(16, 30):
                        kx = w % 2
                        self.op("act", lambda e: e.activation(out=ctmp[kx][:, :W], in_=ubuf[:, c, w:w + W], func=AF.Copy,
                                                              scale=dwT[:, w, c:c + 1]), r=[b_ub, b_dw], w=[b_ctmp[kx]])
                        self.op("pool", lambda e: e.tensor_tensor(out=cacc[:, :W], in0=cacc[:, :W], in1=ctmp[kx][:, :W], op=ALU.add),
                                r=[b_ctmp[kx], b_cacc], w=[b_cacc])
                    self.op("pool", lambda e: e.tensor_tensor(out=hc[:, c, :W], in0=hc[:, c, :W], in1=cacc[:, :W], op=ALU.add),
                            r=[b_cacc, b_hc], w=[b_hc])
                if ti == self.NT - 1 or samp:
                    c0 = 1 if samp else W
                    for c in range(DC):
                        pb = 4 + (c % 2)
                        self.op("pe", lambda e: e.transpose(out=self.psb[pb][0:30, 0:128], in_=ubuf[:, c, c0:c0 + 30],
                                                            identity=self.ident[:]),
                                r=[b_ub, self.b_const], w=[self.bps[pb]])
                        self.op("act", lambda e: e.copy(out=hio[:, c * 128:(c + 1) * 128], in_=self.psb[pb][0:30, 0:128]),
                                r=[self.bps[pb]], w=[b_hio])
                    dst = self.conv_s if samp else self.conv_p
                    self.dma(lambda e: e.dma_start(out=dst[:, :], in_=hio[:]), r=[b_hio])
                if not samp:
                    self.op("pool", lambda e: e.tensor_copy(out=ubuf[:, :, 0:30], in_=ubuf[:, :, W:W + 30]),
                            r=[b_ub], w=[b_ub])
                self.op("act", lambda e: e.activation(out=self.xn[:, :, :W], in_=hc[:, :, :W], func=AF.Square),
                        r=[b_hc], w=[self.b_xn])
                for c in range(DC):
                    self.op("pe", lambda e: e.matmul(self.psb[4][:, :W], lhsT=self.ones_f[:], rhs=hc[:, c, :W],
                                                     start=(c == 0), stop=(c == DC - 1)),
                            r=[b_hc, self.b_const], w=[self.bps[4]])
                for c in range(DC):
                    self.op("pe", lambda e: e.matmul(self.psb[5][:, :W], lhsT=self.ones_b[:], rhs=self.xn[:, c, :W],
                                                     start=(c == 0), stop=(c == DC - 1)),
                            r=[self.b_xn, self.b_const], w=[self.bps[5]])
                self.op("act", lambda e: e.activation(out=mean[:, :W], in_=self.psb[4][:, :W], func=AF.Copy,
                                                      scale=1.0 / D), r=[self.bps[4]], w=[b_mean])
                self.op("dve", lambda e: e.tensor_tensor(out=msq[:, :W], in0=mean[:, :W], in1=mean[:, :W], op=ALU.mult),
                        r=[b_mean], w=[b_msq])
                self.op("dve", lambda e: e.scalar_tensor_tensor(out=msq[:, :W], in0=self.psb[5][:, :W], scalar=1.0 / D,
                                                                in1=msq[:, :W], op0=ALU.mult, op1=ALU.subtract),
                        r=[self.bps[5], b_msq], w=[b_msq])
                self.op("act", lambda e: e.activation(out=self.rstd[:, :W], in_=msq[:, :W], func=AF.Sqrt,
                                                      bias=eps_ln[:, 0:1], scale=1.0), r=[b_msq, b_dw], w=[self.b_rstd])
                self.op("dve", lambda e: e.reciprocal(out=self.rstd[:, :W], in_=self.rstd[:, :W]),
                        r=[self.b_rstd], w=[self.b_rstd])
                for c in range(DC):
                    self.op("pool", lambda e: e.tensor_tensor(out=hc[:, c, :W], in0=hc[:, c, :W], in1=mean[:, :W],
                                                              op=ALU.subtract), r=[b_hc, b_mean], w=[b_hc])
                    self.op("dve", lambda e: e.tensor_tensor(out=hc[:, c, :W], in0=hc[:, c, :W], in1=self.rstd[:, :W],
                                                             op=ALU.mult), r=[b_hc, self.b_rstd], w=[b_hc])
                    self.op("act", lambda e: e.activation(out=self.xn[:, c, :W], in_=hc[:, c, :W], func=AF.Silu,
                                                          bias=prm[:, 2, c:c + 1], scale=prm[:, 1, c:c + 1]),
                            r=[b_hc, b_dw], w=[self.b_xn])

                def ep2(p, psA, bA, psB, bB):
                    self.op("act", lambda e: e.copy(out=self.yy[:, 2 * p, :W], in_=psA[:, :W]), r=[bA], w=[self.b_yy])
                    self.op("dve", lambda e: e.tensor_copy(out=self.yy[:, 2 * p + 1, :W], in_=psB[:, :W]), r=[bB],
                            w=[self.b_yy])

                self.gemm(lambda p, k0, kn, h: w2[:, k0:k0 + kn, p, h, :], DC, DC // 2, self.xn, self.b_xn, W, ep2, wkey='pw2', first=(ti == 0))
                self.post_norm_residual(ti, li * 4 + 1)
            self.tk.barrier()

    def nat_to_scan(self, src_dram_flat, dst, b_dst, tmp64, b_tmp):
        self.dma(lambda e: e.dma_start(out=tmp64[:], in_=src_dram_flat.rearrange("(s g2) p -> s (g2 p)", g2=2)),
                 w=[b_tmp])
        self.op("pe", lambda e: e.transpose(out=self.psb[5][:, 0:64], in_=tmp64[:], identity=self.ident[0:64, 0:64]),
                r=[b_tmp, self.b_const], w=[self.bps[5]])
        self.op("dve", lambda e: e.tensor_copy(out=dst[:], in_=self.psb[5][:, 0:64]), r=[self.bps[5]], w=[b_dst])

    def s5_layer(self, li):
        TWO_PI = 2.0 * math.pi
        with contextlib.ExitStack() as st:
            self.alloc_rowlocal(st)
            rho = self.sb("rho", [128, 64], F32, st)
            c1 = self.sb("c1", [128, 64], F32, st)
            s1 = self.sb("s1", [128, 64], F32, st)
            hpr = self.sb("hpr", [128, 64], F32, st)
            hpi = self.sb("hpi", [128, 64], F32, st)
            k0r = self.sb("k0r", [128, 64], F32, st)
            k0i = self.sb("k0i", [128, 64], F32, st)
            ktm = self.sb("ktm", [128, 64], F32, st)
            dsk = self.sb("dsk", [128, DC], F32, st)
            b_prm = Buf("s5prm")
            b_hp = Buf("hp")
            b_k0 = Buf("k0")
            tmp64 = self.sb("tmp64", [64, 128], F32, st)
            b_t64 = Buf("t64")
            BBs = self.dscr("BBs", [2, 128, 16, 64])
            CCs = self.dscr("CCs", [2, 128, 64, 16])
            PRs = self.dscr("PRs", [3, 128, 64])
            WBs = self.dscr("WBs", [2, 16, 128, 512], BF16)
            WCs = self.dscr("WCs", [2, 16, 128, 512], BF16)
            TAB = self.dscr("TAB", [2, 64, 128, 512])
            b_dr = Buf("s5dram")
            self.dma(lambda e: e.dma_start(out=dsk[:], in_=self.ssm_d.rearrange("o (c p) -> p (o c)", p=128),
                                           allow_slow_non_contiguous=True), w=[b_prm])
            with contextlib.ExitStack() as st2:
                def T2(name, shape, dt=F32):
                    return self.sb(name, shape, dt, st2)
                ar = T2("p_ar", [128, 64]); ai = T2("p_ai", [128, 64]); ldt = T2("p_ldt", [128, 1])
                dt_ = T2("p_dt", [128, 1]); mag = T2("p_mag", [128, 64]); th = T2("p_th", [128, 64])
                r0 = T2("p_r0", [128, 64]); ri_ = T2("p_ri", [128, 64], I32); rf = T2("p_rf", [128, 64])
                m1 = T2("p_m1", [128, 64]); m2 = T2("p_m2", [128, 64])
                cs = T2("p_cs", [128, 64]); sn = T2("p_sn", [128, 64])
                abr = T2("p_abr", [128, 64]); abi = T2("p_abi", [128, 64]); den = T2("p_den", [128, 64])
                cfr = T2("p_cfr", [128, 64]); cfi = T2("p_cfi", [128, 64])
                bre = T2("p_bre", [128, 64, 16]); bim = T2("p_bim", [128, 64, 16])
                bt1 = T2("p_bt1", [128, 64, 16]); bt2 = T2("p_bt2", [128, 64, 16])
                bbT = T2("p_bbT", [128, 16, 64])
                cin = T2("p_cin", [128, 16, 64]); ccT = T2("p_ccT", [128, 64, 16])
                bp = Buf("prep")
                V = "dve"
                self.dma(lambda e: e.dma_start(out=ar[:], in_=self.ssm_a_re[:, :]), w=[bp])
                self.dma(lambda e: e.dma_start(out=ai[:], in_=self.ssm_a_im[:, :]), w=[bp])
                self.dma(lambda e: e.dma_start(out=ldt[:], in_=self.ssm_log_dt.rearrange("o g -> g o"),
                                               allow_slow_non_contiguous=True), w=[bp])
                self.dma(lambda e: e.dma_start(out=bre[:], in_=self.ssm_b_re[:, :, :]), w=[bp])
                self.dma(lambda e: e.dma_start(out=bim[:], in_=self.ssm_b_im[:, :, :]), w=[bp])
                o = lambda q, f: self.op(q, f, r=[bp], w=[bp])
                o("act", lambda e: e.activation(out=dt_[:], in_=ldt[:], func=AF.Exp))
                o(V, lambda e: e.tensor_scalar(out=mag[:], in0=ar[:], scalar1=dt_[:, 0:1], scalar2=None, op0=ALU.mult))
                o("act", lambda e: e.activation(out=mag[:], in_=mag[:], func=AF.Exp))
                o(V, lambda e: e.tensor_scalar(out=th[:], in0=ai[:], scalar1=dt_[:, 0:1], scalar2=1.0 / TWO_PI,
                                               op0=ALU.mult, op1=ALU.mult))

                def sin_turns(dst, off):
                    o(V, lambda e: e.tensor_scalar(out=r0[:], in0=th[:], scalar1=off, scalar2=None, op0=ALU.add))
                    o(V, lambda e: e.tensor_copy(out=ri_[:], in_=r0[:]))
                    o(V, lambda e: e.tensor_copy(out=rf[:], in_=ri_[:]))
                    o(V, lambda e: e.tensor_tensor(out=r0[:], in0=r0[:], in1=rf[:], op=ALU.subtract))
                    o(V, lambda e: e.tensor_scalar(out=m1[:], in0=r0[:], scalar1=0.5, scalar2=None, op0=ALU.is_gt))
                    o(V, lambda e: e.tensor_scalar(out=m2[:], in0=r0[:], scalar1=-0.5, scalar2=None, op0=ALU.is_lt))
                    o(V, lambda e: e.tensor_tensor(out=r0[:], in0=r0[:], in1=m1[:], op=ALU.subtract))
                    o(V, lambda e: e.tensor_tensor(out=r0[:], in0=r0[:], in1=m2[:], op=ALU.add))
                    o("act", lambda e: e.activation(out=dst[:], in_=r0[:], func=AF.Sin, scale=TWO_PI))

                sin_turns(sn, 0.0)
                sin_turns(cs, 0.25)
                o(V, lambda e: e.tensor_tensor(out=abr[:], in0=mag[:], in1=cs[:], op=ALU.mult))
                o(V, lambda e: e.tensor_tensor(out=abi[:], in0=mag[:], in1=sn[:], op=ALU.mult))
                o(V, lambda e: e.tensor_tensor(out=den[:], in0=ar[:], in1=ar[:], op=ALU.mult))
                o(V, lambda e: e.tensor_tensor(out=m1[:], in0=ai[:], in1=ai[:], op=ALU.mult))
                o(V, lambda e: e.tensor_tensor(out=den[:], in0=den[:], in1=m1[:], op=ALU.add))
                o(V, lambda e: e.reciprocal(out=den[:], in_=den[:]))
                o(V, lambda e: e.tensor_scalar(out=m2[:], in0=abr[:], scalar1=-1.0, scalar2=None, op0=ALU.add))
                o(V, lambda e: e.tensor_tensor(out=cfr[:], in0=m2[:], in1=ar[:], op=ALU.mult))
                o(V, lambda e: e.tensor_tensor(out=m1[:], in0=abi[:], in1=ai[:], op=ALU.mult))
                o(V, lambda e: e.tensor_tensor(out=cfr[:], in0=cfr[:], in1=m1[:], op=ALU.add))
                o(V, lambda e: e.tensor_tensor(out=cfr[:], in0=cfr[:], in1=den[:], op=ALU.mult))
                o(V, lambda e: e.tensor_tensor(out=cfi[:], in0=abi[:], in1=ar[:], op=ALU.mult))
                o(V, lambda e: e.tensor_tensor(out=m1[:], in0=m2[:], in1=ai[:], op=ALU.mult))
                o(V, lambda e: e.tensor_tensor(out=cfi[:], in0=cfi[:], in1=m1[:], op=ALU.subtract))
                o(V, lambda e: e.tensor_tensor(out=cfi[:], in0=cfi[:], in1=den[:], op=ALU.mult))
                for k, src in enumerate((mag, cs, sn)):
                    self.dma(lambda e: e.dma_start(out=PRs[k], in_=src[:]), r=[bp], w=[b_dr])
                self.tk.barrier()
                for k, dst in enumerate((rho, c1, s1)):
                    self.nat_to_scan(PRs[k], dst, b_prm, tmp64, b_t64)
                cfr_b = cfr[:].unsqueeze(2).to_broadcast([128, 64, 16])
                cfi_b = cfi[:].unsqueeze(2).to_broadcast([128, 64, 16])
                for rix in range(2):
                    if rix == 0:
                        o(V, lambda e: e.tensor_tensor(out=bt1[:], in0=bre[:], in1=cfr_b, op=ALU.mult))
                        o(V, lambda e: e.tensor_tensor(out=bt2[:], in0=bim[:], in1=cfi_b, op=ALU.mult))
                        o(V, lambda e: e.tensor_tensor(out=bbT[:].rearrange("g c p -> g p c"), in0=bt1[:], in1=bt2[:],
                                                       op=ALU.subtract))
                    else:
                        o(V, lambda e: e.tensor_tensor(out=bt1[:], in0=bim[:], in1=cfr_b, op=ALU.mult))
                        o(V, lambda e: e.tensor_tensor(out=bt2[:], in0=bre[:], in1=cfi_b, op=ALU.mult))
                        o(V, lambda e: e.tensor_tensor(out=bbT[:].rearrange("g c p -> g p c"), in0=bt1[:], in1=bt2[:],
                                                       op=ALU.add))
                    self.dma(lambda e: e.dma_start(out=BBs[rix], in_=bbT[:]), r=[bp], w=[b_dr])
                    self.tk.barrier()
                for rix, src in enumerate((self.ssm_c_re, self.ssm_c_im)):
                    self.dma(lambda e: e.dma_start(out=cin[:], in_=src[:, :, :]), w=[bp])
                    o("act", lambda e: e.activation(out=ccT[:].rearrange("g p c -> g c p"), in_=cin[:], func=AF.Copy,
                                                    scale=(1.0 if rix == 0 else -1.0)))
                    self.dma(lambda e: e.dma_start(out=CCs[rix], in_=ccT[:]), r=[bp], w=[b_dr])
                    self.tk.barrier()
            self.tk.barrier()
            with contextlib.ExitStack() as st2:
                def T2(name, shape, dt=F32):
                    return self.sb(name, shape, dt, st2)
                bp = Buf("prep2")
                o = lambda q, f: self.op(q, f, r=[bp], w=[bp])
                V = "dve"
                bbl = T2("q_bbl", [128, 16, 64]); ccl = T2("q_ccl", [128, 64, 16])
                mkB = T2("q_mkB", [128, 8]); mkC = T2("q_mkC", [128, 4, 8]); mt = T2("q_mt", [128, 4, 8])
                wbt = T2("q_wbt", [128, 8, 64], BF16); wct = T2("q_wct", [128, 4, 8, 16], BF16)
                b_wt = Buf("wt")
                o("pool", lambda e: e.iota(mkB[:], pattern=[[-16, 8]], base=0, channel_multiplier=1,
                                           allow_small_or_imprecise_dtypes=True))
                o(V, lambda e: e.tensor_scalar(out=mt[:, 0, :], in0=mkB[:], scalar1=0.0, scalar2=None, op0=ALU.is_ge))
                o(V, lambda e: e.tensor_scalar(out=mkB[:], in0=mkB[:], scalar1=15.0, scalar2=None, op0=ALU.is_le))
                o(V, lambda e: e.tensor_tensor(out=mkB[:], in0=mkB[:], in1=mt[:, 0, :], op=ALU.mult))
                o("pool", lambda e: e.iota(mkC[:], pattern=[[-128, 4], [64, 8]], base=0, channel_multiplier=-1,
                                           allow_small_or_imprecise_dtypes=True))
                o(V, lambda e: e.tensor_scalar(out=mt[:], in0=mkC[:], scalar1=-63.0, scalar2=None, op0=ALU.is_ge))
                o(V, lambda e: e.tensor_scalar(out=mkC[:], in0=mkC[:], scalar1=0.0, scalar2=None, op0=ALU.is_le))
                o(V, lambda e: e.tensor_tensor(out=mkC[:], in0=mkC[:], in1=mt[:], op=ALU.mult))
                for rix in range(2):
                    self.dma(lambda e: e.dma_start(out=bbl[:], in_=BBs[rix].rearrange("(k g8) c p -> (g8 c) k p", g8=8)),
                             r=[b_dr], w=[bp])
                    self.dma(lambda e: e.dma_start(out=ccl[:], in_=CCs[rix].rearrange("(s g2) p c -> (g2 p) s c", g2=2)),
                             r=[b_dr], w=[bp])
                    for k in range(16):
                        self.op(V, lambda e: e.tensor_tensor(out=wbt[:], in0=bbl[:, k, :].unsqueeze(1).to_broadcast([128, 8, 64]),
                                                             in1=mkB[:].unsqueeze(2).to_broadcast([128, 8, 64]), op=ALU.mult),
                                r=[bp], w=[b_wt])
                        self.dma(lambda e: e.dma_start(out=WBs[rix, k], in_=wbt[:].rearrange("p a b -> p (a b)")),
                                 r=[b_wt], w=[b_dr])
                        for j in range(4):
                            sidx = 4 * k + j
                            self.op(V, lambda e: e.tensor_tensor(out=wct[:, j], in0=ccl[:, sidx, :].unsqueeze(1).to_broadcast([128, 8, 16]),
                                                                 in1=mkC[:, j, :].unsqueeze(2).to_broadcast([128, 8, 16]), op=ALU.mult),
                                    r=[bp], w=[b_wt])
                        self.dma(lambda e: e.dma_start(out=WCs[rix, k], in_=wct[:].rearrange("p j a b -> p (j a b)")),
                                 r=[b_wt], w=[b_dr])
                tc_ = T2("q_tc", [128, 8, 512]); ts_ = T2("q_ts", [128, 8, 512])
                a1 = T2("q_a1", [128, 8, 256]); a2 = T2("q_a2", [128, 8, 256])
                pc = T2("q_pc", [128, 8]); psn = T2("q_ps", [128, 8]); pt1 = T2("q_pt1", [128, 8]); pt2 = T2("q_pt2", [128, 8])
                b_tb = Buf("tb")
                ot = lambda q, f: self.op(q, f, r=[b_tb, b_prm], w=[b_tb])
                for sg in range(8):
                    ot(V, lambda e: e.memset(tc_[:, :, 0:1], 1.0))
                    ot(V, lambda e: e.memset(ts_[:, :, 0:1], 0.0))
                    ot(V, lambda e: e.tensor_copy(out=pc[:], in_=c1[:, sg * 8:(sg + 1) * 8]))
                    ot(V, lambda e: e.tensor_copy(out=psn[:], in_=s1[:, sg * 8:(sg + 1) * 8]))
                    m = 1
                    while m < 512:
                        pcb = pc[:].unsqueeze(2).to_broadcast([128, 8, m])
                        psb_ = psn[:].unsqueeze(2).to_broadcast([128, 8, m])
                        ot(V, lambda e: e.tensor_tensor(out=a1[:, :, :m], in0=tc_[:, :, 0:m], in1=pcb, op=ALU.mult))
                        ot(V, lambda e: e.tensor_tensor(out=a2[:, :, :m], in0=ts_[:, :, 0:m], in1=psb_, op=ALU.mult))
                        ot(V, lambda e: e.tensor_tensor(out=tc_[:, :, m:2 * m], in0=a1[:, :, :m], in1=a2[:, :, :m], op=ALU.subtract))
                        ot(V, lambda e: e.tensor_tensor(out=a1[:, :, :m], in0=tc_[:, :, 0:m], in1=psb_, op=ALU.mult))
                        ot(V, lambda e: e.tensor_tensor(out=a2[:, :, :m], in0=ts_[:, :, 0:m], in1=pcb, op=ALU.mult))
                        ot(V, lambda e: e.tensor_tensor(out=ts_[:, :, m:2 * m], in0=a1[:, :, :m], in1=a2[:, :, :m], op=ALU.add))
                        ot(V, lambda e: e.tensor_tensor(out=pt1[:], in0=pc[:], in1=pc[:], op=ALU.mult))
                        ot(V, lambda e: e.tensor_tensor(out=pt2[:], in0=psn[:], in1=psn[:], op=ALU.mult))
                        ot(V, lambda e: e.tensor_tensor(out=psn[:], in0=psn[:], in1=pc[:], op=ALU.mult))
                        ot(V, lambda e: e.tensor_scalar(out=psn[:], in0=psn[:], scalar1=2.0, scalar2=None, op0=ALU.mult))
                        ot(V, lambda e: e.tensor_tensor(out=pc[:], in0=pt1[:], in1=pt2[:], op=ALU.subtract))
                        m *= 2
                    self.dma(lambda e: e.dma_start(out=TAB[0, sg * 8:(sg + 1) * 8].rearrange("s p t -> p s t"), in_=tc_[:]),
                             r=[b_tb], w=[b_dr])
                    self.dma(lambda e: e.dma_start(out=TAB[1, sg * 8:(sg + 1) * 8].rearrange("s p t -> p s t"), in_=ts_[:]),
                             r=[b_tb], w=[b_dr], q="act")
                self.tk.barrier()
            self.tk.barrier()
            NB = 2
            NBT = 3
            tabc = [self.sb("tabc%d" % i, [128, 512], F32, st) for i in range(NBT)]
            tabs = [self.sb("tabs%d" % i, [128, 512], F32, st) for i in range(NBT)]
            b_tab = [Buf("tab") for _ in range(NBT)]
            bsr = [self.sb("bsr%d" % i, [128, 512], F32, st) for i in range(NB)]
            bsi = [self.sb("bsi%d" % i, [128, 512], F32, st) for i in range(NB)]
            b_bs = [Buf("bs") for _ in range(NB)]
            qa = [self.sb("sqa%d" % i, [128, 512], F32, st) for i in range(4)]
            qb = [self.sb("sqb%d" % i, [128, 512], F32, st) for i in range(4)]
            b_qa = [Buf("qa") for _ in range(4)]; b_qb = [Buf("qb") for _ in range(4)]
            vre = [self.sb("vre%d" % i, [128, 512], F32, st) for i in range(NB)]
            vim = [self.sb("vim%d" % i, [128, 512], F32, st) for i in range(NB)]
            b_vr = [Buf("vr") for _ in range(NB)]; b_vi = [Buf("vi") for _ in range(NB)]
            kre = self.sb("kre", [128, 512], F32, st); kim = self.sb("kim", [128, 512], F32, st)
            hre = self.sb("hre", [128, 512], F32, st); him = self.sb("him", [128, 512], F32, st)
            b_kr = Buf("kr"); b_ki = Buf("ki"); b_h = Buf("h"); b_h2 = Buf("h2")
            hrb = [self.sb("hrb%d" % i, [128, 512], BF16, st) for i in range(NB)]
            hib = [self.sb("hib%d" % i, [128, 512], BF16, st) for i in range(NB)]
            b_hb = [Buf("hb") for _ in range(NB)]
            wBk = [self.sb("wBk%d" % i, [128, 2, 512], BF16, st) for i in range(NB)]
            wCk = [self.sb("wCk%d" % i, [128, 2, 512], BF16, st) for i in range(NB)]
            b_wk = [Buf("wk") for _ in range(NB)]
            yb = self.sb("yb", [128, DC, 512], BF16, st)
            b_yb = Buf("yb")
            sig = [self.sb("s5sig%d" % i, [128, 512], F32, st) for i in range(2)]
            b_sig = [Buf("sig") for _ in range(2)]
            hfo = self.sb("hfo", [64, 128], F32, st)
            b_hfo = Buf("hfo")
            xnf = self.yy
            b_xnf = self.b_yy
            wg = self.ssm_w_glu.rearrange("(kc p) (h f) -> p kc h f", p=128, h=2)
            self.op("dve", lambda e: e.memset(hpr[:], 0.0), w=[b_hp])
            self.op("dve", lambda e: e.memset(hpi[:], 0.0), w=[b_hp])
            sctr = 0
            for ti in range(len(self.tiles)):
                t0, W = self.tiles[ti]
                samp = (ti == len(self.tiles) - 1)
                if samp:
                    self.nat_to_scan(self.state_ssm_re, hpr, b_hp, tmp64, b_t64)
                    self.nat_to_scan(self.state_ssm_im, hpi, b_hp, tmp64, b_t64)
                self.pre_norm(ti, li * 4 + 0, xn_f32=xnf, b_xnf=b_xnf)
                ok = lambda f: self.op("dve", f, r=[b_hp, b_prm, b_k0], w=[b_k0])
                ok(lambda e: e.tensor_tensor(out=k0r[:], in0=c1[:], in1=hpr[:], op=ALU.mult))
                ok(lambda e: e.tensor_tensor(out=ktm[:], in0=s1[:], in1=hpi[:], op=ALU.mult))
                ok(lambda e: e.tensor_tensor(out=k0r[:], in0=k0r[:], in1=ktm[:], op=ALU.subtract))
                ok(lambda e: e.tensor_tensor(out=k0i[:], in0=s1[:], in1=hpr[:], op=ALU.mult))
                ok(lambda e: e.tensor_tensor(out=ktm[:], in0=c1[:], in1=hpi[:], op=ALU.mult))
                ok(lambda e: e.tensor_tensor(out=k0i[:], in0=k0i[:], in1=ktm[:], op=ALU.add))
                lastc = 0 if samp else W - 1

                def stageA(sidx):
                    k, j = sidx // 4, sidx % 4
                    kb = k % NB
                    i2 = sidx % NB
                    if j == 0:
                        self.dma(lambda e: e.dma_start(out=wBk[kb][:], in_=WBs[:, k].rearrange("r p n -> p r n")),
                                 r=[b_dr], w=[b_wk[kb]])
                        self.dma(lambda e: e.dma_start(out=wCk[kb][:], in_=WCs[:, k].rearrange("r p n -> p r n")),
                                 r=[b_dr], w=[b_wk[kb]])
                    i4 = sidx % NBT
                    self.dma(lambda e: e.dma_start(out=tabc[i4][:, :W], in_=TAB[0, sidx, :, 0:W]), r=[b_dr], w=[b_tab[i4]])
                    self.dma(lambda e: e.dma_start(out=tabs[i4][:, :W], in_=TAB[1, sidx, :, 0:W]), r=[b_dr], w=[b_tab[i4]])
                    pr, pi_ = (i2 * 2, i2 * 2 + 1)
                    self.op("pe", lambda e: e.matmul(self.psb[pr][:, :W], lhsT=wBk[kb][:, 0, j * 128:(j + 1) * 128],
                                                     rhs=self.xn[:, k, :W], start=True, stop=True),
                            r=[b_wk[kb], self.b_xn], w=[self.bps[pr]])
                    self.op("pe", lambda e: e.matmul(self.psb[pi_][:, :W], lhsT=wBk[kb][:, 1, j * 128:(j + 1) * 128],
                                                     rhs=self.xn[:, k, :W], start=True, stop=True),
                            r=[b_wk[kb], self.b_xn], w=[self.bps[pi_]])
                    self.op("act", lambda e: e.copy(out=bsr[i2][:, :W], in_=self.psb[pr][:, :W]), r=[self.bps[pr]], w=[b_bs[i2]])
                    self.op("act", lambda e: e.copy(out=bsi[i2][:, :W], in_=self.psb[pi_][:, :W]), r=[self.bps[pi_]], w=[b_bs[i2]])
                    TCt, TSt = tabc[i4], tabs[i4]
                    rd = [b_bs[i2], b_tab[i4]]
                    self.op("pool", lambda e: e.tensor_tensor(out=qa[0][:, :W], in0=TSt[:, :W], in1=bsi[i2][:, :W], op=ALU.mult), r=rd, w=[b_qa[0]])
                    self.op("pool", lambda e: e.tensor_tensor(out=qa[1][:, :W], in0=TCt[:, :W], in1=bsr[i2][:, :W], op=ALU.mult), r=rd, w=[b_qa[1]])
                    self.op("pool", lambda e: e.tensor_tensor(out=vre[i2][:, :W], in0=qa[1][:, :W], in1=qa[0][:, :W], op=ALU.add), r=[b_qa[0], b_qa[1]], w=[b_vr[i2]])
                    self.op("pool", lambda e: e.tensor_tensor(out=qa[2][:, :W], in0=TSt[:, :W], in1=bsr[i2][:, :W], op=ALU.mult), r=rd, w=[b_qa[2]])
                    self.op("pool", lambda e: e.tensor_tensor(out=qa[3][:, :W], in0=TCt[:, :W], in1=bsi[i2][:, :W], op=ALU.mult), r=rd, w=[b_qa[3]])

                def stageA2(sidx):
                    i2 = sidx % NB
                    self.op("dve", lambda e: e.tensor_tensor(out=vim[i2][:, :W], in0=qa[3][:, :W], in1=qa[2][:, :W], op=ALU.subtract), r=[b_qa[2], b_qa[3]], w=[b_vi[i2]])

                def stageB(sidx):
                    k, j = sidx // 4, sidx % 4
                    kb = k % NB
                    i2 = sidx % NB
                    ybank = 6 + (k % 2)
                    i4 = sidx % NBT
                    TCt, TSt = tabc[i4], tabs[i4]
                    rb = rho[:, sidx:sidx + 1].to_broadcast([128, W])
                    self.op("dve", lambda e: e.tensor_tensor_scan(out=kre[:, :W], data0=rb, data1=vre[i2][:, :W],
                                                                  initial=k0r[:, sidx:sidx + 1], op0=ALU.mult, op1=ALU.add),
                            r=[b_vr[i2], b_k0, b_prm], w=[b_kr])
                    self.op("dve", lambda e: e.tensor_tensor_scan(out=kim[:, :W], data0=rb, data1=vim[i2][:, :W],
                                                                  initial=k0i[:, sidx:sidx + 1], op0=ALU.mult, op1=ALU.add),
                            r=[b_vi[i2], b_k0, b_prm], w=[b_ki])
                    rk = [b_kr, b_ki, b_tab[i4]]
                    self.op("dve", lambda e: e.tensor_tensor(out=qb[0][:, :W], in0=TSt[:, :W], in1=kim[:, :W], op=ALU.mult), r=rk, w=[b_qb[0]])
                    self.op("dve", lambda e: e.tensor_tensor(out=qb[1][:, :W], in0=TCt[:, :W], in1=kre[:, :W], op=ALU.mult), r=rk, w=[b_qb[1]])
                    self.op("dve", lambda e: e.tensor_tensor(out=hre[:, :W], in0=qb[1][:, :W], in1=qb[0][:, :W], op=ALU.subtract), r=[b_qb[0], b_qb[1]], w=[b_h])
                    self.op("dve", lambda e: e.tensor_tensor(out=qb[2][:, :W], in0=TSt[:, :W], in1=kre[:, :W], op=ALU.mult), r=rk, w=[b_qb[2]])
                    self.op("dve", lambda e: e.tensor_tensor(out=qb[3][:, :W], in0=TCt[:, :W], in1=kim[:, :W], op=ALU.mult), r=rk, w=[b_qb[3]])
                    self.op("dve", lambda e: e.tensor_tensor(out=him[:, :W], in0=qb[2][:, :W], in1=qb[3][:, :W], op=ALU.add), r=[b_qb[2], b_qb[3]], w=[b_h2])
                    self.op("act", lambda e: e.copy(out=hrb[i2][:, :W], in_=hre[:, :W]), r=[b_h], w=[b_hb[i2]])
                    self.op("act", lambda e: e.copy(out=hib[i2][:, :W], in_=him[:, :W]), r=[b_h2], w=[b_hb[i2]])
                    self.op("act", lambda e: e.copy(out=hpr[:, sidx:sidx + 1], in_=hre[:, lastc:lastc + 1]), r=[b_h, b_k0], w=[b_hp])
                    self.op("act", lambda e: e.copy(out=hpi[:, sidx:sidx + 1], in_=him[:, lastc:lastc + 1]), r=[b_h2, b_k0], w=[b_hp])
                    self.op("pe", lambda e: e.matmul(self.psb[ybank][:, :W], lhsT=wCk[kb][:, 0, j * 128:(j + 1) * 128],
                                                     rhs=hrb[i2][:, :W], start=(j == 0), stop=False),
                            r=[b_wk[kb], b_hb[i2]], w=[self.bps[ybank]])
                    self.op("pe", lambda e: e.matmul(self.psb[ybank][:, :W], lhsT=wCk[kb][:, 1, j * 128:(j + 1) * 128],
                                                     rhs=hib[i2][:, :W], start=False, stop=(j == 3)),
                            r=[b_wk[kb], b_hb[i2]], w=[self.bps[ybank]])
                    if j == 3:
                        self.op("dve", lambda e: e.scalar_tensor_tensor(out=yb[:, k, :W], in0=xnf[:, k, :W], scalar=dsk[:, k:k + 1],
                                                                        in1=self.psb[ybank][:, :W], op0=ALU.mult, op1=ALU.add),
                                r=[b_xnf, b_prm, self.bps[ybank]], w=[b_yb])

                stageA(0)
                stageA2(0)
                for sidx in range(64):
                    if sidx + 1 < 64:
                        stageA(sidx + 1)
                    stageB(sidx)
                    if sidx + 1 < 64:
                        stageA2(sidx + 1)

                def epg(p, psA, bA, psB, bB):
                    kk = p % 2
                    self.op("act", lambda e: e.activation(out=sig[kk][:, :W], in_=psB[:, :W], func=AF.Sigmoid),
                            r=[bB], w=[b_sig[kk]])
                    self.op("dve", lambda e: e.tensor_tensor(out=self.yy[:, p, :W], in0=psA[:, :W], in1=sig[kk][:, :W],
                                                             op=ALU.mult), r=[bA, b_sig[kk]], w=[self.b_yy])

                self.gemm(lambda p, k0, kn, h: wg[:, k0:k0 + kn, h, p * 128:(p + 1) * 128], DC, DC, yb, b_yb, W, epg, wkey='glu', first=(ti == 0))
                self.post_norm_residual(ti, li * 4 + 1)
                if ti == self.NT - 1 or samp:
                    for rix, (src, dstp, dsts) in enumerate(((hpr, self.ssm_re_p, self.ssm_re_s), (hpi, self.ssm_im_p, self.ssm_im_s))):
                        dst = dsts if samp else dstp
                        self.op("pe", lambda e: e.transpose(out=self.psb[5][0:64, 0:128], in_=src[:], identity=self.ident[:]),
                                r=[b_hp, self.b_const], w=[self.bps[5]])
                        self.op("dve", lambda e: e.tensor_copy(out=hfo[:], in_=self.psb[5][0:64, 0:128]), r=[self.bps[5]], w=[b_hfo])
                        self.dma(lambda e: e.dma_start(out=dst.rearrange("(s g2) p -> s (g2 p)", g2=2), in_=hfo[:]), r=[b_hfo])
            self.tk.barrier()

    def nsa_declare(self):
        d = self.din
        o = self.dout
        T = self.T
        for k, shp in STRUCT_SHAPES.items():
            setattr(self, "c_" + k, d(k, shp))
        self.rel_bias = d("rel_bias", [32, 16])
        self.page_table = d("page_table", [1, 128], I32)
        self.nsa_w_q = d("nsa_w_q", [2, D, D])
        self.nsa_w_kv = d("nsa_w_kv", [2, D, 3072])
        self.nsa_cmp_pe = d("nsa_cmp_pe", [2, 32, 2, 128])
        self.nsa_cmp_w1 = d("nsa_cmp_w1", [2, 2, 32, 128, 128])
        self.nsa_cmp_w2 = d("nsa_cmp_w2", [2, 2, 128, 128])
        self.nsa_w_gate = d("nsa_w_gate", [2, D, 48])
        self.nsa_w_o = d("nsa_w_o", [2, D, D])
        self.cache_cmp = [d("cache_cmp%d" % i, [self.NPHYS * 128, 1024]) for i in range(2)]
        self.cache_slc = [d("cache_slc%d" % i, [self.NPHYS * 128 * 4, 256]) for i in range(2)]
        self.cache_win = d("cache_win", [2, 512, 1024])
        WP = min(512, T)
        self.WP = WP
        self.kvo_p = [o(n, [2, T, 1024]) for n in ("cmp_kv_p", "slc_kv_p")] + [o("win_kv_p", [2, WP, 1024])]
        self.kvo_s = [o(n, [2, 1, 1024]) for n in ("cmp_kv_s", "slc_kv_s")] + [o("win_kv_s", [2, 512, 1024])]
        TT = self.TT
        self.QT = self.dscr("QT", [16, 128, TT], BF16)
        self.KVT = self.dscr("KVT", [3, 4, 2, 128, TT], BF16)
        self.VTOK = self.dscr("VTOK", [3, 4, TT, 128], BF16)
        self.GT = self.dscr("GT", [TT, 48])
        self.OT = self.dscr("OT", [16, 128, TT], BF16)
        self.BD = self.dscr("BD", [16, 4, 128, 128])
        self.BCs = self.dscr("BCs", [16, 248, 128])
        self.b_nsa_dr = Buf("nsadram")
        self.bias_built = False

    def build_bias_tables(self, st):
        with contextlib.ExitStack() as s2:
            relb = self.sb("relb", [33, 16], F32, s2)
            b_rb = Buf("relb")
            self.op("dve", lambda e: e.memset(relb[:], -30000.0), w=[b_rb])
            self.dma(lambda e: e.dma_start(out=relb[0:32, :], in_=self.rel_bias[:, :]), w=[b_rb])
            NB = 2
            ohb = [self.sb("ohb%d" % i, [33, 2048], F32, s2) for i in range(NB)]
            b_oh = [Buf("oh") for _ in range(NB)]
            ob = [self.sb("ob%d" % i, [16, 2048], F32, s2) for i in range(NB)]
            b_ob = [Buf("ob") for _ in range(NB)]
            ctr = 0
            for src, dst, n in ((self.c_oh_pk, self.BD.rearrange("h k n q -> h (k n q)"), 65536),
                                (self.c_oh_cmp, self.BCs.rearrange("h m q -> h (m q)"), 248 * 128)):
                for c0 in range(0, n, 2048):
                    cw = min(2048, n - c0)
                    i = ctr % NB
                    ctr += 1
                    self.dma(lambda e: e.dma_start(out=ohb[i][:, :cw], in_=src[:, c0:c0 + cw]), w=[b_oh[i]])
                    for k in range(0, cw, 512):
                        kw = min(512, cw - k)
                        pb = 4 + ((k // 512) % 2)
                        self.op("pe", lambda e: e.matmul(self.psb[pb][0:16, :kw], lhsT=relb[:, :], rhs=ohb[i][:, k:k + kw],
                                                         start=True, stop=True), r=[b_rb, b_oh[i]], w=[self.bps[pb]])
                        self.op("act", lambda e: e.copy(out=ob[i][:, k:k + kw], in_=self.psb[pb][0:16, :kw]),
                                r=[self.bps[pb]], w=[b_ob[i]])
                    self.dma(lambda e: e.dma_start(out=dst[:, c0:c0 + cw], in_=ob[i][:, :cw]), r=[b_ob[i]],
                             w=[self.b_nsa_dr], q="act")
            ohs = self.sb("ohs", [33, 17, 128], F32, s2)
            self.dma(lambda e: e.dma_start(out=ohs[:], in_=self.c_oh_s[:, :, :]), w=[b_oh[0]])
            for t in range(17):
                pb = 4 + (t % 2)
                self.op("pe", lambda e: e.matmul(self.psb[pb][:, 0:16], lhsT=ohs[:, t, :], rhs=relb[:, :], start=True, stop=True),
                        r=[b_rb, b_oh[0]], w=[self.bps[pb]])
                self.op("dve", lambda e: e.tensor_copy(out=self.BS[:, t, :], in_=self.psb[pb][:, 0:16]), r=[self.bps[pb]],
                        w=[self.b_BS])
            self.tk.barrier()

    def nsa_layer(self, li):
        j = 0 if li == 0 else 1
        T = self.T
        NQ = T // 128
        SC = 128.0 ** -0.5
        tk = self.tk
        if not self.bias_built:
            self.BS = self.sb("BS", [128, 17, 16], F32)
            self.b_BS = Buf("BS")
            self.build_bias_tables(None)
            self.bias_built = True
        with contextlib.ExitStack() as st:
            self.alloc_rowlocal(st)
            qt = self.sb("qt", [128, 16, 512], BF16, st); b_qt = Buf("qt")
            ktmp = [self.sb("ktmp%d" % i, [128, 2, 512], F32, st) for i in range(2)]; b_kt = [Buf("kt") for _ in range(2)]
            kbf = [self.sb("kbf%d" % i, [128, 2, 512], BF16, st) for i in range(2)]; b_kb = [Buf("kb") for _ in range(2)]
            stg = [self.sb("stg%d" % i, [128, 4, 2, 128], F32, st) for i in range(2)]; b_stg = [Buf("stg") for _ in range(2)]
            vbf = [self.sb("vbf%d" % i, [128, 4, 128], BF16, st) for i in range(2)]; b_vb = [Buf("vb") for _ in range(2)]
            wgf = self.sb("wgf", [128, 16, 48], F32, st); wgb = self.sb("wgb", [128, 16, 48], BF16, st); b_wg = Buf("wg")
            gsb = self.sb("gsb", [48, 512], F32, st); b_gs = Buf("gs")
            gtk = self.sb("gtk", [128, 4, 48], F32, st); b_gt = Buf("gtk")
            self.dma(lambda e: e.dma_start(out=wgf[:], in_=self.nsa_w_gate[j].rearrange("(kc p) n -> p kc n", p=128)), w=[b_wg])
            self.op("pool", lambda e: e.tensor_copy(out=wgb[:], in_=wgf[:]), r=[b_wg], w=[b_wg])
            wq = self.nsa_w_q[j].rearrange("(kc p) (f h n) -> p kc f h n", p=128, h=2, n=128)
            wkv = self.nsa_w_kv[j].rearrange("(kc p) (f h n) -> p kc f h n", p=128, h=2, n=128)
            QTv = self.QT.rearrange("h p t -> p h t")
            for ti in range(len(self.tiles)):
                t0, W = self.tiles[ti]
                samp = (ti == len(self.tiles) - 1)
                self.pre_norm(ti, li * 4 + 0)

                def epq(p, psA, bA, psB, bB):
                    self.op("act", lambda e: e.activation(out=qt[:, 2 * p, :W], in_=psA[:, :W], func=AF.Copy, scale=SC), r=[bA], w=[b_qt])
                    self.op("dve", lambda e: e.tensor_scalar(out=qt[:, 2 * p + 1, :W], in0=psB[:, :W], scalar1=SC, scalar2=None,
                                                             op0=ALU.mult), r=[bB], w=[b_qt])
                self.gemm(lambda p, k0, kn, h: wq[:, k0:k0 + kn, p, h, :], DC, 8, self.xn, self.b_xn, W, epq, wkey='wq%d' % j, first=(ti == 0))
                self.dma(lambda e: e.dma_start(out=QTv[:, :, t0:t0 + W], in_=qt[:, :, :W]), r=[b_qt], w=[self.b_nsa_dr])

                def epkv(p, psA, bA, psB, bB):
                    k = p % 2
                    br, g = p // 4, p % 4
                    self.op("act", lambda e: e.copy(out=ktmp[k][:, 0, :W], in_=psA[:, :W]), r=[bA], w=[b_kt[k]])
                    self.op("dve", lambda e: e.tensor_copy(out=ktmp[k][:, 1, :W], in_=psB[:, :W]), r=[bB], w=[b_kt[k]])
                    self.op("act", lambda e: e.copy(out=kbf[k][:, :, :W], in_=ktmp[k][:, :, :W]), r=[b_kt[k]], w=[b_kb[k]])
                    self.dma(lambda e: e.dma_start(out=self.KVT[br, g].rearrange("c p t -> p c t")[:, :, t0:t0 + W],
                                                   in_=kbf[k][:, :, :W]), r=[b_kb[k]], w=[self.b_nsa_dr], q="act")
                    nb = 1 if samp else 4
                    bw = W if samp else 128
                    for c in range(2):
                        pb = 4 + c
                        for tb in range(nb):
                            self.op("pe", lambda e: e.transpose(out=self.psb[pb][0:bw, tb * 128:(tb + 1) * 128],
                                                                in_=ktmp[k][:, c, tb * 128:tb * 128 + bw], identity=self.ident[:]),
                                    r=[b_kt[k], self.b_const], w=[self.bps[pb]])
                        if c == 0:
                            self.op("act", lambda e: e.copy(out=stg[k][0:bw, 0:nb, 0, :],
                                                            in_=self.psb[pb][0:bw, 0:nb * 128].rearrange("p (a d) -> p a d", d=128)),
                                    r=[self.bps[pb]], w=[b_stg[k]])
                        else:
                            self.op("dve", lambda e: e.tensor_copy(out=stg[k][0:bw, 0:nb, 1, :],
                                                                   in_=self.psb[pb][0:bw, 0:nb * 128].rearrange("p (a d) -> p a d", d=128)),
                                    r=[self.bps[pb]], w=[b_stg[k]])
                            self.op("dve", lambda e: e.tensor_copy(out=vbf[k][0:bw, 0:nb, :], in_=stg[k][0:bw, 0:nb, 1, :]),
                                    r=[b_stg[k]], w=[b_vb[k]])
                    if samp:
                        if br < 2:
                            self.dma(lambda e: e.dma_start(out=self.kvo_s[br][j, 0:1, g * 256:(g + 1) * 256],
                                                           in_=stg[k][0:1, 0, :, :].rearrange("p c d -> p (c d)")), r=[b_stg[k]])
                        else:
                            self.dma(lambda e: e.dma_start(out=self.kvo_s[2][j, 511:512, g * 256:(g + 1) * 256],
                                                           in_=stg[k][0:1, 0, :, :].rearrange("p c d -> p (c d)")), r=[b_stg[k]])
                        self.dma(lambda e: e.dma_start(out=self.VTOK[br, g, T:T + W, :], in_=vbf[k][0:W, 0, :]),
                                 r=[b_vb[k]], w=[self.b_nsa_dr], q="act")
                    else:
                        if br < 2:
                            dst = self.kvo_p[br][j, t0:t0 + 512, g * 256:(g + 1) * 256].rearrange("(a p) (c d) -> p a c d", p=128, c=2)
                            for c in range(2):
                                self.dma(lambda e: e.dma_start(out=dst[:, :, c, :], in_=stg[k][:, :, c, :]), r=[b_stg[k]])
                        elif t0 + 512 > T - self.WP:
                            r0 = t0 - (T - self.WP)
                            dst = self.kvo_p[2][j, r0:r0 + 512, g * 256:(g + 1) * 256].rearrange("(a p) (c d) -> p a c d", p=128, c=2)
                            for c in range(2):
                                self.dma(lambda e: e.dma_start(out=dst[:, :, c, :], in_=stg[k][:, :, c, :]), r=[b_stg[k]])
                        self.dma(lambda e: e.dma_start(out=self.VTOK[br, g, t0:t0 + 512, :].rearrange("(a p) d -> p a d", p=128),
                                                       in_=vbf[k][:, :, :]), r=[b_vb[k]], w=[self.b_nsa_dr], q="act")
                self.gemm(lambda p, k0, kn, h: wkv[:, k0:k0 + kn, p, h, :], DC, 12, self.xn, self.b_xn, W, epkv, wkey='wkv%d' % j, first=(ti == 0))
                for kc in range(DC):
                    self.op("pe", lambda e: e.matmul(self.psb[5][0:48, :W], lhsT=wgb[:, kc, :], rhs=self.xn[:, kc, :W],
                                                     start=(kc == 0), stop=(kc == DC - 1)), r=[b_wg, self.b_xn], w=[self.bps[5]])
                self.op("act", lambda e: e.activation(out=gsb[:, :W], in_=self.psb[5][0:48, :W], func=AF.Sigmoid), r=[self.bps[5]], w=[b_gs])
                nb = 1 if samp else 4
                bw = W if samp else 128
                for tb in range(nb):
                    self.op("pe", lambda e: e.transpose(out=self.psb[4][0:bw, tb * 48:(tb + 1) * 48], in_=gsb[:, tb * 128:tb * 128 + bw],
                                                        identity=self.ident[0:48, 0:48]), r=[b_gs, self.b_const], w=[self.bps[4]])
                self.op("dve", lambda e: e.tensor_copy(out=gtk[0:bw, 0:nb, :], in_=self.psb[4][0:bw, 0:nb * 48].rearrange("p (a n) -> p a n", n=48)),
                        r=[self.bps[4]], w=[b_gt])
                self.dma(lambda e: e.dma_start(out=self.GT[t0:t0 + nb * bw, :].rearrange("(a p) n -> p a n", p=bw), in_=gtk[0:bw, 0:nb, :]),
                         r=[b_gt], w=[self.b_nsa_dr])
            self.tk.barrier()
        self.dma(lambda e: e.dma_start(out=self.kvo_s[2][j, 0:511, :], in_=self.cache_win[j, 1:512, :]))
        with contextlib.ExitStack() as st:
            w1f = self.sb("w1f", [128, 32, 128], F32, st)
            w1b = self.sb("w1b", [128, 2, 32, 128], BF16, st)
            w2f = self.sb("w2f", [128, 2, 128], F32, st)
            w2b = self.sb("w2b", [128, 2, 128], BF16, st)
            pef = self.sb("pef", [64, 128], F32, st)
            peT = self.sb("peT", [128, 32, 2], BF16, st)
            pebias = self.sb("pebias", [128, 2], F32, st)
            b_cw = Buf("cw")
            for c in range(2):
                self.dma(lambda e: e.dma_start(out=w1f[:], in_=self.nsa_cmp_w1[j, c].rearrange("s d e -> d s e")), w=[b_cw])
                self.op("pool", lambda e: e.tensor_copy(out=w1b[:, c], in_=w1f[:]), r=[b_cw], w=[b_cw])
            self.dma(lambda e: e.dma_start(out=w2f[:], in_=self.nsa_cmp_w2[j].rearrange("c e d -> e c d")), w=[b_cw])
            self.op("pool", lambda e: e.tensor_copy(out=w2b[:], in_=w2f[:]), r=[b_cw], w=[b_cw])
            self.dma(lambda e: e.dma_start(out=pef[:], in_=self.nsa_cmp_pe[j].rearrange("s c d -> (s c) d")), w=[b_cw])
            self.op("pe", lambda e: e.transpose(out=self.psb[5][:, 0:64], in_=pef[:], identity=self.ident[0:64, 0:64]),
                    r=[b_cw, self.b_const], w=[self.bps[5]])
            self.op("dve", lambda e: e.tensor_copy(out=peT[:].rearrange("p s c -> p (s c)"), in_=self.psb[5][:, 0:64]), r=[self.bps[5]], w=[b_cw])
            for c in range(2):
                for s in range(32):
                    self.op("pe", lambda e: e.matmul(self.psb[5][:, 64 + c:65 + c], lhsT=w1b[:, c, s, :], rhs=peT[:, s, c:c + 1],
                                                     start=(s == 0 and c == 0), stop=(s == 31)), r=[b_cw], w=[self.bps[5]])
            self.op("dve", lambda e: e.tensor_copy(out=pebias[:], in_=self.psb[5][:, 64:66]), r=[self.bps[5]], w=[b_cw])
            ght = [self.sb("ght%d" % i, [128, 128], BF16, st) for i in range(2)]
            b_gh = [Buf("ght") for _ in range(2)]

            def compress(xT_ap_fn, bx, ncols, kdst, vdst, bdst, ctr0):
                for c in range(2):
                    i = (ctr0 + c) % 2
                    pb = i
                    for s in range(32):
                        self.op("pe", lambda e: e.matmul(self.psb[pb][:, 0:ncols], lhsT=w1b[:, c, s, :], rhs=xT_ap_fn(c, s),
                                                         start=(s == 0), stop=(s == 31)), r=[b_cw, bx], w=[self.bps[pb]])
                    self.op("act", lambda e: e.activation(out=ght[i][:, 0:ncols], in_=self.psb[pb][:, 0:ncols], func=AF.Gelu_apprx_tanh,
                                                          bias=pebias[:, c:c + 1], scale=1.0), r=[self.bps[pb], b_cw], w=[b_gh[i]])
                    pb2 = 2 + i
                    if c == 0:
                        self.op("pe", lambda e: e.matmul(self.psb[pb2][:, 0:ncols], lhsT=w2b[:, 0, :], rhs=ght[i][:, 0:ncols], start=True, stop=True),
                                r=[b_cw, b_gh[i]], w=[self.bps[pb2]])
                        self.op("dve", lambda e: e.tensor_copy(out=kdst, in_=self.psb[pb2][:, 0:ncols]), r=[self.bps[pb2]], w=[bdst])
                    else:
                        self.op("pe", lambda e: e.matmul(self.psb[pb2][0:ncols, 0:128], lhsT=ght[i][:, 0:ncols], rhs=w2b[:, 1, :], start=True, stop=True),
                                r=[b_cw, b_gh[i]], w=[self.bps[pb2]])
                        self.op("dve", lambda e: e.tensor_copy(out=vdst, in_=self.psb[pb2][0:ncols, 0:128]), r=[self.bps[pb2]], w=[bdst])

            with contextlib.ExitStack() as s2:
                NCB = T // 16 - 1
                xT = self.sb("cxT", [128, 2, T], BF16, s2); b_xT = Buf("cxT")
                kcT = self.sb("kcT", [128, 4, 128], BF16, s2)
                vc = self.sb("vc", [128, 4, 128], BF16, s2)
                b_kc = Buf("kc")
                self.op("dve", lambda e: e.memset(kcT[:], 0.0), w=[b_kc])
                self.op("dve", lambda e: e.memset(vc[:], 0.0), w=[b_kc])
                for g in range(4):
                    self.dma(lambda e: e.dma_start(out=xT[:], in_=self.KVT[0, g].rearrange("c p t -> p c t")[:, :, 0:T]),
                             r=[self.b_nsa_dr], w=[b_xT])
                    compress(lambda c, s: xT[:, c, s:s + 16 * (NCB - 1) + 1:16], b_xT, NCB, kcT[:, g, 0:NCB], vc[0:NCB, g, :], b_kc, 2 * g)
                ks = [self.sb("ks%d" % b, [128, T], BF16, s2) for b in range(2)]
                vs = [self.sb("vs%d" % b, [128, NQ, 128], BF16, s2) for b in range(2)]
                b_kv = Buf("kvres")
                bdt = self.sb("bdt", [128, 4, 4, 128], F32, s2)
                b_bd = Buf("bdt")
                gts = self.sb("gts", [128, NQ, 48], F32, s2); b_gts = Buf("gts")
                self.dma(lambda e: e.dma_start(out=gts[:], in_=self.GT[0:T, :].rearrange("(a p) n -> p a n", p=128)), r=[self.b_nsa_dr], w=[b_gts])
                mselp = self.sb("mselp", [128, 32], F32, s2)
                amp = self.sb("amp", [128, 8, 32], F32, s2)
                eexf = self.sb("eexf", [32, 16, 128], F32, s2)
                eexb = self.sb("eexb", [32, 16, 128], BF16, s2)
                b_sc = Buf("selc")
                self.dma(lambda e: e.dma_start(out=mselp[:], in_=self.c_msel_p[:, :]), w=[b_sc])
                self.dma(lambda e: e.dma_start(out=amp[:], in_=self.c_addmask_p.rearrange("i q j -> q i j")), w=[b_sc])
                self.dma(lambda e: e.dma_start(out=eexf[:], in_=self.c_eexp[:, :, :]), w=[b_sc])
                self.op("dve", lambda e: e.tensor_copy(out=eexb[:], in_=eexf[:]), r=[b_sc], w=[b_sc])
                qs = [self.sb("qs%d" % i, [128, 4, 128], BF16, s2) for i in range(2)]; b_qs = [Buf("qs") for _ in range(2)]
                bcm = [self.sb("bcm%d" % i, [128, 4, 128], F32, s2) for i in range(2)]; b_bcm = [Buf("bcm") for _ in range(2)]
                sT = [self.sb("sT%d" % i, [128, 512], F32, s2) for i in range(3)]; b_sT = [Buf("sT") for _ in range(3)]
                pT = [self.sb("pT%d" % i, [128, 512], BF16, s2) for i in range(3)]; b_pT = [Buf("pT") for _ in range(3)]
                SB_ = [0, 1, 7]
                pf = self.sb("pf", [128, 512], F32, s2); b_pf = Buf("pf")
                rcs = self.sb("rcs", [128, 512], F32, s2); b_rc = Buf("rcs")
                impT = self.sb("impT", [128, 128], F32, s2); b_imp = Buf("imp")
                psl = self.sb("psl", [32, 128], F32, s2); b_psl = Buf("psl")
                sco = self.sb("sco", [128, 32], F32, s2); sco2 = self.sb("sco2", [128, 32], F32, s2)
                mx1 = self.sb("mx1", [128, 8], F32, s2); mx2 = self.sb("mx2", [128, 8], F32, s2)
                ngm = self.sb("ngm", [128, 32], F32, s2); b_sco = Buf("sco")
                ngT = self.sb("ngT", [32, 128], BF16, s2); b_ngT = Buf("ngT")
                osb = self.sb("osb", [128, 4, 128], F32, s2); b_osb = Buf("osb")
                wsc = [self.sb("wsc%d" % i_, [128, 4], F32, s2) for i_ in range(2)]; b_wsc = [Buf("wsc") for _ in range(2)]
                rsc = self.sb("rsc", [128, 4], F32, s2); b_rsc = Buf("rsc")
                osb3 = [self.sb("osb3_%d" % i_, [128, 3, 4, 128], F32, s2) for i_ in range(2)]; b_osb3 = [Buf("osb3") for _ in range(2)]
                otb = [self.sb("otb%d" % i, [128, 4, 128], BF16, s2) for i in range(2)]; b_otb = [Buf("otb") for _ in range(2)]
                for g in range(4):
                    for b_ in range(2):
                        self.dma(lambda e: e.dma_start(out=ks[b_][:], in_=self.KVT[1 + b_, g, 0, :, 0:T]), r=[self.b_nsa_dr], w=[b_kv])
                        self.dma(lambda e: e.dma_start(out=vs[b_][:], in_=self.VTOK[1 + b_, g, 0:T, :].rearrange("(a p) d -> p a d", p=128)),
                                 r=[self.b_nsa_dr], w=[b_kv])
                    for kd in range(4):
                        self.dma(lambda e: e.dma_start(out=bdt[:, kd], in_=self.BD[4 * g:4 * g + 4, kd].rearrange("h n q -> n h q")),
                                 r=[self.b_nsa_dr], w=[b_bd])
                    units = []
                    banks = {}
                    octr = 0
                    for i in range(NQ):
                        for br in (0, 2, 1):
                            if br == 0:
                                kts = [0]
                            elif br == 1:
                                kts = list(range(0, i + 1))
                            else:
                                kts = list(range(max(0, i - 4), i + 1))
                            banks[(i, br)] = (2, 0) if octr % 2 == 0 else (6, 8)
                            octr += 1
                            for ki, kt in enumerate(kts):
                                units.append((i, br, kt, ki, len(kts)))

                    def emit_qk(n):
                        i, br, kt, ki, nk = units[n]
                        qi = i % 2
                        u = n % 3
                        sbk = SB_[u]
                        if br == 0:
                            self.dma(lambda e: e.dma_start(out=qs[qi][:], in_=self.QT[4 * g:4 * g + 4, :, i * 128:(i + 1) * 128].rearrange("h p t -> p h t")),
                                     r=[self.b_nsa_dr], w=[b_qs[qi]])
                            self.dma(lambda e: e.dma_start(out=bcm[qi][:], in_=self.BCs[4 * g:4 * g + 4, 120 - 8 * i:248 - 8 * i, :].rearrange("h n q -> n h q")),
                                     r=[self.b_nsa_dr], w=[b_bcm[qi]])
                        qrhs = qs[qi][:].rearrange("p r q -> p (r q)")
                        if br == 0:
                            klhs, kvb = kcT[:, g, :], b_kc
                        else:
                            klhs, kvb = ks[br - 1][:, kt * 128:(kt + 1) * 128], b_kv
                        msk = (br == 1 and i >= 8)
                        self.op("pe", lambda e: e.matmul(self.psb[sbk][:, :], lhsT=klhs, rhs=qrhs, start=True, stop=(not msk)),
                                r=[kvb, b_qs[qi]], w=[self.bps[sbk]])
                        if msk:
                            for r in range(4):
                                self.op("pe", lambda e: e.matmul(self.psb[sbk][:, r * 128:(r + 1) * 128], lhsT=eexb[:, kt, :], rhs=ngT[:, :],
                                                                 start=False, stop=(r == 3)), r=[b_sc, b_ngT], w=[self.bps[sbk]])

                    def emit_rest(n):
                        i, br, kt, ki, nk = units[n]
                        qi = i % 2
                        u = n % 3
                        sbk = SB_[u]
                        ob_, oc_ = banks[(i, br)]
                        os_ = 3
                        dosel = (i >= 8)
                        if br == 0:
                            vrhs, kvb = vc[:, g, :], b_kc
                            bias_ap, bb = bcm[qi][:].rearrange("p r q -> p (r q)"), b_bcm[qi]
                        else:
                            vrhs, kvb = vs[br - 1][:, kt, :], b_kv
                            if kt == i:
                                kd = 0
                            elif kt == i - 1:
                                kd = 1
                            elif br == 2 and kt == i - 4:
                                kd = 3
                            else:
                                kd = 2
                            bias_ap, bb = bdt[:, kd].rearrange("p r q -> p (r q)"), b_bd
                        self.op("dve", lambda e: e.tensor_tensor(out=sT[u][:], in0=self.psb[sbk][:, :], in1=bias_ap, op=ALU.add),
                                r=[self.bps[sbk], bb], w=[b_sT[u]])
                        if br == 0:
                            if dosel:
                                self.op("act", lambda e: e.activation(out=pf[:], in_=sT[u][:], func=AF.Exp), r=[b_sT[u]], w=[b_pf])
                            self.op("act", lambda e: e.activation(out=pT[u][:], in_=sT[u][:], func=AF.Exp), r=[b_sT[u]], w=[b_pT[u]])
                        else:
                            self.op("act", lambda e: e.activation(out=pT[u][:], in_=sT[u][:], func=AF.Exp), r=[b_sT[u]], w=[b_pT[u]])
                        for r in range(4):
                            self.op("pe", lambda e: e.matmul(self.psb[ob_][:, r * 128:(r + 1) * 128], lhsT=pT[u][:, r * 128:(r + 1) * 128], rhs=vrhs,
                                                             start=(ki == 0 and r == 0), stop=(ki == nk - 1)),
                                    r=[b_pT[u], kvb], w=[self.bps[ob_]])
                            self.op("pe", lambda e: e.matmul(self.psb[os_][:, oc_ + r:oc_ + r + 1], lhsT=pT[u][:, r * 128:(r + 1) * 128], rhs=self.ones_b[:, 0:1],
                                                             start=(ki == 0 and r == 0), stop=(ki == nk - 1)),
                                    r=[b_pT[u], self.b_const], w=[self.bps[os_]])
                        if br == 0 and dosel:
                            for r in range(4):
                                self.op("pe", lambda e: e.matmul(self.psb[5][:, r * 32:(r + 1) * 32], lhsT=pf[:, r * 128:(r + 1) * 128], rhs=mselp[:, :],
                                                                 start=(r == 0), stop=(r == 3)), r=[b_pf, b_sc], w=[self.bps[5]])
                        if ki == nk - 1:
                            self.op("dve", lambda e: e.tensor_scalar(out=rsc[:], in0=self.psb[os_][:, oc_:oc_ + 4], scalar1=1e-30, scalar2=None, op0=ALU.add),
                                    r=[self.bps[os_]], w=[b_rsc])
                            self.op("dve", lambda e: e.reciprocal(out=rsc[:], in_=rsc[:]), r=[b_rsc], w=[b_rsc])
                            gv = gts[:, i, :].rearrange("p (h b) -> p h b", b=3)[:, 4 * g:4 * g + 4, br]
                            wk_ = wsc[(3 * i + br) % 2]
                            bwk_ = b_wsc[(3 * i + br) % 2]
                            self.op("dve", lambda e: e.tensor_tensor(out=wk_[:], in0=rsc[:], in1=gv, op=ALU.mult), r=[b_rsc, b_gts], w=[bwk_])
                            for r in range(4):
                                self.op("act", lambda e: e.activation(out=osb3[qi][:, br, r, :], in_=self.psb[ob_][:, r * 128:(r + 1) * 128], func=AF.Copy,
                                                                      scale=wk_[:, r:r + 1]), r=[self.bps[ob_], bwk_], w=[b_osb3[qi]])
                            if br == 0 and dosel:
                                self.op("dve", lambda e: e.scalar_tensor_tensor(out=sco[:], in0=self.psb[5][:, 0:32], scalar=rsc[:, 0:1], in1=amp[:, i - 8, :],
                                                                                op0=ALU.mult, op1=ALU.add), r=[self.bps[5], b_rsc, b_sc], w=[b_sco])
                                for r in range(1, 4):
                                    self.op("dve", lambda e: e.scalar_tensor_tensor(out=sco[:], in0=self.psb[5][:, r * 32:(r + 1) * 32], scalar=rsc[:, r:r + 1], in1=sco[:],
                                                                                    op0=ALU.mult, op1=ALU.add), r=[self.bps[5], b_rsc, b_sco], w=[b_sco])
                                osc = lambda f: self.op("dve", f, r=[b_sco], w=[b_sco])
                                osc(lambda e: e.max(out=mx1[:], in_=sco[:]))
                                osc(lambda e: e.match_replace(out=sco2[:], in_to_replace=mx1[:], in_values=sco[:], imm_value=-3.0e38))
                                osc(lambda e: e.max(out=mx2[:], in_=sco2[:]))
                                osc(lambda e: e.tensor_scalar(out=ngm[:], in0=sco[:], scalar1=mx2[:, 7:8], scalar2=None, op0=ALU.is_ge))
                                osc(lambda e: e.tensor_scalar(out=ngm[:], in0=ngm[:], scalar1=30000.0, scalar2=-30000.0, op0=ALU.mult, op1=ALU.add))
                                self.op("pe", lambda e: e.transpose(out=self.psb[5][0:32, 256:384], in_=ngm[:], identity=self.ident[:]),
                                        r=[b_sco, self.b_const], w=[self.bps[5]])
                                self.op("act", lambda e: e.copy(out=ngT[:], in_=self.psb[5][0:32, 256:384]), r=[self.bps[5]], w=[b_ngT])
                            if br == 1:
                                self.op("pool", lambda e: e.tensor_tensor(out=osb[:], in0=osb3[qi][:, 0], in1=osb3[qi][:, 2], op=ALU.add),
                                        r=[b_osb3[qi]], w=[b_osb])
                                self.op("pool", lambda e: e.tensor_tensor(out=osb[:], in0=osb[:], in1=osb3[qi][:, 1], op=ALU.add),
                                        r=[b_osb3[qi], b_osb], w=[b_osb])
                                for r in range(4):
                                    self.op("pe", lambda e: e.transpose(out=self.psb[4][:, r * 128:(r + 1) * 128], in_=osb[:, r, :], identity=self.ident[:]),
                                            r=[b_osb, self.b_const], w=[self.bps[4]])
                                self.op("act", lambda e: e.copy(out=otb[qi][:].rearrange("p r q -> p (r q)"), in_=self.psb[4][:, :]), r=[self.bps[4]], w=[b_otb[qi]])
                                self.dma(lambda e: e.dma_start(out=self.OT[4 * g:4 * g + 4, :, i * 128:(i + 1) * 128].rearrange("h p t -> p h t"), in_=otb[qi][:]),
                                         r=[b_otb[qi]], w=[self.b_nsa_dr], q="act")

                    emit_qk(0)
                    emit_qk(1)
                    for n in range(len(units)):
                        if n + 2 < len(units):
                            emit_qk(n + 2)
                        emit_rest(n)
                self.tk.barrier()
            self.nsa_sample(j, st, compress, b_cw)
            self.tk.barrier()
        with contextlib.ExitStack() as st:
            self.alloc_rowlocal(st)
            oin = self.sb("oin_", [128, 16, 512], BF16, st); b_oin = Buf("oin")
            wo = self.nsa_w_o[j].rearrange("(kc p) (f h n) -> p kc f h n", p=128, h=2, n=128)
            OTv = self.OT.rearrange("h p t -> p h t")
            for ti in range(len(self.tiles)):
                t0, W = self.tiles[ti]
                self.load_xr(ti)
                self.dma(lambda e: e.dma_start(out=oin[:, :, :W], in_=OTv[:, :, t0:t0 + W]), r=[self.b_nsa_dr], w=[b_oin], q="act")

                def epo(p, psA, bA, psB, bB):
                    self.op("act", lambda e: e.copy(out=self.yy[:, 2 * p, :W], in_=psA[:, :W]), r=[bA], w=[self.b_yy])
                    self.op("dve", lambda e: e.tensor_copy(out=self.yy[:, 2 * p + 1, :W], in_=psB[:, :W]), r=[bB], w=[self.b_yy])
                self.gemm(lambda p, k0, kn, h: wo[:, k0:k0 + kn, p, h, :], DC, 8, oin, b_oin, W, epo, wkey='wo%d' % j, first=(ti == 0))
                self.post_norm_residual(ti, li * 4 + 1)
            self.tk.barrier()

    def nsa_sample(self, j, st, compress, b_cw):
        T = self.T
        with contextlib.ExitStack() as s2:
            S = lambda n, shp, dt=F32: self.sb(n, shp, dt, s2)
            b_c = Buf("sconst")
            pti = S("pti", [128, 128], I32); ptf = S("ptf", [128, 128]); idxf = S("idxf", [128, 128]); idxa = S("idxa", [128, 128], I32)
            iop = S("iop", [128, 1]); ior = S("ior", [128, 1]); iopg = S("iopg", [128, 128]); gcol = S("gcol", [128, 32])
            selh = S("selh", [4, 2, 128]); eye4 = S("eye4", [4, 4]); msels = S("msels", [128, 8, 257]); ams = S("ams", [4, 257])
            L = lambda dst, src: self.dma(lambda e: e.dma_start(out=dst, in_=src), w=[b_c])
            L(pti[:], self.page_table.partition_broadcast(128))
            L(iop[:], self.c_iota_p[:, :]); L(ior[:], self.c_iota_row[:, :]); L(iopg[:], self.c_iota_pg[:, :]); L(gcol[:], self.c_gcol[:, :])
            L(selh[:], self.c_selhalf.rearrange("s g p -> g s p")); L(eye4[:], self.c_eye4[:, :])
            L(msels[:], self.c_msel_s.rearrange("(t n) j -> n t j", n=128)); L(ams[:], self.c_addmask_s[:, :])
            oc = lambda q, f: self.op(q, f, r=[b_c], w=[b_c])
            oc("dve", lambda e: e.tensor_copy(out=ptf[:], in_=pti[:]))
            oc("dve", lambda e: e.tensor_scalar(out=idxf[:], in0=ptf[:], scalar1=128.0, scalar2=iop[:, 0:1], op0=ALU.mult, op1=ALU.add))
            oc("dve", lambda e: e.tensor_copy(out=idxa[:], in_=idxf[:]))
            qsT = S("qsT", [128, 16], BF16); gsm = S("gsm", [4, 4, 3]); b_q = Buf("qs_s")
            self.dma(lambda e: e.dma_start(out=qsT[:].unsqueeze(2), in_=self.QT.rearrange("h p t -> p h t")[:, :, T:T + 1],
                                           allow_slow_non_contiguous=True), r=[self.b_nsa_dr], w=[b_q])
            self.dma(lambda e: e.dma_start(out=gsm[:], in_=self.GT[T:T + 1, :].rearrange("o (g r b) -> (o r) g b", g=4, r=4),
                                           allow_slow_non_contiguous=True), r=[self.b_nsa_dr], w=[b_q])
            XT = S("XT", [128, 8, 2064], BF16); b_XT = Buf("XT")
            pg = [S("pg%d" % i, [128, 1024]) for i in range(2)]; b_pg = [Buf("pg") for _ in range(2)]
            kcs = S("kcs", [128, 4, 1024], BF16); vcs = S("vcs", [128, 8, 4, 128], BF16); b_kcs = Buf("kcs")
            self.op("pool", lambda e: e.memset(XT[:, :, 0:16], 0.0), w=[b_XT])
            cmp_rows = self.cache_cmp[j]
            for G8 in range(8):
                if G8 > 0:
                    self.op("pool", lambda e: e.tensor_copy(out=XT[:, :, 0:16], in_=XT[:, :, 2048:2064]), r=[b_XT], w=[b_XT])
                for p16 in range(16):
                    page = G8 * 16 + p16
                    k = page % 2
                    self.tk.dma("pool", lambda e: e.indirect_dma_start(out=pg[k][:, :], out_offset=None, in_=cmp_rows[:, :],
                                                                       in_offset=bass.IndirectOffsetOnAxis(ap=idxa[:, page:page + 1], axis=0)),
                                [b_c], [b_pg[k]])
                    for half in range(2):
                        pb = 4 + half
                        for q4 in range(4):
                            gc = half * 4 + q4
                            self.op("pe", lambda e: e.transpose(out=self.psb[pb][:, q4 * 128:(q4 + 1) * 128], in_=pg[k][:, gc * 128:(gc + 1) * 128],
                                                                identity=self.ident[:]), r=[b_pg[k], self.b_const], w=[self.bps[pb]])
                        dst = XT[:, half * 4:(half + 1) * 4, 16 + p16 * 128:16 + (p16 + 1) * 128]
                        src = self.psb[pb][:, :].rearrange("p (a n) -> p a n", a=4)
                        if half == 0:
                            self.op("act", lambda e: e.copy(out=dst, in_=src), r=[self.bps[pb]], w=[b_XT])
                        else:
                            self.op("dve", lambda e: e.tensor_copy(out=dst, in_=src), r=[self.bps[pb]], w=[b_XT])
                for g in range(4):
                    compress(lambda c, s: XT[:, g * 2 + c, s:s + 16 * 127 + 1:16], b_XT, 128,
                             kcs[:, g, G8 * 128:(G8 + 1) * 128], vcs[:, G8, g, :], b_kcs, 2 * g)
            sTs = S("sTs", [128, 144]); pfs = S("pfs", [128, 144]); pns = S("pns", [128, 144]); pbs = S("pbs", [128, 144], BF16)
            tot = S("tot", [128, 16]); b_a = Buf("satt")
            imp = S("imp", [128, 8, 4])

            def softmax_cols(ncol, nt, bias_ap, psbank):
                pass

            for t in range(8):
                for g in range(4):
                    self.op("pe", lambda e: e.matmul(self.psb[0][:, t * 16 + 4 * g:t * 16 + 4 * g + 4], lhsT=kcs[:, g, t * 128:(t + 1) * 128],
                                                     rhs=qsT[:, 4 * g:4 * g + 4], start=True, stop=True), r=[b_kcs, b_q], w=[self.bps[0]])
            oa = lambda q, f, extra=(): self.op(q, f, r=[b_a] + list(extra), w=[b_a])
            oa("dve", lambda e: e.tensor_tensor(out=sTs[:, 0:128], in0=self.psb[0][:, 0:128], in1=self.BS[:, 0:8, :].rearrange("p t h -> p (t h)"),
                                                op=ALU.add), [self.bps[0], self.b_BS])
            oa("act", lambda e: e.activation(out=pfs[:, 0:128], in_=sTs[:, 0:128], func=AF.Exp))
            self.op("pe", lambda e: e.matmul(self.psb[4][:, 0:128], lhsT=self.ones_f[:], rhs=pfs[:, 0:128], start=True, stop=True),
                    r=[b_a, self.b_const], w=[self.bps[4]])
            oa("dve", lambda e: e.tensor_reduce(out=tot[:], in_=self.psb[4][:, 0:128].rearrange("p (t h) -> p h t", h=16), axis=AX.X, op=ALU.add),
               [self.bps[4]])
            oa("dve", lambda e: e.reciprocal(out=tot[:], in_=tot[:]))
            oa("dve", lambda e: e.tensor_tensor(out=pns[:, 0:128].rearrange("p (t h) -> p t h", h=16), in0=pfs[:, 0:128].rearrange("p (t h) -> p t h", h=16),
                                                in1=tot[:].unsqueeze(1).to_broadcast([128, 8, 16]), op=ALU.mult))
            oa("act", lambda e: e.copy(out=pbs[:, 0:128], in_=pns[:, 0:128]))
            oa("dve", lambda e: e.tensor_reduce(out=imp[:], in_=pns[:, 0:128].rearrange("p (t g r) -> p t g r", g=4, r=4), axis=AX.X, op=ALU.add))
            first = True
            for g in range(4):
                for t in range(8):
                    self.op("pe", lambda e: e.matmul(self.psb[2][0:4, g * 128:(g + 1) * 128], lhsT=pbs[:, t * 16 + 4 * g:t * 16 + 4 * g + 4],
                                                     rhs=vcs[:, t, g, :], start=first, stop=(t == 7)), r=[b_a, b_kcs], w=[self.bps[2]])
                    first = False
            osm = S("osm", [4, 4, 128]); otmp = S("otmp", [4, 4, 128]); b_os = Buf("osm")
            self.op("dve", lambda e: e.tensor_tensor(out=osm[:], in0=self.psb[2][0:4, :].rearrange("p (g d) -> p g d", g=4),
                                                     in1=gsm[:, :, 0:1].to_broadcast([4, 4, 128]), op=ALU.mult), r=[self.bps[2], b_q], w=[b_os])
            for t in range(8):
                self.op("pe", lambda e: e.matmul(self.psb[5][0:4, 0:257], lhsT=imp[:, t, :], rhs=msels[:, t, :], start=(t == 0), stop=(t == 7)),
                        r=[b_a, b_c], w=[self.bps[5]])
            sco = S("ssco", [4, 257]); sco2 = S("ssco2", [4, 257]); mx1 = S("smx1", [4, 8]); mx2 = S("smx2", [4, 8])
            ixu = S("ixu", [4, 16], U32); ixf = S("ixf", [4, 16]); bm = S("bm", [4, 2, 4, 8]); b_s = Buf("ssel")
            osl = lambda q, f, extra=(): self.op(q, f, r=[b_s] + list(extra), w=[b_s])
            osl("dve", lambda e: e.tensor_tensor(out=sco[:], in0=self.psb[5][0:4, 0:257], in1=ams[:], op=ALU.add), [self.bps[5], b_c])
            osl("dve", lambda e: e.max(out=mx1[:], in_=sco[:]))
            osl("dve", lambda e: e.match_replace(out=sco2[:], in_to_replace=mx1[:], in_values=sco[:], imm_value=-3.0e38))
            osl("dve", lambda e: e.max(out=mx2[:], in_=sco2[:]))
            osl("dve", lambda e: e.max_index(out=ixu[:, 0:8], in_max=mx1[:], in_values=sco[:]))
            osl("dve", lambda e: e.max_index(out=ixu[:, 8:16], in_max=mx2[:], in_values=sco2[:]))
            osl("dve", lambda e: e.tensor_copy(out=ixf[:], in_=ixu[:]))
            for s2i in range(2):
                osl("dve", lambda e: e.tensor_tensor(out=bm[:, s2i], in0=ixf[:, s2i:16:2].unsqueeze(1).to_broadcast([4, 4, 8]),
                                                     in1=eye4[:, :].unsqueeze(2).to_broadcast([4, 4, 8]), op=ALU.mult), [b_c])
                self.op("pe", lambda e: e.matmul(self.psb[5][:, 300:332], lhsT=selh[:, s2i, :], rhs=bm[:, s2i].rearrange("p g s -> p (g s)"),
                                                 start=(s2i == 0), stop=(s2i == 1)), r=[b_s, b_c], w=[self.bps[5]])
            jvf = S("jvf", [128, 32]); jvi = S("jvi", [128, 32], I32); jhi = S("jhi", [128, 32], I32); jpi = S("jpi", [128, 32], I32)
            jhf = S("jhf", [128, 32]); jpf = S("jpf", [128, 32]); ptsel = S("ptsel", [128, 32]); rowf = S("rowf", [128, 32]); rowi = S("rowi", [128, 32], I32)
            i254 = S("i254", [128, 32]); i255 = S("i255", [128, 32]); i256 = S("i256", [128, 32])
            osl("dve", lambda e: e.tensor_copy(out=jvf[:], in_=self.psb[5][:, 300:332]), [self.bps[5]])
            osl("dve", lambda e: e.tensor_copy(out=jvi[:], in_=jvf[:]))
            osl("dve", lambda e: e.tensor_single_scalar(out=jhi[:], in_=jvi[:], scalar=1, op=ALU.logical_shift_right))
            osl("dve", lambda e: e.tensor_single_scalar(out=jpi[:], in_=jvi[:], scalar=1, op=ALU.bitwise_and))
            osl("dve", lambda e: e.tensor_copy(out=jhf[:], in_=jhi[:]))
            osl("dve", lambda e: e.tensor_copy(out=jpf[:], in_=jpi[:]))
            with contextlib.ExitStack() as s3:
                ohp = self.sb("ohp", [128, 32, 128], F32, s3)
                osl("dve", lambda e: e.tensor_tensor(out=ohp[:], in0=jhf[:].unsqueeze(2).to_broadcast([128, 32, 128]),
                                                     in1=iopg[:].unsqueeze(1).to_broadcast([128, 32, 128]), op=ALU.is_equal), [b_c])
                osl("dve", lambda e: e.tensor_tensor(out=ohp[:], in0=ohp[:], in1=ptf[:].unsqueeze(1).to_broadcast([128, 32, 128]), op=ALU.mult), [b_c])
                osl("dve", lambda e: e.tensor_reduce(out=ptsel[:], in_=ohp[:], axis=AX.X, op=ALU.add))
                self.tk.barrier()
            osl("dve", lambda e: e.tensor_scalar(out=rowf[:], in0=ptsel[:], scalar1=128.0, scalar2=ior[:, 0:1], op0=ALU.mult, op1=ALU.add), [b_c])
            osl("dve", lambda e: e.scalar_tensor_tensor(out=rowf[:], in0=jpf[:], scalar=64.0, in1=rowf[:], op0=ALU.mult, op1=ALU.add))
            osl("dve", lambda e: e.scalar_tensor_tensor(out=rowf[:], in0=rowf[:], scalar=4.0, in1=gcol[:], op0=ALU.mult, op1=ALU.add), [b_c])
            osl("dve", lambda e: e.tensor_copy(out=rowi[:], in_=rowf[:]))
            for tile_, val in ((i254, 254.0), (i255, 255.0), (i256, 256.0)):
                osl("dve", lambda e: e.tensor_scalar(out=tile_[:], in0=jvf[:], scalar1=val, scalar2=None, op0=ALU.is_equal))
            bsl = S("bsl", [128, 4, 9, 4]); dv = S("dv", [128, 2, 16]); tb1 = S("tb1", [128, 8, 4])
            osl("dve", lambda e: e.tensor_tensor(out=dv[:, 0, :], in0=self.BS[:, 13, :], in1=self.BS[:, 15, :], op=ALU.subtract), [self.b_BS])
            osl("dve", lambda e: e.tensor_tensor(out=dv[:, 1, :], in0=self.BS[:, 14, :], in1=self.BS[:, 15, :], op=ALU.subtract), [self.b_BS])
            for g in range(4):
                tgt = bsl[:, g, 0:8, :]
                osl("dve", lambda e: e.tensor_copy(out=tgt, in_=self.BS[:, 15, 4 * g:4 * g + 4].unsqueeze(1).to_broadcast([128, 8, 4])), [self.b_BS])
                for k, ind in enumerate((i254, i255)):
                    osl("dve", lambda e: e.tensor_tensor(out=tb1[:], in0=ind[:, g * 8:(g + 1) * 8].unsqueeze(2).to_broadcast([128, 8, 4]),
                                                         in1=dv[:, k, 4 * g:4 * g + 4].unsqueeze(1).to_broadcast([128, 8, 4]), op=ALU.mult))
                    osl("dve", lambda e: e.tensor_tensor(out=tgt, in0=tgt, in1=tb1[:], op=ALU.add))
                osl("dve", lambda e: e.scalar_tensor_tensor(out=tgt, in0=i256[:, g * 8:(g + 1) * 8].unsqueeze(2).to_broadcast([128, 8, 4]), scalar=-30000.0,
                                                            in1=tgt, op0=ALU.mult, op1=ALU.add))
                osl("dve", lambda e: e.tensor_copy(out=bsl[:, g, 8, :], in_=self.BS[:, 16, 4 * g:4 * g + 4]), [self.b_BS])
            ksel = [S("ksel%d" % i, [128, 256]) for i in range(2)]; b_ks = [Buf("ksel") for _ in range(2)]
            KsT = S("KsT", [128, 4, 9, 128], BF16); Vs = S("Vs", [128, 4, 9, 128], BF16); b_KV = Buf("KVs")
            self.op("pool", lambda e: e.memset(KsT[:, :, 8, :], 0.0), w=[b_KV])
            self.op("pool", lambda e: e.memset(Vs[:, :, 8, :], 0.0), w=[b_KV])
            slc_rows = self.cache_slc[j]
            for g in range(4):
                self.dma(lambda e: e.dma_start(out=KsT[:, g, 8, 0:1], in_=self.KVT[1, g, 0, :, T:T + 1], allow_slow_non_contiguous=True),
                         r=[self.b_nsa_dr], w=[b_KV])
                self.dma(lambda e: e.dma_start(out=Vs[0:1, g, 8, :], in_=self.VTOK[1, g, T:T + 1, :]), r=[self.b_nsa_dr], w=[b_KV])
                for sp in range(8):
                    col = g * 8 + sp
                    k = col % 2
                    self.tk.dma("pool", lambda e: e.indirect_dma_start(out=ksel[k][:, :], out_offset=None, in_=slc_rows[:, :],
                                                                       in_offset=bass.IndirectOffsetOnAxis(ap=rowi[:, col:col + 1], axis=0)),
                                [b_s], [b_ks[k]])
                    pb = 4 + (col % 2)
                    self.op("pe", lambda e: e.transpose(out=self.psb[pb][:, 0:128], in_=ksel[k][:, 0:128], identity=self.ident[:]),
                            r=[b_ks[k], self.b_const], w=[self.bps[pb]])
                    self.op("act", lambda e: e.copy(out=KsT[:, g, sp, :], in_=self.psb[pb][:, 0:128]), r=[self.bps[pb]], w=[b_KV])
                    self.op("dve", lambda e: e.tensor_copy(out=Vs[:, g, sp, :], in_=ksel[k][:, 128:256]), r=[b_ks[k]], w=[b_KV])

            def small_attn(KT_, V_, bKV, nt, bias_ap, brx):
                ncol = 4 * nt * 4
                for g in range(4):
                    for t in range(nt):
                        c0 = (g * nt + t) * 4
                        self.op("pe", lambda e: e.matmul(self.psb[1][:, c0:c0 + 4], lhsT=KT_[:, g, t, :], rhs=qsT[:, 4 * g:4 * g + 4], start=True, stop=True),
                                r=[bKV, b_q], w=[self.bps[1]])
                oa("dve", lambda e: e.tensor_tensor(out=sTs[:, 0:ncol], in0=self.psb[1][:, 0:ncol], in1=bias_ap, op=ALU.add), [self.bps[1], b_s, self.b_BS])
                oa("act", lambda e: e.activation(out=pfs[:, 0:ncol], in_=sTs[:, 0:ncol], func=AF.Exp))
                self.op("pe", lambda e: e.matmul(self.psb[4][:, 0:ncol], lhsT=self.ones_f[:], rhs=pfs[:, 0:ncol], start=True, stop=True),
                        r=[b_a, self.b_const], w=[self.bps[4]])
                oa("dve", lambda e: e.tensor_reduce(out=tot[:].rearrange("p (g r) -> p g r", g=4),
                                                    in_=self.psb[4][:, 0:ncol].rearrange("p (g t r) -> p g r t", g=4, r=4), axis=AX.X, op=ALU.add), [self.bps[4]])
                oa("dve", lambda e: e.reciprocal(out=tot[:], in_=tot[:]))
                oa("dve", lambda e: e.tensor_tensor(out=pns[:, 0:ncol].rearrange("p (g t r) -> p g t r", g=4, r=4),
                                                    in0=pfs[:, 0:ncol].rearrange("p (g t r) -> p g t r", g=4, r=4),
                                                    in1=tot[:].rearrange("p (g r) -> p g r", g=4).unsqueeze(2).to_broadcast([128, 4, nt, 4]), op=ALU.mult))
                oa("act", lambda e: e.copy(out=pbs[:, 0:ncol], in_=pns[:, 0:ncol]))
                first = True
                for g in range(4):
                    for t in range(nt):
                        c0 = (g * nt + t) * 4
                        self.op("pe", lambda e: e.matmul(self.psb[3][0:4, g * 128:(g + 1) * 128], lhsT=pbs[:, c0:c0 + 4], rhs=V_[:, g, t, :],
                                                         start=first, stop=(t == nt - 1)), r=[b_a, bKV], w=[self.bps[3]])
                        first = False
                self.op("dve", lambda e: e.tensor_tensor(out=otmp[:], in0=self.psb[3][0:4, :].rearrange("p (g d) -> p g d", g=4),
                                                         in1=gsm[:, :, brx:brx + 1].to_broadcast([4, 4, 128]), op=ALU.mult), r=[self.bps[3], b_q, b_os], w=[b_os])
                self.op("dve", lambda e: e.tensor_tensor(out=osm[:], in0=osm[:], in1=otmp[:], op=ALU.add), r=[b_os], w=[b_os])

            small_attn(KsT, Vs, b_KV, 9, bsl[:].rearrange("p g t r -> p (g t r)"), 1)
            wld = [S("wld%d" % i, [128, 1024]) for i in range(2)]; b_wl = [Buf("wld") for _ in range(2)]
            KwT = S("KwT", [128, 4, 5, 128], BF16); Vw = S("Vw", [128, 4, 5, 128], BF16); b_KW = Buf("KVw")
            bwl = S("bwl", [128, 4, 5, 4])
            self.op("pool", lambda e: e.memset(KwT[:, :, 4, :], 0.0), w=[b_KW])
            self.op("pool", lambda e: e.memset(Vw[:, :, 4, :], 0.0), w=[b_KW])
            for g in range(4):
                self.dma(lambda e: e.dma_start(out=KwT[:, g, 4, 0:1], in_=self.KVT[2, g, 0, :, T:T + 1], allow_slow_non_contiguous=True),
                         r=[self.b_nsa_dr], w=[b_KW])
                self.dma(lambda e: e.dma_start(out=Vw[0:1, g, 4, :], in_=self.VTOK[2, g, T:T + 1, :]), r=[self.b_nsa_dr], w=[b_KW])
                osl("dve", lambda e: e.tensor_copy(out=bwl[:, g], in_=self.BS[:, 8:13, 4 * g:4 * g + 4]), [self.b_BS])
            for t in range(4):
                k = t % 2
                self.dma(lambda e: e.dma_start(out=wld[k][:], in_=self.cache_win[j, t * 128:(t + 1) * 128, :]), w=[b_wl[k]])
                for g in range(4):
                    pb = 4 + (g % 2)
                    self.op("pe", lambda e: e.transpose(out=self.psb[pb][:, 0:128], in_=wld[k][:, g * 256:g * 256 + 128], identity=self.ident[:]),
                            r=[b_wl[k], self.b_const], w=[self.bps[pb]])
                    self.op("act", lambda e: e.copy(out=KwT[:, g, t, :], in_=self.psb[pb][:, 0:128]), r=[self.bps[pb]], w=[b_KW])
                    self.op("dve", lambda e: e.tensor_copy(out=Vw[:, g, t, :], in_=wld[k][:, g * 256 + 128:(g + 1) * 256]), r=[b_wl[k]], w=[b_KW])
            small_attn(KwT, Vw, b_KW, 5, bwl[:].rearrange("p g t r -> p (g t r)"), 2)
            ots = S("ots", [128, 16, SW], BF16); b_ot = Buf("ots")
            self.op("pool", lambda e: e.memset(ots[:], 0.0), w=[b_ot])
            for g in range(4):
                self.op("pe", lambda e: e.transpose(out=self.psb[4][:, 4 * g:4 * g + 4], in_=osm[:, g, :], identity=self.ident[0:4, 0:4]),
                        r=[b_os, self.b_const], w=[self.bps[4]])
            self.op("dve", lambda e: e.tensor_copy(out=ots[:, :, 0:1], in_=self.psb[4][:, 0:16].unsqueeze(2)), r=[self.bps[4]], w=[b_ot])
            self.dma(lambda e: e.dma_start(out=self.OT.rearrange("h p t -> p h t")[:, :, T:T + SW], in_=ots[:]), r=[b_ot], w=[self.b_nsa_dr])
            self.tk.barrier()

    def build(self):
        self.declare_io()
        self.setup_consts()
        self.eps_rms = self.sb("eps_rms", [128, 1], F32)
        self.op("dve", lambda e: e.memset(self.eps_rms[:], RMS_EPS), w=[self.b_const])
        self.phase_input()
        for li in self.layers:
            if self.do_mixer:
                m = li % 3
                if m == 1:
                    self.conf_layer(li)
                elif m == 2:
                    self.s5_layer(li)
                else:
                    self.nsa_layer(li)
            if self.do_ffn:
                self.ffn_layer(li)
        self.phase_output()
        self.tk.barrier()
        self.es.close()
        return self.nc


_NC_CACHE = {}


def kernel(x_prompt, x_sample, cache_cmp_kv, cache_slc_kv, cache_win_kv, state_conv, state_ssm_re,
           state_ssm_im, state_ffn_conv, page_table, norm_gain, rel_bias, nsa_w_q, nsa_w_kv, nsa_cmp_pe,
           nsa_cmp_w1, nsa_cmp_w2, nsa_w_gate, nsa_w_o, conv_w_pw1, conv_dw, conv_dw_b, conv_ln_g, conv_ln_b,
           conv_w_pw2, ssm_a_re, ssm_a_im, ssm_log_dt, ssm_b_re, ssm_b_im, ssm_c_re, ssm_c_im, ssm_d,
           ssm_w_glu, ffn_w_up, ffn_dw, ffn_dw_b, ffn_w_down):
    A = lambda a: np.ascontiguousarray(np.asarray(a))
    x_prompt = A(x_prompt)
    B, T, _ = x_prompt.shape
    NS = x_sample.shape[0]
    nphys = cache_cmp_kv.shape[1]
    n_cores = 8
    key = (T, nphys)
    if key not in _NC_CACHE:
        _NC_CACHE[key] = Builder(T=T, nphys=nphys).build()
    nc = _NC_CACHE[key]
    f = np.float32
    shared = dict(
        norm_gain=A(norm_gain).reshape(16, D), rel_bias=A(rel_bias),
        nsa_w_q=A(nsa_w_q), nsa_w_kv=A(nsa_w_kv), nsa_cmp_pe=A(nsa_cmp_pe), nsa_cmp_w1=A(nsa_cmp_w1),
        nsa_cmp_w2=A(nsa_cmp_w2), nsa_w_gate=A(nsa_w_gate), nsa_w_o=A(nsa_w_o),
        conv_w_pw1=A(conv_w_pw1)[0], conv_dw=A(conv_dw)[0], conv_dw_b=A(conv_dw_b).reshape(1, D),
        conv_ln_g=A(conv_ln_g).reshape(1, D), conv_ln_b=A(conv_ln_b).reshape(1, D), conv_w_pw2=A(conv_w_pw2)[0],
        ssm_a_re=A(ssm_a_re)[0], ssm_a_im=A(ssm_a_im)[0], ssm_log_dt=A(ssm_log_dt).reshape(1, 128),
        ssm_b_re=A(ssm_b_re)[0], ssm_b_im=A(ssm_b_im)[0], ssm_c_re=A(ssm_c_re)[0], ssm_c_im=A(ssm_c_im)[0],
        ssm_d=A(ssm_d).reshape(1, D), ssm_w_glu=A(ssm_w_glu)[0],
        ffn_w_up=A(ffn_w_up), ffn_dw=A(ffn_dw).reshape(12, DFF), ffn_dw_b=A(ffn_dw_b), ffn_w_down=A(ffn_w_down),
        cache_cmp0=A(cache_cmp_kv)[0].reshape(nphys * 128, 1024), cache_cmp1=A(cache_cmp_kv)[1].reshape(nphys * 128, 1024),
        cache_slc0=A(cache_slc_kv)[0].reshape(nphys * 128 * 4, 256), cache_slc1=A(cache_slc_kv)[1].reshape(nphys * 128 * 4, 256),
    )
    shared.update(structural_consts())
    x_sample = A(x_sample); cache_win_kv = A(cache_win_kv); state_conv = A(state_conv)
    state_ssm_re = A(state_ssm_re); state_ssm_im = A(state_ssm_im); state_ffn_conv = A(state_ffn_conv)
    page_table = A(page_table).astype(np.int32)
    in_maps = []
    for c in range(n_cores):
        b = c % B
        s = c % NS
        m = dict(shared)
        m.update(dict(
            x_p=x_prompt[b], x_s=x_sample[s].reshape(1, D),
            cache_win=A(cache_win_kv[:, s]).reshape(2, 512, 1024),
            state_conv=A(state_conv[0, s]), state_ssm_re=A(state_ssm_re[0, s]), state_ssm_im=A(state_ssm_im[0, s]),
            state_ffn=A(state_ffn_conv[:, s]), page_table=page_table[s:s + 1],
        ))
        in_maps.append(m)
    res = run_bass_kernel_spmd(nc, in_maps, core_ids=list(range(n_cores)))
    R = res.results
    P = lambda name: np.stack([R[b][name] for b in range(B)], 0)
    Sm = lambda name: np.stack([R[s][name] for s in range(NS)], 0)
    WP = min(512, T)
    y_p = P("y_p")
    y_s = Sm("y_s")
    kvp = lambda n, rows: np.moveaxis(P(n), 0, 1).reshape(2, B, rows, 4, 2, 128)
    kvs = lambda n, rows: np.moveaxis(Sm(n), 0, 1).reshape(2, NS, rows, 4, 2, 128)
    outs = (
        y_p, y_s,
        kvp("cmp_kv_p", T), kvs("cmp_kv_s", 1), kvp("slc_kv_p", T), kvs("slc_kv_s", 1),
        kvp("win_kv_p", WP), kvs("win_kv_s", 512),
        P("conv_p")[None], Sm("conv_s")[None],
        P("ssm_re_p")[None], Sm("ssm_re_s")[None], P("ssm_im_p")[None], Sm("ssm_im_s")[None],
        np.moveaxis(P("ffn_p"), 0, 1), np.moveaxis(Sm("ffn_s"), 0, 1),
    )
    return tuple(np.ascontiguousarray(o.astype(np.float32)) for o in outs)
```

```python
import contextlib
import math
import numpy as np
import concourse.bass as bass
import concourse.mybir as mybir
from concourse.bass_utils import run_bass_kernel_spmd

F32 = mybir.dt.float32
BF16 = mybir.dt.bfloat16
I32 = mybir.dt.int32
U32 = mybir.dt.uint32
AF = mybir.ActivationFunctionType
ALU = mybir.AluOpType
AX = mybir.AxisListType

D = 2048
DC = 16
DFF = 5632
FC = 44
DEPTH = 4
SW = 8
EPOCH = 20000
RMS_EPS = 1e-6
LN_EPS = 1e-5


def _bucket(d):
    d = np.asarray(d, dtype=np.int64)
    n = np.maximum(d, 0)
    nf = np.maximum(n, 16).astype(np.float32)
    big = 16 + (np.log(nf / np.float32(16)) / np.float32(math.log(8)) * np.float32(16)).astype(np.int32)
    return np.where(n < 16, n, np.minimum(big, 31)).astype(np.int64)


def _onehot(d, valid):
    b = np.where(valid, _bucket(d), 32)
    oh = np.zeros((33,) + b.shape, np.float32)
    np.put_along_axis(oh, b[None], 1.0, axis=0)
    return oh


def structural_consts():
    c = {}
    n = np.arange(128)[:, None]
    q = np.arange(128)[None, :]
    kinds = []
    kinds.append(_onehot(q - n, (q - n) >= 0))
    kinds.append(_onehot(128 + q - n, np.ones((128, 128), bool)))
    kinds.append(_onehot(np.full((128, 128), 1000), np.ones((128, 128), bool)))
    kinds.append(_onehot(512 + q - n, n >= q))
    c["oh_pk"] = np.stack(kinds, 1).reshape(33, 4 * 128 * 128)
    m = np.arange(248)[:, None] - 120
    d = q - 16 * m - 31
    c["oh_cmp"] = _onehot(d, d >= 0).reshape(33, 248 * 128)
    p = np.arange(128)
    tiles = []
    for t in range(8):
        cc = t * 128 + p
        d = 16384 - 16 * cc - 15
        tiles.append(_onehot(d, cc >= 1))
    for t in range(4):
        idx = t * 128 + p
        tiles.append(_onehot(512 - idx, np.ones(128, bool)))
    tiles.append(_onehot(np.zeros(128, np.int64), p == 0))
    tiles.append(_onehot(128 - (p % 64), np.ones(128, bool)))
    tiles.append(_onehot(64 - (p % 64), np.ones(128, bool)))
    tiles.append(_onehot(np.full(128, 1000), np.ones(128, bool)))
    tiles.append(_onehot(np.zeros(128, np.int64), p == 0))
    c["oh_s"] = np.stack(tiles, 1).astype(np.float32)
    coef = np.array([1, 2, 2, 2, 1], np.float32)
    mp = np.zeros((128, 32), np.float32)
    for nn in range(127):
        for j in range(32):
            o = nn + 1 - 4 * j
            if 0 <= o <= 4:
                mp[nn, j] = coef[o]
    c["msel_p"] = mp
    am = np.zeros((8, 128, 32), np.float32)
    for i in range(8, 16):
        for ql in range(128):
            cur = (i * 128 + ql) // 64
            for j in range(32):
                if j > cur:
                    am[i - 8, ql, j] = -1e30
                elif j == 0 or j == cur or j == cur - 1:
                    am[i - 8, ql, j] = 1e4
    c["addmask_p"] = am
    ms = np.zeros((1024, 257), np.float32)
    for cc in range(1024):
        for j in range(257):
            o = cc - 4 * j
            if 0 <= o <= 4:
                ms[cc, j] = coef[o]
    c["msel_s"] = ms
    ams = np.zeros((4, 257), np.float32)
    ams[:, [0, 255, 256]] = 1e4
    c["addmask_s"] = ams
    ee = np.zeros((32, 16, 128), np.float32)
    for kt in range(16):
        for nn in range(128):
            ee[(kt * 128 + nn) // 64, kt, nn] = 1.0
    c["eexp"] = ee
    sel = np.zeros((2, 4, 128), np.float32)
    sel[0, :, :64] = 1.0
    sel[1, :, 64:] = 1.0
    c["selhalf"] = sel
    c["eye4"] = np.eye(4, dtype=np.float32)
    c["iota_pg"] = np.tile(np.arange(128, dtype=np.float32)[None, :], (128, 1))
    c["iota_row"] = (np.arange(128) % 64).astype(np.float32)[:, None].copy()
    c["iota_p"] = np.arange(128, dtype=np.float32)[:, None].copy()
    c["gcol"] = np.tile((np.arange(32) // 8).astype(np.float32)[None, :], (128, 1))
    return c


STRUCT_SHAPES = dict(oh_pk=[33, 65536], oh_cmp=[33, 248 * 128], oh_s=[33, 17, 128], msel_p=[128, 32],
                     addmask_p=[8, 128, 32], msel_s=[1024, 257], addmask_s=[4, 257], eexp=[32, 16, 128],
                     selhalf=[2, 4, 128], eye4=[4, 4], iota_pg=[128, 128], iota_row=[128, 1], iota_p=[128, 1], gcol=[128, 32])

class Buf:
    __slots__ = ("name", "last_w", "readers")

    def __init__(self, name):
        self.name = name
        self.last_w = None
        self.readers = []


class TK:
    def __init__(self, nc, es, same_engine_sync=True):
        self.nc = nc
        self.es = es
        self.eng = {"pe": nc.tensor, "dve": nc.vector, "act": nc.scalar, "pool": nc.gpsimd, "sp": nc.sync}
        self.cur = {}
        self.nsem = 0
        for q in self.eng:
            self.cur[q] = [self._newsem(q), 0]
        self.seen = {q: {} for q in self.eng}
        self.dpool = {}
        for q, n in (("sp", 20), ("act", 8), ("pool", 12)):
            self.dpool[q] = [[self._newsem("d" + q), 0] for _ in range(n)]
        self.dnext = {q: 0 for q in self.dpool}
        self.same = same_engine_sync
        self.ninstr = 0
        self.nwait = 0

    def _newsem(self, tag):
        self.nsem += 1
        return self.es.enter_context(self.nc.semaphore("s_%s_%d" % (tag, self.nsem)))

    def _wait(self, q, ev):
        sem, val, owner = ev
        key = id(sem)
        if self.seen[q].get(key, 0) >= val:
            return
        self.eng[q].wait_ge(sem, val)
        self.nwait += 1
        self.seen[q][key] = val

    def _deps(self, q, reads, writes, is_dma):
        evs = []
        for b in reads:
            if b.last_w is not None:
                evs.append(b.last_w)
        for b in writes:
            if b.last_w is not None:
                evs.append(b.last_w)
            evs.extend(b.readers)
        for ev in evs:
            if ev[2] == q and not is_dma:
                if q == "pe" or not self.same:
                    continue
            self._wait(q, ev)

    def _record(self, ev, reads, writes):
        for b in writes:
            b.last_w = ev
            b.readers = []
        for b in reads:
            b.readers = [e for e in b.readers if not (e[2] == ev[2] and e[0] is ev[0])] + [ev]

    def op(self, q, fn, reads=(), writes=()):
        self._deps(q, reads, writes, False)
        c = self.cur[q]
        if c[1] >= EPOCH:
            c[0] = self._newsem(q)
            c[1] = 0
        ins = fn(self.eng[q])
        c[1] += 1
        ins.then_inc(c[0], 1)
        ev = (c[0], c[1], q)
        self._record(ev, reads, writes)
        self.ninstr += 1
        return ev

    def dma(self, q, fn, reads=(), writes=()):
        self._deps(q, reads, writes, True)
        pool = self.dpool[q]
        i = self.dnext[q]
        self.dnext[q] = (i + 1) % len(pool)
        s = pool[i]
        if s[1] > 0:
            self._wait(q, (s[0], s[1], "dma" + q))
        ins = fn(self.eng[q])
        s[1] += 16
        ins.then_inc(s[0], 16)
        ev = (s[0], s[1], "dma" + q)
        self._record(ev, reads, writes)
        self.ninstr += 1
        return ev

    def barrier(self):
        evs = []
        for q, c in self.cur.items():
            if c[1] > 0:
                evs.append((c[0], c[1], q))
        for q, pool in self.dpool.items():
            for s in pool:
                if s[1] > 0:
                    evs.append((s[0], s[1], "dma" + q))
        for q in self.eng:
            for ev in evs:
                if ev[2] == q:
                    continue
                self._wait(q, ev)


class Builder:
    def __init__(self, T=2048, layers=(0, 1, 2, 3), do_mixer=True, do_ffn=True, dbg=False, nphys=1280):
        self.NPHYS = nphys
        self.T = T
        self.NT = T // 512
        self.TT = T + SW
        self.layers = layers
        self.do_mixer = do_mixer
        self.do_ffn = do_ffn
        self.dbg = dbg
        self.nc = bass.Bass("TRN2", target_bir_lowering=False)
        self.es = contextlib.ExitStack()
        self.tk = TK(self.nc, self.es)
        self.tiles = [(i * 512, 512) for i in range(self.NT)] + [(T, SW)]
        self.dq = 0
        self.wcache = {}

    def din(self, name, shape, dt=F32):
        return self.nc.dram_tensor(name, list(shape), dt, kind="ExternalInput").ap()

    def dout(self, name, shape, dt=F32):
        return self.nc.dram_tensor(name, list(shape), dt, kind="ExternalOutput").ap()

    def dscr(self, name, shape, dt=F32):
        return self.nc.dram_tensor(name, list(shape), dt, kind="Internal").ap()

    def sb(self, name, shape, dt=F32, stack=None):
        self.uid = getattr(self, "uid", 0) + 1
        return (stack or self.es).enter_context(self.nc.sbuf_tensor("%s_u%d" % (name, self.uid), list(shape), dt))

    def ps(self, name, shape, dt=F32, stack=None):
        return (stack or self.es).enter_context(self.nc.psum_tensor(name, list(shape), dt))

    def op(self, q, fn, r=(), w=()):
        return self.tk.op(q, fn, r, w)

    def dma(self, fn, r=(), w=(), q=None):
        if q is None:
            q = "sp"
        return self.tk.dma(q, fn, r, w)

    def declare_io(self):
        T = self.T
        d = self.din
        self.x_p = d("x_p", [T, D])
        self.x_s = d("x_s", [1, D])
        self.norm_gain = d("norm_gain", [DEPTH * 4, D])
        self.ffn_w_up = d("ffn_w_up", [DEPTH, D, 2 * DFF])
        self.ffn_dw = d("ffn_dw", [DEPTH * 3, DFF])
        self.ffn_dw_b = d("ffn_dw_b", [DEPTH, DFF])
        self.ffn_w_down = d("ffn_w_down", [DEPTH, DFF, D])
        self.state_ffn = d("state_ffn", [DEPTH, 2, DFF])
        self.conv_w_pw1 = d("conv_w_pw1", [D, 2 * D])
        self.conv_dw = d("conv_dw", [31, D])
        self.conv_dw_b = d("conv_dw_b", [1, D])
        self.conv_ln_g = d("conv_ln_g", [1, D])
        self.conv_ln_b = d("conv_ln_b", [1, D])
        self.conv_w_pw2 = d("conv_w_pw2", [D, D])
        self.state_conv = d("state_conv", [30, D])
        self.ssm_a_re = d("ssm_a_re", [128, 64])
        self.ssm_a_im = d("ssm_a_im", [128, 64])
        self.ssm_log_dt = d("ssm_log_dt", [1, 128])
        self.ssm_b_re = d("ssm_b_re", [128, 64, 16])
        self.ssm_b_im = d("ssm_b_im", [128, 64, 16])
        self.ssm_c_re = d("ssm_c_re", [128, 16, 64])
        self.ssm_c_im = d("ssm_c_im", [128, 16, 64])
        self.ssm_d = d("ssm_d", [1, D])
        self.ssm_w_glu = d("ssm_w_glu", [D, 2 * D])
        self.state_ssm_re = d("state_ssm_re", [128, 64])
        self.state_ssm_im = d("state_ssm_im", [128, 64])
        o = self.dout
        self.ssm_re_p = o("ssm_re_p", [128, 64])
        self.ssm_re_s = o("ssm_re_s", [128, 64])
        self.ssm_im_p = o("ssm_im_p", [128, 64])
        self.ssm_im_s = o("ssm_im_s", [128, 64])
        self.conv_p = o("conv_p", [30, D])
        self.conv_s = o("conv_s", [30, D])
        self.y_p = o("y_p", [T, D])
        self.y_s = o("y_s", [1, D])
        self.ffn_p = o("ffn_p", [DEPTH, 2, DFF])
        self.ffn_s = o("ffn_s", [DEPTH, 2, DFF])
        self.nsa_declare()
        self.XR = self.dscr("XR", [DC, 128, self.TT])
        self.bXR = [Buf("XR%d" % i) for i in range(len(self.tiles))]

    def setup_consts(self):
        tk = self.tk
        self.ident = self.sb("ident", [128, 128], F32)
        self.ident_b = self.sb("ident_b", [128, 128], BF16)
        self.ones_b = self.sb("ones_b", [128, 128], BF16)
        self.gains = self.sb("gains", [128, DEPTH * 4, DC], F32)
        self.b_const = Buf("const")
        nc = self.nc
        self.iot = self.sb("iot", [128, 128], F32)
        self.op("pool", lambda e: e.iota(self.iot[:], pattern=[[1, 128]], base=0, channel_multiplier=-1,
                                         allow_small_or_imprecise_dtypes=True), w=[self.b_const])
        self.op("dve", lambda e: e.tensor_scalar(out=self.ident[:], in0=self.iot[:], scalar1=0.0, scalar2=None,
                                                 op0=ALU.is_equal), r=[self.b_const], w=[self.b_const])
        self.op("dve", lambda e: e.tensor_copy(out=self.ident_b[:], in_=self.ident[:]), r=[self.b_const],
                w=[self.b_const])
        self.op("dve", lambda e: e.memset(self.ones_b[:], 1.0), w=[self.b_const])
        self.ones_f = self.sb("ones_f", [128, 128], F32)
        self.op("dve", lambda e: e.memset(self.ones_f[:], 1.0), w=[self.b_const])
        self.dma(lambda e: e.dma_start(out=self.gains[:], in_=self.norm_gain.rearrange("l (c p) -> p l c", p=128),
                                       allow_slow_non_contiguous=True), w=[self.b_const])
        self.psb = [self.ps("psb%d" % i, [128, 512], F32) for i in range(8)]
        self.bps = [Buf("ps%d" % i) for i in range(8)]

    def phase_input(self):
        with contextlib.ExitStack() as st:
            NB = 2
            tin = [self.sb("tin%d" % i, [128, D], F32, st) for i in range(NB)]
            tout = [self.sb("tout%d" % i, [128, DC, 128], F32, st) for i in range(NB)]
            btin = [Buf("tin") for _ in range(NB)]
            btout = [Buf("tout") for _ in range(NB)]
            XRv = self.XR.rearrange("c p t -> p c t")
            nblk = self.T // 128
            for b in range(nblk + 1):
                k = b % NB
                samp = (b == nblk)
                rows = 1 if samp else 128
                if samp:
                    self.op("pool", lambda e: e.memset(tin[k][:], 0.0), w=[btin[k]])
                    self.dma(lambda e: e.dma_start(out=tin[k][0:1, :], in_=self.x_s[0:1, :]), w=[btin[k]])
                else:
                    self.dma(lambda e: e.dma_start(out=tin[k][:], in_=self.x_p[b * 128:(b + 1) * 128, :]),
                             w=[btin[k]])
                for g in range(4):
                    pb = 4 + (g % 2)
                    for j in range(4):
                        c = g * 4 + j
                        self.op("pe", lambda e: e.transpose(out=self.psb[pb][:, j * 128:(j + 1) * 128],
                                                            in_=tin[k][:, c * 128:(c + 1) * 128],
                                                            identity=self.ident[:]),
                                r=[btin[k], self.b_const], w=[self.bps[pb]])
                    eng = "act" if g % 2 == 0 else "dve"
                    if eng == "act":
                        self.op("act", lambda e: e.copy(out=tout[k][:, g * 4:(g + 1) * 4, :],
                                                        in_=self.psb[pb][:].rearrange("p (j t) -> p j t", j=4)),
                                r=[self.bps[pb]], w=[btout[k]])
                    else:
                        self.op("dve", lambda e: e.tensor_copy(out=tout[k][:, g * 4:(g + 1) * 4, :],
                                                               in_=self.psb[pb][:].rearrange("p (j t) -> p j t", j=4)),
                                r=[self.bps[pb]], w=[btout[k]])
                if samp:
                    ti = len(self.tiles) - 1
                    self.dma(lambda e: e.dma_start(out=XRv[:, :, self.T:self.T + SW], in_=tout[k][:, :, 0:SW]),
                             r=[btout[k]], w=[self.bXR[ti]])
                else:
                    ti = b // 4
                    self.dma(lambda e: e.dma_start(out=XRv[:, :, b * 128:(b + 1) * 128], in_=tout[k][:]),
                             r=[btout[k]], w=[self.bXR[ti]])
            self.tk.barrier()

    def phase_output(self):
        with contextlib.ExitStack() as st:
            NB = 2
            tin = [self.sb("oin%d" % i, [128, DC, 128], F32, st) for i in range(NB)]
            tout = [self.sb("oout%d" % i, [128, D], F32, st) for i in range(NB)]
            btin = [Buf("oin") for _ in range(NB)]
            btout = [Buf("oout") for _ in range(NB)]
            XRv = self.XR.rearrange("c p t -> p c t")
            nblk = self.T // 128
            for b in range(nblk + 1):
                k = b % NB
                samp = (b == nblk)
                if samp:
                    ti = len(self.tiles) - 1
                    self.op("pool", lambda e: e.memset(tin[k][:], 0.0), w=[btin[k]])
                    self.dma(lambda e: e.dma_start(out=tin[k][:, :, 0:SW], in_=XRv[:, :, self.T:self.T + SW]),
                             r=[self.bXR[ti]], w=[btin[k]])
                else:
                    ti = b // 4
                    self.dma(lambda e: e.dma_start(out=tin[k][:], in_=XRv[:, :, b * 128:(b + 1) * 128]),
                             r=[self.bXR[ti]], w=[btin[k]])
                for g in range(4):
                    pb = 4 + (g % 2)
                    for j in range(4):
                        c = g * 4 + j
                        self.op("pe", lambda e: e.transpose(out=self.psb[pb][:, j * 128:(j + 1) * 128],
                                                            in_=tin[k][:, c, :], identity=self.ident[:]),
                                r=[btin[k], self.b_const], w=[self.bps[pb]])
                    if g % 2 == 0:
                        self.op("act", lambda e: e.copy(out=tout[k][:, g * 512:(g + 1) * 512], in_=self.psb[pb][:]),
                                r=[self.bps[pb]], w=[btout[k]])
                    else:
                        self.op("dve", lambda e: e.tensor_copy(out=tout[k][:, g * 512:(g + 1) * 512],
                                                               in_=self.psb[pb][:]),
                                r=[self.bps[pb]], w=[btout[k]])
                if samp:
                    self.dma(lambda e: e.dma_start(out=self.y_s[0:1, :], in_=tout[k][0:1, :]), r=[btout[k]])
                else:
                    self.dma(lambda e: e.dma_start(out=self.y_p[b * 128:(b + 1) * 128, :], in_=tout[k][:]),
                             r=[btout[k]])
            self.tk.barrier()

    def alloc_rowlocal(self, st):
        self.xr = self.sb("xr", [128, DC, 512], F32, st)
        self.b_xr = Buf("xr")
        self.xn = self.sb("xn", [128, DC, 512], BF16, st)
        self.b_xn = Buf("xn")
        self.yy = self.sb("yy", [128, DC, 512], F32, st)
        self.b_yy = Buf("yy")
        self.rstd = self.sb("rstd", [128, 512], F32, st)
        self.b_rstd = Buf("rstd")
        self.NWB = 4
        self.NWF = 2
        self.wf = [self.sb("wf%d" % i, [128, 8, 2, 128], F32, st) for i in range(self.NWF)]
        self.wb = [self.sb("wb%d" % i, [128, 8, 2, 128], BF16, st) for i in range(self.NWB)]
        self.b_wf = [Buf("wf") for _ in range(self.NWF)]
        self.b_wb = [Buf("wb") for _ in range(self.NWB)]
        self.wctr = 0
        self.wfctr = 0

    def load_xr(self, ti):
        t0, W = self.tiles[ti]
        XRv = self.XR.rearrange("c p t -> p c t")
        self.dma(lambda e: e.dma_start(out=self.xr[:, :, :W], in_=XRv[:, :, t0:t0 + W]),
                 r=[self.bXR[ti]], w=[self.b_xr])

    def store_xr(self, ti):
        t0, W = self.tiles[ti]
        XRv = self.XR.rearrange("c p t -> p c t")
        self.dma(lambda e: e.dma_start(out=XRv[:, :, t0:t0 + W], in_=self.xr[:, :, :W]),
                 r=[self.b_xr], w=[self.bXR[ti]])

    def rms_stats(self, src, bsrc, W, sq, bsq):
        self.op("act", lambda e: e.activation(out=sq[:, :, :W], in_=src[:, :, :W], func=AF.Square),
                r=[bsrc], w=[bsq])
        pb = 4
        for c in range(DC):
            self.op("pe", lambda e: e.matmul(self.psb[pb][:, :W], lhsT=self.ones_b[:], rhs=sq[:, c, :W],
                                             start=(c == 0), stop=(c == DC - 1)),
                    r=[bsq, self.b_const], w=[self.bps[pb]])
        self.op("act", lambda e: e.activation(out=self.rstd[:, :W], in_=self.psb[pb][:, :W], func=AF.Sqrt,
                                              bias=self.eps_rms[:, 0:1], scale=1.0 / D),
                r=[self.bps[pb], self.b_const], w=[self.b_rstd])
        self.op("dve", lambda e: e.reciprocal(out=self.rstd[:, :W], in_=self.rstd[:, :W]),
                r=[self.b_rstd], w=[self.b_rstd])

    def pre_norm(self, ti, gidx, xn_f32=None, b_xnf=None):
        t0, W = self.tiles[ti]
        self.load_xr(ti)
        self.rms_stats(self.xr, self.b_xr, W, self.xn, self.b_xn)
        for c in range(DC):
            if xn_f32 is not None:
                self.op("dve", lambda e: e.scalar_tensor_tensor(out=xn_f32[:, c, :W], in0=self.xr[:, c, :W],
                                                                scalar=self.gains[:, gidx, c:c + 1],
                                                                in1=self.rstd[:, :W], op0=ALU.mult, op1=ALU.mult),
                        r=[self.b_xr, self.b_rstd, self.b_const], w=[b_xnf])
                self.op("act", lambda e: e.copy(out=self.xn[:, c, :W], in_=xn_f32[:, c, :W]),
                        r=[b_xnf], w=[self.b_xn])
            else:
                self.op("dve", lambda e: e.scalar_tensor_tensor(out=self.xn[:, c, :W], in0=self.xr[:, c, :W],
                                                                scalar=self.gains[:, gidx, c:c + 1],
                                                                in1=self.rstd[:, :W], op0=ALU.mult, op1=ALU.mult),
                        r=[self.b_xr, self.b_rstd, self.b_const], w=[self.b_xn])

    def post_norm_residual(self, ti, gidx):
        t0, W = self.tiles[ti]
        self.rms_stats(self.yy, self.b_yy, W, self.xn, self.b_xn)
        for c in range(DC):
            self.op("dve", lambda e: e.scalar_tensor_tensor(out=self.yy[:, c, :W], in0=self.yy[:, c, :W],
                                                            scalar=self.gains[:, gidx, c:c + 1],
                                                            in1=self.rstd[:, :W], op0=ALU.mult, op1=ALU.mult),
                    r=[self.b_yy, self.b_rstd, self.b_const], w=[self.b_yy])
            self.op("pool", lambda e: e.tensor_tensor(out=self.xr[:, c, :W], in0=self.xr[:, c, :W],
                                                      in1=self.yy[:, c, :W], op=ALU.add),
                    r=[self.b_yy, self.b_xr], w=[self.b_xr])
        self.store_xr(ti)

    def gemm(self, wview, KC, npairs, xin, bxin, W, epilogue, wkey=None, first=True):
        nkb = (KC + 7) // 8
        cache = None
        if wkey is not None:
            if wkey not in self.wcache:
                self.wcache[wkey] = (self.dscr("wc_" + wkey, [npairs * nkb, 128, 2048], BF16), Buf("wc_" + wkey))
            cache, b_cache = self.wcache[wkey]
        for p in range(npairs):
            pa = (p % 2) * 2
            pbk = pa + 1
            for kb in range(nkb):
                k0 = kb * 8
                kn = min(8, KC - k0)
                i = self.wctr % self.NWB
                self.wctr += 1
                blk = p * nkb + kb
                if first or cache is None:
                    fi = self.wfctr % self.NWF
                    self.wfctr += 1
                    for h in range(2):
                        src = wview(p, k0, kn, h)
                        self.dma(lambda e: e.dma_start(out=self.wf[fi][:, :kn, h, :], in_=src), w=[self.b_wf[fi]], q="sp")
                    ceng = "pool" if (self.wfctr % 2 == 0) else "act"
                    if ceng == "pool":
                        self.op("pool", lambda e: e.tensor_copy(out=self.wb[i][:, :kn], in_=self.wf[fi][:, :kn]),
                                r=[self.b_wf[fi]], w=[self.b_wb[i]])
                    else:
                        self.op("act", lambda e: e.copy(out=self.wb[i][:, :kn], in_=self.wf[fi][:, :kn]),
                                r=[self.b_wf[fi]], w=[self.b_wb[i]])
                    if cache is not None:
                        self.dma(lambda e: e.dma_start(out=cache[blk, :, 0:kn * 256],
                                                       in_=self.wb[i][:, :kn].rearrange("p k h n -> p (k h n)")),
                                 r=[self.b_wb[i]], w=[b_cache], q=ceng)
                else:
                    self.dma(lambda e: e.dma_start(out=self.wb[i][:, :kn].rearrange("p k h n -> p (k h n)"),
                                                   in_=cache[blk, :, 0:kn * 256]), r=[b_cache], w=[self.b_wb[i]], q="sp")
                for kk in range(kn):
                    kc = k0 + kk
                    for h, pbank in ((0, pa), (1, pbk)):
                        self.op("pe", lambda e: e.matmul(self.psb[pbank][:, :W], lhsT=self.wb[i][:, kk, h, :],
                                                         rhs=xin[:, kc, :W], start=(kc == 0), stop=(kc == KC - 1)),
                                r=[self.b_wb[i], bxin], w=[self.bps[pbank]])
            epilogue(p, self.psb[pa], self.bps[pa], self.psb[pbk], self.bps[pbk])

    def ffn_layer(self, li):
        tk = self.tk
        with contextlib.ExitStack() as st:
            self.alloc_rowlocal(st)
            hh = self.sb("hh", [128, FC, 512], BF16, st)
            b_hh = Buf("hh")
            ghist = self.sb("ghist", [128, 2, FC], F32, st)
            b_gh = Buf("ghist")
            gbuf = [self.sb("gbuf%d" % i, [128, 514], F32, st) for i in range(2)]
            b_gb = [Buf("gbuf") for _ in range(2)]
            acc = [self.sb("acc%d" % i, [128, 512], F32, st) for i in range(2)]
            b_acc = [Buf("acc") for _ in range(2)]
            dwT = self.sb("dwT", [128, 3, FC], F32, st)
            dbT = self.sb("dbT", [128, FC], F32, st)
            b_dw = Buf("dw")
            hist_in = self.sb("hist_in", [88, 128], F32, st)
            b_hi = Buf("hi")
            hsT = self.sb("hsT", [128, 2, FC], F32, st)
            b_hs = Buf("hsT")
            hout = self.sb("hout", [88, 128], F32, st)
            b_ho = Buf("hout")
            self.dma(lambda e: e.dma_start(out=dwT[:], in_=self.ffn_dw[li * 3:(li + 1) * 3, :]
                                           .rearrange("w (c p) -> p w c", p=128), allow_slow_non_contiguous=True),
                     w=[b_dw])
            self.dma(lambda e: e.dma_start(out=dbT[:], in_=self.ffn_dw_b[li:li + 1, :]
                                           .rearrange("o (c p) -> p (o c)", p=128), allow_slow_non_contiguous=True),
                     w=[b_dw])
            self.dma(lambda e: e.dma_start(out=hist_in[:], in_=self.state_ffn[li].rearrange("t (c p) -> (t c) p", p=128)),
                     w=[b_hi])
            self.op("pe", lambda e: e.transpose(out=self.psb[5][:, 0:88], in_=hist_in[:], identity=self.ident[0:88, 0:88]),
                    r=[b_hi, self.b_const], w=[self.bps[5]])
            self.op("dve", lambda e: e.tensor_copy(out=hsT[:].rearrange("p t c -> p (t c)"), in_=self.psb[5][:, 0:88]),
                    r=[self.bps[5]], w=[b_hs])
            self.op("dve", lambda e: e.memset(ghist[:], 0.0), w=[b_gh])

            w_up = self.ffn_w_up[li].rearrange("(kc p) (h f) -> p kc h f", p=128, h=2)
            w_dn = self.ffn_w_down[li].rearrange("(kc p) (f h n) -> p kc f h n", p=128, h=2, n=128)

            for ti in range(len(self.tiles)):
                t0, W = self.tiles[ti]
                samp = (ti == len(self.tiles) - 1)
                self.pre_norm(ti, li * 4 + 2)
                if samp:
                    self.op("dve", lambda e: e.tensor_copy(out=ghist[:], in_=hsT[:]), r=[b_hs], w=[b_gh])

                def ep_up(p, psA, bA, psB, bB):
                    k = p % 2
                    self.op("act", lambda e: e.copy(out=gbuf[k][:, 2:2 + W], in_=psA[:, :W]), r=[bA], w=[b_gb[k]])
                    self.op("dve", lambda e: e.tensor_copy(out=gbuf[k][:, 0:2], in_=ghist[:, :, p]),
                            r=[b_gh], w=[b_gb[k]])
                    self.op("act", lambda e: e.activation(out=acc[k][:, :W], in_=gbuf[k][:, 2:2 + W], func=AF.Identity,
                                                          bias=dbT[:, p:p + 1], scale=dwT[:, 2, p:p + 1]),
                            r=[b_gb[k], b_dw], w=[b_acc[k]])
                    self.op("dve", lambda e: e.scalar_tensor_tensor(out=acc[k][:, :W], in0=gbuf[k][:, 1:1 + W],
                                                                    scalar=dwT[:, 1, p:p + 1], in1=acc[k][:, :W],
                                                                    op0=ALU.mult, op1=ALU.add),
                            r=[b_gb[k], b_dw, b_acc[k]], w=[b_acc[k]])
                    self.op("dve", lambda e: e.scalar_tensor_tensor(out=acc[k][:, :W], in0=gbuf[k][:, 0:W],
                                                                    scalar=dwT[:, 0, p:p + 1], in1=acc[k][:, :W],
                                                                    op0=ALU.mult, op1=ALU.add),
                            r=[b_gb[k], b_dw, b_acc[k]], w=[b_acc[k]])
                    if samp:
                        self.op("act", lambda e: e.copy(out=ghist[:, 0, p:p + 1], in_=gbuf[k][:, 1:2]),
                                r=[b_gb[k]], w=[b_gh])
                        self.op("act", lambda e: e.copy(out=ghist[:, 1, p:p + 1], in_=gbuf[k][:, 2:3]),
                                r=[b_gb[k]], w=[b_gh])
                    else:
                        self.op("act", lambda e: e.copy(out=ghist[:, :, p], in_=gbuf[k][:, W:W + 2]),
                                r=[b_gb[k]], w=[b_gh])
                    self.op("act", lambda e: e.activation(out=acc[k][:, :W], in_=acc[k][:, :W],
                                                          func=AF.Gelu_apprx_tanh),
                            r=[b_acc[k]], w=[b_acc[k]])
                    self.op("dve", lambda e: e.tensor_tensor(out=hh[:, p, :W], in0=acc[k][:, :W], in1=psB[:, :W],
                                                             op=ALU.mult),
                            r=[b_acc[k], bB], w=[b_hh])

                self.gemm(lambda p, k0, kn, h: w_up[:, k0:k0 + kn, h, p * 128:(p + 1) * 128], DC, FC,
                          self.xn, self.b_xn, W, ep_up, wkey='up%d' % li, first=(ti == 0))

                def ep_dn(p, psA, bA, psB, bB):
                    self.op("act", lambda e: e.copy(out=self.yy[:, 2 * p, :W], in_=psA[:, :W]), r=[bA], w=[self.b_yy])
                    self.op("dve", lambda e: e.tensor_copy(out=self.yy[:, 2 * p + 1, :W], in_=psB[:, :W]), r=[bB],
                            w=[self.b_yy])

                self.gemm(lambda p, k0, kn, h: w_dn[:, k0:k0 + kn, p, h, :], FC, DC // 2, hh, b_hh, W, ep_dn, wkey='dn%d' % li, first=(ti == 0))
                self.post_norm_residual(ti, li * 4 + 3)

                if ti == self.NT - 1 or samp:
                    dst = self.ffn_s if samp else self.ffn_p
                    self.op("pe", lambda e: e.transpose(out=self.psb[5][0:88, 0:128],
                                                        in_=ghist[:].rearrange("p t c -> p (t c)"),
                                                        identity=self.ident[:]),
                            r=[b_gh, self.b_const], w=[self.bps[5]])
                    self.op("dve", lambda e: e.tensor_copy(out=hout[:], in_=self.psb[5][0:88, 0:128]),
                            r=[self.bps[5]], w=[b_ho])
                    self.dma(lambda e: e.dma_start(out=dst[li].rearrange("t (c p) -> (t c) p", p=128), in_=hout[:]),
                             r=[b_ho])
            self.tk.barrier()

    def conf_layer(self, li):
        with contextlib.ExitStack() as st:
            self.alloc_rowlocal(st)
            CW = 31
            ubuf = self.sb("ubuf", [128, DC, 30 + 512], F32, st)
            b_ub = Buf("ubuf")
            sig = [self.sb("sig%d" % i, [128, 512], F32, st) for i in range(2)]
            b_sig = [Buf("sig") for _ in range(2)]
            dwT = self.sb("cdwT", [128, CW, DC], F32, st)
            prm = self.sb("cprm", [128, 3, DC], F32, st)
            b_dw = Buf("cdw")
            mean = self.sb("mean", [128, 512], F32, st)
            b_mean = Buf("mean")
            msq = self.sb("msq", [128, 512], F32, st)
            b_msq = Buf("msq")
            hio = self.sb("hio", [30, D], F32, st)
            b_hio = Buf("hio")
            cacc = self.sb("cacc", [128, 512], F32, st); b_cacc = Buf("cacc")
            ctmp = [self.sb("ctmp%d" % i, [128, 512], F32, st) for i in range(2)]; b_ctmp = [Buf("ctmp") for _ in range(2)]
            eps_ln = self.sb("eps_ln", [128, 1], F32, st)
            self.op("dve", lambda e: e.memset(eps_ln[:], LN_EPS), w=[b_dw])
            self.dma(lambda e: e.dma_start(out=dwT[:], in_=self.conv_dw.rearrange("w (c p) -> p w c", p=128),
                                           allow_slow_non_contiguous=True), w=[b_dw])
            for k, src in enumerate((self.conv_dw_b, self.conv_ln_g, self.conv_ln_b)):
                self.dma(lambda e: e.dma_start(out=prm[:, k, :], in_=src.rearrange("o (c p) -> p (o c)", p=128),
                                               allow_slow_non_contiguous=True), w=[b_dw])
            self.op("dve", lambda e: e.memset(ubuf[:, :, 0:30], 0.0), w=[b_ub])
            w1 = self.conv_w_pw1.rearrange("(kc p) (h f) -> p kc h f", p=128, h=2)
            w2 = self.conv_w_pw2.rearrange("(kc p) (f h n) -> p kc f h n", p=128, h=2, n=128)
            hc = self.yy
            b_hc = self.b_yy
            for ti in range(len(self.tiles)):
                t0, W = self.tiles[ti]
                samp = (ti == len(self.tiles) - 1)
                self.pre_norm(ti, li * 4 + 0)
                if samp:
                    self.dma(lambda e: e.dma_start(out=hio[:], in_=self.state_conv[:, :]), w=[b_hio])
                    for c in range(DC):
                        pb = 4 + (c % 2)
                        self.op("pe", lambda e: e.transpose(out=self.psb[pb][:, 0:30], in_=hio[:, c * 128:(c + 1) * 128],
                                                            identity=self.ident[0:30, 0:30]),
                                r=[b_hio, self.b_const], w=[self.bps[pb]])
                        self.op("dve", lambda e: e.tensor_copy(out=ubuf[:, c, 0:30], in_=self.psb[pb][:, 0:30]),
                                r=[self.bps[pb]], w=[b_ub])

                def ep1(p, psA, bA, psB, bB):
                    k = p % 2
                    self.op("act", lambda e: e.activation(out=sig[k][:, :W], in_=psB[:, :W], func=AF.Sigmoid),
                            r=[bB], w=[b_sig[k]])
                    self.op("dve", lambda e: e.tensor_tensor(out=ubuf[:, p, 30:30 + W], in0=psA[:, :W],
                                                             in1=sig[k][:, :W], op=ALU.mult),
                            r=[bA, b_sig[k]], w=[b_ub])

                self.gemm(lambda p, k0, kn, h: w1[:, k0:k0 + kn, h, p * 128:(p + 1) * 128], DC, DC,
                          self.xn, self.b_xn, W, ep1, wkey='pw1', first=(ti == 0))
                for c in range(DC):
                    self.op("act", lambda e: e.activation(out=hc[:, c, :W], in_=ubuf[:, c, 30:30 + W], func=AF.Identity,
                                                          bias=prm[:, 0, c:c + 1], scale=dwT[:, 30, c:c + 1]),
                            r=[b_ub, b_dw], w=[b_hc])
                    for w in range(20):
                        self.op("dve", lambda e: e.scalar_tensor_tensor(out=hc[:, c, :W], in0=ubuf[:, c, w:w + W],
                                                                        scalar=dwT[:, w, c:c + 1], in1=hc[:, c, :W],
                                                                        op0=ALU.mult, op1=ALU.add),
                                r=[b_ub, b_dw, b_hc], w=[b_hc])
                    self.op("act", lambda e: e.activation(out=cacc[:, :W], in_=ubuf[:, c, 20:20 + W], func=AF.Copy,
                                                          scale=dwT[:, 20, c:c + 1]), r=[b_ub, b_dw], w=[b_cacc])
                    for w in range(21, 30):
                        kx = w % 2
                        self.op("act", lambda e: e.activation(out=ctmp[kx][:, :W], in_=ubuf[:, c, w:w + W], func=AF.Copy,
                                                              scale=dwT[:, w, c:c + 1]), r=[b_ub, b_dw], w=[b_ctmp[kx]])
                        self.op("pool", lambda e: e.tensor_tensor(out=cacc[:, :W], in0=cacc[:, :W], in1=ctmp[kx][:, :W], op=ALU.add),
                                r=[b_ctmp[kx], b_cacc], w=[b_cacc])
                    self.op("pool", lambda e: e.tensor_tensor(out=hc[:, c, :W], in0=hc[:, c, :W], in1=cacc[:, :W], op=ALU.add),
                            r=[b_cacc, b_hc], w=[b_hc])
                if ti == self.NT - 1 or samp:
                    c0 = 1 if samp else W
                    for c in range(DC):
                        pb = 4 + (c % 2)
                        self.op("pe", lambda e: e.transpose(out=self.psb[pb][0:30, 0:128], in_=ubuf[:, c, c0:c0 + 30],
                                                            identity=self.ident[:]),
                                r=[b_ub, self.b_const], w=[self.bps[pb]])
                        self.op("act", lambda e: e.copy(out=hio[:, c * 128:(c + 1) * 128], in_=self.psb[pb][0:30, 0:128]),
                                r=[self.bps[pb]], w=[b_hio])
                    dst = self.conv_s if samp else self.conv_p
                    self.dma(lambda e: e.dma_start(out=dst[:, :], in_=hio[:]), r=[b_hio])
                if not samp:
                    self.op("pool", lambda e: e.tensor_copy(out=ubuf[:, :, 0:30], in_=ubuf[:, :, W:W + 30]),
                            r=[b_ub], w=[b_ub])
                self.op("act", lambda e: e.activation(out=self.xn[:, :, :W], in_=hc[:, :, :W], func=AF.Square),
                        r=[b_hc], w=[self.b_xn])
                for c in range(DC):
                    self.op("pe", lambda e: e.matmul(self.psb[4][:, :W], lhsT=self.ones_f[:], rhs=hc[:, c, :W],
                                                     start=(c == 0), stop=(c == DC - 1)),
                            r=[b_hc, self.b_const], w=[self.bps[4]])
                for c in range(DC):
                    self.op("pe", lambda e: e.matmul(self.psb[5][:, :W], lhsT=self.ones_b[:], rhs=self.xn[:, c, :W],
                                                     start=(c == 0), stop=(c == DC - 1)),
                            r=[self.b_xn, self.b_const], w=[self.bps[5]])
                self.op("act", lambda e: e.activation(out=mean[:, :W], in_=self.psb[4][:, :W], func=AF.Copy,
                                                      scale=1.0 / D), r=[self.bps[4]], w=[b_mean])
                self.op("dve", lambda e: e.tensor_tensor(out=msq[:, :W], in0=mean[:, :W], in1=mean[:, :W], op=ALU.mult),
                        r=[b_mean], w=[b_msq])
                self.op("dve", lambda e: e.scalar_tensor_tensor(out=msq[:, :W], in0=self.psb[5][:, :W], scalar=1.0 / D,
                                                                in1=msq[:, :W], op0=ALU.mult, op1=ALU.subtract),
                        r=[self.bps[5], b_msq], w=[b_msq])
                self.op("act", lambda e: e.activation(out=self.rstd[:, :W], in_=msq[:, :W], func=AF.Sqrt,
                                                      bias=eps_ln[:, 0:1], scale=1.0), r=[b_msq, b_dw], w=[self.b_rstd])
                self.op("dve", lambda e: e.reciprocal(out=self.rstd[:, :W], in_=self.rstd[:, :W]),
                        r=[self.b_rstd], w=[self.b_rstd])
                for c in range(DC):
                    self.op("pool", lambda e: e.tensor_tensor(out=hc[:, c, :W], in0=hc[:, c, :W], in1=mean[:, :W],
                                                              op=ALU.subtract), r=[b_hc, b_mean], w=[b_hc])
                    self.op("dve", lambda e: e.tensor_tensor(out=hc[:, c, :W], in0=hc[:, c, :W], in1=self.rstd[:, :W],
                                                             op=ALU.mult), r=[b_hc, self.b_rstd], w=[b_hc])
                    self.op("act", lambda e: e.activation(out=self.xn[:, c, :W], in_=hc[:, c, :W], func=AF.Silu,
                                                          bias=prm[:, 2, c:c + 1], scale=prm[:, 1, c:c + 1]),
                            r=[b_hc, b_dw], w=[self.b_xn])

                def ep2(p, psA, bA, psB, bB):
                    self.op("act", lambda e: e.copy(out=self.yy[:, 2 * p, :W], in_=psA[:, :W]), r=[bA], w=[self.b_yy])
                    self.op("dve", lambda e: e.tensor_copy(out=self.yy[:, 2 * p + 1, :W], in_=psB[:, :W]), r=[bB],
                            w=[self.b_yy])

                self.gemm(lambda p, k0, kn, h: w2[:, k0:k0 + kn, p, h, :], DC, DC // 2, self.xn, self.b_xn, W, ep2, wkey='pw2', first=(ti == 0))
                self.post_norm_residual(ti, li * 4 + 1)
            self.tk.barrier()

    def nat_to_scan(self, src_dram_flat, dst, b_dst, tmp64, b_tmp):
        self.dma(lambda e: e.dma_start(out=tmp64[:], in_=src_dram_flat.rearrange("(s g2) p -> s (g2 p)", g2=2)),
                 w=[b_tmp])
        self.op("pe", lambda e: e.transpose(out=self.psb[5][:, 0:64], in_=tmp64[:], identity=self.ident[0:64, 0:64]),
                r=[b_tmp, self.b_const], w=[self.bps[5]])
        self.op("dve", lambda e: e.tensor_copy(out=dst[:], in_=self.psb[5][:, 0:64]), r=[self.bps[5]], w=[b_dst])

    def s5_layer(self, li):
        TWO_PI = 2.0 * math.pi
        with contextlib.ExitStack() as st:
            self.alloc_rowlocal(st)
            rho = self.sb("rho", [128, 64], F32, st)
            c1 = self.sb("c1", [128, 64], F32, st)
            s1 = self.sb("s1", [128, 64], F32, st)
            hpr = self.sb("hpr", [128, 64], F32, st)
            hpi = self.sb("hpi", [128, 64], F32, st)
            k0r = self.sb("k0r", [128, 64], F32, st)
            k0i = self.sb("k0i", [128, 64], F32, st)
            ktm = self.sb("ktm", [128, 64], F32, st)
            dsk = self.sb("dsk", [128, DC], F32, st)
            b_prm = Buf("s5prm")
            b_hp = Buf("hp")
            b_k0 = Buf("k0")
            tmp64 = self.sb("tmp64", [64, 128], F32, st)
            b_t64 = Buf("t64")
            BBs = self.dscr("BBs", [2, 128, 16, 64])
            CCs = self.dscr("CCs", [2, 128, 64, 16])
            PRs = self.dscr("PRs", [3, 128, 64])
            WBs = self.dscr("WBs", [2, 16, 128, 512], BF16)
            WCs = self.dscr("WCs", [2, 16, 128, 512], BF16)
            TAB = self.dscr("TAB", [2, 64, 128, 512])
            b_dr = Buf("s5dram")
            self.dma(lambda e: e.dma_start(out=dsk[:], in_=self.ssm_d.rearrange("o (c p) -> p (o c)", p=128),
                                           allow_slow_non_contiguous=True), w=[b_prm])
            with contextlib.ExitStack() as st2:
                def T2(name, shape, dt=F32):
                    return self.sb(name, shape, dt, st2)
                ar = T2("p_ar", [128, 64]); ai = T2("p_ai", [128, 64]); ldt = T2("p_ldt", [128, 1])
                dt_ = T2("p_dt", [128, 1]); mag = T2("p_mag", [128, 64]); th = T2("p_th", [128, 64])
                r0 = T2("p_r0", [128, 64]); ri_ = T2("p_ri", [128, 64], I32); rf = T2("p_rf", [128, 64])
                m1 = T2("p_m1", [128, 64]); m2 = T2("p_m2", [128, 64])
                cs = T2("p_cs", [128, 64]); sn = T2("p_sn", [128, 64])
                abr = T2("p_abr", [128, 64]); abi = T2("p_abi", [128, 64]); den = T2("p_den", [128, 64])
                cfr = T2("p_cfr", [128, 64]); cfi = T2("p_cfi", [128, 64])
                bre = T2("p_bre", [128, 64, 16]); bim = T2("p_bim", [128, 64, 16])
                bt1 = T2("p_bt1", [128, 64, 16]); bt2 = T2("p_bt2", [128, 64, 16])
                bbT = T2("p_bbT", [128, 16, 64])
                cin = T2("p_cin", [128, 16, 64]); ccT = T2("p_ccT", [128, 64, 16])
                bp = Buf("prep")
                V = "dve"
                self.dma(lambda e: e.dma_start(out=ar[:], in_=self.ssm_a_re[:, :]), w=[bp])
                self.dma(lambda e: e.dma_start(out=ai[:], in_=self.ssm_a_im[:, :]), w=[bp])
                self.dma(lambda e: e.dma_start(out=ldt[:], in_=self.ssm_log_dt.rearrange("o g -> g o"),
                                               allow_slow_non_contiguous=True), w=[bp])
                self.dma(lambda e: e.dma_start(out=bre[:], in_=self.ssm_b_re[:, :, :]), w=[bp])
                self.dma(lambda e: e.dma_start(out=bim[:], in_=self.ssm_b_im[:, :, :]), w=[bp])
                o = lambda q, f: self.op(q, f, r=[bp], w=[bp])
                o("act", lambda e: e.activation(out=dt_[:], in_=ldt[:], func=AF.Exp))
                o(V, lambda e: e.tensor_scalar(out=mag[:], in0=ar[:], scalar1=dt_[:, 0:1], scalar2=None, op0=ALU.mult))
                o("act", lambda e: e.activation(out=mag[:], in_=mag[:], func=AF.Exp))
                o(V, lambda e: e.tensor_scalar(out=th[:], in0=ai[:], scalar1=dt_[:, 0:1], scalar2=1.0 / TWO_PI,
                                               op0=ALU.mult, op1=ALU.mult))

                def sin_turns(dst, off):
                    o(V, lambda e: e.tensor_scalar(out=r0[:], in0=th[:], scalar1=off, scalar2=None, op0=ALU.add))
                    o(V, lambda e: e.tensor_copy(out=ri_[:], in_=r0[:]))
                    o(V, lambda e: e.tensor_copy(out=rf[:], in_=ri_[:]))
                    o(V, lambda e: e.tensor_tensor(out=r0[:], in0=r0[:], in1=rf[:], op=ALU.subtract))
                    o(V, lambda e: e.tensor_scalar(out=m1[:], in0=r0[:], scalar1=0.5, scalar2=None, op0=ALU.is_gt))
                    o(V, lambda e: e.tensor_scalar(out=m2[:], in0=r0[:], scalar1=-0.5, scalar2=None, op0=ALU.is_lt))
                    o(V, lambda e: e.tensor_tensor(out=r0[:], in0=r0[:], in1=m1[:], op=ALU.subtract))
                    o(V, lambda e: e.tensor_tensor(out=r0[:], in0=r0[:], in1=m2[:], op=ALU.add))
                    o("act", lambda e: e.activation(out=dst[:], in_=r0[:], func=AF.Sin, scale=TWO_PI))

                sin_turns(sn, 0.0)
                sin_turns(cs, 0.25)
                o(V, lambda e: e.tensor_tensor(out=abr[:], in0=mag[:], in1=cs[:], op=ALU.mult))
                o(V, lambda e: e.tensor_tensor(out=abi[:], in0=mag[:], in1=sn[:], op=ALU.mult))
                o(V, lambda e: e.tensor_tensor(out=den[:], in0=ar[:], in1=ar[:], op=ALU.mult))
                o(V, lambda e: e.tensor_tensor(out=m1[:], in0=ai[:], in1=ai[:], op=ALU.mult))
                o(V, lambda e: e.tensor_tensor(out=den[:], in0=den[:], in1=m1[:], op=ALU.add))
                o(V, lambda e: e.reciprocal(out=den[:], in_=den[:]))
                o(V, lambda e: e.tensor_scalar(out=m2[:], in0=abr[:], scalar1=-1.0, scalar2=None, op0=ALU.add))
                o(V, lambda e: e.tensor_tensor(out=cfr[:], in0=m2[:], in1=ar[:], op=ALU.mult))
                o(V, lambda e: e.tensor_tensor(out=m1[:], in0=abi[:], in1=ai[:], op=ALU.mult))
                o(V, lambda e: e.tensor_tensor(out=cfr[:], in0=cfr[:], in1=m1[:], op=ALU.add))
                o(V, lambda e: e.tensor_tensor(out=cfr[:], in0=cfr[:], in1=den[:], op=ALU.mult))
                o(V, lambda e: e.tensor_tensor(out=cfi[:], in0=abi[:], in1=ar[:], op=ALU.mult))
                o(V, lambda e: e.tensor_tensor(out=m1[:], in0=m2[:], in1=ai[:], op=ALU.mult))
                o(V, lambda e: e.tensor_tensor(out=cfi[:], in0=cfi[:], in1=m1[:], op=ALU.subtract))
                o(V, lambda e: e.tensor_tensor(out=cfi[:], in0=cfi[:], in1=den[:], op=ALU.mult))
                for k, src in enumerate((mag, cs, sn)):
                    self.dma(lambda e: e.dma_start(out=PRs[k], in_=src[:]), r=[bp], w=[b_dr])
                self.tk.barrier()
                for k, dst in enumerate((rho, c1, s1)):
                    self.nat_to_scan(PRs[k], dst, b_prm, tmp64, b_t64)
                cfr_b = cfr[:].unsqueeze(2).to_broadcast([128, 64, 16])
                cfi_b = cfi[:].unsqueeze(2).to_broadcast([128, 64, 16])
                for rix in range(2):
                    if rix == 0:
                        o(V, lambda e: e.tensor_tensor(out=bt1[:], in0=bre[:], in1=cfr_b, op=ALU.mult))
                        o(V, lambda e: e.tensor_tensor(out=bt2[:], in0=bim[:], in1=cfi_b, op=ALU.mult))
                        o(V, lambda e: e.tensor_tensor(out=bbT[:].rearrange("g c p -> g p c"), in0=bt1[:], in1=bt2[:],
                                                       op=ALU.subtract))
                    else:
                        o(V, lambda e: e.tensor_tensor(out=bt1[:], in0=bim[:], in1=cfr_b, op=ALU.mult))
                        o(V, lambda e: e.tensor_tensor(out=bt2[:], in0=bre[:], in1=cfi_b, op=ALU.mult))
                        o(V, lambda e: e.tensor_tensor(out=bbT[:].rearrange("g c p -> g p c"), in0=bt1[:], in1=bt2[:],
                                                       op=ALU.add))
                    self.dma(lambda e: e.dma_start(out=BBs[rix], in_=bbT[:]), r=[bp], w=[b_dr])
                    self.tk.barrier()
                for rix, src in enumerate((self.ssm_c_re, self.ssm_c_im)):
                    self.dma(lambda e: e.dma_start(out=cin[:], in_=src[:, :, :]), w=[bp])
                    o("act", lambda e: e.activation(out=ccT[:].rearrange("g p c -> g c p"), in_=cin[:], func=AF.Copy,
                                                    scale=(1.0 if rix == 0 else -1.0)))
                    self.dma(lambda e: e.dma_start(out=CCs[rix], in_=ccT[:]), r=[bp], w=[b_dr])
                    self.tk.barrier()
            self.tk.barrier()
            with contextlib.ExitStack() as st2:
                def T2(name, shape, dt=F32):
                    return self.sb(name, shape, dt, st2)
                bp = Buf("prep2")
                o = lambda q, f: self.op(q, f, r=[bp], w=[bp])
                V = "dve"
                bbl = T2("q_bbl", [128, 16, 64]); ccl = T2("q_ccl", [128, 64, 16])
                mkB = T2("q_mkB", [128, 8]); mkC = T2("q_mkC", [128, 4, 8]); mt = T2("q_mt", [128, 4, 8])
                wbt = T2("q_wbt", [128, 8, 64], BF16); wct = T2("q_wct", [128, 4, 8, 16], BF16)
                b_wt = Buf("wt")
                o("pool", lambda e: e.iota(mkB[:], pattern=[[-16, 8]], base=0, channel_multiplier=1,
                                           allow_small_or_imprecise_dtypes=True))
                o(V, lambda e: e.tensor_scalar(out=mt[:, 0, :], in0=mkB[:], scalar1=0.0, scalar2=None, op0=ALU.is_ge))
                o(V, lambda e: e.tensor_scalar(out=mkB[:], in0=mkB[:], scalar1=15.0, scalar2=None, op0=ALU.is_le))
                o(V, lambda e: e.tensor_tensor(out=mkB[:], in0=mkB[:], in1=mt[:, 0, :], op=ALU.mult))
                o("pool", lambda e: e.iota(mkC[:], pattern=[[-128, 4], [64, 8]], base=0, channel_multiplier=-1,
                                           allow_small_or_imprecise_dtypes=True))
                o(V, lambda e: e.tensor_scalar(out=mt[:], in0=mkC[:], scalar1=-63.0, scalar2=None, op0=ALU.is_ge))
                o(V, lambda e: e.tensor_scalar(out=mkC[:], in0=mkC[:], scalar1=0.0, scalar2=None, op0=ALU.is_le))
                o(V, lambda e: e.tensor_tensor(out=mkC[:], in0=mkC[:], in1=mt[:], op=ALU.mult))
                for rix in range(2):
                    self.dma(lambda e: e.dma_start(out=bbl[:], in_=BBs[rix].rearrange("(k g8) c p -> (g8 c) k p", g8=8)),
                             r=[b_dr], w=[bp])
                    self.dma(lambda e: e.dma_start(out=ccl[:], in_=CCs[rix].rearrange("(s g2) p c -> (g2 p) s c", g2=2)),
                             r=[b_dr], w=[bp])
                    for k in range(16):
                        self.op(V, lambda e: e.tensor_tensor(out=wbt[:], in0=bbl[:, k, :].unsqueeze(1).to_broadcast([128, 8, 64]),
                                                             in1=mkB[:].unsqueeze(2).to_broadcast([128, 8, 64]), op=ALU.mult),
                                r=[bp], w=[b_wt])
                        self.dma(lambda e: e.dma_start(out=WBs[rix, k], in_=wbt[:].rearrange("p a b -> p (a b)")),
                                 r=[b_wt], w=[b_dr])
                        for j in range(4):
                            sidx = 4 * k + j
                            self.op(V, lambda e: e.tensor_tensor(out=wct[:, j], in0=ccl[:, sidx, :].unsqueeze(1).to_broadcast([128, 8, 16]),
                                                                 in1=mkC[:, j, :].unsqueeze(2).to_broadcast([128, 8, 16]), op=ALU.mult),
                                    r=[bp], w=[b_wt])
                        self.dma(lambda e: e.dma_start(out=WCs[rix, k], in_=wct[:].rearrange("p j a b -> p (j a b)")),
                                 r=[b_wt], w=[b_dr])
                tc_ = T2("q_tc", [128, 8, 512]); ts_ = T2("q_ts", [128, 8, 512])
                a1 = T2("q_a1", [128, 8, 256]); a2 = T2("q_a2", [128, 8, 256])
                pc = T2("q_pc", [128, 8]); psn = T2("q_ps", [128, 8]); pt1 = T2("q_pt1", [128, 8]); pt2 = T2("q_pt2", [128, 8])
                b_tb = Buf("tb")
                ot = lambda q, f: self.op(q, f, r=[b_tb, b_prm], w=[b_tb])
                for sg in range(8):
                    ot(V, lambda e: e.memset(tc_[:, :, 0:1], 1.0))
                    ot(V, lambda e: e.memset(ts_[:, :, 0:1], 0.0))
                    ot(V, lambda e: e.tensor_copy(out=pc[:], in_=c1[:, sg * 8:(sg + 1) * 8]))
                    ot(V, lambda e: e.tensor_copy(out=psn[:], in_=s1[:, sg * 8:(sg + 1) * 8]))
                    m = 1
                    while m < 512:
                        pcb = pc[:].unsqueeze(2).to_broadcast([128, 8, m])
                        psb_ = psn[:].unsqueeze(2).to_broadcast([128, 8, m])
                        ot(V, lambda e: e.tensor_tensor(out=a1[:, :, :m], in0=tc_[:, :, 0:m], in1=pcb, op=ALU.mult))
                        ot(V, lambda e: e.tensor_tensor(out=a2[:, :, :m], in0=ts_[:, :, 0:m], in1=psb_, op=ALU.mult))
                        ot(V, lambda e: e.tensor_tensor(out=tc_[:, :, m:2 * m], in0=a1[:, :, :m], in1=a2[:, :, :m], op=ALU.subtract))
                        ot(V, lambda e: e.tensor_tensor(out=a1[:, :, :m], in0=tc_[:, :, 0:m], in1=psb_, op=ALU.mult))
                        ot(V, lambda e: e.tensor_tensor(out=a2[:, :, :m], in0=ts_[:, :, 0:m], in1=pcb, op=ALU.mult))
                        ot(V, lambda e: e.tensor_tensor(out=ts_[:, :, m:2 * m], in0=a1[:, :, :m], in1=a2[:, :, :m], op=ALU.add))
                        ot(V, lambda e: e.tensor_tensor(out=pt1[:], in0=pc[:], in1=pc[:], op=ALU.mult))
                        ot(V, lambda e: e.tensor_tensor(out=pt2[:], in0=psn[:], in1=psn[:], op=ALU.mult))
                        ot(V, lambda e: e.tensor_tensor(out=psn[:], in0=psn[:], in1=pc[:], op=ALU.mult))
                        ot(V, lambda e: e.tensor_scalar(out=psn[:], in0=psn[:], scalar1=2.0, scalar2=None, op0=ALU.mult))
                        ot(V, lambda e: e.tensor_tensor(out=pc[:], in0=pt1[:], in1=pt2[:], op=ALU.subtract))
                        m *= 2
                    self.dma(lambda e: e.dma_start(out=TAB[0, sg * 8:(sg + 1) * 8].rearrange("s p t -> p s t"), in_=tc_[:]),
                             r=[b_tb], w=[b_dr])
                    self.dma(lambda e: e.dma_start(out=TAB[1, sg * 8:(sg + 1) * 8].rearrange("s p t -> p s t"), in_=ts_[:]),
                             r=[b_tb], w=[b_dr], q="act")
                self.tk.barrier()
            self.tk.barrier()
            NB = 2
            NBT = 3
            tabc = [self.sb("tabc%d" % i, [128, 512], F32, st) for i in range(NBT)]
            tabs = [self.sb("tabs%d" % i, [128, 512], F32, st) for i in range(NBT)]
            b_tab = [Buf("tab") for _ in range(NBT)]
            bsr = [self.sb("bsr%d" % i, [128, 512], F32, st) for i in range(NB)]
            bsi = [self.sb("bsi%d" % i, [128, 512], F32, st) for i in range(NB)]
            b_bs = [Buf("bs") for _ in range(NB)]
            qa = [self.sb("sqa%d" % i, [128, 512], F32, st) for i in range(4)]
            qb = [self.sb("sqb%d" % i, [128, 512], F32, st) for i in range(4)]
            b_qa = [Buf("qa") for _ in range(4)]; b_qb = [Buf("qb") for _ in range(4)]
            vre = [self.sb("vre%d" % i, [128, 512], F32, st) for i in range(NB)]
            vim = [self.sb("vim%d" % i, [128, 512], F32, st) for i in range(NB)]
            b_vr = [Buf("vr") for _ in range(NB)]; b_vi = [Buf("vi") for _ in range(NB)]
            kre = self.sb("kre", [128, 512], F32, st); kim = self.sb("kim", [128, 512], F32, st)
            hre = self.sb("hre", [128, 512], F32, st); him = self.sb("him", [128, 512], F32, st)
            b_kr = Buf("kr"); b_ki = Buf("ki"); b_h = Buf("h"); b_h2 = Buf("h2")
            hrb = [self.sb("hrb%d" % i, [128, 512], BF16, st) for i in range(NB)]
            hib = [self.sb("hib%d" % i, [128, 512], BF16, st) for i in range(NB)]
            b_hb = [Buf("hb") for _ in range(NB)]
            wBk = [self.sb("wBk%d" % i, [128, 2, 512], BF16, st) for i in range(NB)]
            wCk = [self.sb("wCk%d" % i, [128, 2, 512], BF16, st) for i in range(NB)]
            b_wk = [Buf("wk") for _ in range(NB)]
            yb = self.sb("yb", [128, DC, 512], BF16, st)
            b_yb = Buf("yb")
            sig = [self.sb("s5sig%d" % i, [128, 512], F32, st) for i in range(2)]
            b_sig = [Buf("sig") for _ in range(2)]
            hfo = self.sb("hfo", [64, 128], F32, st)
            b_hfo = Buf("hfo")
            xnf = self.yy
            b_xnf = self.b_yy
            wg = self.ssm_w_glu.rearrange("(kc p) (h f) -> p kc h f", p=128, h=2)
            self.op("dve", lambda e: e.memset(hpr[:], 0.0), w=[b_hp])
            self.op("dve", lambda e: e.memset(hpi[:], 0.0), w=[b_hp])
            sctr = 0
            for ti in range(len(self.tiles)):
                t0, W = self.tiles[ti]
                samp = (ti == len(self.tiles) - 1)
                if samp:
                    self.nat_to_scan(self.state_ssm_re, hpr, b_hp, tmp64, b_t64)
                    self.nat_to_scan(self.state_ssm_im, hpi, b_hp, tmp64, b_t64)
                self.pre_norm(ti, li * 4 + 0, xn_f32=xnf, b_xnf=b_xnf)
                ok = lambda f: self.op("dve", f, r=[b_hp, b_prm, b_k0], w=[b_k0])
                ok(lambda e: e.tensor_tensor(out=k0r[:], in0=c1[:], in1=hpr[:], op=ALU.mult))
                ok(lambda e: e.tensor_tensor(out=ktm[:], in0=s1[:], in1=hpi[:], op=ALU.mult))
                ok(lambda e: e.tensor_tensor(out=k0r[:], in0=k0r[:], in1=ktm[:], op=ALU.subtract))
                ok(lambda e: e.tensor_tensor(out=k0i[:], in0=s1[:], in1=hpr[:], op=ALU.mult))
                ok(lambda e: e.tensor_tensor(out=ktm[:], in0=c1[:], in1=hpi[:], op=ALU.mult))
                ok(lambda e: e.tensor_tensor(out=k0i[:], in0=k0i[:], in1=ktm[:], op=ALU.add))
                lastc = 0 if samp else W - 1

                def stageA(sidx):
                    k, j = sidx // 4, sidx % 4
                    kb = k % NB
                    i2 = sidx % NB
                    if j == 0:
                        self.dma(lambda e: e.dma_start(out=wBk[kb][:], in_=WBs[:, k].rearrange("r p n -> p r n")),
                                 r=[b_dr], w=[b_wk[kb]])
                        self.dma(lambda e: e.dma_start(out=wCk[kb][:], in_=WCs[:, k].rearrange("r p n -> p r n")),
                                 r=[b_dr], w=[b_wk[kb]])
                    i4 = sidx % NBT
                    self.dma(lambda e: e.dma_start(out=tabc[i4][:, :W], in_=TAB[0, sidx, :, 0:W]), r=[b_dr], w=[b_tab[i4]])
                    self.dma(lambda e: e.dma_start(out=tabs[i4][:, :W], in_=TAB[1, sidx, :, 0:W]), r=[b_dr], w=[b_tab[i4]])
                    pr, pi_ = (i2 * 2, i2 * 2 + 1)
                    self.op("pe", lambda e: e.matmul(self.psb[pr][:, :W], lhsT=wBk[kb][:, 0, j * 128:(j + 1) * 128],
                                                     rhs=self.xn[:, k, :W], start=True, stop=True),
                            r=[b_wk[kb], self.b_xn], w=[self.bps[pr]])
                    self.op("pe", lambda e: e.matmul(self.psb[pi_][:, :W], lhsT=wBk[kb][:, 1, j * 128:(j + 1) * 128],
                                                     rhs=self.xn[:, k, :W], start=True, stop=True),
                            r=[b_wk[kb], self.b_xn], w=[self.bps[pi_]])
                    self.op("act", lambda e: e.copy(out=bsr[i2][:, :W], in_=self.psb[pr][:, :W]), r=[self.bps[pr]], w=[b_bs[i2]])
                    self.op("act", lambda e: e.copy(out=bsi[i2][:, :W], in_=self.psb[pi_][:, :W]), r=[self.bps[pi_]], w=[b_bs[i2]])
                    TCt, TSt = tabc[i4], tabs[i4]
                    rd = [b_bs[i2], b_tab[i4]]
                    self.op("pool", lambda e: e.tensor_tensor(out=qa[0][:, :W], in0=TSt[:, :W], in1=bsi[i2][:, :W], op=ALU.mult), r=rd, w=[b_qa[0]])
                    self.op("pool", lambda e: e.tensor_tensor(out=qa[1][:, :W], in0=TCt[:, :W], in1=bsr[i2][:, :W], op=ALU.mult), r=rd, w=[b_qa[1]])
                    self.op("pool", lambda e: e.tensor_tensor(out=vre[i2][:, :W], in0=qa[1][:, :W], in1=qa[0][:, :W], op=ALU.add), r=[b_qa[0], b_qa[1]], w=[b_vr[i2]])
                    self.op("pool", lambda e: e.tensor_tensor(out=qa[2][:, :W], in0=TSt[:, :W], in1=bsr[i2][:, :W], op=ALU.mult), r=rd, w=[b_qa[2]])
                    self.op("pool", lambda e: e.tensor_tensor(out=qa[3][:, :W], in0=TCt[:, :W], in1=bsi[i2][:, :W], op=ALU.mult), r=rd, w=[b_qa[3]])

                def stageA2(sidx):
                    i2 = sidx % NB
                    self.op("dve", lambda e: e.tensor_tensor(out=vim[i2][:, :W], in0=qa[3][:, :W], in1=qa[2][:, :W], op=ALU.subtract), r=[b_qa[2], b_qa[3]], w=[b_vi[i2]])

                def stageB(sidx):
                    k, j = sidx // 4, sidx % 4
                    kb = k % NB
                    i2 = sidx % NB
                    ybank = 6 + (k % 2)
                    i4 = sidx % NBT
                    TCt, TSt = tabc[i4], tabs[i4]
                    rb = rho[:, sidx:sidx + 1].to_broadcast([128, W])
                    self.op("dve", lambda e: e.tensor_tensor_scan(out=kre[:, :W], data0=rb, data1=vre[i2][:, :W],
                                                                  initial=k0r[:, sidx:sidx + 1], op0=ALU.mult, op1=ALU.add),
                            r=[b_vr[i2], b_k0, b_prm], w=[b_kr])
                    self.op("dve", lambda e: e.tensor_tensor_scan(out=kim[:, :W], data0=rb, data1=vim[i2][:, :W],
                                                                  initial=k0i[:, sidx:sidx + 1], op0=ALU.mult, op1=ALU.add),
                            r=[b_vi[i2], b_k0, b_prm], w=[b_ki])
                    rk = [b_kr, b_ki, b_tab[i4]]
                    self.op("dve", lambda e: e.tensor_tensor(out=qb[0][:, :W], in0=TSt[:, :W], in1=kim[:, :W], op=ALU.mult), r=rk, w=[b_qb[0]])
                    self.op("dve", lambda e: e.tensor_tensor(out=qb[1][:, :W], in0=TCt[:, :W], in1=kre[:, :W], op=ALU.mult), r=rk, w=[b_qb[1]])
                    self.op("dve", lambda e: e.tensor_tensor(out=hre[:, :W], in0=qb[1][:, :W], in1=qb[0][:, :W], op=ALU.subtract), r=[b_qb[0], b_qb[1]], w=[b_h])
                    self.op("dve", lambda e: e.tensor_tensor(out=qb[2][:, :W], in0=TSt[:, :W], in1=kre[:, :W], op=ALU.mult), r=rk, w=[b_qb[2]])
                    self.op("dve", lambda e: e.tensor_tensor(out=qb[3][:, :W], in0=TCt[:, :W], in1=kim[:, :W], op=ALU.mult), r=rk, w=[b_qb[3]])
                    self.op("dve", lambda e: e.tensor_tensor(out=him[:, :W], in0=qb[2][:, :W], in1=qb[3][:, :W], op=ALU.add), r=[b_qb[2], b_qb[3]], w=[b_h2])
                    self.op("act", lambda e: e.copy(out=hrb[i2][:, :W], in_=hre[:, :W]), r=[b_h], w=[b_hb[i2]])
                    self.op("act", lambda e: e.copy(out=hib[i2][:, :W], in_=him[:, :W]), r=[b_h2], w=[b_hb[i2]])
                    self.op("act", lambda e: e.copy(out=hpr[:, sidx:sidx + 1], in_=hre[:, lastc:lastc + 1]), r=[b_h, b_k0], w=[b_hp])
                    self.op("act", lambda e: e.copy(out=hpi[:, sidx:sidx + 1], in_=him[:, lastc:lastc + 1]), r=[b_h2, b_k0], w=[b_hp])
                    self.op("pe", lambda e: e.matmul(self.psb[ybank][:, :W], lhsT=wCk[kb][:, 0, j * 128:(j + 1) * 128],
                                                     rhs=hrb[i2][:, :W], start=(j == 0), stop=False),
                            r=[b_wk[kb], b_hb[i2]], w=[self.bps[ybank]])
                    self.op("pe", lambda e: e.matmul(self.psb[ybank][:, :W], lhsT=wCk[kb][:, 1, j * 128:(j + 1) * 128],
                                                     rhs=hib[i2][:, :W], start=False, stop=(j == 3)),
                            r=[b_wk[kb], b_hb[i2]], w=[self.bps[ybank]])
                    if j == 3:
                        self.op("dve", lambda e: e.scalar_tensor_tensor(out=yb[:, k, :W], in0=xnf[:, k, :W], scalar=dsk[:, k:k + 1],
                                                                        in1=self.psb[ybank][:, :W], op0=ALU.mult, op1=ALU.add),
                                r=[b_xnf, b_prm, self.bps[ybank]], w=[b_yb])

                stageA(0)
                stageA2(0)
                for sidx in range(64):
                    if sidx + 1 < 64:
                        stageA(sidx + 1)
                    stageB(sidx)
                    if sidx + 1 < 64:
                        stageA2(sidx + 1)

                def epg(p, psA, bA, psB, bB):
                    kk = p % 2
                    self.op("act", lambda e: e.activation(out=sig[kk][:, :W], in_=psB[:, :W], func=AF.Sigmoid),
                            r=[bB], w=[b_sig[kk]])
                    self.op("dve", lambda e: e.tensor_tensor(out=self.yy[:, p, :W], in0=psA[:, :W], in1=sig[kk][:, :W],
                                                             op=ALU.mult), r=[bA, b_sig[kk]], w=[self.b_yy])

                self.gemm(lambda p, k0, kn, h: wg[:, k0:k0 + kn, h, p * 128:(p + 1) * 128], DC, DC, yb, b_yb, W, epg, wkey='glu', first=(ti == 0))
                self.post_norm_residual(ti, li * 4 + 1)
                if ti == self.NT - 1 or samp:
                    for rix, (src, dstp, dsts) in enumerate(((hpr, self.ssm_re_p, self.ssm_re_s), (hpi, self.ssm_im_p, self.ssm_im_s))):
                        dst = dsts if samp else dstp
                        self.op("pe", lambda e: e.transpose(out=self.psb[5][0:64, 0:128], in_=src[:], identity=self.ident[:]),
                                r=[b_hp, self.b_const], w=[self.bps[5]])
                        self.op("dve", lambda e: e.tensor_copy(out=hfo[:], in_=self.psb[5][0:64, 0:128]), r=[self.bps[5]], w=[b_hfo])
                        self.dma(lambda e: e.dma_start(out=dst.rearrange("(s g2) p -> s (g2 p)", g2=2), in_=hfo[:]), r=[b_hfo])
            self.tk.barrier()

    def nsa_declare(self):
        d = self.din
        o = self.dout
        T = self.T
        for k, shp in STRUCT_SHAPES.items():
            setattr(self, "c_" + k, d(k, shp))
        self.rel_bias = d("rel_bias", [32, 16])
        self.page_table = d("page_table", [1, 128], I32)
        self.nsa_w_q = d("nsa_w_q", [2, D, D])
        self.nsa_w_kv = d("nsa_w_kv", [2, D, 3072])
        self.nsa_cmp_pe = d("nsa_cmp_pe", [2, 32, 2, 128])
        self.nsa_cmp_w1 = d("nsa_cmp_w1", [2, 2, 32, 128, 128])
        self.nsa_cmp_w2 = d("nsa_cmp_w2", [2, 2, 128, 128])
        self.nsa_w_gate = d("nsa_w_gate", [2, D, 48])
        self.nsa_w_o = d("nsa_w_o", [2, D, D])
        self.cache_cmp = [d("cache_cmp%d" % i, [self.NPHYS * 128, 1024]) for i in range(2)]
        self.cache_slc = [d("cache_slc%d" % i, [self.NPHYS * 128 * 4, 256]) for i in range(2)]
        self.cache_win = d("cache_win", [2, 512, 1024])
        WP = min(512, T)
        self.WP = WP
        self.kvo_p = [o(n, [2, T, 1024]) for n in ("cmp_kv_p", "slc_kv_p")] + [o("win_kv_p", [2, WP, 1024])]
        self.kvo_s = [o(n, [2, 1, 1024]) for n in ("cmp_kv_s", "slc_kv_s")] + [o("win_kv_s", [2, 512, 1024])]
        TT = self.TT
        self.QT = self.dscr("QT", [16, 128, TT], BF16)
        self.KVT = self.dscr("KVT", [3, 4, 2, 128, TT], BF16)
        self.VTOK = self.dscr("VTOK", [3, 4, TT, 128], BF16)
        self.GT = self.dscr("GT", [TT, 48])
        self.OT = self.dscr("OT", [16, 128, TT], BF16)
        self.BD = self.dscr("BD", [16, 4, 128, 128])
        self.BCs = self.dscr("BCs", [16, 248, 128])
        self.b_nsa_dr = Buf("nsadram")
        self.bias_built = False

    def build_bias_tables(self, st):
        with contextlib.ExitStack() as s2:
            relb = self.sb("relb", [33, 16], F32, s2)
            b_rb = Buf("relb")
            self.op("dve", lambda e: e.memset(relb[:], -30000.0), w=[b_rb])
            self.dma(lambda e: e.dma_start(out=relb[0:32, :], in_=self.rel_bias[:, :]), w=[b_rb])
            r31 = self.sb("r31", [32, 16], F32, s2)
            self.dma(lambda e: e.dma_start(out=r31[:], in_=self.rel_bias[31:32, :].partition_broadcast(32)), w=[b_rb])
            self.op("dve", lambda e: e.tensor_tensor(out=relb[0:32, :], in0=relb[0:32, :], in1=r31[:], op=ALU.subtract),
                    r=[b_rb], w=[b_rb])
            NB = 2
            ohb = [self.sb("ohb%d" % i, [33, 2048], F32, s2) for i in range(NB)]
            b_oh = [Buf("oh") for _ in range(NB)]
            ob = [self.sb("ob%d" % i, [16, 2048], F32, s2) for i in range(NB)]
            b_ob = [Buf("ob") for _ in range(NB)]
            ctr = 0
            for src, dst, n in ((self.c_oh_pk, self.BD.rearrange("h k n q -> h (k n q)"), 65536),
                                (self.c_oh_cmp, self.BCs.rearrange("h m q -> h (m q)"), 248 * 128)):
                for c0 in range(0, n, 2048):
                    cw = min(2048, n - c0)
                    i = ctr % NB
                    ctr += 1
                    self.dma(lambda e: e.dma_start(out=ohb[i][:, :cw], in_=src[:, c0:c0 + cw]), w=[b_oh[i]])
                    for k in range(0, cw, 512):
                        kw = min(512, cw - k)
                        pb = 4 + ((k // 512) % 2)
                        self.op("pe", lambda e: e.matmul(self.psb[pb][0:16, :kw], lhsT=relb[:, :], rhs=ohb[i][:, k:k + kw],
                                                         start=True, stop=True), r=[b_rb, b_oh[i]], w=[self.bps[pb]])
                        self.op("act", lambda e: e.copy(out=ob[i][:, k:k + kw], in_=self.psb[pb][0:16, :kw]),
                                r=[self.bps[pb]], w=[b_ob[i]])
                    self.dma(lambda e: e.dma_start(out=dst[:, c0:c0 + cw], in_=ob[i][:, :cw]), r=[b_ob[i]],
                             w=[self.b_nsa_dr], q="act")
            ohs = self.sb("ohs", [33, 17, 128], F32, s2)
            self.dma(lambda e: e.dma_start(out=ohs[:], in_=self.c_oh_s[:, :, :]), w=[b_oh[0]])
            for t in range(17):
                pb = 4 + (t % 2)
                self.op("pe", lambda e: e.matmul(self.psb[pb][:, 0:16], lhsT=ohs[:, t, :], rhs=relb[:, :], start=True, stop=True),
                        r=[b_rb, b_oh[0]], w=[self.bps[pb]])
                self.op("dve", lambda e: e.tensor_copy(out=self.BS[:, t, :], in_=self.psb[pb][:, 0:16]), r=[self.bps[pb]],
                        w=[self.b_BS])
            self.tk.barrier()

    def nsa_layer(self, li):
        j = 0 if li == 0 else 1
        T = self.T
        NQ = T // 128
        SC = 128.0 ** -0.5
        tk = self.tk
        if not self.bias_built:
            self.BS = self.sb("BS", [128, 17, 16], F32)
            self.b_BS = Buf("BS")
            self.build_bias_tables(None)
            self.bias_built = True
        with contextlib.ExitStack() as st:
            self.alloc_rowlocal(st)
            qt = self.sb("qt", [128, 16, 512], BF16, st); b_qt = Buf("qt")
            ktmp = [self.sb("ktmp%d" % i, [128, 2, 512], F32, st) for i in range(2)]; b_kt = [Buf("kt") for _ in range(2)]
            kbf = [self.sb("kbf%d" % i, [128, 2, 512], BF16, st) for i in range(2)]; b_kb = [Buf("kb") for _ in range(2)]
            stg = [self.sb("stg%d" % i, [128, 4, 2, 128], F32, st) for i in range(2)]; b_stg = [Buf("stg") for _ in range(2)]
            vbf = [self.sb("vbf%d" % i, [128, 4, 128], BF16, st) for i in range(2)]; b_vb = [Buf("vb") for _ in range(2)]
            wgf = self.sb("wgf", [128, 16, 48], F32, st); wgb = self.sb("wgb", [128, 16, 48], BF16, st); b_wg = Buf("wg")
            gsb = self.sb("gsb", [48, 512], F32, st); b_gs = Buf("gs")
            gtk = self.sb("gtk", [128, 4, 48], F32, st); b_gt = Buf("gtk")
            self.dma(lambda e: e.dma_start(out=wgf[:], in_=self.nsa_w_gate[j].rearrange("(kc p) n -> p kc n", p=128)), w=[b_wg])
            self.op("pool", lambda e: e.tensor_copy(out=wgb[:], in_=wgf[:]), r=[b_wg], w=[b_wg])
            wq = self.nsa_w_q[j].rearrange("(kc p) (f h n) -> p kc f h n", p=128, h=2, n=128)
            wkv = self.nsa_w_kv[j].rearrange("(kc p) (f h n) -> p kc f h n", p=128, h=2, n=128)
            QTv = self.QT.rearrange("h p t -> p h t")
            for ti in range(len(self.tiles)):
                t0, W = self.tiles[ti]
                samp = (ti == len(self.tiles) - 1)
                self.pre_norm(ti, li * 4 + 0)

                def epq(p, psA, bA, psB, bB):
                    self.op("act", lambda e: e.activation(out=qt[:, 2 * p, :W], in_=psA[:, :W], func=AF.Copy, scale=SC), r=[bA], w=[b_qt])
                    self.op("dve", lambda e: e.tensor_scalar(out=qt[:, 2 * p + 1, :W], in0=psB[:, :W], scalar1=SC, scalar2=None,
                                                             op0=ALU.mult), r=[bB], w=[b_qt])
                self.gemm(lambda p, k0, kn, h: wq[:, k0:k0 + kn, p, h, :], DC, 8, self.xn, self.b_xn, W, epq, wkey='wq%d' % j, first=(ti == 0))
                self.dma(lambda e: e.dma_start(out=QTv[:, :, t0:t0 + W], in_=qt[:, :, :W]), r=[b_qt], w=[self.b_nsa_dr])

                def epkv(p, psA, bA, psB, bB):
                    k = p % 2
                    br, g = p // 4, p % 4
                    self.op("act", lambda e: e.copy(out=ktmp[k][:, 0, :W], in_=psA[:, :W]), r=[bA], w=[b_kt[k]])
                    self.op("dve", lambda e: e.tensor_copy(out=ktmp[k][:, 1, :W], in_=psB[:, :W]), r=[bB], w=[b_kt[k]])
                    self.op("act", lambda e: e.copy(out=kbf[k][:, :, :W], in_=ktmp[k][:, :, :W]), r=[b_kt[k]], w=[b_kb[k]])
                    self.dma(lambda e: e.dma_start(out=self.KVT[br, g].rearrange("c p t -> p c t")[:, :, t0:t0 + W],
                                                   in_=kbf[k][:, :, :W]), r=[b_kb[k]], w=[self.b_nsa_dr], q="act")
                    nb = 1 if samp else 4
                    bw = W if samp else 128
                    for c in range(2):
                        pb = 4 + c
                        for tb in range(nb):
                            self.op("pe", lambda e: e.transpose(out=self.psb[pb][0:bw, tb * 128:(tb + 1) * 128],
                                                                in_=ktmp[k][:, c, tb * 128:tb * 128 + bw], identity=self.ident[:]),
                                    r=[b_kt[k], self.b_const], w=[self.bps[pb]])
                        if c == 0:
                            self.op("act", lambda e: e.copy(out=stg[k][0:bw, 0:nb, 0, :],
                                                            in_=self.psb[pb][0:bw, 0:nb * 128].rearrange("p (a d) -> p a d", d=128)),
                                    r=[self.bps[pb]], w=[b_stg[k]])
                        else:
                            self.op("dve", lambda e: e.tensor_copy(out=stg[k][0:bw, 0:nb, 1, :],
                                                                   in_=self.psb[pb][0:bw, 0:nb * 128].rearrange("p (a d) -> p a d", d=128)),
                                    r=[self.bps[pb]], w=[b_stg[k]])
                            self.op("dve", lambda e: e.tensor_copy(out=vbf[k][0:bw, 0:nb, :], in_=stg[k][0:bw, 0:nb, 1, :]),
                                    r=[b_stg[k]], w=[b_vb[k]])
                    if samp:
                        if br < 2:
                            self.dma(lambda e: e.dma_start(out=self.kvo_s[br][j, 0:1, g * 256:(g + 1) * 256],
                                                           in_=stg[k][0:1, 0, :, :].rearrange("p c d -> p (c d)")), r=[b_stg[k]])
                        else:
                            self.dma(lambda e: e.dma_start(out=self.kvo_s[2][j, 511:512, g * 256:(g + 1) * 256],
                                                           in_=stg[k][0:1, 0, :, :].rearrange("p c d -> p (c d)")), r=[b_stg[k]])
                        self.dma(lambda e: e.dma_start(out=self.VTOK[br, g, T:T + W, :], in_=vbf[k][0:W, 0, :]),
                                 r=[b_vb[k]], w=[self.b_nsa_dr], q="act")
                    else:
                        if br < 2:
                            dst = self.kvo_p[br][j, t0:t0 + 512, g * 256:(g + 1) * 256].rearrange("(a p) (c d) -> p a c d", p=128, c=2)
                            for c in range(2):
                                self.dma(lambda e: e.dma_start(out=dst[:, :, c, :], in_=stg[k][:, :, c, :]), r=[b_stg[k]])
                        elif t0 + 512 > T - self.WP:
                            r0 = t0 - (T - self.WP)
                            dst = self.kvo_p[2][j, r0:r0 + 512, g * 256:(g + 1) * 256].rearrange("(a p) (c d) -> p a c d", p=128, c=2)
                            for c in range(2):
                                self.dma(lambda e: e.dma_start(out=dst[:, :, c, :], in_=stg[k][:, :, c, :]), r=[b_stg[k]])
                        self.dma(lambda e: e.dma_start(out=self.VTOK[br, g, t0:t0 + 512, :].rearrange("(a p) d -> p a d", p=128),
                                                       in_=vbf[k][:, :, :]), r=[b_vb[k]], w=[self.b_nsa_dr], q="act")
                self.gemm(lambda p, k0, kn, h: wkv[:, k0:k0 + kn, p, h, :], DC, 12, self.xn, self.b_xn, W, epkv, wkey='wkv%d' % j, first=(ti == 0))
                for kc in range(DC):
                    self.op("pe", lambda e: e.matmul(self.psb[5][0:48, :W], lhsT=wgb[:, kc, :], rhs=self.xn[:, kc, :W],
                                                     start=(kc == 0), stop=(kc == DC - 1)), r=[b_wg, self.b_xn], w=[self.bps[5]])
                self.op("act", lambda e: e.activation(out=gsb[:, :W], in_=self.psb[5][0:48, :W], func=AF.Sigmoid), r=[self.bps[5]], w=[b_gs])
                nb = 1 if samp else 4
                bw = W if samp else 128
                for tb in range(nb):
                    self.op("pe", lambda e: e.transpose(out=self.psb[4][0:bw, tb * 48:(tb + 1) * 48], in_=gsb[:, tb * 128:tb * 128 + bw],
                                                        identity=self.ident[0:48, 0:48]), r=[b_gs, self.b_const], w=[self.bps[4]])
                self.op("dve", lambda e: e.tensor_copy(out=gtk[0:bw, 0:nb, :], in_=self.psb[4][0:bw, 0:nb * 48].rearrange("p (a n) -> p a n", n=48)),
                        r=[self.bps[4]], w=[b_gt])
                self.dma(lambda e: e.dma_start(out=self.GT[t0:t0 + nb * bw, :].rearrange("(a p) n -> p a n", p=bw), in_=gtk[0:bw, 0:nb, :]),
                         r=[b_gt], w=[self.b_nsa_dr])
            self.tk.barrier()
        self.dma(lambda e: e.dma_start(out=self.kvo_s[2][j, 0:511, :], in_=self.cache_win[j, 1:512, :]))
        with contextlib.ExitStack() as st:
            w1f = self.sb("w1f", [128, 32, 128], F32, st)
            w1b = self.sb("w1b", [128, 2, 32, 128], BF16, st)
            w2f = self.sb("w2f", [128, 2, 128], F32, st)
            w2b = self.sb("w2b", [128, 2, 128], BF16, st)
            pef = self.sb("pef", [64, 128], F32, st)
            peT = self.sb("peT", [128, 32, 2], BF16, st)
            pebias = self.sb("pebias", [128, 2], F32, st)
            b_cw = Buf("cw")
            for c in range(2):
                self.dma(lambda e: e.dma_start(out=w1f[:], in_=self.nsa_cmp_w1[j, c].rearrange("s d e -> d s e")), w=[b_cw])
                self.op("pool", lambda e: e.tensor_copy(out=w1b[:, c], in_=w1f[:]), r=[b_cw], w=[b_cw])
            self.dma(lambda e: e.dma_start(out=w2f[:], in_=self.nsa_cmp_w2[j].rearrange("c e d -> e c d")), w=[b_cw])
            self.op("pool", lambda e: e.tensor_copy(out=w2b[:], in_=w2f[:]), r=[b_cw], w=[b_cw])
            self.dma(lambda e: e.dma_start(out=pef[:], in_=self.nsa_cmp_pe[j].rearrange("s c d -> (s c) d")), w=[b_cw])
            self.op("pe", lambda e: e.transpose(out=self.psb[5][:, 0:64], in_=pef[:], identity=self.ident[0:64, 0:64]),
                    r=[b_cw, self.b_const], w=[self.bps[5]])
            self.op("dve", lambda e: e.tensor_copy(out=peT[:].rearrange("p s c -> p (s c)"), in_=self.psb[5][:, 0:64]), r=[self.bps[5]], w=[b_cw])
            for c in range(2):
                for s in range(32):
                    self.op("pe", lambda e: e.matmul(self.psb[5][:, 64 + c:65 + c], lhsT=w1b[:, c, s, :], rhs=peT[:, s, c:c + 1],
                                                     start=(s == 0 and c == 0), stop=(s == 31)), r=[b_cw], w=[self.bps[5]])
            self.op("dve", lambda e: e.tensor_copy(out=pebias[:], in_=self.psb[5][:, 64:66]), r=[self.bps[5]], w=[b_cw])
            ght = [self.sb("ght%d" % i, [128, 128], BF16, st) for i in range(2)]
            b_gh = [Buf("ght") for _ in range(2)]

            def compress(xT_ap_fn, bx, ncols, kdst, vdst, bdst, ctr0):
                for c in range(2):
                    i = (ctr0 + c) % 2
                    pb = i
                    for s in range(32):
                        self.op("pe", lambda e: e.matmul(self.psb[pb][:, 0:ncols], lhsT=w1b[:, c, s, :], rhs=xT_ap_fn(c, s),
                                                         start=(s == 0), stop=(s == 31)), r=[b_cw, bx], w=[self.bps[pb]])
                    self.op("act", lambda e: e.activation(out=ght[i][:, 0:ncols], in_=self.psb[pb][:, 0:ncols], func=AF.Gelu_apprx_tanh,
                                                          bias=pebias[:, c:c + 1], scale=1.0), r=[self.bps[pb], b_cw], w=[b_gh[i]])
                    pb2 = 2 + i
                    if c == 0:
                        self.op("pe", lambda e: e.matmul(self.psb[pb2][:, 0:ncols], lhsT=w2b[:, 0, :], rhs=ght[i][:, 0:ncols], start=True, stop=True),
                                r=[b_cw, b_gh[i]], w=[self.bps[pb2]])
                        self.op("dve", lambda e: e.tensor_copy(out=kdst, in_=self.psb[pb2][:, 0:ncols]), r=[self.bps[pb2]], w=[bdst])
                    else:
                        self.op("pe", lambda e: e.matmul(self.psb[pb2][0:ncols, 0:128], lhsT=ght[i][:, 0:ncols], rhs=w2b[:, 1, :], start=True, stop=True),
                                r=[b_cw, b_gh[i]], w=[self.bps[pb2]])
                        self.op("dve", lambda e: e.tensor_copy(out=vdst, in_=self.psb[pb2][0:ncols, 0:128]), r=[self.bps[pb2]], w=[bdst])

            with contextlib.ExitStack() as s2:
                NCB = T // 16 - 1
                xT = self.sb("cxT", [128, 2, T], BF16, s2); b_xT = Buf("cxT")
                kcT = self.sb("kcT", [128, 4, 128], BF16, s2)
                vc = self.sb("vc", [128, 4, 128], BF16, s2)
                b_kc = Buf("kc")
                self.op("dve", lambda e: e.memset(kcT[:], 0.0), w=[b_kc])
                self.op("dve", lambda e: e.memset(vc[:], 0.0), w=[b_kc])
                for g in range(4):
                    self.dma(lambda e: e.dma_start(out=xT[:], in_=self.KVT[0, g].rearrange("c p t -> p c t")[:, :, 0:T]),
                             r=[self.b_nsa_dr], w=[b_xT])
                    compress(lambda c, s: xT[:, c, s:s + 16 * (NCB - 1) + 1:16], b_xT, NCB, kcT[:, g, 0:NCB], vc[0:NCB, g, :], b_kc, 2 * g)
                ks = [self.sb("ks%d" % b, [128, T], BF16, s2) for b in range(2)]
                vs = [self.sb("vs%d" % b, [128, NQ, 128], BF16, s2) for b in range(2)]
                b_kv = Buf("kvres")
                bdt = self.sb("bdt", [128, 4, 4, 128], F32, s2)
                b_bd = Buf("bdt")
                gts = self.sb("gts", [128, NQ, 48], F32, s2); b_gts = Buf("gts")
                self.dma(lambda e: e.dma_start(out=gts[:], in_=self.GT[0:T, :].rearrange("(a p) n -> p a n", p=128)), r=[self.b_nsa_dr], w=[b_gts])
                mselp = self.sb("mselp", [128, 32], F32, s2)
                amp = self.sb("amp", [128, 8, 32], F32, s2)
                eexf = self.sb("eexf", [32, 16, 128], F32, s2)
                eexb = self.sb("eexb", [32, 16, 128], BF16, s2)
                b_sc = Buf("selc")
                self.dma(lambda e: e.dma_start(out=mselp[:], in_=self.c_msel_p[:, :]), w=[b_sc])
                self.dma(lambda e: e.dma_start(out=amp[:], in_=self.c_addmask_p.rearrange("i q j -> q i j")), w=[b_sc])
                self.dma(lambda e: e.dma_start(out=eexf[:], in_=self.c_eexp[:, :, :]), w=[b_sc])
                self.op("dve", lambda e: e.tensor_copy(out=eexb[:], in_=eexf[:]), r=[b_sc], w=[b_sc])
                qs = [self.sb("qs%d" % i, [128, 4, 128], BF16, s2) for i in range(2)]; b_qs = [Buf("qs") for _ in range(2)]
                bcm = [self.sb("bcm%d" % i, [128, 4, 128], F32, s2) for i in range(2)]; b_bcm = [Buf("bcm") for _ in range(2)]
                sT = [self.sb("sT%d" % i, [128, 512], F32, s2) for i in range(3)]; b_sT = [Buf("sT") for _ in range(3)]
                pT = [self.sb("pT%d" % i, [128, 512], BF16, s2) for i in range(3)]; b_pT = [Buf("pT") for _ in range(3)]
                SB_ = [0, 1, 7]
                pf = self.sb("pf", [128, 512], F32, s2); b_pf = Buf("pf")
                rcs = self.sb("rcs", [128, 512], F32, s2); b_rc = Buf("rcs")
                impT = self.sb("impT", [128, 128], F32, s2); b_imp = Buf("imp")
                psl = self.sb("psl", [32, 128], F32, s2); b_psl = Buf("psl")
                sco = self.sb("sco", [128, 32], F32, s2); sco2 = self.sb("sco2", [128, 32], F32, s2)
                mx1 = self.sb("mx1", [128, 8], F32, s2); mx2 = self.sb("mx2", [128, 8], F32, s2)
                ngm = self.sb("ngm", [128, 32], F32, s2); b_sco = Buf("sco")
                ngT = self.sb("ngT", [32, 128], BF16, s2); b_ngT = Buf("ngT")
                osb = self.sb("osb", [128, 4, 128], F32, s2); b_osb = Buf("osb")
                wsc = [self.sb("wsc%d" % i_, [128, 4], F32, s2) for i_ in range(2)]; b_wsc = [Buf("wsc") for _ in range(2)]
                rsc = self.sb("rsc", [128, 4], F32, s2); b_rsc = Buf("rsc")
                osb3 = [self.sb("osb3_%d" % i_, [128, 3, 4, 128], F32, s2) for i_ in range(2)]; b_osb3 = [Buf("osb3") for _ in range(2)]
                otb = [self.sb("otb%d" % i, [128, 4, 128], BF16, s2) for i in range(2)]; b_otb = [Buf("otb") for _ in range(2)]
                for g in range(4):
                    for b_ in range(2):
                        self.dma(lambda e: e.dma_start(out=ks[b_][:], in_=self.KVT[1 + b_, g, 0, :, 0:T]), r=[self.b_nsa_dr], w=[b_kv])
                        self.dma(lambda e: e.dma_start(out=vs[b_][:], in_=self.VTOK[1 + b_, g, 0:T, :].rearrange("(a p) d -> p a d", p=128)),
                                 r=[self.b_nsa_dr], w=[b_kv])
                    for kd in range(4):
                        self.dma(lambda e: e.dma_start(out=bdt[:, kd], in_=self.BD[4 * g:4 * g + 4, kd].rearrange("h n q -> n h q")),
                                 r=[self.b_nsa_dr], w=[b_bd])
                    units = []
                    banks = {}
                    octr = 0
                    for i in range(NQ):
                        for br in (0, 2, 1):
                            if br == 0:
                                kts = [0]
                            elif br == 1:
                                kts = list(range(0, i + 1))
                            else:
                                kts = list(range(max(0, i - 4), i + 1))
                            banks[(i, br)] = (2, 0) if octr % 2 == 0 else (6, 8)
                            octr += 1
                            for ki, kt in enumerate(kts):
                                units.append((i, br, kt, ki, len(kts)))

                    def emit_qk(n):
                        i, br, kt, ki, nk = units[n]
                        qi = i % 2
                        u = n % 3
                        sbk = SB_[u]
                        if br == 0:
                            self.dma(lambda e: e.dma_start(out=qs[qi][:], in_=self.QT[4 * g:4 * g + 4, :, i * 128:(i + 1) * 128].rearrange("h p t -> p h t")),
                                     r=[self.b_nsa_dr], w=[b_qs[qi]])
                            self.dma(lambda e: e.dma_start(out=bcm[qi][:], in_=self.BCs[4 * g:4 * g + 4, 120 - 8 * i:248 - 8 * i, :].rearrange("h n q -> n h q")),
                                     r=[self.b_nsa_dr], w=[b_bcm[qi]])
                        qrhs = qs[qi][:].rearrange("p r q -> p (r q)")
                        if br == 0:
                            klhs, kvb = kcT[:, g, :], b_kc
                        else:
                            klhs, kvb = ks[br - 1][:, kt * 128:(kt + 1) * 128], b_kv
                        msk = (br == 1 and i >= 8)
                        self.op("pe", lambda e: e.matmul(self.psb[sbk][:, :], lhsT=klhs, rhs=qrhs, start=True, stop=(not msk)),
                                r=[kvb, b_qs[qi]], w=[self.bps[sbk]])
                        if msk:
                            for r in range(4):
                                self.op("pe", lambda e: e.matmul(self.psb[sbk][:, r * 128:(r + 1) * 128], lhsT=eexb[:, kt, :], rhs=ngT[:, :],
                                                                 start=False, stop=(r == 3)), r=[b_sc, b_ngT], w=[self.bps[sbk]])

                    def emit_rest(n):
                        i, br, kt, ki, nk = units[n]
                        qi = i % 2
                        u = n % 3
                        sbk = SB_[u]
                        ob_, oc_ = banks[(i, br)]
                        os_ = 3
                        dosel = (i >= 8)
                        if br == 0:
                            vrhs, kvb = vc[:, g, :], b_kc
                            bias_ap, bb = bcm[qi][:].rearrange("p r q -> p (r q)"), b_bcm[qi]
                        else:
                            vrhs, kvb = vs[br - 1][:, kt, :], b_kv
                            if kt == i:
                                kd = 0
                            elif kt == i - 1:
                                kd = 1
                            elif br == 2 and kt == i - 4:
                                kd = 3
                            else:
                                kd = 2
                            bias_ap, bb = bdt[:, kd].rearrange("p r q -> p (r q)"), b_bd
                        far = (br != 0 and kd == 2)
                        if not far:
                            self.op("dve", lambda e: e.tensor_tensor(out=sT[u][:], in0=self.psb[sbk][:, :], in1=bias_ap, op=ALU.add),
                                    r=[self.bps[sbk], bb], w=[b_sT[u]])
                        if br == 0:
                            if dosel:
                                self.op("act", lambda e: e.activation(out=pf[:], in_=sT[u][:], func=AF.Exp), r=[b_sT[u]], w=[b_pf])
                            self.op("act", lambda e: e.activation(out=pT[u][:], in_=sT[u][:], func=AF.Exp), r=[b_sT[u]], w=[b_pT[u]])
                        elif far:
                            self.op("act", lambda e: e.activation(out=pT[u][:], in_=self.psb[sbk][:, :], func=AF.Exp), r=[self.bps[sbk]], w=[b_pT[u]])
                        else:
                            self.op("act", lambda e: e.activation(out=pT[u][:], in_=sT[u][:], func=AF.Exp), r=[b_sT[u]], w=[b_pT[u]])
                        for r in range(4):
                            self.op("pe", lambda e: e.matmul(self.psb[ob_][:, r * 128:(r + 1) * 128], lhsT=pT[u][:, r * 128:(r + 1) * 128], rhs=vrhs,
                                                             start=(ki == 0 and r == 0), stop=(ki == nk - 1)),
                                    r=[b_pT[u], kvb], w=[self.bps[ob_]])
                            self.op("pe", lambda e: e.matmul(self.psb[os_][:, oc_ + r:oc_ + r + 1], lhsT=pT[u][:, r * 128:(r + 1) * 128], rhs=self.ones_b[:, 0:1],
                                                             start=(ki == 0 and r == 0), stop=(ki == nk - 1)),
                                    r=[b_pT[u], self.b_const], w=[self.bps[os_]])
                        if br == 0 and dosel:
                            for r in range(4):
                                self.op("pe", lambda e: e.matmul(self.psb[5][:, r * 32:(r + 1) * 32], lhsT=pf[:, r * 128:(r + 1) * 128], rhs=mselp[:, :],
                                                                 start=(r == 0), stop=(r == 3)), r=[b_pf, b_sc], w=[self.bps[5]])
                        if ki == nk - 1:
                            self.op("dve", lambda e: e.tensor_scalar(out=rsc[:], in0=self.psb[os_][:, oc_:oc_ + 4], scalar1=1e-30, scalar2=None, op0=ALU.add),
                                    r=[self.bps[os_]], w=[b_rsc])
                            self.op("dve", lambda e: e.reciprocal(out=rsc[:], in_=rsc[:]), r=[b_rsc], w=[b_rsc])
                            gv = gts[:, i, :].rearrange("p (h b) -> p h b", b=3)[:, 4 * g:4 * g + 4, br]
                            wk_ = wsc[(3 * i + br) % 2]
                            bwk_ = b_wsc[(3 * i + br) % 2]
                            self.op("dve", lambda e: e.tensor_tensor(out=wk_[:], in0=rsc[:], in1=gv, op=ALU.mult), r=[b_rsc, b_gts], w=[bwk_])
                            for r in range(4):
                                self.op("act", lambda e: e.activation(out=osb3[qi][:, br, r, :], in_=self.psb[ob_][:, r * 128:(r + 1) * 128], func=AF.Copy,
                                                                      scale=wk_[:, r:r + 1]), r=[self.bps[ob_], bwk_], w=[b_osb3[qi]])
                            if br == 0 and dosel:
                                self.op("dve", lambda e: e.scalar_tensor_tensor(out=sco[:], in0=self.psb[5][:, 0:32], scalar=rsc[:, 0:1], in1=amp[:, i - 8, :],
                                                                                op0=ALU.mult, op1=ALU.add), r=[self.bps[5], b_rsc, b_sc], w=[b_sco])
                                for r in range(1, 4):
                                    self.op("dve", lambda e: e.scalar_tensor_tensor(out=sco[:], in0=self.psb[5][:, r * 32:(r + 1) * 32], scalar=rsc[:, r:r + 1], in1=sco[:],
                                                                                    op0=ALU.mult, op1=ALU.add), r=[self.bps[5], b_rsc, b_sco], w=[b_sco])
                                osc = lambda f: self.op("dve", f, r=[b_sco], w=[b_sco])
                                osc(lambda e: e.max(out=mx1[:], in_=sco[:]))
                                osc(lambda e: e.match_replace(out=sco2[:], in_to_replace=mx1[:], in_values=sco[:], imm_value=-3.0e38))
                                osc(lambda e: e.max(out=mx2[:], in_=sco2[:]))
                                osc(lambda e: e.tensor_scalar(out=ngm[:], in0=sco[:], scalar1=mx2[:, 7:8], scalar2=None, op0=ALU.is_ge))
                                osc(lambda e: e.tensor_scalar(out=ngm[:], in0=ngm[:], scalar1=30000.0, scalar2=-30000.0, op0=ALU.mult, op1=ALU.add))
                                self.op("pe", lambda e: e.transpose(out=self.psb[5][0:32, 256:384], in_=ngm[:], identity=self.ident[:]),
                                        r=[b_sco, self.b_const], w=[self.bps[5]])
                                self.op("act", lambda e: e.copy(out=ngT[:], in_=self.psb[5][0:32, 256:384]), r=[self.bps[5]], w=[b_ngT])
                            if br == 1:
                                self.op("pool", lambda e: e.tensor_tensor(out=osb[:], in0=osb3[qi][:, 0], in1=osb3[qi][:, 2], op=ALU.add),
                                        r=[b_osb3[qi]], w=[b_osb])
                                self.op("pool", lambda e: e.tensor_tensor(out=osb[:], in0=osb[:], in1=osb3[qi][:, 1], op=ALU.add),
                                        r=[b_osb3[qi], b_osb], w=[b_osb])
                                for r in range(4):
                                    self.op("pe", lambda e: e.transpose(out=self.psb[4][:, r * 128:(r + 1) * 128], in_=osb[:, r, :], identity=self.ident[:]),
                                            r=[b_osb, self.b_const], w=[self.bps[4]])
                                self.op("act", lambda e: e.copy(out=otb[qi][:].rearrange("p r q -> p (r q)"), in_=self.psb[4][:, :]), r=[self.bps[4]], w=[b_otb[qi]])
                                self.dma(lambda e: e.dma_start(out=self.OT[4 * g:4 * g + 4, :, i * 128:(i + 1) * 128].rearrange("h p t -> p h t"), in_=otb[qi][:]),
                                         r=[b_otb[qi]], w=[self.b_nsa_dr], q="act")

                    emit_qk(0)
                    emit_qk(1)
                    for n in range(len(units)):
                        if n + 2 < len(units):
                            emit_qk(n + 2)
                        emit_rest(n)
                self.tk.barrier()
            self.nsa_sample(j, st, compress, b_cw)
            self.tk.barrier()
        with contextlib.ExitStack() as st:
            self.alloc_rowlocal(st)
            oin = self.sb("oin_", [128, 16, 512], BF16, st); b_oin = Buf("oin")
            wo = self.nsa_w_o[j].rearrange("(kc p) (f h n) -> p kc f h n", p=128, h=2, n=128)
            OTv = self.OT.rearrange("h p t -> p h t")
            for ti in range(len(self.tiles)):
                t0, W = self.tiles[ti]
                self.load_xr(ti)
                self.dma(lambda e: e.dma_start(out=oin[:, :, :W], in_=OTv[:, :, t0:t0 + W]), r=[self.b_nsa_dr], w=[b_oin], q="act")

                def epo(p, psA, bA, psB, bB):
                    self.op("act", lambda e: e.copy(out=self.yy[:, 2 * p, :W], in_=psA[:, :W]), r=[bA], w=[self.b_yy])
                    self.op("dve", lambda e: e.tensor_copy(out=self.yy[:, 2 * p + 1, :W], in_=psB[:, :W]), r=[bB], w=[self.b_yy])
                self.gemm(lambda p, k0, kn, h: wo[:, k0:k0 + kn, p, h, :], DC, 8, oin, b_oin, W, epo, wkey='wo%d' % j, first=(ti == 0))
                self.post_norm_residual(ti, li * 4 + 1)
            self.tk.barrier()

    def nsa_sample(self, j, st, compress, b_cw):
        T = self.T
        with contextlib.ExitStack() as s2:
            S = lambda n, shp, dt=F32: self.sb(n, shp, dt, s2)
            b_c = Buf("sconst")
            pti = S("pti", [128, 128], I32); ptf = S("ptf", [128, 128]); idxf = S("idxf", [128, 128]); idxa = S("idxa", [128, 128], I32)
            iop = S("iop", [128, 1]); ior = S("ior", [128, 1]); iopg = S("iopg", [128, 128]); gcol = S("gcol", [128, 32])
            selh = S("selh", [4, 2, 128]); eye4 = S("eye4", [4, 4]); msels = S("msels", [128, 8, 257]); ams = S("ams", [4, 257])
            L = lambda dst, src: self.dma(lambda e: e.dma_start(out=dst, in_=src), w=[b_c])
            L(pti[:], self.page_table.partition_broadcast(128))
            L(iop[:], self.c_iota_p[:, :]); L(ior[:], self.c_iota_row[:, :]); L(iopg[:], self.c_iota_pg[:, :]); L(gcol[:], self.c_gcol[:, :])
            L(selh[:], self.c_selhalf.rearrange("s g p -> g s p")); L(eye4[:], self.c_eye4[:, :])
            L(msels[:], self.c_msel_s.rearrange("(t n) j -> n t j", n=128)); L(ams[:], self.c_addmask_s[:, :])
            oc = lambda q, f: self.op(q, f, r=[b_c], w=[b_c])
            oc("dve", lambda e: e.tensor_copy(out=ptf[:], in_=pti[:]))
            oc("dve", lambda e: e.tensor_scalar(out=idxf[:], in0=ptf[:], scalar1=128.0, scalar2=iop[:, 0:1], op0=ALU.mult, op1=ALU.add))
            oc("dve", lambda e: e.tensor_copy(out=idxa[:], in_=idxf[:]))
            qsT = S("qsT", [128, 16], BF16); gsm = S("gsm", [4, 4, 3]); b_q = Buf("qs_s")
            self.dma(lambda e: e.dma_start(out=qsT[:].unsqueeze(2), in_=self.QT.rearrange("h p t -> p h t")[:, :, T:T + 1],
                                           allow_slow_non_contiguous=True), r=[self.b_nsa_dr], w=[b_q])
            self.dma(lambda e: e.dma_start(out=gsm[:], in_=self.GT[T:T + 1, :].rearrange("o (g r b) -> (o r) g b", g=4, r=4),
                                           allow_slow_non_contiguous=True), r=[self.b_nsa_dr], w=[b_q])
            XT = S("XT", [128, 8, 2064], BF16); b_XT = Buf("XT")
            pg = [S("pg%d" % i, [128, 1024]) for i in range(2)]; b_pg = [Buf("pg") for _ in range(2)]
            kcs = S("kcs", [128, 4, 1024], BF16); vcs = S("vcs", [128, 8, 4, 128], BF16); b_kcs = Buf("kcs")
            self.op("pool", lambda e: e.memset(XT[:, :, 0:16], 0.0), w=[b_XT])
            cmp_rows = self.cache_cmp[j]
            for G8 in range(8):
                if G8 > 0:
                    self.op("pool", lambda e: e.tensor_copy(out=XT[:, :, 0:16], in_=XT[:, :, 2048:2064]), r=[b_XT], w=[b_XT])
                for p16 in range(16):
                    page = G8 * 16 + p16
                    k = page % 2
                    self.tk.dma("pool", lambda e: e.indirect_dma_start(out=pg[k][:, :], out_offset=None, in_=cmp_rows[:, :],
                                                                       in_offset=bass.IndirectOffsetOnAxis(ap=idxa[:, page:page + 1], axis=0)),
                                [b_c], [b_pg[k]])
                    for half in range(2):
                        pb = 4 + half
                        for q4 in range(4):
                            gc = half * 4 + q4
                            self.op("pe", lambda e: e.transpose(out=self.psb[pb][:, q4 * 128:(q4 + 1) * 128], in_=pg[k][:, gc * 128:(gc + 1) * 128],
                                                                identity=self.ident[:]), r=[b_pg[k], self.b_const], w=[self.bps[pb]])
                        dst = XT[:, half * 4:(half + 1) * 4, 16 + p16 * 128:16 + (p16 + 1) * 128]
                        src = self.psb[pb][:, :].rearrange("p (a n) -> p a n", a=4)
                        if half == 0:
                            self.op("act", lambda e: e.copy(out=dst, in_=src), r=[self.bps[pb]], w=[b_XT])
                        else:
                            self.op("dve", lambda e: e.tensor_copy(out=dst, in_=src), r=[self.bps[pb]], w=[b_XT])
                for g in range(4):
                    compress(lambda c, s: XT[:, g * 2 + c, s:s + 16 * 127 + 1:16], b_XT, 128,
                             kcs[:, g, G8 * 128:(G8 + 1) * 128], vcs[:, G8, g, :], b_kcs, 2 * g)
            sTs = S("sTs", [128, 144]); pfs = S("pfs", [128, 144]); pns = S("pns", [128, 144]); pbs = S("pbs", [128, 144], BF16)
            tot = S("tot", [128, 16]); b_a = Buf("satt")
            imp = S("imp", [128, 8, 4])

            def softmax_cols(ncol, nt, bias_ap, psbank):
                pass

            for t in range(8):
                for g in range(4):
                    self.op("pe", lambda e: e.matmul(self.psb[0][:, t * 16 + 4 * g:t * 16 + 4 * g + 4], lhsT=kcs[:, g, t * 128:(t + 1) * 128],
                                                     rhs=qsT[:, 4 * g:4 * g + 4], start=True, stop=True), r=[b_kcs, b_q], w=[self.bps[0]])
            oa = lambda q, f, extra=(): self.op(q, f, r=[b_a] + list(extra), w=[b_a])
            oa("dve", lambda e: e.tensor_tensor(out=sTs[:, 0:128], in0=self.psb[0][:, 0:128], in1=self.BS[:, 0:8, :].rearrange("p t h -> p (t h)"),
                                                op=ALU.add), [self.bps[0], self.b_BS])
            oa("act", lambda e: e.activation(out=pfs[:, 0:128], in_=sTs[:, 0:128], func=AF.Exp))
            self.op("pe", lambda e: e.matmul(self.psb[4][:, 0:128], lhsT=self.ones_f[:], rhs=pfs[:, 0:128], start=True, stop=True),
                    r=[b_a, self.b_const], w=[self.bps[4]])
            oa("dve", lambda e: e.tensor_reduce(out=tot[:], in_=self.psb[4][:, 0:128].rearrange("p (t h) -> p h t", h=16), axis=AX.X, op=ALU.add),
               [self.bps[4]])
            oa("dve", lambda e: e.reciprocal(out=tot[:], in_=tot[:]))
            oa("dve", lambda e: e.tensor_tensor(out=pns[:, 0:128].rearrange("p (t h) -> p t h", h=16), in0=pfs[:, 0:128].rearrange("p (t h) -> p t h", h=16),
                                                in1=tot[:].unsqueeze(1).to_broadcast([128, 8, 16]), op=ALU.mult))
            oa("act", lambda e: e.copy(out=pbs[:, 0:128], in_=pns[:, 0:128]))
            oa("dve", lambda e: e.tensor_reduce(out=imp[:], in_=pns[:, 0:128].rearrange("p (t g r) -> p t g r", g=4, r=4), axis=AX.X, op=ALU.add))
            first = True
            for g in range(4):
                for t in range(8):
                    self.op("pe", lambda e: e.matmul(self.psb[2][0:4, g * 128:(g + 1) * 128], lhsT=pbs[:, t * 16 + 4 * g:t * 16 + 4 * g + 4],
                                                     rhs=vcs[:, t, g, :], start=first, stop=(t == 7)), r=[b_a, b_kcs], w=[self.bps[2]])
                    first = False
            osm = S("osm", [4, 4, 128]); otmp = S("otmp", [4, 4, 128]); b_os = Buf("osm")
            self.op("dve", lambda e: e.tensor_tensor(out=osm[:], in0=self.psb[2][0:4, :].rearrange("p (g d) -> p g d", g=4),
                                                     in1=gsm[:, :, 0:1].to_broadcast([4, 4, 128]), op=ALU.mult), r=[self.bps[2], b_q], w=[b_os])
            for t in range(8):
                self.op("pe", lambda e: e.matmul(self.psb[5][0:4, 0:257], lhsT=imp[:, t, :], rhs=msels[:, t, :], start=(t == 0), stop=(t == 7)),
                        r=[b_a, b_c], w=[self.bps[5]])
            sco = S("ssco", [4, 257]); sco2 = S("ssco2", [4, 257]); mx1 = S("smx1", [4, 8]); mx2 = S("smx2", [4, 8])
            ixu = S("ixu", [4, 16], U32); ixf = S("ixf", [4, 16]); bm = S("bm", [4, 2, 4, 8]); b_s = Buf("ssel")
            osl = lambda q, f, extra=(): self.op(q, f, r=[b_s] + list(extra), w=[b_s])
            osl("dve", lambda e: e.tensor_tensor(out=sco[:], in0=self.psb[5][0:4, 0:257], in1=ams[:], op=ALU.add), [self.bps[5], b_c])
            osl("dve", lambda e: e.max(out=mx1[:], in_=sco[:]))
            osl("dve", lambda e: e.match_replace(out=sco2[:], in_to_replace=mx1[:], in_values=sco[:], imm_value=-3.0e38))
            osl("dve", lambda e: e.max(out=mx2[:], in_=sco2[:]))
            osl("dve", lambda e: e.max_index(out=ixu[:, 0:8], in_max=mx1[:], in_values=sco[:]))
            osl("dve", lambda e: e.max_index(out=ixu[:, 8:16], in_max=mx2[:], in_values=sco2[:]))
            osl("dve", lambda e: e.tensor_copy(out=ixf[:], in_=ixu[:]))
            for s2i in range(2):
                osl("dve", lambda e: e.tensor_tensor(out=bm[:, s2i], in0=ixf[:, s2i:16:2].unsqueeze(1).to_broadcast([4, 4, 8]),
                                                     in1=eye4[:, :].unsqueeze(2).to_broadcast([4, 4, 8]), op=ALU.mult), [b_c])
                self.op("pe", lambda e: e.matmul(self.psb[5][:, 300:332], lhsT=selh[:, s2i, :], rhs=bm[:, s2i].rearrange("p g s -> p (g s)"),
                                                 start=(s2i == 0), stop=(s2i == 1)), r=[b_s, b_c], w=[self.bps[5]])
            jvf = S("jvf", [128, 32]); jvi = S("jvi", [128, 32], I32); jhi = S("jhi", [128, 32], I32); jpi = S("jpi", [128, 32], I32)
            jhf = S("jhf", [128, 32]); jpf = S("jpf", [128, 32]); ptsel = S("ptsel", [128, 32]); rowf = S("rowf", [128, 32]); rowi = S("rowi", [128, 32], I32)
            i254 = S("i254", [128, 32]); i255 = S("i255", [128, 32]); i256 = S("i256", [128, 32])
            osl("dve", lambda e: e.tensor_copy(out=jvf[:], in_=self.psb[5][:, 300:332]), [self.bps[5]])
            osl("dve", lambda e: e.tensor_copy(out=jvi[:], in_=jvf[:]))
            osl("dve", lambda e: e.tensor_single_scalar(out=jhi[:], in_=jvi[:], scalar=1, op=ALU.logical_shift_right))
            osl("dve", lambda e: e.tensor_single_scalar(out=jpi[:], in_=jvi[:], scalar=1, op=ALU.bitwise_and))
            osl("dve", lambda e: e.tensor_copy(out=jhf[:], in_=jhi[:]))
            osl("dve", lambda e: e.tensor_copy(out=jpf[:], in_=jpi[:]))
            with contextlib.ExitStack() as s3:
                ohp = self.sb("ohp", [128, 32, 128], F32, s3)
                osl("dve", lambda e: e.tensor_tensor(out=ohp[:], in0=jhf[:].unsqueeze(2).to_broadcast([128, 32, 128]),
                                                     in1=iopg[:].unsqueeze(1).to_broadcast([128, 32, 128]), op=ALU.is_equal), [b_c])
                osl("dve", lambda e: e.tensor_tensor(out=ohp[:], in0=ohp[:], in1=ptf[:].unsqueeze(1).to_broadcast([128, 32, 128]), op=ALU.mult), [b_c])
                osl("dve", lambda e: e.tensor_reduce(out=ptsel[:], in_=ohp[:], axis=AX.X, op=ALU.add))
                self.tk.barrier()
            osl("dve", lambda e: e.tensor_scalar(out=rowf[:], in0=ptsel[:], scalar1=128.0, scalar2=ior[:, 0:1], op0=ALU.mult, op1=ALU.add), [b_c])
            osl("dve", lambda e: e.scalar_tensor_tensor(out=rowf[:], in0=jpf[:], scalar=64.0, in1=rowf[:], op0=ALU.mult, op1=ALU.add))
            osl("dve", lambda e: e.scalar_tensor_tensor(out=rowf[:], in0=rowf[:], scalar=4.0, in1=gcol[:], op0=ALU.mult, op1=ALU.add), [b_c])
            osl("dve", lambda e: e.tensor_copy(out=rowi[:], in_=rowf[:]))
            for tile_, val in ((i254, 254.0), (i255, 255.0), (i256, 256.0)):
                osl("dve", lambda e: e.tensor_scalar(out=tile_[:], in0=jvf[:], scalar1=val, scalar2=None, op0=ALU.is_equal))
            bsl = S("bsl", [128, 4, 9, 4]); dv = S("dv", [128, 2, 16]); tb1 = S("tb1", [128, 8, 4])
            osl("dve", lambda e: e.tensor_tensor(out=dv[:, 0, :], in0=self.BS[:, 13, :], in1=self.BS[:, 15, :], op=ALU.subtract), [self.b_BS])
            osl("dve", lambda e: e.tensor_tensor(out=dv[:, 1, :], in0=self.BS[:, 14, :], in1=self.BS[:, 15, :], op=ALU.subtract), [self.b_BS])
            for g in range(4):
                tgt = bsl[:, g, 0:8, :]
                osl("dve", lambda e: e.tensor_copy(out=tgt, in_=self.BS[:, 15, 4 * g:4 * g + 4].unsqueeze(1).to_broadcast([128, 8, 4])), [self.b_BS])
                for k, ind in enumerate((i254, i255)):
                    osl("dve", lambda e: e.tensor_tensor(out=tb1[:], in0=ind[:, g * 8:(g + 1) * 8].unsqueeze(2).to_broadcast([128, 8, 4]),
                                                         in1=dv[:, k, 4 * g:4 * g + 4].unsqueeze(1).to_broadcast([128, 8, 4]), op=ALU.mult))
                    osl("dve", lambda e: e.tensor_tensor(out=tgt, in0=tgt, in1=tb1[:], op=ALU.add))
                osl("dve", lambda e: e.scalar_tensor_tensor(out=tgt, in0=i256[:, g * 8:(g + 1) * 8].unsqueeze(2).to_broadcast([128, 8, 4]), scalar=-30000.0,
                                                            in1=tgt, op0=ALU.mult, op1=ALU.add))
                osl("dve", lambda e: e.tensor_copy(out=bsl[:, g, 8, :], in_=self.BS[:, 16, 4 * g:4 * g + 4]), [self.b_BS])
            ksel = [S("ksel%d" % i, [128, 256]) for i in range(2)]; b_ks = [Buf("ksel") for _ in range(2)]
            KsT = S("KsT", [128, 4, 9, 128], BF16); Vs = S("Vs", [128, 4, 9, 128], BF16); b_KV = Buf("KVs")
            self.op("pool", lambda e: e.memset(KsT[:, :, 8, :], 0.0), w=[b_KV])
            self.op("pool", lambda e: e.memset(Vs[:, :, 8, :], 0.0), w=[b_KV])
            slc_rows = self.cache_slc[j]
            for g in range(4):
                self.dma(lambda e: e.dma_start(out=KsT[:, g, 8, 0:1], in_=self.KVT[1, g, 0, :, T:T + 1], allow_slow_non_contiguous=True),
                         r=[self.b_nsa_dr], w=[b_KV])
                self.dma(lambda e: e.dma_start(out=Vs[0:1, g, 8, :], in_=self.VTOK[1, g, T:T + 1, :]), r=[self.b_nsa_dr], w=[b_KV])
                for sp in range(8):
                    col = g * 8 + sp
                    k = col % 2
                    self.tk.dma("pool", lambda e: e.indirect_dma_start(out=ksel[k][:, :], out_offset=None, in_=slc_rows[:, :],
                                                                       in_offset=bass.IndirectOffsetOnAxis(ap=rowi[:, col:col + 1], axis=0)),
                                [b_s], [b_ks[k]])
                    pb = 4 + (col % 2)
                    self.op("pe", lambda e: e.transpose(out=self.psb[pb][:, 0:128], in_=ksel[k][:, 0:128], identity=self.ident[:]),
                            r=[b_ks[k], self.b_const], w=[self.bps[pb]])
                    self.op("act", lambda e: e.copy(out=KsT[:, g, sp, :], in_=self.psb[pb][:, 0:128]), r=[self.bps[pb]], w=[b_KV])
                    self.op("dve", lambda e: e.tensor_copy(out=Vs[:, g, sp, :], in_=ksel[k][:, 128:256]), r=[b_ks[k]], w=[b_KV])

            def small_attn(KT_, V_, bKV, nt, bias_ap, brx):
                ncol = 4 * nt * 4
                for g in range(4):
                    for t in range(nt):
                        c0 = (g * nt + t) * 4
                        self.op("pe", lambda e: e.matmul(self.psb[1][:, c0:c0 + 4], lhsT=KT_[:, g, t, :], rhs=qsT[:, 4 * g:4 * g + 4], start=True, stop=True),
                                r=[bKV, b_q], w=[self.bps[1]])
                oa("dve", lambda e: e.tensor_tensor(out=sTs[:, 0:ncol], in0=self.psb[1][:, 0:ncol], in1=bias_ap, op=ALU.add), [self.bps[1], b_s, self.b_BS])
                oa("act", lambda e: e.activation(out=pfs[:, 0:ncol], in_=sTs[:, 0:ncol], func=AF.Exp))
                self.op("pe", lambda e: e.matmul(self.psb[4][:, 0:ncol], lhsT=self.ones_f[:], rhs=pfs[:, 0:ncol], start=True, stop=True),
                        r=[b_a, self.b_const], w=[self.bps[4]])
                oa("dve", lambda e: e.tensor_reduce(out=tot[:].rearrange("p (g r) -> p g r", g=4),
                                                    in_=self.psb[4][:, 0:ncol].rearrange("p (g t r) -> p g r t", g=4, r=4), axis=AX.X, op=ALU.add), [self.bps[4]])
                oa("dve", lambda e: e.reciprocal(out=tot[:], in_=tot[:]))
                oa("dve", lambda e: e.tensor_tensor(out=pns[:, 0:ncol].rearrange("p (g t r) -> p g t r", g=4, r=4),
                                                    in0=pfs[:, 0:ncol].rearrange("p (g t r) -> p g t r", g=4, r=4),
                                                    in1=tot[:].rearrange("p (g r) -> p g r", g=4).unsqueeze(2).to_broadcast([128, 4, nt, 4]), op=ALU.mult))
                oa("act", lambda e: e.copy(out=pbs[:, 0:ncol], in_=pns[:, 0:ncol]))
                first = True
                for g in range(4):
                    for t in range(nt):
                        c0 = (g * nt + t) * 4
                        self.op("pe", lambda e: e.matmul(self.psb[3][0:4, g * 128:(g + 1) * 128], lhsT=pbs[:, c0:c0 + 4], rhs=V_[:, g, t, :],
                                                         start=first, stop=(t == nt - 1)), r=[b_a, bKV], w=[self.bps[3]])
                        first = False
                self.op("dve", lambda e: e.tensor_tensor(out=otmp[:], in0=self.psb[3][0:4, :].rearrange("p (g d) -> p g d", g=4),
                                                         in1=gsm[:, :, brx:brx + 1].to_broadcast([4, 4, 128]), op=ALU.mult), r=[self.bps[3], b_q, b_os], w=[b_os])
                self.op("dve", lambda e: e.tensor_tensor(out=osm[:], in0=osm[:], in1=otmp[:], op=ALU.add), r=[b_os], w=[b_os])

            small_attn(KsT, Vs, b_KV, 9, bsl[:].rearrange("p g t r -> p (g t r)"), 1)
            wld = [S("wld%d" % i, [128, 1024]) for i in range(2)]; b_wl = [Buf("wld") for _ in range(2)]
            KwT = S("KwT", [128, 4, 5, 128], BF16); Vw = S("Vw", [128, 4, 5, 128], BF16); b_KW = Buf("KVw")
            bwl = S("bwl", [128, 4, 5, 4])
            self.op("pool", lambda e: e.memset(KwT[:, :, 4, :], 0.0), w=[b_KW])
            self.op("pool", lambda e: e.memset(Vw[:, :, 4, :], 0.0), w=[b_KW])
            for g in range(4):
                self.dma(lambda e: e.dma_start(out=KwT[:, g, 4, 0:1], in_=self.KVT[2, g, 0, :, T:T + 1], allow_slow_non_contiguous=True),
                         r=[self.b_nsa_dr], w=[b_KW])
                self.dma(lambda e: e.dma_start(out=Vw[0:1, g, 4, :], in_=self.VTOK[2, g, T:T + 1, :]), r=[self.b_nsa_dr], w=[b_KW])
                osl("dve", lambda e: e.tensor_copy(out=bwl[:, g], in_=self.BS[:, 8:13, 4 * g:4 * g + 4]), [self.b_BS])
            for t in range(4):
                k = t % 2
                self.dma(lambda e: e.dma_start(out=wld[k][:], in_=self.cache_win[j, t * 128:(t + 1) * 128, :]), w=[b_wl[k]])
                for g in range(4):
                    pb = 4 + (g % 2)
                    self.op("pe", lambda e: e.transpose(out=self.psb[pb][:, 0:128], in_=wld[k][:, g * 256:g * 256 + 128], identity=self.ident[:]),
                            r=[b_wl[k], self.b_const], w=[self.bps[pb]])
                    self.op("act", lambda e: e.copy(out=KwT[:, g, t, :], in_=self.psb[pb][:, 0:128]), r=[self.bps[pb]], w=[b_KW])
                    self.op("dve", lambda e: e.tensor_copy(out=Vw[:, g, t, :], in_=wld[k][:, g * 256 + 128:(g + 1) * 256]), r=[b_wl[k]], w=[b_KW])
            small_attn(KwT, Vw, b_KW, 5, bwl[:].rearrange("p g t r -> p (g t r)"), 2)
            ots = S("ots", [128, 16, SW], BF16); b_ot = Buf("ots")
            self.op("pool", lambda e: e.memset(ots[:], 0.0), w=[b_ot])
            for g in range(4):
                self.op("pe", lambda e: e.transpose(out=self.psb[4][:, 4 * g:4 * g + 4], in_=osm[:, g, :], identity=self.ident[0:4, 0:4]),
                        r=[b_os, self.b_const], w=[self.bps[4]])
            self.op("dve", lambda e: e.tensor_copy(out=ots[:, :, 0:1], in_=self.psb[4][:, 0:16].unsqueeze(2)), r=[self.bps[4]], w=[b_ot])
            self.dma(lambda e: e.dma_start(out=self.OT.rearrange("h p t -> p h t")[:, :, T:T + SW], in_=ots[:]), r=[b_ot], w=[self.b_nsa_dr])
            self.tk.barrier()

    def build(self):
        self.declare_io()
        self.setup_consts()
        self.eps_rms = self.sb("eps_rms", [128, 1], F32)
        self.op("dve", lambda e: e.memset(self.eps_rms[:], RMS_EPS), w=[self.b_const])
        self.phase_input()
        for li in self.layers:
            if self.do_mixer:
                m = li % 3
                if m == 1:
                    self.conf_layer(li)
                elif m == 2:
                    self.s5_layer(li)
                else:
                    self.nsa_layer(li)
            if self.do_ffn:
                self.ffn_layer(li)
        self.phase_output()
        self.tk.barrier()
        self.es.close()
        return self.nc


_NC_CACHE = {}


def kernel(x_prompt, x_sample, cache_cmp_kv, cache_slc_kv, cache_win_kv, state_conv, state_ssm_re,
           state_ssm_im, state_ffn_conv, page_table, norm_gain, rel_bias, nsa_w_q, nsa_w_kv, nsa_cmp_pe,
           nsa_cmp_w1, nsa_cmp_w2, nsa_w_gate, nsa_w_o, conv_w_pw1, conv_dw, conv_dw_b, conv_ln_g, conv_ln_b,
           conv_w_pw2, ssm_a_re, ssm_a_im, ssm_log_dt, ssm_b_re, ssm_b_im, ssm_c_re, ssm_c_im, ssm_d,
           ssm_w_glu, ffn_w_up, ffn_dw, ffn_dw_b, ffn_w_down):
    A = lambda a: np.ascontiguousarray(np.asarray(a))
    x_prompt = A(x_prompt)
    B, T, _ = x_prompt.shape
    NS = x_sample.shape[0]
    nphys = cache_cmp_kv.shape[1]
    n_cores = 8
    key = (T, nphys)
    if key not in _NC_CACHE:
        _NC_CACHE[key] = Builder(T=T, nphys=nphys).build()
    nc = _NC_CACHE[key]
    f = np.float32
    shared = dict(
        norm_gain=A(norm_gain).reshape(16, D), rel_bias=A(rel_bias),
        nsa_w_q=A(nsa_w_q), nsa_w_kv=A(nsa_w_kv), nsa_cmp_pe=A(nsa_cmp_pe), nsa_cmp_w1=A(nsa_cmp_w1),
        nsa_cmp_w2=A(nsa_cmp_w2), nsa_w_gate=A(nsa_w_gate), nsa_w_o=A(nsa_w_o),
        conv_w_pw1=A(conv_w_pw1)[0], conv_dw=A(conv_dw)[0], conv_dw_b=A(conv_dw_b).reshape(1, D),
        conv_ln_g=A(conv_ln_g).reshape(1, D), conv_ln_b=A(conv_ln_b).reshape(1, D), conv_w_pw2=A(conv_w_pw2)[0],
        ssm_a_re=A(ssm_a_re)[0], ssm_a_im=A(ssm_a_im)[0], ssm_log_dt=A(ssm_log_dt).reshape(1, 128),
        ssm_b_re=A(ssm_b_re)[0], ssm_b_im=A(ssm_b_im)[0], ssm_c_re=A(ssm_c_re)[0], ssm_c_im=A(ssm_c_im)[0],
        ssm_d=A(ssm_d).reshape(1, D), ssm_w_glu=A(ssm_w_glu)[0],
        ffn_w_up=A(ffn_w_up), ffn_dw=A(ffn_dw).reshape(12, DFF), ffn_dw_b=A(ffn_dw_b), ffn_w_down=A(ffn_w_down),
        cache_cmp0=A(cache_cmp_kv)[0].reshape(nphys * 128, 1024), cache_cmp1=A(cache_cmp_kv)[1].reshape(nphys * 128, 1024),
        cache_slc0=A(cache_slc_kv)[0].reshape(nphys * 128 * 4, 256), cache_slc1=A(cache_slc_kv)[1].reshape(nphys * 128 * 4, 256),
    )
    shared.update(structural_consts())
    x_sample = A(x_sample); cache_win_kv = A(cache_win_kv); state_conv = A(state_conv)
    state_ssm_re = A(state_ssm_re); state_ssm_im = A(state_ssm_im); state_ffn_conv = A(state_ffn_conv)
    page_table = A(page_table).astype(np.int32)
    in_maps = []
    for c in range(n_cores):
        b = c % B
        s = c % NS
        m = dict(shared)
        m.update(dict(
            x_p=x_prompt[b], x_s=x_sample[s].reshape(1, D),
            cache_win=A(cache_win_kv[:, s]).reshape(2, 512, 1024),
            state_conv=A(state_conv[0, s]), state_ssm_re=A(state_ssm_re[0, s]), state_ssm_im=A(state_ssm_im[0, s]),
            state_ffn=A(state_ffn_conv[:, s]), page_table=page_table[s:s + 1],
        ))
        in_maps.append(m)
    res = run_bass_kernel_spmd(nc, in_maps, core_ids=list(range(n_cores)))
    R = res.results
    P = lambda name: np.stack([R[b][name] for b in range(B)], 0)
    Sm = lambda name: np.stack([R[s][name] for s in range(NS)], 0)
    WP = min(512, T)
    y_p = P("y_p")
    y_s = Sm("y_s")
    kvp = lambda n, rows: np.moveaxis(P(n), 0, 1).reshape(2, B, rows, 4, 2, 128)
    kvs = lambda n, rows: np.moveaxis(Sm(n), 0, 1).reshape(2, NS, rows, 4, 2, 128)
    outs = (
        y_p, y_s,
        kvp("cmp_kv_p", T), kvs("cmp_kv_s", 1), kvp("slc_kv_p", T), kvs("slc_kv_s", 1),
        kvp("win_kv_p", WP), kvs("win_kv_s", 512),
        P("conv_p")[None], Sm("conv_s")[None],
        P("ssm_re_p")[None], Sm("ssm_re_s")[None], P("ssm_im_p")[None], Sm("ssm_im_s")[None],
        np.moveaxis(P("ffn_p"), 0, 1), np.moveaxis(Sm("ffn_s"), 0, 1),
    )
    return tuple(np.ascontiguousarray(o.astype(np.float32)) for o in outs)
```

```python
import contextlib
import math
import numpy as np
import concourse.bass as bass
import concourse.mybir as mybir
from concourse.bass_utils import run_bass_kernel_spmd

F32 = mybir.dt.float32
BF16 = mybir.dt.bfloat16
I32 = mybir.dt.int32
U32 = mybir.dt.uint32
AF = mybir.ActivationFunctionType
ALU = mybir.AluOpType
AX = mybir.AxisListType

D = 2048
DC = 16
DFF = 5632
FC = 44
DEPTH = 4
SW = 8
EPOCH = 20000
RMS_EPS = 1e-6
LN_EPS = 1e-5


def _bucket(d):
    d = np.asarray(d, dtype=np.int64)
    n = np.maximum(d, 0)
    nf = np.maximum(n, 16).astype(np.float32)
    big = 16 + (np.log(nf / np.float32(16)) / np.float32(math.log(8)) * np.float32(16)).astype(np.int32)
    return np.where(n < 16, n, np.minimum(big, 31)).astype(np.int64)


def _onehot(d, valid):
    b = np.where(valid, _bucket(d), 32)
    oh = np.zeros((33,) + b.shape, np.float32)
    np.put_along_axis(oh, b[None], 1.0, axis=0)
    return oh


def structural_consts():
    c = {}
    n = np.arange(128)[:, None]
    q = np.arange(128)[None, :]
    kinds = []
    kinds.append(_onehot(q - n, (q - n) >= 0))
    kinds.append(_onehot(128 + q - n, np.ones((128, 128), bool)))
    kinds.append(_onehot(np.full((128, 128), 1000), np.ones((128, 128), bool)))
    kinds.append(_onehot(512 + q - n, n >= q))
    c["oh_pk"] = np.stack(kinds, 1).reshape(33, 4 * 128 * 128)
    m = np.arange(248)[:, None] - 120
    d = q - 16 * m - 31
    c["oh_cmp"] = _onehot(d, d >= 0).reshape(33, 248 * 128)
    p = np.arange(128)
    tiles = []
    for t in range(8):
        cc = t * 128 + p
        d = 16384 - 16 * cc - 15
        tiles.append(_onehot(d, cc >= 1))
    for t in range(4):
        idx = t * 128 + p
        tiles.append(_onehot(512 - idx, np.ones(128, bool)))
    tiles.append(_onehot(np.zeros(128, np.int64), p == 0))
    tiles.append(_onehot(128 - (p % 64), np.ones(128, bool)))
    tiles.append(_onehot(64 - (p % 64), np.ones(128, bool)))
    tiles.append(_onehot(np.full(128, 1000), np.ones(128, bool)))
    tiles.append(_onehot(np.zeros(128, np.int64), p == 0))
    c["oh_s"] = np.stack(tiles, 1).astype(np.float32)
    coef = np.array([1, 2, 2, 2, 1], np.float32)
    mp = np.zeros((128, 32), np.float32)
    for nn in range(127):
        for j in range(32):
            o = nn + 1 - 4 * j
            if 0 <= o <= 4:
                mp[nn, j] = coef[o]
    c["msel_p"] = mp
    am = np.zeros((8, 128, 32), np.float32)
    for i in range(8, 16):
        for ql in range(128):
            cur = (i * 128 + ql) // 64
            for j in range(32):
                if j > cur:
                    am[i - 8, ql, j] = -1e30
                elif j == 0 or j == cur or j == cur - 1:
                    am[i - 8, ql, j] = 1e4
    c["addmask_p"] = am
    ms = np.zeros((1024, 257), np.float32)
    for cc in range(1024):
        for j in range(257):
            o = cc - 4 * j
            if 0 <= o <= 4:
                ms[cc, j] = coef[o]
    c["msel_s"] = ms
    ams = np.zeros((4, 257), np.float32)
    ams[:, [0, 255, 256]] = 1e4
    c["addmask_s"] = ams
    ee = np.zeros((32, 16, 128), np.float32)
    for kt in range(16):
        for nn in range(128):
            ee[(kt * 128 + nn) // 64, kt, nn] = 1.0
    c["eexp"] = ee
    sel = np.zeros((2, 4, 128), np.float32)
    sel[0, :, :64] = 1.0
    sel[1, :, 64:] = 1.0
    c["selhalf"] = sel
    c["eye4"] = np.eye(4, dtype=np.float32)
    c["iota_pg"] = np.tile(np.arange(128, dtype=np.float32)[None, :], (128, 1))
    c["iota_row"] = (np.arange(128) % 64).astype(np.float32)[:, None].copy()
    c["iota_p"] = np.arange(128, dtype=np.float32)[:, None].copy()
    c["gcol"] = np.tile((np.arange(32) // 8).astype(np.float32)[None, :], (128, 1))
    return c


STRUCT_SHAPES = dict(oh_pk=[33, 65536], oh_cmp=[33, 248 * 128], oh_s=[33, 17, 128], msel_p=[128, 32],
                     addmask_p=[8, 128, 32], msel_s=[1024, 257], addmask_s=[4, 257], eexp=[32, 16, 128],
                     selhalf=[2, 4, 128], eye4=[4, 4], iota_pg=[128, 128], iota_row=[128, 1], iota_p=[128, 1], gcol=[128, 32])

class Buf:
    __slots__ = ("name", "last_w", "readers")

    def __init__(self, name):
        self.name = name
        self.last_w = None
        self.readers = []


class TK:
    def __init__(self, nc, es, same_engine_sync=True):
        self.nc = nc
        self.es = es
        self.eng = {"pe": nc.tensor, "dve": nc.vector, "act": nc.scalar, "pool": nc.gpsimd, "sp": nc.sync}
        self.cur = {}
        self.nsem = 0
        for q in self.eng:
            self.cur[q] = [self._newsem(q), 0]
        self.seen = {q: {} for q in self.eng}
        self.dpool = {}
        for q, n in (("sp", 20), ("act", 8), ("pool", 12)):
            self.dpool[q] = [[self._newsem("d" + q), 0] for _ in range(n)]
        self.dnext = {q: 0 for q in self.dpool}
        self.same = same_engine_sync
        self.ninstr = 0
        self.nwait = 0

    def _newsem(self, tag):
        self.nsem += 1
        return self.es.enter_context(self.nc.semaphore("s_%s_%d" % (tag, self.nsem)))

    def _wait(self, q, ev):
        sem, val, owner = ev
        key = id(sem)
        if self.seen[q].get(key, 0) >= val:
            return
        self.eng[q].wait_ge(sem, val)
        self.nwait += 1
        self.seen[q][key] = val

    def _deps(self, q, reads, writes, is_dma):
        evs = []
        for b in reads:
            if b.last_w is not None:
                evs.append(b.last_w)
        for b in writes:
            if b.last_w is not None:
                evs.append(b.last_w)
            evs.extend(b.readers)
        for ev in evs:
            if ev[2] == q and not is_dma:
                if q == "pe" or not self.same:
                    continue
            self._wait(q, ev)

    def _record(self, ev, reads, writes):
        for b in writes:
            b.last_w = ev
            b.readers = []
        for b in reads:
            b.readers = [e for e in b.readers if not (e[2] == ev[2] and e[0] is ev[0])] + [ev]

    def op(self, q, fn, reads=(), writes=()):
        self._deps(q, reads, writes, False)
        c = self.cur[q]
        if c[1] >= EPOCH:
            c[0] = self._newsem(q)
            c[1] = 0
        ins = fn(self.eng[q])
        c[1] += 1
        ins.then_inc(c[0], 1)
        ev = (c[0], c[1], q)
        self._record(ev, reads, writes)
        self.ninstr += 1
        return ev

    def dma(self, q, fn, reads=(), writes=()):
        self._deps(q, reads, writes, True)
        pool = self.dpool[q]
        i = self.dnext[q]
        self.dnext[q] = (i + 1) % len(pool)
        s = pool[i]
        if s[1] > 0:
            self._wait(q, (s[0], s[1], "dma" + q))
        ins = fn(self.eng[q])
        s[1] += 16
        ins.then_inc(s[0], 16)
        ev = (s[0], s[1], "dma" + q)
        self._record(ev, reads, writes)
        self.ninstr += 1
        return ev

    def barrier(self):
        evs = []
        for q, c in self.cur.items():
            if c[1] > 0:
                evs.append((c[0], c[1], q))
        for q, pool in self.dpool.items():
            for s in pool:
                if s[1] > 0:
                    evs.append((s[0], s[1], "dma" + q))
        for q in self.eng:
            for ev in evs:
                if ev[2] == q:
                    continue
                self._wait(q, ev)


class Builder:
    def __init__(self, T=2048, layers=(0, 1, 2, 3), do_mixer=True, do_ffn=True, dbg=False, nphys=1280):
        self.NPHYS = nphys
        self.T = T
        self.NT = T // 512
        self.TT = T + SW
        self.layers = layers
        self.do_mixer = do_mixer
        self.do_ffn = do_ffn
        self.dbg = dbg
        self.nc = bass.Bass("TRN2", target_bir_lowering=False)
        self.es = contextlib.ExitStack()
        self.tk = TK(self.nc, self.es)
        self.tiles = [(i * 512, 512) for i in range(self.NT)] + [(T, SW)]
        self.dq = 0
        self.wcache = {}

    def din(self, name, shape, dt=F32):
        return self.nc.dram_tensor(name, list(shape), dt, kind="ExternalInput").ap()

    def dout(self, name, shape, dt=F32):
        return self.nc.dram_tensor(name, list(shape), dt, kind="ExternalOutput").ap()

    def dscr(self, name, shape, dt=F32):
        return self.nc.dram_tensor(name, list(shape), dt, kind="Internal").ap()

    def sb(self, name, shape, dt=F32, stack=None):
        self.uid = getattr(self, "uid", 0) + 1
        return (stack or self.es).enter_context(self.nc.sbuf_tensor("%s_u%d" % (name, self.uid), list(shape), dt))

    def ps(self, name, shape, dt=F32, stack=None):
        return (stack or self.es).enter_context(self.nc.psum_tensor(name, list(shape), dt))

    def op(self, q, fn, r=(), w=()):
        return self.tk.op(q, fn, r, w)

    def dma(self, fn, r=(), w=(), q=None):
        if q is None:
            q = "sp"
        return self.tk.dma(q, fn, r, w)

    def declare_io(self):
        T = self.T
        d = self.din
        self.x_p = d("x_p", [T, D])
        self.x_s = d("x_s", [1, D])
        self.norm_gain = d("norm_gain", [DEPTH * 4, D])
        self.ffn_w_up = d("ffn_w_up", [DEPTH, D, 2 * DFF])
        self.ffn_dw = d("ffn_dw", [DEPTH * 3, DFF])
        self.ffn_dw_b = d("ffn_dw_b", [DEPTH, DFF])
        self.ffn_w_down = d("ffn_w_down", [DEPTH, DFF, D])
        self.state_ffn = d("state_ffn", [DEPTH, 2, DFF])
        self.conv_w_pw1 = d("conv_w_pw1", [D, 2 * D])
        self.conv_dw = d("conv_dw", [31, D])
        self.conv_dw_b = d("conv_dw_b", [1, D])
        self.conv_ln_g = d("conv_ln_g", [1, D])
        self.conv_ln_b = d("conv_ln_b", [1, D])
        self.conv_w_pw2 = d("conv_w_pw2", [D, D])
        self.state_conv = d("state_conv", [30, D])
        self.ssm_a_re = d("ssm_a_re", [128, 64])
        self.ssm_a_im = d("ssm_a_im", [128, 64])
        self.ssm_log_dt = d("ssm_log_dt", [1, 128])
        self.ssm_b_re = d("ssm_b_re", [128, 64, 16])
        self.ssm_b_im = d("ssm_b_im", [128, 64, 16])
        self.ssm_c_re = d("ssm_c_re", [128, 16, 64])
        self.ssm_c_im = d("ssm_c_im", [128, 16, 64])
        self.ssm_d = d("ssm_d", [1, D])
        self.ssm_w_glu = d("ssm_w_glu", [D, 2 * D])
        self.state_ssm_re = d("state_ssm_re", [128, 64])
        self.state_ssm_im = d("state_ssm_im", [128, 64])
        o = self.dout
        self.ssm_re_p = o("ssm_re_p", [128, 64])
        self.ssm_re_s = o("ssm_re_s", [128, 64])
        self.ssm_im_p = o("ssm_im_p", [128, 64])
        self.ssm_im_s = o("ssm_im_s", [128, 64])
        self.conv_p = o("conv_p", [30, D])
        self.conv_s = o("conv_s", [30, D])
        self.y_p = o("y_p", [T, D])
        self.y_s = o("y_s", [1, D])
        self.ffn_p = o("ffn_p", [DEPTH, 2, DFF])
        self.ffn_s = o("ffn_s", [DEPTH, 2, DFF])
        self.nsa_declare()
        self.XR = self.dscr("XR", [DC, 128, self.TT])
        self.bXR = [Buf("XR%d" % i) for i in range(len(self.tiles))]

    def setup_consts(self):
        tk = self.tk
        self.ident = self.sb("ident", [128, 128], F32)
        self.ident_b = self.sb("ident_b", [128, 128], BF16)
        self.ones_b = self.sb("ones_b", [128, 128], BF16)
        self.gains = self.sb("gains", [128, DEPTH * 4, DC], F32)
        self.b_const = Buf("const")
        nc = self.nc
        self.iot = self.sb("iot", [128, 128], F32)
        self.op("pool", lambda e: e.iota(self.iot[:], pattern=[[1, 128]], base=0, channel_multiplier=-1,
                                         allow_small_or_imprecise_dtypes=True), w=[self.b_const])
        self.op("dve", lambda e: e.tensor_scalar(out=self.ident[:], in0=self.iot[:], scalar1=0.0, scalar2=None,
                                                 op0=ALU.is_equal), r=[self.b_const], w=[self.b_const])
        self.op("dve", lambda e: e.tensor_copy(out=self.ident_b[:], in_=self.ident[:]), r=[self.b_const],
                w=[self.b_const])
        self.op("dve", lambda e: e.memset(self.ones_b[:], 1.0), w=[self.b_const])
        self.ones_f = self.sb("ones_f", [128, 128], F32)
        self.op("dve", lambda e: e.memset(self.ones_f[:], 1.0), w=[self.b_const])
        self.dma(lambda e: e.dma_start(out=self.gains[:], in_=self.norm_gain.rearrange("l (c p) -> p l c", p=128),
                                       allow_slow_non_contiguous=True), w=[self.b_const])
        self.psb = [self.ps("psb%d" % i, [128, 512], F32) for i in range(8)]
        self.bps = [Buf("ps%d" % i) for i in range(8)]

    def phase_input(self):
        with contextlib.ExitStack() as st:
            NB = 2
            tin = [self.sb("tin%d" % i, [128, D], F32, st) for i in range(NB)]
            tout = [self.sb("tout%d" % i, [128, DC, 128], F32, st) for i in range(NB)]
            btin = [Buf("tin") for _ in range(NB)]
            btout = [Buf("tout") for _ in range(NB)]
            XRv = self.XR.rearrange("c p t -> p c t")
            nblk = self.T // 128
            for b in range(nblk + 1):
                k = b % NB
                samp = (b == nblk)
                rows = 1 if samp else 128
                if samp:
                    self.op("pool", lambda e: e.memset(tin[k][:], 0.0), w=[btin[k]])
                    self.dma(lambda e: e.dma_start(out=tin[k][0:1, :], in_=self.x_s[0:1, :]), w=[btin[k]])
                else:
                    self.dma(lambda e: e.dma_start(out=tin[k][:], in_=self.x_p[b * 128:(b + 1) * 128, :]),
                             w=[btin[k]])
                for g in range(4):
                    pb = 4 + (g % 2)
                    for j in range(4):
                        c = g * 4 + j
                        self.op("pe", lambda e: e.transpose(out=self.psb[pb][:, j * 128:(j + 1) * 128],
                                                            in_=tin[k][:, c * 128:(c + 1) * 128],
                                                            identity=self.ident[:]),
                                r=[btin[k], self.b_const], w=[self.bps[pb]])
                    eng = "act" if g % 2 == 0 else "dve"
                    if eng == "act":
                        self.op("act", lambda e: e.copy(out=tout[k][:, g * 4:(g + 1) * 4, :],
                                                        in_=self.psb[pb][:].rearrange("p (j t) -> p j t", j=4)),
                                r=[self.bps[pb]], w=[btout[k]])
                    else:
                        self.op("dve", lambda e: e.tensor_copy(out=tout[k][:, g * 4:(g + 1) * 4, :],
                                                               in_=self.psb[pb][:].rearrange("p (j t) -> p j t", j=4)),
                                r=[self.bps[pb]], w=[btout[k]])
                if samp:
                    ti = len(self.tiles) - 1
                    self.dma(lambda e: e.dma_start(out=XRv[:, :, self.T:self.T + SW], in_=tout[k][:, :, 0:SW]),
                             r=[btout[k]], w=[self.bXR[ti]])
                else:
                    ti = b // 4
                    self.dma(lambda e: e.dma_start(out=XRv[:, :, b * 128:(b + 1) * 128], in_=tout[k][:]),
                             r=[btout[k]], w=[self.bXR[ti]])
            self.tk.barrier()

    def phase_output(self):
        with contextlib.ExitStack() as st:
            NB = 2
            tin = [self.sb("oin%d" % i, [128, DC, 128], F32, st) for i in range(NB)]
            tout = [self.sb("oout%d" % i, [128, D], F32, st) for i in range(NB)]
            btin = [Buf("oin") for _ in range(NB)]
            btout = [Buf("oout") for _ in range(NB)]
            XRv = self.XR.rearrange("c p t -> p c t")
            nblk = self.T // 128
            for b in range(nblk + 1):
                k = b % NB
                samp = (b == nblk)
                if samp:
                    ti = len(self.tiles) - 1
                    self.op("pool", lambda e: e.memset(tin[k][:], 0.0), w=[btin[k]])
                    self.dma(lambda e: e.dma_start(out=tin[k][:, :, 0:SW], in_=XRv[:, :, self.T:self.T + SW]),
                             r=[self.bXR[ti]], w=[btin[k]])
                else:
                    ti = b // 4
                    self.dma(lambda e: e.dma_start(out=tin[k][:], in_=XRv[:, :, b * 128:(b + 1) * 128]),
                             r=[self.bXR[ti]], w=[btin[k]])
                for g in range(4):
                    pb = 4 + (g % 2)
                    for j in range(4):
                        c = g * 4 + j
                        self.op("pe", lambda e: e.transpose(out=self.psb[pb][:, j * 128:(j + 1) * 128],
                                                            in_=tin[k][:, c, :], identity=self.ident[:]),
                                r=[btin[k], self.b_const], w=[self.bps[pb]])
                    if g % 2 == 0:
                        self.op("act", lambda e: e.copy(out=tout[k][:, g * 512:(g + 1) * 512], in_=self.psb[pb][:]),
                                r=[self.bps[pb]], w=[btout[k]])
                    else:
                        self.op("dve", lambda e: e.tensor_copy(out=tout[k][:, g * 512:(g + 1) * 512],
                                                               in_=self.psb[pb][:]),
                                r=[self.bps[pb]], w=[btout[k]])
                if samp:
                    self.dma(lambda e: e.dma_start(out=self.y_s[0:1, :], in_=tout[k][0:1, :]), r=[btout[k]])
                else:
                    self.dma(lambda e: e.dma_start(out=self.y_p[b * 128:(b + 1) * 128, :], in_=tout[k][:]),
                             r=[btout[k]])
            self.tk.barrier()

    def alloc_rowlocal(self, st):
        self.xr = self.sb("xr", [128, DC, 512], F32, st)
        self.b_xr = Buf("xr")
        self.xn = self.sb("xn", [128, DC, 512], BF16, st)
        self.b_xn = Buf("xn")
        self.yy = self.sb("yy", [128, DC, 512], F32, st)
        self.b_yy = Buf("yy")
        self.rstd = self.sb("rstd", [128, 512], F32, st)
        self.b_rstd = Buf("rstd")
        self.NWB = 4
        self.NWF = 2
        self.wf = [self.sb("wf%d" % i, [128, 8, 2, 128], F32, st) for i in range(self.NWF)]
        self.wb = [self.sb("wb%d" % i, [128, 8, 2, 128], BF16, st) for i in range(self.NWB)]
        self.b_wf = [Buf("wf") for _ in range(self.NWF)]
        self.b_wb = [Buf("wb") for _ in range(self.NWB)]
        self.wctr = 0
        self.wfctr = 0

    def load_xr(self, ti):
        t0, W = self.tiles[ti]
        XRv = self.XR.rearrange("c p t -> p c t")
        self.dma(lambda e: e.dma_start(out=self.xr[:, :, :W], in_=XRv[:, :, t0:t0 + W]),
                 r=[self.bXR[ti]], w=[self.b_xr])

    def store_xr(self, ti):
        t0, W = self.tiles[ti]
        XRv = self.XR.rearrange("c p t -> p c t")
        self.dma(lambda e: e.dma_start(out=XRv[:, :, t0:t0 + W], in_=self.xr[:, :, :W]),
                 r=[self.b_xr], w=[self.bXR[ti]])

    def rms_stats(self, src, bsrc, W, sq, bsq):
        self.op("act", lambda e: e.activation(out=sq[:, :, :W], in_=src[:, :, :W], func=AF.Square),
                r=[bsrc], w=[bsq])
        pb = 4
        for c in range(DC):
            self.op("pe", lambda e: e.matmul(self.psb[pb][:, :W], lhsT=self.ones_b[:], rhs=sq[:, c, :W],
                                             start=(c == 0), stop=(c == DC - 1)),
                    r=[bsq, self.b_const], w=[self.bps[pb]])
        self.op("act", lambda e: e.activation(out=self.rstd[:, :W], in_=self.psb[pb][:, :W], func=AF.Sqrt,
                                              bias=self.eps_rms[:, 0:1], scale=1.0 / D),
                r=[self.bps[pb], self.b_const], w=[self.b_rstd])
        self.op("dve", lambda e: e.reciprocal(out=self.rstd[:, :W], in_=self.rstd[:, :W]),
                r=[self.b_rstd], w=[self.b_rstd])

    def pre_norm(self, ti, gidx, xn_f32=None, b_xnf=None):
        t0, W = self.tiles[ti]
        self.load_xr(ti)
        self.rms_stats(self.xr, self.b_xr, W, self.xn, self.b_xn)
        for c in range(DC):
            if xn_f32 is not None:
                self.op("dve", lambda e: e.scalar_tensor_tensor(out=xn_f32[:, c, :W], in0=self.xr[:, c, :W],
                                                                scalar=self.gains[:, gidx, c:c + 1],
                                                                in1=self.rstd[:, :W], op0=ALU.mult, op1=ALU.mult),
                        r=[self.b_xr, self.b_rstd, self.b_const], w=[b_xnf])
                self.op("act", lambda e: e.copy(out=self.xn[:, c, :W], in_=xn_f32[:, c, :W]),
                        r=[b_xnf], w=[self.b_xn])
            else:
                self.op("dve", lambda e: e.scalar_tensor_tensor(out=self.xn[:, c, :W], in0=self.xr[:, c, :W],
                                                                scalar=self.gains[:, gidx, c:c + 1],
                                                                in1=self.rstd[:, :W], op0=ALU.mult, op1=ALU.mult),
                        r=[self.b_xr, self.b_rstd, self.b_const], w=[self.b_xn])

    def post_norm_residual(self, ti, gidx):
        t0, W = self.tiles[ti]
        self.rms_stats(self.yy, self.b_yy, W, self.xn, self.b_xn)
        for c in range(DC):
            self.op("dve", lambda e: e.scalar_tensor_tensor(out=self.yy[:, c, :W], in0=self.yy[:, c, :W],
                                                            scalar=self.gains[:, gidx, c:c + 1],
                                                            in1=self.rstd[:, :W], op0=ALU.mult, op1=ALU.mult),
                    r=[self.b_yy, self.b_rstd, self.b_const], w=[self.b_yy])
            self.op("pool", lambda e: e.tensor_tensor(out=self.xr[:, c, :W], in0=self.xr[:, c, :W],
                                                      in1=self.yy[:, c, :W], op=ALU.add),
                    r=[self.b_yy, self.b_xr], w=[self.b_xr])
        self.store_xr(ti)

    def gemm(self, wview, KC, npairs, xin, bxin, W, epilogue, wkey=None, first=True):
        nkb = (KC + 7) // 8
        cache = None
        if wkey is not None:
            if wkey not in self.wcache:
                self.wcache[wkey] = (self.dscr("wc_" + wkey, [npairs * nkb, 128, 2048], BF16), Buf("wc_" + wkey))
            cache, b_cache = self.wcache[wkey]
        for p in range(npairs):
            pa = (p % 2) * 2
            pbk = pa + 1
            for kb in range(nkb):
                k0 = kb * 8
                kn = min(8, KC - k0)
                i = self.wctr % self.NWB
                self.wctr += 1
                blk = p * nkb + kb
                if first or cache is None:
                    fi = self.wfctr % self.NWF
                    self.wfctr += 1
                    for h in range(2):
                        src = wview(p, k0, kn, h)
                        self.dma(lambda e: e.dma_start(out=self.wf[fi][:, :kn, h, :], in_=src), w=[self.b_wf[fi]], q="sp")
                    ceng = "pool" if (self.wfctr % 2 == 0) else "act"
                    if ceng == "pool":
                        self.op("pool", lambda e: e.tensor_copy(out=self.wb[i][:, :kn], in_=self.wf[fi][:, :kn]),
                                r=[self.b_wf[fi]], w=[self.b_wb[i]])
                    else:
                        self.op("act", lambda e: e.copy(out=self.wb[i][:, :kn], in_=self.wf[fi][:, :kn]),
                                r=[self.b_wf[fi]], w=[self.b_wb[i]])
                    if cache is not None:
                        self.dma(lambda e: e.dma_start(out=cache[blk, :, 0:kn * 256],
                                                       in_=self.wb[i][:, :kn].rearrange("p k h n -> p (k h n)")),
                                 r=[self.b_wb[i]], w=[b_cache], q=ceng)
                else:
                    self.dma(lambda e: e.dma_start(out=self.wb[i][:, :kn].rearrange("p k h n -> p (k h n)"),
                                                   in_=cache[blk, :, 0:kn * 256]), r=[b_cache], w=[self.b_wb[i]], q="sp")
                for kk in range(kn):
                    kc = k0 + kk
                    for h, pbank in ((0, pa), (1, pbk)):
                        self.op("pe", lambda e: e.matmul(self.psb[pbank][:, :W], lhsT=self.wb[i][:, kk, h, :],
                                                         rhs=xin[:, kc, :W], start=(kc == 0), stop=(kc == KC - 1)),
                                r=[self.b_wb[i], bxin], w=[self.bps[pbank]])
            epilogue(p, self.psb[pa], self.bps[pa], self.psb[pbk], self.bps[pbk])

    def ffn_layer(self, li):
        tk = self.tk
        with contextlib.ExitStack() as st:
            self.alloc_rowlocal(st)
            hh = self.sb("hh", [128, FC, 512], BF16, st)
            b_hh = Buf("hh")
            ghist = self.sb("ghist", [128, 2, FC], F32, st)
            b_gh = Buf("ghist")
            gbuf = [self.sb("gbuf%d" % i, [128, 514], F32, st) for i in range(2)]
            b_gb = [Buf("gbuf") for _ in range(2)]
            acc = [self.sb("acc%d" % i, [128, 512], F32, st) for i in range(2)]
            b_acc = [Buf("acc") for _ in range(2)]
            dwT = self.sb("dwT", [128, 3, FC], F32, st)
            dbT = self.sb("dbT", [128, FC], F32, st)
            b_dw = Buf("dw")
            hist_in = self.sb("hist_in", [88, 128], F32, st)
            b_hi = Buf("hi")
            hsT = self.sb("hsT", [128, 2, FC], F32, st)
            b_hs = Buf("hsT")
            hout = self.sb("hout", [88, 128], F32, st)
            b_ho = Buf("hout")
            self.dma(lambda e: e.dma_start(out=dwT[:], in_=self.ffn_dw[li * 3:(li + 1) * 3, :]
                                           .rearrange("w (c p) -> p w c", p=128), allow_slow_non_contiguous=True),
                     w=[b_dw])
            self.dma(lambda e: e.dma_start(out=dbT[:], in_=self.ffn_dw_b[li:li + 1, :]
                                           .rearrange("o (c p) -> p (o c)", p=128), allow_slow_non_contiguous=True),
                     w=[b_dw])
            self.dma(lambda e: e.dma_start(out=hist_in[:], in_=self.state_ffn[li].rearrange("t (c p) -> (t c) p", p=128)),
                     w=[b_hi])
            self.op("pe", lambda e: e.transpose(out=self.psb[5][:, 0:88], in_=hist_in[:], identity=self.ident[0:88, 0:88]),
                    r=[b_hi, self.b_const], w=[self.bps[5]])
            self.op("dve", lambda e: e.tensor_copy(out=hsT[:].rearrange("p t c -> p (t c)"), in_=self.psb[5][:, 0:88]),
                    r=[self.bps[5]], w=[b_hs])
            self.op("dve", lambda e: e.memset(ghist[:], 0.0), w=[b_gh])

            w_up = self.ffn_w_up[li].rearrange("(kc p) (h f) -> p kc h f", p=128, h=2)
            w_dn = self.ffn_w_down[li].rearrange("(kc p) (f h n) -> p kc f h n", p=128, h=2, n=128)

            for ti in range(len(self.tiles)):
                t0, W = self.tiles[ti]
                samp = (ti == len(self.tiles) - 1)
                self.pre_norm(ti, li * 4 + 2)
                if samp:
                    self.op("dve", lambda e: e.tensor_copy(out=ghist[:], in_=hsT[:]), r=[b_hs], w=[b_gh])

                def ep_up(p, psA, bA, psB, bB):
                    k = p % 2
                    self.op("act", lambda e: e.copy(out=gbuf[k][:, 2:2 + W], in_=psA[:, :W]), r=[bA], w=[b_gb[k]])
                    self.op("dve", lambda e: e.tensor_copy(out=gbuf[k][:, 0:2], in_=ghist[:, :, p]),
                            r=[b_gh], w=[b_gb[k]])
                    self.op("act", lambda e: e.activation(out=acc[k][:, :W], in_=gbuf[k][:, 2:2 + W], func=AF.Identity,
                                                          bias=dbT[:, p:p + 1], scale=dwT[:, 2, p:p + 1]),
                            r=[b_gb[k], b_dw], w=[b_acc[k]])
                    self.op("dve", lambda e: e.scalar_tensor_tensor(out=acc[k][:, :W], in0=gbuf[k][:, 1:1 + W],
                                                                    scalar=dwT[:, 1, p:p + 1], in1=acc[k][:, :W],
                                                                    op0=ALU.mult, op1=ALU.add),
                            r=[b_gb[k], b_dw, b_acc[k]], w=[b_acc[k]])
                    self.op("dve", lambda e: e.scalar_tensor_tensor(out=acc[k][:, :W], in0=gbuf[k][:, 0:W],
                                                                    scalar=dwT[:, 0, p:p + 1], in1=acc[k][:, :W],
                                                                    op0=ALU.mult, op1=ALU.add),
                            r=[b_gb[k], b_dw, b_acc[k]], w=[b_acc[k]])
                    if samp:
                        self.op("act", lambda e: e.copy(out=ghist[:, 0, p:p + 1], in_=gbuf[k][:, 1:2]),
                                r=[b_gb[k]], w=[b_gh])
                        self.op("act", lambda e: e.copy(out=ghist[:, 1, p:p + 1], in_=gbuf[k][:, 2:3]),
                                r=[b_gb[k]], w=[b_gh])
                    else:
                        self.op("act", lambda e: e.copy(out=ghist[:, :, p], in_=gbuf[k][:, W:W + 2]),
                                r=[b_gb[k]], w=[b_gh])
                    self.op("act", lambda e: e.activation(out=acc[k][:, :W], in_=acc[k][:, :W],
                                                          func=AF.Gelu_apprx_tanh),
                            r=[b_acc[k]], w=[b_acc[k]])
                    self.op("dve", lambda e: e.tensor_tensor(out=hh[:, p, :W], in0=acc[k][:, :W], in1=psB[:, :W],
                                                             op=ALU.mult),
                            r=[b_acc[k], bB], w=[b_hh])

                self.gemm(lambda p, k0, kn, h: w_up[:, k0:k0 + kn, h, p * 128:(p + 1) * 128], DC, FC,
                          self.xn, self.b_xn, W, ep_up, wkey='up%d' % li, first=(ti == 0))

                def ep_dn(p, psA, bA, psB, bB):
                    self.op("act", lambda e: e.copy(out=self.yy[:, 2 * p, :W], in_=psA[:, :W]), r=[bA], w=[self.b_yy])
                    self.op("dve", lambda e: e.tensor_copy(out=self.yy[:, 2 * p + 1, :W], in_=psB[:, :W]), r=[bB],
                            w=[self.b_yy])

                self.gemm(lambda p, k0, kn, h: w_dn[:, k0:k0 + kn, p, h, :], FC, DC // 2, hh, b_hh, W, ep_dn, wkey='dn%d' % li, first=(ti == 0))
                self.post_norm_residual(ti, li * 4 + 3)

                if ti == self.NT - 1 or samp:
                    dst = self.ffn_s if samp else self.ffn_p
                    self.op("pe", lambda e: e.transpose(out=self.psb[5][0:88, 0:128],
                                                        in_=ghist[:].rearrange("p t c -> p (t c)"),
                                                        identity=self.ident[:]),
                            r=[b_gh, self.b_const], w=[self.bps[5]])
                    self.op("dve", lambda e: e.tensor_copy(out=hout[:], in_=self.psb[5][0:88, 0:128]),
                            r=[self.bps[5]], w=[b_ho])
                    self.dma(lambda e: e.dma_start(out=dst[li].rearrange("t (c p) -> (t c) p", p=128), in_=hout[:]),
                             r=[b_ho])
            self.tk.barrier()

    def conf_layer(self, li):
        with contextlib.ExitStack() as st:
            self.alloc_rowlocal(st)
            CW = 31
            ubuf = self.sb("ubuf", [128, DC, 30 + 512], F32, st)
            b_ub = Buf("ubuf")
            sig = [self.sb("sig%d" % i, [128, 512], F32, st) for i in range(2)]
            b_sig = [Buf("sig") for _ in range(2)]
            dwT = self.sb("cdwT", [128, CW, DC], F32, st)
            prm = self.sb("cprm", [128, 3, DC], F32, st)
            b_dw = Buf("cdw")
            mean = self.sb("mean", [128, 512], F32, st)
            b_mean = Buf("mean")
            msq = self.sb("msq", [128, 512], F32, st)
            b_msq = Buf("msq")
            hio = self.sb("hio", [30, D], F32, st)
            b_hio = Buf("hio")
            cacc = self.sb("cacc", [128, 512], F32, st); b_cacc = Buf("cacc")
            ctmp = [self.sb("ctmp%d" % i, [128, 512], F32, st) for i in range(2)]; b_ctmp = [Buf("ctmp") for _ in range(2)]
            eps_ln = self.sb("eps_ln", [128, 1], F32, st)
            self.op("dve", lambda e: e.memset(eps_ln[:], LN_EPS), w=[b_dw])
            self.dma(lambda e: e.dma_start(out=dwT[:], in_=self.conv_dw.rearrange("w (c p) -> p w c", p=128),
                                           allow_slow_non_contiguous=True), w=[b_dw])
            for k, src in enumerate((self.conv_dw_b, self.conv_ln_g, self.conv_ln_b)):
                self.dma(lambda e: e.dma_start(out=prm[:, k, :], in_=src.rearrange("o (c p) -> p (o c)", p=128),
                                               allow_slow_non_contiguous=True), w=[b_dw])
            self.op("dve", lambda e: e.memset(ubuf[:, :, 0:30], 0.0), w=[b_ub])
            w1 = self.conv_w_pw1.rearrange("(kc p) (h f) -> p kc h f", p=128, h=2)
            w2 = self.conv_w_pw2.rearrange("(kc p) (f h n) -> p kc f h n", p=128, h=2, n=128)
            hc = self.yy
            b_hc = self.b_yy
            for ti in range(len(self.tiles)):
                t0, W = self.tiles[ti]
                samp = (ti == len(self.tiles) - 1)
                self.pre_norm(ti, li * 4 + 0)
                if samp:
                    self.dma(lambda e: e.dma_start(out=hio[:], in_=self.state_conv[:, :]), w=[b_hio])
                    for c in range(DC):
                        pb = 4 + (c % 2)
                        self.op("pe", lambda e: e.transpose(out=self.psb[pb][:, 0:30], in_=hio[:, c * 128:(c + 1) * 128],
                                                            identity=self.ident[0:30, 0:30]),
                                r=[b_hio, self.b_const], w=[self.bps[pb]])
                        self.op("dve", lambda e: e.tensor_copy(out=ubuf[:, c, 0:30], in_=self.psb[pb][:, 0:30]),
                                r=[self.bps[pb]], w=[b_ub])

                def ep1(p, psA, bA, psB, bB):
                    k = p % 2
                    self.op("act", lambda e: e.activation(out=sig[k][:, :W], in_=psB[:, :W], func=AF.Sigmoid),
                            r=[bB], w=[b_sig[k]])
                    self.op("dve", lambda e: e.tensor_tensor(out=ubuf[:, p, 30:30 + W], in0=psA[:, :W],
                                                             in1=sig[k][:, :W], op=ALU.mult),
                            r=[bA, b_sig[k]], w=[b_ub])

                self.gemm(lambda p, k0, kn, h: w1[:, k0:k0 + kn, h, p * 128:(p + 1) * 128], DC, DC,
                          self.xn, self.b_xn, W, ep1, wkey='pw1', first=(ti == 0))
                for c in range(DC):
                    self.op("act", lambda e: e.activation(out=hc[:, c, :W], in_=ubuf[:, c, 30:30 + W], func=AF.Identity,
                                                          bias=prm[:, 0, c:c + 1], scale=dwT[:, 30, c:c + 1]),
                            r=[b_ub, b_dw], w=[b_hc])
                    for w in range(20):
                        self.op("dve", lambda e: e.scalar_tensor_tensor(out=hc[:, c, :W], in0=ubuf[:, c, w:w + W],
                                                                        scalar=dwT[:, w, c:c + 1], in1=hc[:, c, :W],
                                                                        op0=ALU.mult, op1=ALU.add),
                                r=[b_ub, b_dw, b_hc], w=[b_hc])
                    self.op("act", lambda e: e.activation(out=cacc[:, :W], in_=ubuf[:, c, 20:20 + W], func=AF.Copy,
                                                          scale=dwT[:, 20, c:c + 1]), r=[b_ub, b_dw], w=[b_cacc])
                    for w in range(21, 30):
                        kx = w % 2
                        self.op("act", lambda e: e.activation(out=ctmp[kx][:, :W], in_=ubuf[:, c, w:w + W], func=AF.Copy,
                                                              scale=dwT[:, w, c:c + 1]), r=[b_ub, b_dw], w=[b_ctmp[kx]])
                        self.op("pool", lambda e: e.tensor_tensor(out=cacc[:, :W], in0=cacc[:, :W], in1=ctmp[kx][:, :W], op=ALU.add),
                                r=[b_ctmp[kx], b_cacc], w=[b_cacc])
                    self.op("pool", lambda e: e.tensor_tensor(out=hc[:, c, :W], in0=hc[:, c, :W], in1=cacc[:, :W], op=ALU.add),
                            r=[b_cacc, b_hc], w=[b_hc])
                if ti == self.NT - 1 or samp:
                    c0 = 1 if samp else W
                    for c in range(DC):
                        pb = 4 + (c % 2)
                        self.op("pe", lambda e: e.transpose(out=self.psb[pb][0:30, 0:128], in_=ubuf[:, c, c0:c0 + 30],
                                                            identity=self.ident[:]),
                                r=[b_ub, self.b_const], w=[self.bps[pb]])
                        self.op("act", lambda e: e.copy(out=hio[:, c * 128:(c + 1) * 128], in_=self.psb[pb][0:30, 0:128]),
                                r=[self.bps[pb]], w=[b_hio])
                    dst = self.conv_s if samp else self.conv_p
                    self.dma(lambda e: e.dma_start(out=dst[:, :], in_=hio[:]), r=[b_hio])
                if not samp:
                    self.op("pool", lambda e: e.tensor_copy(out=ubuf[:, :, 0:30], in_=ubuf[:, :, W:W + 30]),
                            r=[b_ub], w=[b_ub])
                self.op("act", lambda e: e.activation(out=self.xn[:, :, :W], in_=hc[:, :, :W], func=AF.Square),
                        r=[b_hc], w=[self.b_xn])
                for c in range(DC):
                    self.op("pe", lambda e: e.matmul(self.psb[4][:, :W], lhsT=self.ones_f[:], rhs=hc[:, c, :W],
                                                     start=(c == 0), stop=(c == DC - 1)),
                            r=[b_hc, self.b_const], w=[self.bps[4]])
                for c in range(DC):
                    self.op("pe", lambda e: e.matmul(self.psb[5][:, :W], lhsT=self.ones_b[:], rhs=self.xn[:, c, :W],
                                                     start=(c == 0), stop=(c == DC - 1)),
                            r=[self.b_xn, self.b_const], w=[self.bps[5]])
                self.op("act", lambda e: e.activation(out=mean[:, :W], in_=self.psb[4][:, :W], func=AF.Copy,
                                                      scale=1.0 / D), r=[self.bps[4]], w=[b_mean])
                self.op("dve", lambda e: e.tensor_tensor(out=msq[:, :W], in0=mean[:, :W], in1=mean[:, :W], op=ALU.mult),
                        r=[b_mean], w=[b_msq])
                self.op("dve", lambda e: e.scalar_tensor_tensor(out=msq[:, :W], in0=self.psb[5][:, :W], scalar=1.0 / D,
                                                                in1=msq[:, :W], op0=ALU.mult, op1=ALU.subtract),
                        r=[self.bps[5], b_msq], w=[b_msq])
                self.op("act", lambda e: e.activation(out=self.rstd[:, :W], in_=msq[:, :W], func=AF.Sqrt,
                                                      bias=eps_ln[:, 0:1], scale=1.0), r=[b_msq, b_dw], w=[self.b_rstd])
                self.op("dve", lambda e: e.reciprocal(out=self.rstd[:, :W], in_=self.rstd[:, :W]),
                        r=[self.b_rstd], w=[self.b_rstd])
                for c in range(DC):
                    self.op("pool", lambda e: e.tensor_tensor(out=hc[:, c, :W], in0=hc[:, c, :W], in1=mean[:, :W],
                                                              op=ALU.subtract), r=[b_hc, b_mean], w=[b_hc])
                    self.op("dve", lambda e: e.tensor_tensor(out=hc[:, c, :W], in0=hc[:, c, :W], in1=self.rstd[:, :W],
                                                             op=ALU.mult), r=[b_hc, self.b_rstd], w=[b_hc])
                    self.op("act", lambda e: e.activation(out=self.xn[:, c, :W], in_=hc[:, c, :W], func=AF.Silu,
                                                          bias=prm[:, 2, c:c + 1], scale=prm[:, 1, c:c + 1]),
                            r=[b_hc, b_dw], w=[self.b_xn])

                def ep2(p, psA, bA, psB, bB):
                    self.op("act", lambda e: e.copy(out=self.yy[:, 2 * p, :W], in_=psA[:, :W]), r=[bA], w=[self.b_yy])
                    self.op("dve", lambda e: e.tensor_copy(out=self.yy[:, 2 * p + 1, :W], in_=psB[:, :W]), r=[bB],
                            w=[self.b_yy])

                self.gemm(lambda p, k0, kn, h: w2[:, k0:k0 + kn, p, h, :], DC, DC // 2, self.xn, self.b_xn, W, ep2, wkey='pw2', first=(ti == 0))
                self.post_norm_residual(ti, li * 4 + 1)
            self.tk.barrier()

    def nat_to_scan(self, src_dram_flat, dst, b_dst, tmp64, b_tmp):
        self.dma(lambda e: e.dma_start(out=tmp64[:], in_=src_dram_flat.rearrange("(s g2) p -> s (g2 p)", g2=2)),
                 w=[b_tmp])
        self.op("pe", lambda e: e.transpose(out=self.psb[5][:, 0:64], in_=tmp64[:], identity=self.ident[0:64, 0:64]),
                r=[b_tmp, self.b_const], w=[self.bps[5]])
        self.op("dve", lambda e: e.tensor_copy(out=dst[:], in_=self.psb[5][:, 0:64]), r=[self.bps[5]], w=[b_dst])

    def s5_layer(self, li):
        TWO_PI = 2.0 * math.pi
        with contextlib.ExitStack() as st:
            self.alloc_rowlocal(st)
            rho = self.sb("rho", [128, 64], F32, st)
            c1 = self.sb("c1", [128, 64], F32, st)
            s1 = self.sb("s1", [128, 64], F32, st)
            hpr = self.sb("hpr", [128, 64], F32, st)
            hpi = self.sb("hpi", [128, 64], F32, st)
            k0r = self.sb("k0r", [128, 64], F32, st)
            k0i = self.sb("k0i", [128, 64], F32, st)
            ktm = self.sb("ktm", [128, 64], F32, st)
            dsk = self.sb("dsk", [128, DC], F32, st)
            b_prm = Buf("s5prm")
            b_hp = Buf("hp")
            b_k0 = Buf("k0")
            tmp64 = self.sb("tmp64", [64, 128], F32, st)
            b_t64 = Buf("t64")
            BBs = self.dscr("BBs", [2, 128, 16, 64])
            CCs = self.dscr("CCs", [2, 128, 64, 16])
            PRs = self.dscr("PRs", [3, 128, 64])
            WBs = self.dscr("WBs", [2, 16, 128, 512], BF16)
            WCs = self.dscr("WCs", [2, 16, 128, 512], BF16)
            TAB = self.dscr("TAB", [2, 64, 128, 512])
            b_dr = Buf("s5dram")
            self.dma(lambda e: e.dma_start(out=dsk[:], in_=self.ssm_d.rearrange("o (c p) -> p (o c)", p=128),
                                           allow_slow_non_contiguous=True), w=[b_prm])
            with contextlib.ExitStack() as st2:
                def T2(name, shape, dt=F32):
                    return self.sb(name, shape, dt, st2)
                ar = T2("p_ar", [128, 64]); ai = T2("p_ai", [128, 64]); ldt = T2("p_ldt", [128, 1])
                dt_ = T2("p_dt", [128, 1]); mag = T2("p_mag", [128, 64]); th = T2("p_th", [128, 64])
                r0 = T2("p_r0", [128, 64]); ri_ = T2("p_ri", [128, 64], I32); rf = T2("p_rf", [128, 64])
                m1 = T2("p_m1", [128, 64]); m2 = T2("p_m2", [128, 64])
                cs = T2("p_cs", [128, 64]); sn = T2("p_sn", [128, 64])
                abr = T2("p_abr", [128, 64]); abi = T2("p_abi", [128, 64]); den = T2("p_den", [128, 64])
                cfr = T2("p_cfr", [128, 64]); cfi = T2("p_cfi", [128, 64])
                bre = T2("p_bre", [128, 64, 16]); bim = T2("p_bim", [128, 64, 16])
                bt1 = T2("p_bt1", [128, 64, 16]); bt2 = T2("p_bt2", [128, 64, 16])
                bbT = T2("p_bbT", [128, 16, 64])
                cin = T2("p_cin", [128, 16, 64]); ccT = T2("p_ccT", [128, 64, 16])
                bp = Buf("prep")
                V = "dve"
                self.dma(lambda e: e.dma_start(out=ar[:], in_=self.ssm_a_re[:, :]), w=[bp])
                self.dma(lambda e: e.dma_start(out=ai[:], in_=self.ssm_a_im[:, :]), w=[bp])
                self.dma(lambda e: e.dma_start(out=ldt[:], in_=self.ssm_log_dt.rearrange("o g -> g o"),
                                               allow_slow_non_contiguous=True), w=[bp])
                self.dma(lambda e: e.dma_start(out=bre[:], in_=self.ssm_b_re[:, :, :]), w=[bp])
                self.dma(lambda e: e.dma_start(out=bim[:], in_=self.ssm_b_im[:, :, :]), w=[bp])
                o = lambda q, f: self.op(q, f, r=[bp], w=[bp])
                o("act", lambda e: e.activation(out=dt_[:], in_=ldt[:], func=AF.Exp))
                o(V, lambda e: e.tensor_scalar(out=mag[:], in0=ar[:], scalar1=dt_[:, 0:1], scalar2=None, op0=ALU.mult))
                o("act", lambda e: e.activation(out=mag[:], in_=mag[:], func=AF.Exp))
                o(V, lambda e: e.tensor_scalar(out=th[:], in0=ai[:], scalar1=dt_[:, 0:1], scalar2=1.0 / TWO_PI,
                                               op0=ALU.mult, op1=ALU.mult))

                def sin_turns(dst, off):
                    o(V, lambda e: e.tensor_scalar(out=r0[:], in0=th[:], scalar1=off, scalar2=None, op0=ALU.add))
                    o(V, lambda e: e.tensor_copy(out=ri_[:], in_=r0[:]))
                    o(V, lambda e: e.tensor_copy(out=rf[:], in_=ri_[:]))
                    o(V, lambda e: e.tensor_tensor(out=r0[:], in0=r0[:], in1=rf[:], op=ALU.subtract))
                    o(V, lambda e: e.tensor_scalar(out=m1[:], in0=r0[:], scalar1=0.5, scalar2=None, op0=ALU.is_gt))
                    o(V, lambda e: e.tensor_scalar(out=m2[:], in0=r0[:], scalar1=-0.5, scalar2=None, op0=ALU.is_lt))
                    o(V, lambda e: e.tensor_tensor(out=r0[:], in0=r0[:], in1=m1[:], op=ALU.subtract))
                    o(V, lambda e: e.tensor_tensor(out=r0[:], in0=r0[:], in1=m2[:], op=ALU.add))
                    o("act", lambda e: e.activation(out=dst[:], in_=r0[:], func=AF.Sin, scale=TWO_PI))

                sin_turns(sn, 0.0)
                sin_turns(cs, 0.25)
                o(V, lambda e: e.tensor_tensor(out=abr[:], in0=mag[:], in1=cs[:], op=ALU.mult))
                o(V, lambda e: e.tensor_tensor(out=abi[:], in0=mag[:], in1=sn[:], op=ALU.mult))
                o(V, lambda e: e.tensor_tensor(out=den[:], in0=ar[:], in1=ar[:], op=ALU.mult))
                o(V, lambda e: e.tensor_tensor(out=m1[:], in0=ai[:], in1=ai[:], op=ALU.mult))
                o(V, lambda e: e.tensor_tensor(out=den[:], in0=den[:], in1=m1[:], op=ALU.add))
                o(V, lambda e: e.reciprocal(out=den[:], in_=den[:]))
                o(V, lambda e: e.tensor_scalar(out=m2[:], in0=abr[:], scalar1=-1.0, scalar2=None, op0=ALU.add))
                o(V, lambda e: e.tensor_tensor(out=cfr[:], in0=m2[:], in1=ar[:], op=ALU.mult))
                o(V, lambda e: e.tensor_tensor(out=m1[:], in0=abi[:], in1=ai[:], op=ALU.mult))
                o(V, lambda e: e.tensor_tensor(out=cfr[:], in0=cfr[:], in1=m1[:], op=ALU.add))
                o(V, lambda e: e.tensor_tensor(out=cfr[:], in0=cfr[:], in1=den[:], op=ALU.mult))
                o(V, lambda e: e.tensor_tensor(out=cfi[:], in0=abi[:], in1=ar[:], op=ALU.mult))
                o(V, lambda e: e.tensor_tensor(out=m1[:], in0=m2[:], in1=ai[:], op=ALU.mult))
                o(V, lambda e: e.tensor_tensor(out=cfi[:], in0=cfi[:], in1=m1[:], op=ALU.subtract))
                o(V, lambda e: e.tensor_tensor(out=cfi[:], in0=cfi[:], in1=den[:], op=ALU.mult))
                for k, src in enumerate((mag, cs, sn)):
                    self.dma(lambda e: e.dma_start(out=PRs[k], in_=src[:]), r=[bp], w=[b_dr])
                self.tk.barrier()
                for k, dst in enumerate((rho, c1, s1)):
                    self.nat_to_scan(PRs[k], dst, b_prm, tmp64, b_t64)
                cfr_b = cfr[:].unsqueeze(2).to_broadcast([128, 64, 16])
                cfi_b = cfi[:].unsqueeze(2).to_broadcast([128, 64, 16])
                for rix in range(2):
                    if rix == 0:
                        o(V, lambda e: e.tensor_tensor(out=bt1[:], in0=bre[:], in1=cfr_b, op=ALU.mult))
                        o(V, lambda e: e.tensor_tensor(out=bt2[:], in0=bim[:], in1=cfi_b, op=ALU.mult))
                        o(V, lambda e: e.tensor_tensor(out=bbT[:].rearrange("g c p -> g p c"), in0=bt1[:], in1=bt2[:],
                                                       op=ALU.subtract))
                    else:
                        o(V, lambda e: e.tensor_tensor(out=bt1[:], in0=bim[:], in1=cfr_b, op=ALU.mult))
                        o(V, lambda e: e.tensor_tensor(out=bt2[:], in0=bre[:], in1=cfi_b, op=ALU.mult))
                        o(V, lambda e: e.tensor_tensor(out=bbT[:].rearrange("g c p -> g p c"), in0=bt1[:], in1=bt2[:],
                                                       op=ALU.add))
                    self.dma(lambda e: e.dma_start(out=BBs[rix], in_=bbT[:]), r=[bp], w=[b_dr])
                    self.tk.barrier()
                for rix, src in enumerate((self.ssm_c_re, self.ssm_c_im)):
                    self.dma(lambda e: e.dma_start(out=cin[:], in_=src[:, :, :]), w=[bp])
                    o("act", lambda e: e.activation(out=ccT[:].rearrange("g p c -> g c p"), in_=cin[:], func=AF.Copy,
                                                    scale=(1.0 if rix == 0 else -1.0)))
                    self.dma(lambda e: e.dma_start(out=CCs[rix], in_=ccT[:]), r=[bp], w=[b_dr])
                    self.tk.barrier()
            self.tk.barrier()
            with contextlib.ExitStack() as st2:
                def T2(name, shape, dt=F32):
                    return self.sb(name, shape, dt, st2)
                bp = Buf("prep2")
                o = lambda q, f: self.op(q, f, r=[bp], w=[bp])
                V = "dve"
                bbl = T2("q_bbl", [128, 16, 64]); ccl = T2("q_ccl", [128, 64, 16])
                mkB = T2("q_mkB", [128, 8]); mkC = T2("q_mkC", [128, 4, 8]); mt = T2("q_mt", [128, 4, 8])
                wbt = T2("q_wbt", [128, 8, 64], BF16); wct = T2("q_wct", [128, 4, 8, 16], BF16)
                b_wt = Buf("wt")
                o("pool", lambda e: e.iota(mkB[:], pattern=[[-16, 8]], base=0, channel_multiplier=1,
                                           allow_small_or_imprecise_dtypes=True))
                o(V, lambda e: e.tensor_scalar(out=mt[:, 0, :], in0=mkB[:], scalar1=0.0, scalar2=None, op0=ALU.is_ge))
                o(V, lambda e: e.tensor_scalar(out=mkB[:], in0=mkB[:], scalar1=15.0, scalar2=None, op0=ALU.is_le))
                o(V, lambda e: e.tensor_tensor(out=mkB[:], in0=mkB[:], in1=mt[:, 0, :], op=ALU.mult))
                o("pool", lambda e: e.iota(mkC[:], pattern=[[-128, 4], [64, 8]], base=0, channel_multiplier=-1,
                                           allow_small_or_imprecise_dtypes=True))
                o(V, lambda e: e.tensor_scalar(out=mt[:], in0=mkC[:], scalar1=-63.0, scalar2=None, op0=ALU.is_ge))
                o(V, lambda e: e.tensor_scalar(out=mkC[:], in0=mkC[:], scalar1=0.0, scalar2=None, op0=ALU.is_le))
                o(V, lambda e: e.tensor_tensor(out=mkC[:], in0=mkC[:], in1=mt[:], op=ALU.mult))
                for rix in range(2):
                    self.dma(lambda e: e.dma_start(out=bbl[:], in_=BBs[rix].rearrange("(k g8) c p -> (g8 c) k p", g8=8)),
                             r=[b_dr], w=[bp])
                    self.dma(lambda e: e.dma_start(out=ccl[:], in_=CCs[rix].rearrange("(s g2) p c -> (g2 p) s c", g2=2)),
                             r=[b_dr], w=[bp])
                    for k in range(16):
                        self.op(V, lambda e: e.tensor_tensor(out=wbt[:], in0=bbl[:, k, :].unsqueeze(1).to_broadcast([128, 8, 64]),
                                                             in1=mkB[:].unsqueeze(2).to_broadcast([128, 8, 64]), op=ALU.mult),
                                r=[bp], w=[b_wt])
                        self.dma(lambda e: e.dma_start(out=WBs[rix, k], in_=wbt[:].rearrange("p a b -> p (a b)")),
                                 r=[b_wt], w=[b_dr])
                        for j in range(4):
                            sidx = 4 * k + j
                            self.op(V, lambda e: e.tensor_tensor(out=wct[:, j], in0=ccl[:, sidx, :].unsqueeze(1).to_broadcast([128, 8, 16]),
                                                                 in1=mkC[:, j, :].unsqueeze(2).to_broadcast([128, 8, 16]), op=ALU.mult),
                                    r=[bp], w=[b_wt])
                        self.dma(lambda e: e.dma_start(out=WCs[rix, k], in_=wct[:].rearrange("p j a b -> p (j a b)")),
                                 r=[b_wt], w=[b_dr])
                tc_ = T2("q_tc", [128, 8, 512]); ts_ = T2("q_ts", [128, 8, 512])
                a1 = T2("q_a1", [128, 8, 256]); a2 = T2("q_a2", [128, 8, 256])
                pc = T2("q_pc", [128, 8]); psn = T2("q_ps", [128, 8]); pt1 = T2("q_pt1", [128, 8]); pt2 = T2("q_pt2", [128, 8])
                b_tb = Buf("tb")
                ot = lambda q, f: self.op(q, f, r=[b_tb, b_prm], w=[b_tb])
                for sg in range(8):
                    ot(V, lambda e: e.memset(tc_[:, :, 0:1], 1.0))
                    ot(V, lambda e: e.memset(ts_[:, :, 0:1], 0.0))
                    ot(V, lambda e: e.tensor_copy(out=pc[:], in_=c1[:, sg * 8:(sg + 1) * 8]))
                    ot(V, lambda e: e.tensor_copy(out=psn[:], in_=s1[:, sg * 8:(sg + 1) * 8]))
                    m = 1
                    while m < 512:
                        pcb = pc[:].unsqueeze(2).to_broadcast([128, 8, m])
                        psb_ = psn[:].unsqueeze(2).to_broadcast([128, 8, m])
                        ot(V, lambda e: e.tensor_tensor(out=a1[:, :, :m], in0=tc_[:, :, 0:m], in1=pcb, op=ALU.mult))
                        ot(V, lambda e: e.tensor_tensor(out=a2[:, :, :m], in0=ts_[:, :, 0:m], in1=psb_, op=ALU.mult))
                        ot(V, lambda e: e.tensor_tensor(out=tc_[:, :, m:2 * m], in0=a1[:, :, :m], in1=a2[:, :, :m], op=ALU.subtract))
                        ot(V, lambda e: e.tensor_tensor(out=a1[:, :, :m], in0=tc_[:, :, 0:m], in1=psb_, op=ALU.mult))
                        ot(V, lambda e: e.tensor_tensor(out=a2[:, :, :m], in0=ts_[:, :, 0:m], in1=pcb, op=ALU.mult))
                        ot(V, lambda e: e.tensor_tensor(out=ts_[:, :, m:2 * m], in0=a1[:, :, :m], in1=a2[:, :, :m], op=ALU.add))
                        ot(V, lambda e: e.tensor_tensor(out=pt1[:], in0=pc[:], in1=pc[:], op=ALU.mult))
                        ot(V, lambda e: e.tensor_tensor(out=pt2[:], in0=psn[:], in1=psn[:], op=ALU.mult))
                        ot(V, lambda e: e.tensor_tensor(out=psn[:], in0=psn[:], in1=pc[:], op=ALU.mult))
                        ot(V, lambda e: e.tensor_scalar(out=psn[:], in0=psn[:], scalar1=2.0, scalar2=None, op0=ALU.mult))
                        ot(V, lambda e: e.tensor_tensor(out=pc[:], in0=pt1[:], in1=pt2[:], op=ALU.subtract))
                        m *= 2
                    self.dma(lambda e: e.dma_start(out=TAB[0, sg * 8:(sg + 1) * 8].rearrange("s p t -> p s t"), in_=tc_[:]),
                             r=[b_tb], w=[b_dr])
                    self.dma(lambda e: e.dma_start(out=TAB[1, sg * 8:(sg + 1) * 8].rearrange("s p t -> p s t"), in_=ts_[:]),
                             r=[b_tb], w=[b_dr], q="act")
                self.tk.barrier()
            self.tk.barrier()
            NB = 2
            NBT = 3
            tabc = [self.sb("tabc%d" % i, [128, 512], F32, st) for i in range(NBT)]
            tabs = [self.sb("tabs%d" % i, [128, 512], F32, st) for i in range(NBT)]
            b_tab = [Buf("tab") for _ in range(NBT)]
            bsr = [self.sb("bsr%d" % i, [128, 512], F32, st) for i in range(NB)]
            bsi = [self.sb("bsi%d" % i, [128, 512], F32, st) for i in range(NB)]
            b_bs = [Buf("bs") for _ in range(NB)]
            qa = [self.sb("sqa%d" % i, [128, 512], F32, st) for i in range(4)]
            qb = [self.sb("sqb%d" % i, [128, 512], F32, st) for i in range(4)]
            b_qa = [Buf("qa") for _ in range(4)]; b_qb = [Buf("qb") for _ in range(4)]
            vre = [self.sb("vre%d" % i, [128, 512], F32, st) for i in range(NB)]
            vim = [self.sb("vim%d" % i, [128, 512], F32, st) for i in range(NB)]
            b_vr = [Buf("vr") for _ in range(NB)]; b_vi = [Buf("vi") for _ in range(NB)]
            kre = self.sb("kre", [128, 512], F32, st); kim = self.sb("kim", [128, 512], F32, st)
            hre = self.sb("hre", [128, 512], F32, st); him = self.sb("him", [128, 512], F32, st)
            b_kr = Buf("kr"); b_ki = Buf("ki"); b_h = Buf("h"); b_h2 = Buf("h2")
            hrb = [self.sb("hrb%d" % i, [128, 512], BF16, st) for i in range(NB)]
            hib = [self.sb("hib%d" % i, [128, 512], BF16, st) for i in range(NB)]
            b_hb = [Buf("hb") for _ in range(NB)]
            wBk = [self.sb("wBk%d" % i, [128, 2, 512], BF16, st) for i in range(NB)]
            wCk = [self.sb("wCk%d" % i, [128, 2, 512], BF16, st) for i in range(NB)]
            b_wk = [Buf("wk") for _ in range(NB)]
            yb = self.sb("yb", [128, DC, 512], BF16, st)
            b_yb = Buf("yb")
            sig = [self.sb("s5sig%d" % i, [128, 512], F32, st) for i in range(2)]
            b_sig = [Buf("sig") for _ in range(2)]
            hfo = self.sb("hfo", [64, 128], F32, st)
            b_hfo = Buf("hfo")
            xnf = self.yy
            b_xnf = self.b_yy
            wg = self.ssm_w_glu.rearrange("(kc p) (h f) -> p kc h f", p=128, h=2)
            self.op("dve", lambda e: e.memset(hpr[:], 0.0), w=[b_hp])
            self.op("dve", lambda e: e.memset(hpi[:], 0.0), w=[b_hp])
            sctr = 0
            for ti in range(len(self.tiles)):
                t0, W = self.tiles[ti]
                samp = (ti == len(self.tiles) - 1)
                if samp:
                    self.nat_to_scan(self.state_ssm_re, hpr, b_hp, tmp64, b_t64)
                    self.nat_to_scan(self.state_ssm_im, hpi, b_hp, tmp64, b_t64)
                self.pre_norm(ti, li * 4 + 0, xn_f32=xnf, b_xnf=b_xnf)
                ok = lambda f: self.op("dve", f, r=[b_hp, b_prm, b_k0], w=[b_k0])
                ok(lambda e: e.tensor_tensor(out=k0r[:], in0=c1[:], in1=hpr[:], op=ALU.mult))
                ok(lambda e: e.tensor_tensor(out=ktm[:], in0=s1[:], in1=hpi[:], op=ALU.mult))
                ok(lambda e: e.tensor_tensor(out=k0r[:], in0=k0r[:], in1=ktm[:], op=ALU.subtract))
                ok(lambda e: e.tensor_tensor(out=k0i[:], in0=s1[:], in1=hpr[:], op=ALU.mult))
                ok(lambda e: e.tensor_tensor(out=ktm[:], in0=c1[:], in1=hpi[:], op=ALU.mult))
                ok(lambda e: e.tensor_tensor(out=k0i[:], in0=k0i[:], in1=ktm[:], op=ALU.add))
                lastc = 0 if samp else W - 1

                def stageA(sidx):
                    k, j = sidx // 4, sidx % 4
                    kb = k % NB
                    i2 = sidx % NB
                    if j == 0:
                        self.dma(lambda e: e.dma_start(out=wBk[kb][:], in_=WBs[:, k].rearrange("r p n -> p r n")),
                                 r=[b_dr], w=[b_wk[kb]])
                        self.dma(lambda e: e.dma_start(out=wCk[kb][:], in_=WCs[:, k].rearrange("r p n -> p r n")),
                                 r=[b_dr], w=[b_wk[kb]])
                    i4 = sidx % NBT
                    self.dma(lambda e: e.dma_start(out=tabc[i4][:, :W], in_=TAB[0, sidx, :, 0:W]), r=[b_dr], w=[b_tab[i4]])
                    self.dma(lambda e: e.dma_start(out=tabs[i4][:, :W], in_=TAB[1, sidx, :, 0:W]), r=[b_dr], w=[b_tab[i4]])
                    pr, pi_ = (i2 * 2, i2 * 2 + 1)
                    self.op("pe", lambda e: e.matmul(self.psb[pr][:, :W], lhsT=wBk[kb][:, 0, j * 128:(j + 1) * 128],
                                                     rhs=self.xn[:, k, :W], start=True, stop=True),
                            r=[b_wk[kb], self.b_xn], w=[self.bps[pr]])
                    self.op("pe", lambda e: e.matmul(self.psb[pi_][:, :W], lhsT=wBk[kb][:, 1, j * 128:(j + 1) * 128],
                                                     rhs=self.xn[:, k, :W], start=True, stop=True),
                            r=[b_wk[kb], self.b_xn], w=[self.bps[pi_]])
                    self.op("act", lambda e: e.copy(out=bsr[i2][:, :W], in_=self.psb[pr][:, :W]), r=[self.bps[pr]], w=[b_bs[i2]])
                    self.op("act", lambda e: e.copy(out=bsi[i2][:, :W], in_=self.psb[pi_][:, :W]), r=[self.bps[pi_]], w=[b_bs[i2]])
                    TCt, TSt = tabc[i4], tabs[i4]
                    rd = [b_bs[i2], b_tab[i4]]
                    self.op("pool", lambda e: e.tensor_tensor(out=qa[0][:, :W], in0=TSt[:, :W], in1=bsi[i2][:, :W], op=ALU.mult), r=rd, w=[b_qa[0]])
                    self.op("pool", lambda e: e.tensor_tensor(out=qa[1][:, :W], in0=TCt[:, :W], in1=bsr[i2][:, :W], op=ALU.mult), r=rd, w=[b_qa[1]])
                    self.op("pool", lambda e: e.tensor_tensor(out=vre[i2][:, :W], in0=qa[1][:, :W], in1=qa[0][:, :W], op=ALU.add), r=[b_qa[0], b_qa[1]], w=[b_vr[i2]])
                    self.op("pool", lambda e: e.tensor_tensor(out=qa[2][:, :W], in0=TSt[:, :W], in1=bsr[i2][:, :W], op=ALU.mult), r=rd, w=[b_qa[2]])
                    self.op("pool", lambda e: e.tensor_tensor(out=qa[3][:, :W], in0=TCt[:, :W], in1=bsi[i2][:, :W], op=ALU.mult), r=rd, w=[b_qa[3]])

                def stageA2(sidx):
                    i2 = sidx % NB
                    self.op("dve", lambda e: e.tensor_tensor(out=vim[i2][:, :W], in0=qa[3][:, :W], in1=qa[2][:, :W], op=ALU.subtract), r=[b_qa[2], b_qa[3]], w=[b_vi[i2]])

                def stageB(sidx):
                    k, j = sidx // 4, sidx % 4
                    kb = k % NB
                    i2 = sidx % NB
                    ybank = 6 + (k % 2)
                    i4 = sidx % NBT
                    TCt, TSt = tabc[i4], tabs[i4]
                    rb = rho[:, sidx:sidx + 1].to_broadcast([128, W])
                    self.op("dve", lambda e: e.tensor_tensor_scan(out=kre[:, :W], data0=rb, data1=vre[i2][:, :W],
                                                                  initial=k0r[:, sidx:sidx + 1], op0=ALU.mult, op1=ALU.add),
                            r=[b_vr[i2], b_k0, b_prm], w=[b_kr])
                    self.op("dve", lambda e: e.tensor_tensor_scan(out=kim[:, :W], data0=rb, data1=vim[i2][:, :W],
                                                                  initial=k0i[:, sidx:sidx + 1], op0=ALU.mult, op1=ALU.add),
                            r=[b_vi[i2], b_k0, b_prm], w=[b_ki])
                    rk = [b_kr, b_ki, b_tab[i4]]
                    self.op("dve", lambda e: e.tensor_tensor(out=qb[0][:, :W], in0=TSt[:, :W], in1=kim[:, :W], op=ALU.mult), r=rk, w=[b_qb[0]])
                    self.op("dve", lambda e: e.tensor_tensor(out=qb[1][:, :W], in0=TCt[:, :W], in1=kre[:, :W], op=ALU.mult), r=rk, w=[b_qb[1]])
                    self.op("dve", lambda e: e.tensor_tensor(out=hre[:, :W], in0=qb[1][:, :W], in1=qb[0][:, :W], op=ALU.subtract), r=[b_qb[0], b_qb[1]], w=[b_h])
                    self.op("dve", lambda e: e.tensor_tensor(out=qb[2][:, :W], in0=TSt[:, :W], in1=kre[:, :W], op=ALU.mult), r=rk, w=[b_qb[2]])
                    self.op("dve", lambda e: e.tensor_tensor(out=qb[3][:, :W], in0=TCt[:, :W], in1=kim[:, :W], op=ALU.mult), r=rk, w=[b_qb[3]])
                    self.op("dve", lambda e: e.tensor_tensor(out=him[:, :W], in0=qb[2][:, :W], in1=qb[3][:, :W], op=ALU.add), r=[b_qb[2], b_qb[3]], w=[b_h2])
                    self.op("act", lambda e: e.copy(out=hrb[i2][:, :W], in_=hre[:, :W]), r=[b_h], w=[b_hb[i2]])
                    self.op("act", lambda e: e.copy(out=hib[i2][:, :W], in_=him[:, :W]), r=[b_h2], w=[b_hb[i2]])
                    self.op("act", lambda e: e.copy(out=hpr[:, sidx:sidx + 1], in_=hre[:, lastc:lastc + 1]), r=[b_h, b_k0], w=[b_hp])
                    self.op("act", lambda e: e.copy(out=hpi[:, sidx:sidx + 1], in_=him[:, lastc:lastc + 1]), r=[b_h2, b_k0], w=[b_hp])
                    self.op("pe", lambda e: e.matmul(self.psb[ybank][:, :W], lhsT=wCk[kb][:, 0, j * 128:(j + 1) * 128],
                                                     rhs=hrb[i2][:, :W], start=(j == 0), stop=False),
                            r=[b_wk[kb], b_hb[i2]], w=[self.bps[ybank]])
                    self.op("pe", lambda e: e.matmul(self.psb[ybank][:, :W], lhsT=wCk[kb][:, 1, j * 128:(j + 1) * 128],
                                                     rhs=hib[i2][:, :W], start=False, stop=(j == 3)),
                            r=[b_wk[kb], b_hb[i2]], w=[self.bps[ybank]])
                    if j == 3:
                        self.op("dve", lambda e: e.scalar_tensor_tensor(out=yb[:, k, :W], in0=xnf[:, k, :W], scalar=dsk[:, k:k + 1],
                                                                        in1=self.psb[ybank][:, :W], op0=ALU.mult, op1=ALU.add),
                                r=[b_xnf, b_prm, self.bps[ybank]], w=[b_yb])

                stageA(0)
                stageA2(0)
                for sidx in range(64):
                    if sidx + 1 < 64:
                        stageA(sidx + 1)
                    stageB(sidx)
                    if sidx + 1 < 64:
                        stageA2(sidx + 1)

                def epg(p, psA, bA, psB, bB):
                    kk = p % 2
                    self.op("act", lambda e: e.activation(out=sig[kk][:, :W], in_=psB[:, :W], func=AF.Sigmoid),
                            r=[bB], w=[b_sig[kk]])
                    self.op("dve", lambda e: e.tensor_tensor(out=self.yy[:, p, :W], in0=psA[:, :W], in1=sig[kk][:, :W],
                                                             op=ALU.mult), r=[bA, b_sig[kk]], w=[self.b_yy])

                self.gemm(lambda p, k0, kn, h: wg[:, k0:k0 + kn, h, p * 128:(p + 1) * 128], DC, DC, yb, b_yb, W, epg, wkey='glu', first=(ti == 0))
                self.post_norm_residual(ti, li * 4 + 1)
                if ti == self.NT - 1 or samp:
                    for rix, (src, dstp, dsts) in enumerate(((hpr, self.ssm_re_p, self.ssm_re_s), (hpi, self.ssm_im_p, self.ssm_im_s))):
                        dst = dsts if samp else dstp
                        self.op("pe", lambda e: e.transpose(out=self.psb[5][0:64, 0:128], in_=src[:], identity=self.ident[:]),
                                r=[b_hp, self.b_const], w=[self.bps[5]])
                        self.op("dve", lambda e: e.tensor_copy(out=hfo[:], in_=self.psb[5][0:64, 0:128]), r=[self.bps[5]], w=[b_hfo])
                        self.dma(lambda e: e.dma_start(out=dst.rearrange("(s g2) p -> s (g2 p)", g2=2), in_=hfo[:]), r=[b_hfo])
            self.tk.barrier()

    def nsa_declare(self):
        d = self.din
        o = self.dout
        T = self.T
        for k, shp in STRUCT_SHAPES.items():
            setattr(self, "c_" + k, d(k, shp))
        self.rel_bias = d("rel_bias", [32, 16])
        self.page_table = d("page_table", [1, 128], I32)
        self.nsa_w_q = d("nsa_w_q", [2, D, D])
        self.nsa_w_kv = d("nsa_w_kv", [2, D, 3072])
        self.nsa_cmp_pe = d("nsa_cmp_pe", [2, 32, 2, 128])
        self.nsa_cmp_w1 = d("nsa_cmp_w1", [2, 2, 32, 128, 128])
        self.nsa_cmp_w2 = d("nsa_cmp_w2", [2, 2, 128, 128])
        self.nsa_w_gate = d("nsa_w_gate", [2, D, 48])
        self.nsa_w_o = d("nsa_w_o", [2, D, D])
        self.cache_cmp = [d("cache_cmp%d" % i, [self.NPHYS * 128, 1024]) for i in range(2)]
        self.cache_slc = [d("cache_slc%d" % i, [self.NPHYS * 128 * 4, 256]) for i in range(2)]
        self.cache_win = d("cache_win", [2, 512, 1024])
        WP = min(512, T)
        self.WP = WP
        self.kvo_p = [o(n, [2, T, 1024]) for n in ("cmp_kv_p", "slc_kv_p")] + [o("win_kv_p", [2, WP, 1024])]
        self.kvo_s = [o(n, [2, 1, 1024]) for n in ("cmp_kv_s", "slc_kv_s")] + [o("win_kv_s", [2, 512, 1024])]
        TT = self.TT
        self.QT = self.dscr("QT", [16, 128, TT], BF16)
        self.KVT = self.dscr("KVT", [3, 4, 2, 128, TT], BF16)
        self.VTOK = self.dscr("VTOK", [3, 4, TT, 128], BF16)
        self.GT = self.dscr("GT", [TT, 48])
        self.OT = self.dscr("OT", [16, 128, TT], BF16)
        self.BD = self.dscr("BD", [16, 4, 128, 128])
        self.BCs = self.dscr("BCs", [16, 248, 128])
        self.b_nsa_dr = Buf("nsadram")
        self.bias_built = False

    def build_bias_tables(self, st):
        with contextlib.ExitStack() as s2:
            relb = self.sb("relb", [33, 16], F32, s2)
            b_rb = Buf("relb")
            self.op("dve", lambda e: e.memset(relb[:], -30000.0), w=[b_rb])
            self.dma(lambda e: e.dma_start(out=relb[0:32, :], in_=self.rel_bias[:, :]), w=[b_rb])
            r31 = self.sb("r31", [32, 16], F32, s2)
            self.dma(lambda e: e.dma_start(out=r31[:], in_=self.rel_bias[31:32, :].partition_broadcast(32)), w=[b_rb])
            self.op("dve", lambda e: e.tensor_tensor(out=relb[0:32, :], in0=relb[0:32, :], in1=r31[:], op=ALU.subtract),
                    r=[b_rb], w=[b_rb])
            NB = 2
            ohb = [self.sb("ohb%d" % i, [33, 2048], F32, s2) for i in range(NB)]
            b_oh = [Buf("oh") for _ in range(NB)]
            ob = [self.sb("ob%d" % i, [16, 2048], F32, s2) for i in range(NB)]
            b_ob = [Buf("ob") for _ in range(NB)]
            ctr = 0
            for src, dst, n in ((self.c_oh_pk, self.BD.rearrange("h k n q -> h (k n q)"), 65536),
                                (self.c_oh_cmp, self.BCs.rearrange("h m q -> h (m q)"), 248 * 128)):
                for c0 in range(0, n, 2048):
                    cw = min(2048, n - c0)
                    i = ctr % NB
                    ctr += 1
                    self.dma(lambda e: e.dma_start(out=ohb[i][:, :cw], in_=src[:, c0:c0 + cw]), w=[b_oh[i]])
                    for k in range(0, cw, 512):
                        kw = min(512, cw - k)
                        pb = 4 + ((k // 512) % 2)
                        self.op("pe", lambda e: e.matmul(self.psb[pb][0:16, :kw], lhsT=relb[:, :], rhs=ohb[i][:, k:k + kw],
                                                         start=True, stop=True), r=[b_rb, b_oh[i]], w=[self.bps[pb]])
                        self.op("act", lambda e: e.copy(out=ob[i][:, k:k + kw], in_=self.psb[pb][0:16, :kw]),
                                r=[self.bps[pb]], w=[b_ob[i]])
                    self.dma(lambda e: e.dma_start(out=dst[:, c0:c0 + cw], in_=ob[i][:, :cw]), r=[b_ob[i]],
                             w=[self.b_nsa_dr], q="act")
            ohs = self.sb("ohs", [33, 17, 128], F32, s2)
            self.dma(lambda e: e.dma_start(out=ohs[:], in_=self.c_oh_s[:, :, :]), w=[b_oh[0]])
            for t in range(17):
                pb = 4 + (t % 2)
                self.op("pe", lambda e: e.matmul(self.psb[pb][:, 0:16], lhsT=ohs[:, t, :], rhs=relb[:, :], start=True, stop=True),
                        r=[b_rb, b_oh[0]], w=[self.bps[pb]])
                self.op("dve", lambda e: e.tensor_copy(out=self.BS[:, t, :], in_=self.psb[pb][:, 0:16]), r=[self.bps[pb]],
                        w=[self.b_BS])
            self.tk.barrier()

    def nsa_layer(self, li):
        j = 0 if li == 0 else 1
        T = self.T
        NQ = T // 128
        SC = 128.0 ** -0.5
        tk = self.tk
        if not self.bias_built:
            self.BS = self.sb("BS", [128, 17, 16], F32)
            self.b_BS = Buf("BS")
            self.build_bias_tables(None)
            self.bias_built = True
        with contextlib.ExitStack() as st:
            self.alloc_rowlocal(st)
            qt = self.sb("qt", [128, 16, 512], BF16, st); b_qt = Buf("qt")
            ktmp = [self.sb("ktmp%d" % i, [128, 2, 512], F32, st) for i in range(2)]; b_kt = [Buf("kt") for _ in range(2)]
            kbf = [self.sb("kbf%d" % i, [128, 2, 512], BF16, st) for i in range(2)]; b_kb = [Buf("kb") for _ in range(2)]
            stg = [self.sb("stg%d" % i, [128, 4, 2, 128], F32, st) for i in range(2)]; b_stg = [Buf("stg") for _ in range(2)]
            vbf = [self.sb("vbf%d" % i, [128, 4, 128], BF16, st) for i in range(2)]; b_vb = [Buf("vb") for _ in range(2)]
            wgf = self.sb("wgf", [128, 16, 48], F32, st); wgb = self.sb("wgb", [128, 16, 48], BF16, st); b_wg = Buf("wg")
            gsb = self.sb("gsb", [48, 512], F32, st); b_gs = Buf("gs")
            gtk = self.sb("gtk", [128, 4, 48], F32, st); b_gt = Buf("gtk")
            self.dma(lambda e: e.dma_start(out=wgf[:], in_=self.nsa_w_gate[j].rearrange("(kc p) n -> p kc n", p=128)), w=[b_wg])
            self.op("pool", lambda e: e.tensor_copy(out=wgb[:], in_=wgf[:]), r=[b_wg], w=[b_wg])
            wq = self.nsa_w_q[j].rearrange("(kc p) (f h n) -> p kc f h n", p=128, h=2, n=128)
            wkv = self.nsa_w_kv[j].rearrange("(kc p) (f h n) -> p kc f h n", p=128, h=2, n=128)
            QTv = self.QT.rearrange("h p t -> p h t")
            for ti in range(len(self.tiles)):
                t0, W = self.tiles[ti]
                samp = (ti == len(self.tiles) - 1)
                self.pre_norm(ti, li * 4 + 0)

                def epq(p, psA, bA, psB, bB):
                    self.op("act", lambda e: e.activation(out=qt[:, 2 * p, :W], in_=psA[:, :W], func=AF.Copy, scale=SC), r=[bA], w=[b_qt])
                    self.op("dve", lambda e: e.tensor_scalar(out=qt[:, 2 * p + 1, :W], in0=psB[:, :W], scalar1=SC, scalar2=None,
                                                             op0=ALU.mult), r=[bB], w=[b_qt])
                self.gemm(lambda p, k0, kn, h: wq[:, k0:k0 + kn, p, h, :], DC, 8, self.xn, self.b_xn, W, epq, wkey='wq%d' % j, first=(ti == 0))
                self.dma(lambda e: e.dma_start(out=QTv[:, :, t0:t0 + W], in_=qt[:, :, :W]), r=[b_qt], w=[self.b_nsa_dr])

                def epkv(p, psA, bA, psB, bB):
                    k = p % 2
                    br, g = p // 4, p % 4
                    self.op("act", lambda e: e.copy(out=ktmp[k][:, 0, :W], in_=psA[:, :W]), r=[bA], w=[b_kt[k]])
                    self.op("dve", lambda e: e.tensor_copy(out=ktmp[k][:, 1, :W], in_=psB[:, :W]), r=[bB], w=[b_kt[k]])
                    self.op("act", lambda e: e.copy(out=kbf[k][:, :, :W], in_=ktmp[k][:, :, :W]), r=[b_kt[k]], w=[b_kb[k]])
                    self.dma(lambda e: e.dma_start(out=self.KVT[br, g].rearrange("c p t -> p c t")[:, :, t0:t0 + W],
                                                   in_=kbf[k][:, :, :W]), r=[b_kb[k]], w=[self.b_nsa_dr], q="act")
                    nb = 1 if samp else 4
                    bw = W if samp else 128
                    for c in range(2):
                        pb = 4 + c
                        for tb in range(nb):
                            self.op("pe", lambda e: e.transpose(out=self.psb[pb][0:bw, tb * 128:(tb + 1) * 128],
                                                                in_=ktmp[k][:, c, tb * 128:tb * 128 + bw], identity=self.ident[:]),
                                    r=[b_kt[k], self.b_const], w=[self.bps[pb]])
                        if c == 0:
                            self.op("act", lambda e: e.copy(out=stg[k][0:bw, 0:nb, 0, :],
                                                            in_=self.psb[pb][0:bw, 0:nb * 128].rearrange("p (a d) -> p a d", d=128)),
                                    r=[self.bps[pb]], w=[b_stg[k]])
                        else:
                            self.op("dve", lambda e: e.tensor_copy(out=stg[k][0:bw, 0:nb, 1, :],
                                                                   in_=self.psb[pb][0:bw, 0:nb * 128].rearrange("p (a d) -> p a d", d=128)),
                                    r=[self.bps[pb]], w=[b_stg[k]])
                            self.op("dve", lambda e: e.tensor_copy(out=vbf[k][0:bw, 0:nb, :], in_=stg[k][0:bw, 0:nb, 1, :]),
                                    r=[b_stg[k]], w=[b_vb[k]])
                    if samp:
                        if br < 2:
                            self.dma(lambda e: e.dma_start(out=self.kvo_s[br][j, 0:1, g * 256:(g + 1) * 256],
                                                           in_=stg[k][0:1, 0, :, :].rearrange("p c d -> p (c d)")), r=[b_stg[k]])
                        else:
                            self.dma(lambda e: e.dma_start(out=self.kvo_s[2][j, 511:512, g * 256:(g + 1) * 256],
                                                           in_=stg[k][0:1, 0, :, :].rearrange("p c d -> p (c d)")), r=[b_stg[k]])
                        self.dma(lambda e: e.dma_start(out=self.VTOK[br, g, T:T + W, :], in_=vbf[k][0:W, 0, :]),
                                 r=[b_vb[k]], w=[self.b_nsa_dr], q="act")
                    else:
                        if br < 2:
                            dst = self.kvo_p[br][j, t0:t0 + 512, g * 256:(g + 1) * 256].rearrange("(a p) (c d) -> p a c d", p=128, c=2)
                            for c in range(2):
                                self.dma(lambda e: e.dma_start(out=dst[:, :, c, :], in_=stg[k][:, :, c, :]), r=[b_stg[k]])
                        elif t0 + 512 > T - self.WP:
                            r0 = t0 - (T - self.WP)
                            dst = self.kvo_p[2][j, r0:r0 + 512, g * 256:(g + 1) * 256].rearrange("(a p) (c d) -> p a c d", p=128, c=2)
                            for c in range(2):
                                self.dma(lambda e: e.dma_start(out=dst[:, :, c, :], in_=stg[k][:, :, c, :]), r=[b_stg[k]])
                        self.dma(lambda e: e.dma_start(out=self.VTOK[br, g, t0:t0 + 512, :].rearrange("(a p) d -> p a d", p=128),
                                                       in_=vbf[k][:, :, :]), r=[b_vb[k]], w=[self.b_nsa_dr], q="act")
                self.gemm(lambda p, k0, kn, h: wkv[:, k0:k0 + kn, p, h, :], DC, 12, self.xn, self.b_xn, W, epkv, wkey='wkv%d' % j, first=(ti == 0))
                for kc in range(DC):
                    self.op("pe", lambda e: e.matmul(self.psb[5][0:48, :W], lhsT=wgb[:, kc, :], rhs=self.xn[:, kc, :W],
                                                     start=(kc == 0), stop=(kc == DC - 1)), r=[b_wg, self.b_xn], w=[self.bps[5]])
                self.op("act", lambda e: e.activation(out=gsb[:, :W], in_=self.psb[5][0:48, :W], func=AF.Sigmoid), r=[self.bps[5]], w=[b_gs])
                nb = 1 if samp else 4
                bw = W if samp else 128
                for tb in range(nb):
                    self.op("pe", lambda e: e.transpose(out=self.psb[4][0:bw, tb * 48:(tb + 1) * 48], in_=gsb[:, tb * 128:tb * 128 + bw],
                                                        identity=self.ident[0:48, 0:48]), r=[b_gs, self.b_const], w=[self.bps[4]])
                self.op("dve", lambda e: e.tensor_copy(out=gtk[0:bw, 0:nb, :], in_=self.psb[4][0:bw, 0:nb * 48].rearrange("p (a n) -> p a n", n=48)),
                        r=[self.bps[4]], w=[b_gt])
                self.dma(lambda e: e.dma_start(out=self.GT[t0:t0 + nb * bw, :].rearrange("(a p) n -> p a n", p=bw), in_=gtk[0:bw, 0:nb, :]),
                         r=[b_gt], w=[self.b_nsa_dr])
            self.tk.barrier()
        self.dma(lambda e: e.dma_start(out=self.kvo_s[2][j, 0:511, :], in_=self.cache_win[j, 1:512, :]))
        with contextlib.ExitStack() as st:
            w1f = self.sb("w1f", [128, 32, 128], F32, st)
            w1b = self.sb("w1b", [128, 2, 32, 128], BF16, st)
            w2f = self.sb("w2f", [128, 2, 128], F32, st)
            w2b = self.sb("w2b", [128, 2, 128], BF16, st)
            pef = self.sb("pef", [64, 128], F32, st)
            peT = self.sb("peT", [128, 32, 2], BF16, st)
            pebias = self.sb("pebias", [128, 2], F32, st)
            b_cw = Buf("cw")
            for c in range(2):
                self.dma(lambda e: e.dma_start(out=w1f[:], in_=self.nsa_cmp_w1[j, c].rearrange("s d e -> d s e")), w=[b_cw])
                self.op("pool", lambda e: e.tensor_copy(out=w1b[:, c], in_=w1f[:]), r=[b_cw], w=[b_cw])
            self.dma(lambda e: e.dma_start(out=w2f[:], in_=self.nsa_cmp_w2[j].rearrange("c e d -> e c d")), w=[b_cw])
            self.op("pool", lambda e: e.tensor_copy(out=w2b[:], in_=w2f[:]), r=[b_cw], w=[b_cw])
            self.dma(lambda e: e.dma_start(out=pef[:], in_=self.nsa_cmp_pe[j].rearrange("s c d -> (s c) d")), w=[b_cw])
            self.op("pe", lambda e: e.transpose(out=self.psb[5][:, 0:64], in_=pef[:], identity=self.ident[0:64, 0:64]),
                    r=[b_cw, self.b_const], w=[self.bps[5]])
            self.op("dve", lambda e: e.tensor_copy(out=peT[:].rearrange("p s c -> p (s c)"), in_=self.psb[5][:, 0:64]), r=[self.bps[5]], w=[b_cw])
            for c in range(2):
                for s in range(32):
                    self.op("pe", lambda e: e.matmul(self.psb[5][:, 64 + c:65 + c], lhsT=w1b[:, c, s, :], rhs=peT[:, s, c:c + 1],
                                                     start=(s == 0 and c == 0), stop=(s == 31)), r=[b_cw], w=[self.bps[5]])
            self.op("dve", lambda e: e.tensor_copy(out=pebias[:], in_=self.psb[5][:, 64:66]), r=[self.bps[5]], w=[b_cw])
            ght = [self.sb("ght%d" % i, [128, 128], BF16, st) for i in range(2)]
            b_gh = [Buf("ght") for _ in range(2)]

            def compress(xT_ap_fn, bx, ncols, kdst, vdst, bdst, ctr0):
                for c in range(2):
                    i = (ctr0 + c) % 2
                    pb = i
                    for s in range(32):
                        self.op("pe", lambda e: e.matmul(self.psb[pb][:, 0:ncols], lhsT=w1b[:, c, s, :], rhs=xT_ap_fn(c, s),
                                                         start=(s == 0), stop=(s == 31)), r=[b_cw, bx], w=[self.bps[pb]])
                    self.op("act", lambda e: e.activation(out=ght[i][:, 0:ncols], in_=self.psb[pb][:, 0:ncols], func=AF.Gelu_apprx_tanh,
                                                          bias=pebias[:, c:c + 1], scale=1.0), r=[self.bps[pb], b_cw], w=[b_gh[i]])
                    pb2 = 2 + i
                    if c == 0:
                        self.op("pe", lambda e: e.matmul(self.psb[pb2][:, 0:ncols], lhsT=w2b[:, 0, :], rhs=ght[i][:, 0:ncols], start=True, stop=True),
                                r=[b_cw, b_gh[i]], w=[self.bps[pb2]])
                        self.op("dve", lambda e: e.tensor_copy(out=kdst, in_=self.psb[pb2][:, 0:ncols]), r=[self.bps[pb2]], w=[bdst])
                    else:
                        self.op("pe", lambda e: e.matmul(self.psb[pb2][0:ncols, 0:128], lhsT=ght[i][:, 0:ncols], rhs=w2b[:, 1, :], start=True, stop=True),
                                r=[b_cw, b_gh[i]], w=[self.bps[pb2]])
                        self.op("dve", lambda e: e.tensor_copy(out=vdst, in_=self.psb[pb2][0:ncols, 0:128]), r=[self.bps[pb2]], w=[bdst])

            with contextlib.ExitStack() as s2:
                NCB = T // 16 - 1
                xT = self.sb("cxT", [128, 2, T], BF16, s2); b_xT = Buf("cxT")
                kcT = self.sb("kcT", [128, 4, 128], BF16, s2)
                vc = self.sb("vc", [128, 4, 128], BF16, s2)
                b_kc = Buf("kc")
                self.op("dve", lambda e: e.memset(kcT[:], 0.0), w=[b_kc])
                self.op("dve", lambda e: e.memset(vc[:], 0.0), w=[b_kc])
                for g in range(4):
                    self.dma(lambda e: e.dma_start(out=xT[:], in_=self.KVT[0, g].rearrange("c p t -> p c t")[:, :, 0:T]),
                             r=[self.b_nsa_dr], w=[b_xT])
                    compress(lambda c, s: xT[:, c, s:s + 16 * (NCB - 1) + 1:16], b_xT, NCB, kcT[:, g, 0:NCB], vc[0:NCB, g, :], b_kc, 2 * g)
                ks = [self.sb("ks%d" % b, [128, T], BF16, s2) for b in range(2)]
                vs = [self.sb("vs%d" % b, [128, NQ, 128], BF16, s2) for b in range(2)]
                b_kv = Buf("kvres")
                bdt = self.sb("bdt", [128, 4, 4, 128], F32, s2)
                b_bd = Buf("bdt")
                gts = self.sb("gts", [128, NQ, 48], F32, s2); b_gts = Buf("gts")
                self.dma(lambda e: e.dma_start(out=gts[:], in_=self.GT[0:T, :].rearrange("(a p) n -> p a n", p=128)), r=[self.b_nsa_dr], w=[b_gts])
                mselp = self.sb("mselp", [128, 32], F32, s2)
                amp = self.sb("amp", [128, 8, 32], F32, s2)
                eexf = self.sb("eexf", [32, 16, 128], F32, s2)
                eexb = self.sb("eexb", [32, 16, 128], BF16, s2)
                b_sc = Buf("selc")
                self.dma(lambda e: e.dma_start(out=mselp[:], in_=self.c_msel_p[:, :]), w=[b_sc])
                self.dma(lambda e: e.dma_start(out=amp[:], in_=self.c_addmask_p.rearrange("i q j -> q i j")), w=[b_sc])
                self.dma(lambda e: e.dma_start(out=eexf[:], in_=self.c_eexp[:, :, :]), w=[b_sc])
                self.op("dve", lambda e: e.tensor_copy(out=eexb[:], in_=eexf[:]), r=[b_sc], w=[b_sc])
                qs = [self.sb("qs%d" % i, [128, 4, 128], BF16, s2) for i in range(2)]; b_qs = [Buf("qs") for _ in range(2)]
                bcm = [self.sb("bcm%d" % i, [128, 4, 128], F32, s2) for i in range(2)]; b_bcm = [Buf("bcm") for _ in range(2)]
                sT = [self.sb("sT%d" % i, [128, 512], F32, s2) for i in range(3)]; b_sT = [Buf("sT") for _ in range(3)]
                pT = [self.sb("pT%d" % i, [128, 512], BF16, s2) for i in range(3)]; b_pT = [Buf("pT") for _ in range(3)]
                SB_ = [0, 1, 7]
                pf = self.sb("pf", [128, 512], F32, s2); b_pf = Buf("pf")
                rcs = self.sb("rcs", [128, 512], F32, s2); b_rc = Buf("rcs")
                impT = self.sb("impT", [128, 128], F32, s2); b_imp = Buf("imp")
                psl = self.sb("psl", [32, 128], F32, s2); b_psl = Buf("psl")
                sco = self.sb("sco", [128, 32], F32, s2); sco2 = self.sb("sco2", [128, 32], F32, s2)
                mx1 = self.sb("mx1", [128, 8], F32, s2); mx2 = self.sb("mx2", [128, 8], F32, s2)
                ngm = self.sb("ngm", [128, 32], F32, s2); b_sco = Buf("sco")
                ngT = self.sb("ngT", [32, 128], BF16, s2); b_ngT = Buf("ngT")
                osb = self.sb("osb", [128, 4, 128], F32, s2); b_osb = Buf("osb")
                wsc = [self.sb("wsc%d" % i_, [128, 4], F32, s2) for i_ in range(2)]; b_wsc = [Buf("wsc") for _ in range(2)]
                rsc = self.sb("rsc", [128, 4], F32, s2); b_rsc = Buf("rsc")
                osb3 = [self.sb("osb3_%d" % i_, [128, 3, 4, 128], F32, s2) for i_ in range(2)]; b_osb3 = [Buf("osb3") for _ in range(2)]
                otb = [self.sb("otb%d" % i, [128, 4, 128], BF16, s2) for i in range(2)]; b_otb = [Buf("otb") for _ in range(2)]
                for g in range(4):
                    for b_ in range(2):
                        self.dma(lambda e: e.dma_start(out=ks[b_][:], in_=self.KVT[1 + b_, g, 0, :, 0:T]), r=[self.b_nsa_dr], w=[b_kv])
                        self.dma(lambda e: e.dma_start(out=vs[b_][:], in_=self.VTOK[1 + b_, g, 0:T, :].rearrange("(a p) d -> p a d", p=128)),
                                 r=[self.b_nsa_dr], w=[b_kv])
                    for kd in range(4):
                        self.dma(lambda e: e.dma_start(out=bdt[:, kd], in_=self.BD[4 * g:4 * g + 4, kd].rearrange("h n q -> n h q")),
                                 r=[self.b_nsa_dr], w=[b_bd])
                    units = []
                    banks = {}
                    octr = 0
                    for i in range(NQ):
                        for br in (0, 2, 1):
                            if br == 0:
                                kts = [0]
                            elif br == 1:
                                kts = list(range(0, i + 1))
                            else:
                                kts = list(range(max(0, i - 4), i + 1))
                            banks[(i, br)] = (2, 0) if octr % 2 == 0 else (6, 8)
                            octr += 1
                            for ki, kt in enumerate(kts):
                                units.append((i, br, kt, ki, len(kts)))

                    def emit_qk(n):
                        i, br, kt, ki, nk = units[n]
                        qi = i % 2
                        u = n % 3
                        sbk = SB_[u]
                        if br == 0:
                            self.dma(lambda e: e.dma_start(out=qs[qi][:], in_=self.QT[4 * g:4 * g + 4, :, i * 128:(i + 1) * 128].rearrange("h p t -> p h t")),
                                     r=[self.b_nsa_dr], w=[b_qs[qi]])
                            self.dma(lambda e: e.dma_start(out=bcm[qi][:], in_=self.BCs[4 * g:4 * g + 4, 120 - 8 * i:248 - 8 * i, :].rearrange("h n q -> n h q")),
                                     r=[self.b_nsa_dr], w=[b_bcm[qi]])
                        qrhs = qs[qi][:].rearrange("p r q -> p (r q)")
                        if br == 0:
                            klhs, kvb = kcT[:, g, :], b_kc
                        else:
                            klhs, kvb = ks[br - 1][:, kt * 128:(kt + 1) * 128], b_kv
                        msk = (br == 1 and i >= 8)
                        self.op("pe", lambda e: e.matmul(self.psb[sbk][:, :], lhsT=klhs, rhs=qrhs, start=True, stop=(not msk)),
                                r=[kvb, b_qs[qi]], w=[self.bps[sbk]])
                        if msk:
                            for r in range(4):
                                self.op("pe", lambda e: e.matmul(self.psb[sbk][:, r * 128:(r + 1) * 128], lhsT=eexb[:, kt, :], rhs=ngT[:, :],
                                                                 start=False, stop=(r == 3)), r=[b_sc, b_ngT], w=[self.bps[sbk]])

                    def emit_rest(n):
                        i, br, kt, ki, nk = units[n]
                        qi = i % 2
                        u = n % 3
                        sbk = SB_[u]
                        ob_, oc_ = banks[(i, br)]
                        os_ = 3
                        dosel = (i >= 8)
                        if br == 0:
                            vrhs, kvb = vc[:, g, :], b_kc
                            bias_ap, bb = bcm[qi][:].rearrange("p r q -> p (r q)"), b_bcm[qi]
                        else:
                            vrhs, kvb = vs[br - 1][:, kt, :], b_kv
                            if kt == i:
                                kd = 0
                            elif kt == i - 1:
                                kd = 1
                            elif br == 2 and kt == i - 4:
                                kd = 3
                            else:
                                kd = 2
                            bias_ap, bb = bdt[:, kd].rearrange("p r q -> p (r q)"), b_bd
                        far = (br != 0 and kd == 2)
                        if not far:
                            self.op("dve", lambda e: e.tensor_tensor(out=sT[u][:], in0=self.psb[sbk][:, :], in1=bias_ap, op=ALU.add),
                                    r=[self.bps[sbk], bb], w=[b_sT[u]])
                        if br == 0:
                            if dosel:
                                self.op("act", lambda e: e.activation(out=pf[:], in_=sT[u][:], func=AF.Exp), r=[b_sT[u]], w=[b_pf])
                            self.op("act", lambda e: e.activation(out=pT[u][:], in_=sT[u][:], func=AF.Exp), r=[b_sT[u]], w=[b_pT[u]])
                        elif far:
                            self.op("act", lambda e: e.activation(out=pT[u][:], in_=self.psb[sbk][:, :], func=AF.Exp), r=[self.bps[sbk]], w=[b_pT[u]])
                        else:
                            self.op("act", lambda e: e.activation(out=pT[u][:], in_=sT[u][:], func=AF.Exp), r=[b_sT[u]], w=[b_pT[u]])
                        for r in range(4):
                            self.op("pe", lambda e: e.matmul(self.psb[ob_][:, r * 128:(r + 1) * 128], lhsT=pT[u][:, r * 128:(r + 1) * 128], rhs=vrhs,
                                                             start=(ki == 0 and r == 0), stop=(ki == nk - 1)),
                                    r=[b_pT[u], kvb], w=[self.bps[ob_]])
                            self.op("pe", lambda e: e.matmul(self.psb[os_][:, oc_ + r:oc_ + r + 1], lhsT=pT[u][:, r * 128:(r + 1) * 128], rhs=self.ones_b[:, 0:1],
                                                             start=(ki == 0 and r == 0), stop=(ki == nk - 1)),
                                    r=[b_pT[u], self.b_const], w=[self.bps[os_]])
                        if br == 0 and dosel:
                            for r in range(4):
                                self.op("pe", lambda e: e.matmul(self.psb[5][:, r * 32:(r + 1) * 32], lhsT=pf[:, r * 128:(r + 1) * 128], rhs=mselp[:, :],
                                                                 start=(r == 0), stop=(r == 3)), r=[b_pf, b_sc], w=[self.bps[5]])
                        if ki == nk - 1:
                            self.op("dve", lambda e: e.tensor_scalar(out=rsc[:], in0=self.psb[os_][:, oc_:oc_ + 4], scalar1=1e-30, scalar2=None, op0=ALU.add),
                                    r=[self.bps[os_]], w=[b_rsc])
                            self.op("dve", lambda e: e.reciprocal(out=rsc[:], in_=rsc[:]), r=[b_rsc], w=[b_rsc])
                            gv = gts[:, i, :].rearrange("p (h b) -> p h b", b=3)[:, 4 * g:4 * g + 4, br]
                            wk_ = wsc[(3 * i + br) % 2]
                            bwk_ = b_wsc[(3 * i + br) % 2]
                            self.op("dve", lambda e: e.tensor_tensor(out=wk_[:], in0=rsc[:], in1=gv, op=ALU.mult), r=[b_rsc, b_gts], w=[bwk_])
                            for r in range(4):
                                self.op("act", lambda e: e.activation(out=osb3[qi][:, br, r, :], in_=self.psb[ob_][:, r * 128:(r + 1) * 128], func=AF.Copy,
                                                                      scale=wk_[:, r:r + 1]), r=[self.bps[ob_], bwk_], w=[b_osb3[qi]])
                            if br == 0 and dosel:
                                self.op("dve", lambda e: e.scalar_tensor_tensor(out=sco[:], in0=self.psb[5][:, 0:32], scalar=rsc[:, 0:1], in1=amp[:, i - 8, :],
                                                                                op0=ALU.mult, op1=ALU.add), r=[self.bps[5], b_rsc, b_sc], w=[b_sco])
                                for r in range(1, 4):
                                    self.op("dve", lambda e: e.scalar_tensor_tensor(out=sco[:], in0=self.psb[5][:, r * 32:(r + 1) * 32], scalar=rsc[:, r:r + 1], in1=sco[:],
                                                                                    op0=ALU.mult, op1=ALU.add), r=[self.bps[5], b_rsc, b_sco], w=[b_sco])
                                osc = lambda f: self.op("dve", f, r=[b_sco], w=[b_sco])
                                osc(lambda e: e.max(out=mx1[:], in_=sco[:]))
                                osc(lambda e: e.match_replace(out=sco2[:], in_to_replace=mx1[:], in_values=sco[:], imm_value=-3.0e38))
                                osc(lambda e: e.max(out=mx2[:], in_=sco2[:]))
                                osc(lambda e: e.tensor_scalar(out=ngm[:], in0=sco[:], scalar1=mx2[:, 7:8], scalar2=None, op0=ALU.is_ge))
                                osc(lambda e: e.tensor_scalar(out=ngm[:], in0=ngm[:], scalar1=30000.0, scalar2=-30000.0, op0=ALU.mult, op1=ALU.add))
                                self.op("pe", lambda e: e.transpose(out=self.psb[5][0:32, 256:384], in_=ngm[:], identity=self.ident[:]),
                                        r=[b_sco, self.b_const], w=[self.bps[5]])
                                self.op("act", lambda e: e.copy(out=ngT[:], in_=self.psb[5][0:32, 256:384]), r=[self.bps[5]], w=[b_ngT])
                            if br == 1:
                                self.op("pool", lambda e: e.tensor_tensor(out=osb[:], in0=osb3[qi][:, 0], in1=osb3[qi][:, 2], op=ALU.add),
                                        r=[b_osb3[qi]], w=[b_osb])
                                self.op("dve", lambda e: e.tensor_tensor(out=osb[:], in0=osb[:], in1=osb3[qi][:, 1], op=ALU.add),
                                        r=[b_osb3[qi], b_osb], w=[b_osb])
                                for r in range(4):
                                    self.op("pe", lambda e: e.transpose(out=self.psb[4][:, r * 128:(r + 1) * 128], in_=osb[:, r, :], identity=self.ident[:]),
                                            r=[b_osb, self.b_const], w=[self.bps[4]])
                                self.op("act", lambda e: e.copy(out=otb[qi][:].rearrange("p r q -> p (r q)"), in_=self.psb[4][:, :]), r=[self.bps[4]], w=[b_otb[qi]])
                                self.dma(lambda e: e.dma_start(out=self.OT[4 * g:4 * g + 4, :, i * 128:(i + 1) * 128].rearrange("h p t -> p h t"), in_=otb[qi][:]),
                                         r=[b_otb[qi]], w=[self.b_nsa_dr], q="act")

                    emit_qk(0)
                    emit_qk(1)
                    for n in range(len(units)):
                        if n + 2 < len(units):
                            emit_qk(n + 2)
                        emit_rest(n)
                self.tk.barrier()
            self.nsa_sample(j, st, compress, b_cw)
            self.tk.barrier()
        with contextlib.ExitStack() as st:
            self.alloc_rowlocal(st)
            oin = self.sb("oin_", [128, 16, 512], BF16, st); b_oin = Buf("oin")
            wo = self.nsa_w_o[j].rearrange("(kc p) (f h n) -> p kc f h n", p=128, h=2, n=128)
            OTv = self.OT.rearrange("h p t -> p h t")
            for ti in range(len(self.tiles)):
                t0, W = self.tiles[ti]
                self.load_xr(ti)
                self.dma(lambda e: e.dma_start(out=oin[:, :, :W], in_=OTv[:, :, t0:t0 + W]), r=[self.b_nsa_dr], w=[b_oin], q="act")

                def epo(p, psA, bA, psB, bB):
                    self.op("act", lambda e: e.copy(out=self.yy[:, 2 * p, :W], in_=psA[:, :W]), r=[bA], w=[self.b_yy])
                    self.op("dve", lambda e: e.tensor_copy(out=self.yy[:, 2 * p + 1, :W], in_=psB[:, :W]), r=[bB], w=[self.b_yy])
                self.gemm(lambda p, k0, kn, h: wo[:, k0:k0 + kn, p, h, :], DC, 8, oin, b_oin, W, epo, wkey='wo%d' % j, first=(ti == 0))
                self.post_norm_residual(ti, li * 4 + 1)
            self.tk.barrier()

    def nsa_sample(self, j, st, compress, b_cw):
        T = self.T
        with contextlib.ExitStack() as s2:
            S = lambda n, shp, dt=F32: self.sb(n, shp, dt, s2)
            b_c = Buf("sconst")
            pti = S("pti", [128, 128], I32); ptf = S("ptf", [128, 128]); idxf = S("idxf", [128, 128]); idxa = S("idxa", [128, 128], I32)
            iop = S("iop", [128, 1]); ior = S("ior", [128, 1]); iopg = S("iopg", [128, 128]); gcol = S("gcol", [128, 32])
            selh = S("selh", [4, 2, 128]); eye4 = S("eye4", [4, 4]); msels = S("msels", [128, 8, 257]); ams = S("ams", [4, 257])
            L = lambda dst, src: self.dma(lambda e: e.dma_start(out=dst, in_=src), w=[b_c])
            L(pti[:], self.page_table.partition_broadcast(128))
            L(iop[:], self.c_iota_p[:, :]); L(ior[:], self.c_iota_row[:, :]); L(iopg[:], self.c_iota_pg[:, :]); L(gcol[:], self.c_gcol[:, :])
            L(selh[:], self.c_selhalf.rearrange("s g p -> g s p")); L(eye4[:], self.c_eye4[:, :])
            L(msels[:], self.c_msel_s.rearrange("(t n) j -> n t j", n=128)); L(ams[:], self.c_addmask_s[:, :])
            oc = lambda q, f: self.op(q, f, r=[b_c], w=[b_c])
            oc("dve", lambda e: e.tensor_copy(out=ptf[:], in_=pti[:]))
            oc("dve", lambda e: e.tensor_scalar(out=idxf[:], in0=ptf[:], scalar1=128.0, scalar2=iop[:, 0:1], op0=ALU.mult, op1=ALU.add))
            oc("dve", lambda e: e.tensor_copy(out=idxa[:], in_=idxf[:]))
            qsT = S("qsT", [128, 16], BF16); gsm = S("gsm", [4, 4, 3]); b_q = Buf("qs_s")
            self.dma(lambda e: e.dma_start(out=qsT[:].unsqueeze(2), in_=self.QT.rearrange("h p t -> p h t")[:, :, T:T + 1],
                                           allow_slow_non_contiguous=True), r=[self.b_nsa_dr], w=[b_q])
            self.dma(lambda e: e.dma_start(out=gsm[:], in_=self.GT[T:T + 1, :].rearrange("o (g r b) -> (o r) g b", g=4, r=4),
                                           allow_slow_non_contiguous=True), r=[self.b_nsa_dr], w=[b_q])
            XT = S("XT", [128, 8, 2064], BF16); b_XT = Buf("XT")
            pg = [S("pg%d" % i, [128, 1024]) for i in range(4)]; b_pg = [Buf("pg") for _ in range(4)]
            kcs = S("kcs", [128, 4, 1024], BF16); vcs = S("vcs", [128, 8, 4, 128], BF16); b_kcs = Buf("kcs")
            self.op("pool", lambda e: e.memset(XT[:, :, 0:16], 0.0), w=[b_XT])
            cmp_rows = self.cache_cmp[j]
            for G8 in range(8):
                if G8 > 0:
                    self.op("pool", lambda e: e.tensor_copy(out=XT[:, :, 0:16], in_=XT[:, :, 2048:2064]), r=[b_XT], w=[b_XT])
                for p16 in range(16):
                    page = G8 * 16 + p16
                    k = page % 4
                    self.tk.dma("pool", lambda e: e.indirect_dma_start(out=pg[k][:, :], out_offset=None, in_=cmp_rows[:, :],
                                                                       in_offset=bass.IndirectOffsetOnAxis(ap=idxa[:, page:page + 1], axis=0)),
                                [b_c], [b_pg[k]])
                    for half in range(2):
                        pb = 4 + half
                        for q4 in range(4):
                            gc = half * 4 + q4
                            self.op("pe", lambda e: e.transpose(out=self.psb[pb][:, q4 * 128:(q4 + 1) * 128], in_=pg[k][:, gc * 128:(gc + 1) * 128],
                                                                identity=self.ident[:]), r=[b_pg[k], self.b_const], w=[self.bps[pb]])
                        dst = XT[:, half * 4:(half + 1) * 4, 16 + p16 * 128:16 + (p16 + 1) * 128]
                        src = self.psb[pb][:, :].rearrange("p (a n) -> p a n", a=4)
                        if half == 0:
                            self.op("act", lambda e: e.copy(out=dst, in_=src), r=[self.bps[pb]], w=[b_XT])
                        else:
                            self.op("dve", lambda e: e.tensor_copy(out=dst, in_=src), r=[self.bps[pb]], w=[b_XT])
                for g in range(4):
                    compress(lambda c, s: XT[:, g * 2 + c, s:s + 16 * 127 + 1:16], b_XT, 128,
                             kcs[:, g, G8 * 128:(G8 + 1) * 128], vcs[:, G8, g, :], b_kcs, 2 * g)
            sTs = S("sTs", [128, 144]); pfs = S("pfs", [128, 144]); pns = S("pns", [128, 144]); pbs = S("pbs", [128, 144], BF16)
            tot = S("tot", [128, 16]); b_a = Buf("satt")
            imp = S("imp", [128, 8, 4])

            def softmax_cols(ncol, nt, bias_ap, psbank):
                pass

            for t in range(8):
                for g in range(4):
                    self.op("pe", lambda e: e.matmul(self.psb[0][:, t * 16 + 4 * g:t * 16 + 4 * g + 4], lhsT=kcs[:, g, t * 128:(t + 1) * 128],
                                                     rhs=qsT[:, 4 * g:4 * g + 4], start=True, stop=True), r=[b_kcs, b_q], w=[self.bps[0]])
            oa = lambda q, f, extra=(): self.op(q, f, r=[b_a] + list(extra), w=[b_a])
            oa("dve", lambda e: e.tensor_tensor(out=sTs[:, 0:128], in0=self.psb[0][:, 0:128], in1=self.BS[:, 0:8, :].rearrange("p t h -> p (t h)"),
                                                op=ALU.add), [self.bps[0], self.b_BS])
            oa("act", lambda e: e.activation(out=pfs[:, 0:128], in_=sTs[:, 0:128], func=AF.Exp))
            self.op("pe", lambda e: e.matmul(self.psb[4][:, 0:128], lhsT=self.ones_f[:], rhs=pfs[:, 0:128], start=True, stop=True),
                    r=[b_a, self.b_const], w=[self.bps[4]])
            oa("dve", lambda e: e.tensor_reduce(out=tot[:], in_=self.psb[4][:, 0:128].rearrange("p (t h) -> p h t", h=16), axis=AX.X, op=ALU.add),
               [self.bps[4]])
            oa("dve", lambda e: e.reciprocal(out=tot[:], in_=tot[:]))
            oa("dve", lambda e: e.tensor_tensor(out=pns[:, 0:128].rearrange("p (t h) -> p t h", h=16), in0=pfs[:, 0:128].rearrange("p (t h) -> p t h", h=16),
                                                in1=tot[:].unsqueeze(1).to_broadcast([128, 8, 16]), op=ALU.mult))
            oa("act", lambda e: e.copy(out=pbs[:, 0:128], in_=pns[:, 0:128]))
            oa("dve", lambda e: e.tensor_reduce(out=imp[:], in_=pns[:, 0:128].rearrange("p (t g r) -> p t g r", g=4, r=4), axis=AX.X, op=ALU.add))
            first = True
            for g in range(4):
                for t in range(8):
                    self.op("pe", lambda e: e.matmul(self.psb[2][0:4, g * 128:(g + 1) * 128], lhsT=pbs[:, t * 16 + 4 * g:t * 16 + 4 * g + 4],
                                                     rhs=vcs[:, t, g, :], start=first, stop=(t == 7)), r=[b_a, b_kcs], w=[self.bps[2]])
                    first = False
            osm = S("osm", [4, 4, 128]); otmp = S("otmp", [4, 4, 128]); b_os = Buf("osm")
            self.op("dve", lambda e: e.tensor_tensor(out=osm[:], in0=self.psb[2][0:4, :].rearrange("p (g d) -> p g d", g=4),
                                                     in1=gsm[:, :, 0:1].to_broadcast([4, 4, 128]), op=ALU.mult), r=[self.bps[2], b_q], w=[b_os])
            for t in range(8):
                self.op("pe", lambda e: e.matmul(self.psb[5][0:4, 0:257], lhsT=imp[:, t, :], rhs=msels[:, t, :], start=(t == 0), stop=(t == 7)),
                        r=[b_a, b_c], w=[self.bps[5]])
            sco = S("ssco", [4, 257]); sco2 = S("ssco2", [4, 257]); mx1 = S("smx1", [4, 8]); mx2 = S("smx2", [4, 8])
            ixu = S("ixu", [4, 16], U32); ixf = S("ixf", [4, 16]); bm = S("bm", [4, 2, 4, 8]); b_s = Buf("ssel")
            osl = lambda q, f, extra=(): self.op(q, f, r=[b_s] + list(extra), w=[b_s])
            osl("dve", lambda e: e.tensor_tensor(out=sco[:], in0=self.psb[5][0:4, 0:257], in1=ams[:], op=ALU.add), [self.bps[5], b_c])
            osl("dve", lambda e: e.max(out=mx1[:], in_=sco[:]))
            osl("dve", lambda e: e.match_replace(out=sco2[:], in_to_replace=mx1[:], in_values=sco[:], imm_value=-3.0e38))
            osl("dve", lambda e: e.max(out=mx2[:], in_=sco2[:]))
            osl("dve", lambda e: e.max_index(out=ixu[:, 0:8], in_max=mx1[:], in_values=sco[:]))
            osl("dve", lambda e: e.max_index(out=ixu[:, 8:16], in_max=mx2[:], in_values=sco2[:]))
            osl("dve", lambda e: e.tensor_copy(out=ixf[:], in_=ixu[:]))
            for s2i in range(2):
                osl("dve", lambda e: e.tensor_tensor(out=bm[:, s2i], in0=ixf[:, s2i:16:2].unsqueeze(1).to_broadcast([4, 4, 8]),
                                                     in1=eye4[:, :].unsqueeze(2).to_broadcast([4, 4, 8]), op=ALU.mult), [b_c])
                self.op("pe", lambda e: e.matmul(self.psb[5][:, 300:332], lhsT=selh[:, s2i, :], rhs=bm[:, s2i].rearrange("p g s -> p (g s)"),
                                                 start=(s2i == 0), stop=(s2i == 1)), r=[b_s, b_c], w=[self.bps[5]])
            jvf = S("jvf", [128, 32]); jvi = S("jvi", [128, 32], I32); jhi = S("jhi", [128, 32], I32); jpi = S("jpi", [128, 32], I32)
            jhf = S("jhf", [128, 32]); jpf = S("jpf", [128, 32]); ptsel = S("ptsel", [128, 32]); rowf = S("rowf", [128, 32]); rowi = S("rowi", [128, 32], I32)
            i254 = S("i254", [128, 32]); i255 = S("i255", [128, 32]); i256 = S("i256", [128, 32])
            osl("dve", lambda e: e.tensor_copy(out=jvf[:], in_=self.psb[5][:, 300:332]), [self.bps[5]])
            osl("dve", lambda e: e.tensor_copy(out=jvi[:], in_=jvf[:]))
            osl("dve", lambda e: e.tensor_single_scalar(out=jhi[:], in_=jvi[:], scalar=1, op=ALU.logical_shift_right))
            osl("dve", lambda e: e.tensor_single_scalar(out=jpi[:], in_=jvi[:], scalar=1, op=ALU.bitwise_and))
            osl("dve", lambda e: e.tensor_copy(out=jhf[:], in_=jhi[:]))
            osl("dve", lambda e: e.tensor_copy(out=jpf[:], in_=jpi[:]))
            with contextlib.ExitStack() as s3:
                ohp = self.sb("ohp", [128, 32, 128], F32, s3)
                osl("dve", lambda e: e.tensor_tensor(out=ohp[:], in0=jhf[:].unsqueeze(2).to_broadcast([128, 32, 128]),
                                                     in1=iopg[:].unsqueeze(1).to_broadcast([128, 32, 128]), op=ALU.is_equal), [b_c])
                osl("dve", lambda e: e.tensor_tensor(out=ohp[:], in0=ohp[:], in1=ptf[:].unsqueeze(1).to_broadcast([128, 32, 128]), op=ALU.mult), [b_c])
                osl("dve", lambda e: e.tensor_reduce(out=ptsel[:], in_=ohp[:], axis=AX.X, op=ALU.add))
                self.tk.barrier()
            osl("dve", lambda e: e.tensor_scalar(out=rowf[:], in0=ptsel[:], scalar1=128.0, scalar2=ior[:, 0:1], op0=ALU.mult, op1=ALU.add), [b_c])
            osl("dve", lambda e: e.scalar_tensor_tensor(out=rowf[:], in0=jpf[:], scalar=64.0, in1=rowf[:], op0=ALU.mult, op1=ALU.add))
            osl("dve", lambda e: e.scalar_tensor_tensor(out=rowf[:], in0=rowf[:], scalar=4.0, in1=gcol[:], op0=ALU.mult, op1=ALU.add), [b_c])
            osl("dve", lambda e: e.tensor_copy(out=rowi[:], in_=rowf[:]))
            for tile_, val in ((i254, 254.0), (i255, 255.0), (i256, 256.0)):
                osl("dve", lambda e: e.tensor_scalar(out=tile_[:], in0=jvf[:], scalar1=val, scalar2=None, op0=ALU.is_equal))
            bsl = S("bsl", [128, 4, 9, 4]); dv = S("dv", [128, 2, 16]); tb1 = S("tb1", [128, 8, 4])
            osl("dve", lambda e: e.tensor_tensor(out=dv[:, 0, :], in0=self.BS[:, 13, :], in1=self.BS[:, 15, :], op=ALU.subtract), [self.b_BS])
            osl("dve", lambda e: e.tensor_tensor(out=dv[:, 1, :], in0=self.BS[:, 14, :], in1=self.BS[:, 15, :], op=ALU.subtract), [self.b_BS])
            for g in range(4):
                tgt = bsl[:, g, 0:8, :]
                osl("dve", lambda e: e.tensor_copy(out=tgt, in_=self.BS[:, 15, 4 * g:4 * g + 4].unsqueeze(1).to_broadcast([128, 8, 4])), [self.b_BS])
                for k, ind in enumerate((i254, i255)):
                    osl("dve", lambda e: e.tensor_tensor(out=tb1[:], in0=ind[:, g * 8:(g + 1) * 8].unsqueeze(2).to_broadcast([128, 8, 4]),
                                                         in1=dv[:, k, 4 * g:4 * g + 4].unsqueeze(1).to_broadcast([128, 8, 4]), op=ALU.mult))
                    osl("dve", lambda e: e.tensor_tensor(out=tgt, in0=tgt, in1=tb1[:], op=ALU.add))
                osl("dve", lambda e: e.scalar_tensor_tensor(out=tgt, in0=i256[:, g * 8:(g + 1) * 8].unsqueeze(2).to_broadcast([128, 8, 4]), scalar=-30000.0,
                                                            in1=tgt, op0=ALU.mult, op1=ALU.add))
                osl("dve", lambda e: e.tensor_copy(out=bsl[:, g, 8, :], in_=self.BS[:, 16, 4 * g:4 * g + 4]), [self.b_BS])
            ksel = [S("ksel%d" % i, [128, 256]) for i in range(4)]; b_ks = [Buf("ksel") for _ in range(4)]
            KsT = S("KsT", [128, 4, 9, 128], BF16); Vs = S("Vs", [128, 4, 9, 128], BF16); b_KV = Buf("KVs")
            self.op("pool", lambda e: e.memset(KsT[:, :, 8, :], 0.0), w=[b_KV])
            self.op("pool", lambda e: e.memset(Vs[:, :, 8, :], 0.0), w=[b_KV])
            slc_rows = self.cache_slc[j]
            for g in range(4):
                self.dma(lambda e: e.dma_start(out=KsT[:, g, 8, 0:1], in_=self.KVT[1, g, 0, :, T:T + 1], allow_slow_non_contiguous=True),
                         r=[self.b_nsa_dr], w=[b_KV])
                self.dma(lambda e: e.dma_start(out=Vs[0:1, g, 8, :], in_=self.VTOK[1, g, T:T + 1, :]), r=[self.b_nsa_dr], w=[b_KV])
                for sp in range(8):
                    col = g * 8 + sp
                    k = col % 4
                    self.tk.dma("pool", lambda e: e.indirect_dma_start(out=ksel[k][:, :], out_offset=None, in_=slc_rows[:, :],
                                                                       in_offset=bass.IndirectOffsetOnAxis(ap=rowi[:, col:col + 1], axis=0)),
                                [b_s], [b_ks[k]])
                    pb = 4 + (col % 2)
                    self.op("pe", lambda e: e.transpose(out=self.psb[pb][:, 0:128], in_=ksel[k][:, 0:128], identity=self.ident[:]),
                            r=[b_ks[k], self.b_const], w=[self.bps[pb]])
                    self.op("act", lambda e: e.copy(out=KsT[:, g, sp, :], in_=self.psb[pb][:, 0:128]), r=[self.bps[pb]], w=[b_KV])
                    self.op("dve", lambda e: e.tensor_copy(out=Vs[:, g, sp, :], in_=ksel[k][:, 128:256]), r=[b_ks[k]], w=[b_KV])

            def small_attn(KT_, V_, bKV, nt, bias_ap, brx):
                ncol = 4 * nt * 4
                for g in range(4):
                    for t in range(nt):
                        c0 = (g * nt + t) * 4
                        self.op("pe", lambda e: e.matmul(self.psb[1][:, c0:c0 + 4], lhsT=KT_[:, g, t, :], rhs=qsT[:, 4 * g:4 * g + 4], start=True, stop=True),
                                r=[bKV, b_q], w=[self.bps[1]])
                oa("dve", lambda e: e.tensor_tensor(out=sTs[:, 0:ncol], in0=self.psb[1][:, 0:ncol], in1=bias_ap, op=ALU.add), [self.bps[1], b_s, self.b_BS])
                oa("act", lambda e: e.activation(out=pfs[:, 0:ncol], in_=sTs[:, 0:ncol], func=AF.Exp))
                self.op("pe", lambda e: e.matmul(self.psb[4][:, 0:ncol], lhsT=self.ones_f[:], rhs=pfs[:, 0:ncol], start=True, stop=True),
                        r=[b_a, self.b_const], w=[self.bps[4]])
                oa("dve", lambda e: e.tensor_reduce(out=tot[:].rearrange("p (g r) -> p g r", g=4),
                                                    in_=self.psb[4][:, 0:ncol].rearrange("p (g t r) -> p g r t", g=4, r=4), axis=AX.X, op=ALU.add), [self.bps[4]])
                oa("dve", lambda e: e.reciprocal(out=tot[:], in_=tot[:]))
                oa("dve", lambda e: e.tensor_tensor(out=pns[:, 0:ncol].rearrange("p (g t r) -> p g t r", g=4, r=4),
                                                    in0=pfs[:, 0:ncol].rearrange("p (g t r) -> p g t r", g=4, r=4),
                                                    in1=tot[:].rearrange("p (g r) -> p g r", g=4).unsqueeze(2).to_broadcast([128, 4, nt, 4]), op=ALU.mult))
                oa("act", lambda e: e.copy(out=pbs[:, 0:ncol], in_=pns[:, 0:ncol]))
                first = True
                for g in range(4):
                    for t in range(nt):
                        c0 = (g * nt + t) * 4
                        self.op("pe", lambda e: e.matmul(self.psb[3][0:4, g * 128:(g + 1) * 128], lhsT=pbs[:, c0:c0 + 4], rhs=V_[:, g, t, :],
                                                         start=first, stop=(t == nt - 1)), r=[b_a, bKV], w=[self.bps[3]])
                        first = False
                self.op("dve", lambda e: e.tensor_tensor(out=otmp[:], in0=self.psb[3][0:4, :].rearrange("p (g d) -> p g d", g=4),
                                                         in1=gsm[:, :, brx:brx + 1].to_broadcast([4, 4, 128]), op=ALU.mult), r=[self.bps[3], b_q, b_os], w=[b_os])
                self.op("dve", lambda e: e.tensor_tensor(out=osm[:], in0=osm[:], in1=otmp[:], op=ALU.add), r=[b_os], w=[b_os])

            small_attn(KsT, Vs, b_KV, 9, bsl[:].rearrange("p g t r -> p (g t r)"), 1)
            wld = [S("wld%d" % i, [128, 1024]) for i in range(2)]; b_wl = [Buf("wld") for _ in range(2)]
            KwT = S("KwT", [128, 4, 5, 128], BF16); Vw = S("Vw", [128, 4, 5, 128], BF16); b_KW = Buf("KVw")
            bwl = S("bwl", [128, 4, 5, 4])
            self.op("pool", lambda e: e.memset(KwT[:, :, 4, :], 0.0), w=[b_KW])
            self.op("pool", lambda e: e.memset(Vw[:, :, 4, :], 0.0), w=[b_KW])
            for g in range(4):
                self.dma(lambda e: e.dma_start(out=KwT[:, g, 4, 0:1], in_=self.KVT[2, g, 0, :, T:T + 1], allow_slow_non_contiguous=True),
                         r=[self.b_nsa_dr], w=[b_KW])
                self.dma(lambda e: e.dma_start(out=Vw[0:1, g, 4, :], in_=self.VTOK[2, g, T:T + 1, :]), r=[self.b_nsa_dr], w=[b_KW])
                osl("dve", lambda e: e.tensor_copy(out=bwl[:, g], in_=self.BS[:, 8:13, 4 * g:4 * g + 4]), [self.b_BS])
            for t in range(4):
                k = t % 2
                self.dma(lambda e: e.dma_start(out=wld[k][:], in_=self.cache_win[j, t * 128:(t + 1) * 128, :]), w=[b_wl[k]])
                for g in range(4):
                    pb = 4 + (g % 2)
                    self.op("pe", lambda e: e.transpose(out=self.psb[pb][:, 0:128], in_=wld[k][:, g * 256:g * 256 + 128], identity=self.ident[:]),
                            r=[b_wl[k], self.b_const], w=[self.bps[pb]])
                    self.op("act", lambda e: e.copy(out=KwT[:, g, t, :], in_=self.psb[pb][:, 0:128]), r=[self.bps[pb]], w=[b_KW])
                    self.op("dve", lambda e: e.tensor_copy(out=Vw[:, g, t, :], in_=wld[k][:, g * 256 + 128:(g + 1) * 256]), r=[b_wl[k]], w=[b_KW])
            small_attn(KwT, Vw, b_KW, 5, bwl[:].rearrange("p g t r -> p (g t r)"), 2)
            ots = S("ots", [128, 16, SW], BF16); b_ot = Buf("ots")
            self.op("pool", lambda e: e.memset(ots[:], 0.0), w=[b_ot])
            for g in range(4):
                self.op("pe", lambda e: e.transpose(out=self.psb[4][:, 4 * g:4 * g + 4], in_=osm[:, g, :], identity=self.ident[0:4, 0:4]),
                        r=[b_os, self.b_const], w=[self.bps[4]])
            self.op("dve", lambda e: e.tensor_copy(out=ots[:, :, 0:1], in_=self.psb[4][:, 0:16].unsqueeze(2)), r=[self.bps[4]], w=[b_ot])
            self.dma(lambda e: e.dma_start(out=self.OT.rearrange("h p t -> p h t")[:, :, T:T + SW], in_=ots[:]), r=[b_ot], w=[self.b_nsa_dr])
            self.tk.barrier()

    def build(self):
        self.declare_io()
        self.setup_consts()
        self.eps_rms = self.sb("eps_rms", [128, 1], F32)
        self.op("dve", lambda e: e.memset(self.eps_rms[:], RMS_EPS), w=[self.b_const])
        self.phase_input()
        for li in self.layers:
            if self.do_mixer:
                m = li % 3
                if m == 1:
                    self.conf_layer(li)
                elif m == 2:
                    self.s5_layer(li)
                else:
                    self.nsa_layer(li)
            if self.do_ffn:
                self.ffn_layer(li)
        self.phase_output()
        self.tk.barrier()
        self.es.close()
        return self.nc


_NC_CACHE = {}


def kernel(x_prompt, x_sample, cache_cmp_kv, cache_slc_kv, cache_win_kv, state_conv, state_ssm_re,
           state_ssm_im, state_ffn_conv, page_table, norm_gain, rel_bias, nsa_w_q, nsa_w_kv, nsa_cmp_pe,
           nsa_cmp_w1, nsa_cmp_w2, nsa_w_gate, nsa_w_o, conv_w_pw1, conv_dw, conv_dw_b, conv_ln_g, conv_ln_b,
           conv_w_pw2, ssm_a_re, ssm_a_im, ssm_log_dt, ssm_b_re, ssm_b_im, ssm_c_re, ssm_c_im, ssm_d,
           ssm_w_glu, ffn_w_up, ffn_dw, ffn_dw_b, ffn_w_down):
    A = lambda a: np.ascontiguousarray(np.asarray(a))
    x_prompt = A(x_prompt)
    B, T, _ = x_prompt.shape
    NS = x_sample.shape[0]
    nphys = cache_cmp_kv.shape[1]
    n_cores = 8
    key = (T, nphys)
    if key not in _NC_CACHE:
        _NC_CACHE[key] = Builder(T=T, nphys=nphys).build()
    nc = _NC_CACHE[key]
    f = np.float32
    shared = dict(
        norm_gain=A(norm_gain).reshape(16, D), rel_bias=A(rel_bias),
        nsa_w_q=A(nsa_w_q), nsa_w_kv=A(nsa_w_kv), nsa_cmp_pe=A(nsa_cmp_pe), nsa_cmp_w1=A(nsa_cmp_w1),
        nsa_cmp_w2=A(nsa_cmp_w2), nsa_w_gate=A(nsa_w_gate), nsa_w_o=A(nsa_w_o),
        conv_w_pw1=A(conv_w_pw1)[0], conv_dw=A(conv_dw)[0], conv_dw_b=A(conv_dw_b).reshape(1, D),
        conv_ln_g=A(conv_ln_g).reshape(1, D), conv_ln_b=A(conv_ln_b).reshape(1, D), conv_w_pw2=A(conv_w_pw2)[0],
        ssm_a_re=A(ssm_a_re)[0], ssm_a_im=A(ssm_a_im)[0], ssm_log_dt=A(ssm_log_dt).reshape(1, 128),
        ssm_b_re=A(ssm_b_re)[0], ssm_b_im=A(ssm_b_im)[0], ssm_c_re=A(ssm_c_re)[0], ssm_c_im=A(ssm_c_im)[0],
        ssm_d=A(ssm_d).reshape(1, D), ssm_w_glu=A(ssm_w_glu)[0],
        ffn_w_up=A(ffn_w_up), ffn_dw=A(ffn_dw).reshape(12, DFF), ffn_dw_b=A(ffn_dw_b), ffn_w_down=A(ffn_w_down),
        cache_cmp0=A(cache_cmp_kv)[0].reshape(nphys * 128, 1024), cache_cmp1=A(cache_cmp_kv)[1].reshape(nphys * 128, 1024),
        cache_slc0=A(cache_slc_kv)[0].reshape(nphys * 128 * 4, 256), cache_slc1=A(cache_slc_kv)[1].reshape(nphys * 128 * 4, 256),
    )
    shared.update(structural_consts())
    x_sample = A(x_sample); cache_win_kv = A(cache_win_kv); state_conv = A(state_conv)
    state_ssm_re = A(state_ssm_re); state_ssm_im = A(state_ssm_im); state_ffn_conv = A(state_ffn_conv)
    page_table = A(page_table).astype(np.int32)
    in_maps = []
    for c in range(n_cores):
        b = c % B
        s = c % NS
        m = dict(shared)
        m.update(dict(
            x_p=x_prompt[b], x_s=x_sample[s].reshape(1, D),
            cache_win=A(cache_win_kv[:, s]).reshape(2, 512, 1024),
            state_conv=A(state_conv[0, s]), state_ssm_re=A(state_ssm_re[0, s]), state_ssm_im=A(state_ssm_im[0, s]),
            state_ffn=A(state_ffn_conv[:, s]), page_table=page_table[s:s + 1],
        ))
        in_maps.append(m)
    res = run_bass_kernel_spmd(nc, in_maps, core_ids=list(range(n_cores)))
    R = res.results
    P = lambda name: np.stack([R[b][name] for b in range(B)], 0)
    Sm = lambda name: np.stack([R[s][name] for s in range(NS)], 0)
    WP = min(512, T)
    y_p = P("y_p")
    y_s = Sm("y_s")
    kvp = lambda n, rows: np.moveaxis(P(n), 0, 1).reshape(2, B, rows, 4, 2, 128)
    kvs = lambda n, rows: np.moveaxis(Sm(n), 0, 1).reshape(2, NS, rows, 4, 2, 128)
    outs = (
        y_p, y_s,
        kvp("cmp_kv_p", T), kvs("cmp_kv_s", 1), kvp("slc_kv_p", T), kvs("slc_kv_s", 1),
        kvp("win_kv_p", WP), kvs("win_kv_s", 512),
        P("conv_p")[None], Sm("conv_s")[None],
        P("ssm_re_p")[None], Sm("ssm_re_s")[None], P("ssm_im_p")[None], Sm("ssm_im_s")[None],
        np.moveaxis(P("ffn_p"), 0, 1), np.moveaxis(Sm("ffn_s"), 0, 1),
    )
    return tuple(np.ascontiguousarray(o.astype(np.float32)) for o in outs)
```
